# Optimizing a Trainium2 kernel written in Bass

```python
import math
import jax, jax.numpy as jnp
from jax import lax
import numpy as np

D_MODEL = 1024
BATCH = 8
SEQ = 4096
DEPTH = 4

HEAD_DIM = 64
SC_WIDTH = 512
SC_TAPS = 3
DA_HEADS = 8
DA_WIDTH = DA_HEADS * HEAD_DIM
DA_PAIRS = ((128, 1), (512, 4), (2048, 16))
CC_WIDTH = 512
CC_TAPS = 31
SB_HEADS = 8
SB_WIDTH = SB_HEADS * HEAD_DIM
BLOCK_Q = 128
D_FF = 2816
FFN_TAPS = 3
REL_BUCKETS = 32
REL_MAX_DIST = 2048
EPS = 1e-6

N_EVEN = (DEPTH + 1) // 2
N_ODD = DEPTH // 2
EVEN_IN = 3 * SC_WIDTH + 3 * DA_WIDTH
EVEN_MIX = SC_WIDTH + DA_WIDTH
ODD_IN = 2 * CC_WIDTH + 3 * SB_WIDTH
ODD_MIX = CC_WIDTH + SB_WIDTH

kernel_name = "hybrid_shortconv_dilated_conformer_stickbreak"


def rms_norm(x, g):
    xf = x.astype(jnp.float32)
    y = xf * lax.rsqrt(jnp.mean(xf * xf, axis=-1, keepdims=True) + EPS)
    return (y * g.astype(jnp.float32)).astype(x.dtype)


def layer_norm(x, g, b):
    xf = x.astype(jnp.float32)
    mu = jnp.mean(xf, axis=-1, keepdims=True)
    var = jnp.mean(jnp.square(xf - mu), axis=-1, keepdims=True)
    y = (xf - mu) * lax.rsqrt(var + EPS) * g.astype(jnp.float32) + b.astype(jnp.float32)
    return y.astype(x.dtype)


def causal_dwconv(x, w):
    taps, c = w.shape
    return lax.conv_general_dilated(
        x, w[:, None, :].astype(x.dtype), window_strides=(1,),
        padding=((taps - 1, 0),), dimension_numbers=("NWC", "WIO", "NWC"),
        feature_group_count=c)


def t5_bucket(dist):
    max_exact = REL_BUCKETS // 2
    d = jnp.maximum(dist, 1).astype(jnp.float32)
    large = max_exact + (jnp.log(d / max_exact) / math.log(REL_MAX_DIST / max_exact)
                         * (REL_BUCKETS - max_exact)).astype(jnp.int32)
    large = jnp.minimum(large, REL_BUCKETS - 1)
    return jnp.where(dist < max_exact, dist, large)


def dilated_branch(q, k, v, span, dilation, rel_bias):
    b, s, h, dh = q.shape
    L = s // dilation
    n = b * dilation

    def to_sub(t):
        return jnp.swapaxes(t.reshape(b, L, dilation, h, dh), 1, 2).reshape(n, L, h, dh)

    def from_sub(t):
        t = t.reshape((b, dilation, L) + t.shape[3:])
        return jnp.swapaxes(t, 1, 2).reshape((b, s) + t.shape[3:])

    qs, ks, vs = to_sub(q), to_sub(k), to_sub(v)
    bq = math.gcd(BLOCK_Q, L)
    nb = L // bq
    kp = jnp.pad(ks, ((0, 0), (span, 0), (0, 0), (0, 0)))
    vp = jnp.pad(vs, ((0, 0), (span, 0), (0, 0), (0, 0)))
    idx = jnp.arange(nb)[:, None] * bq + jnp.arange(bq + span)[None, :]
    kb = kp[:, idx]
    vb = vp[:, idx]
    qb = qs.reshape(n, nb, bq, h, dh)
    logits = jnp.einsum("nbqhd,nbkhd->nbhqk", qb, kb) / math.sqrt(dh)
    rel = jnp.arange(bq)[:, None] - jnp.arange(bq + span)[None, :] + span
    valid = ((rel >= 0) & (rel <= span))[None] & ((idx - span) >= 0)[:, None, :]
    bias = rel_bias.astype(jnp.float32)[t5_bucket(jnp.clip(rel, 0, span) * dilation)]
    logits = logits + jnp.transpose(bias, (2, 0, 1))[None, None]
    logits = jnp.where(valid[None, :, None], logits, -jnp.inf)
    m = jnp.max(logits, axis=-1)
    p = jnp.exp(logits - m[..., None])
    den = jnp.sum(p, axis=-1)
    u = jnp.einsum("nbhqk,nbkhd->nbqhd", p, vb)
    m = jnp.transpose(m, (0, 1, 3, 2))
    den = jnp.transpose(den, (0, 1, 3, 2))
    return from_sub(u), from_sub(m), from_sub(den)


def stick_breaking(q, k, v):
    b, s, h, dh = q.shape
    nb = s // BLOCK_Q
    qb = jnp.moveaxis(q.reshape(b, nb, BLOCK_Q, h, dh), 1, 0)
    kpos = jnp.arange(s)

    def one_block(args):
        qi, start = args
        z = jnp.einsum("bqhd,bkhd->bhqk", qi, k) / math.sqrt(dh)
        qpos = start + jnp.arange(BLOCK_Q)
        before = (kpos[None, :] < qpos[:, None])[None, None]
        log_beta = jax.nn.log_sigmoid(z)
        log_rest = jnp.where(before, log_beta - z, 0.0)
        later = lax.cumsum(log_rest, axis=3, reverse=True) - log_rest
        a = jnp.where(before, jnp.exp(log_beta + later), 0.0)
        return jnp.einsum("bhqk,bkhd->bqhd", a, v)

    o = lax.map(one_block, (qb, jnp.arange(nb) * BLOCK_Q))
    return jnp.moveaxis(o, 0, 1).reshape(b, s, h, dh)


def even_mixer(h, w_in, w_out, w_sc, rel_bias):
    b, s, _ = h.shape
    p = h @ w_in
    o1 = SC_WIDTH
    o3 = 3 * SC_WIDTH
    gate_b, gate_c, xa, q, k, v = jnp.split(
        p, [o1, 2 * o1, o3, o3 + DA_WIDTH, o3 + 2 * DA_WIDTH], axis=-1)
    y_a = gate_b * causal_dwconv(gate_c * xa, w_sc)
    heads = lambda t: t.reshape(b, s, DA_HEADS, HEAD_DIM).astype(jnp.float32)
    q, k, v = heads(q), heads(k), heads(v)
    outs = [dilated_branch(q, k, v, win // dil, dil, rel_bias) for win, dil in DA_PAIRS]
    u = jnp.stack([o[0] for o in outs])
    m = jnp.stack([o[1] for o in outs])
    den = jnp.stack([o[2] for o in outs])
    wgt = jnp.exp(m - jnp.max(m, axis=0, keepdims=True))
    o = jnp.einsum("gbsh,gbshd->bshd", wgt, u) / jnp.sum(wgt * den, axis=0)[..., None]
    y_b = o.reshape(b, s, DA_WIDTH).astype(h.dtype)
    return jnp.concatenate([y_a, y_b], axis=-1) @ w_out


def odd_mixer(h, w_in, w_out, w_cc, b_cc, ln_g, ln_b):
    b, s, _ = h.shape
    p = h @ w_in
    c2 = 2 * CC_WIDTH
    a, gate, q, k, v = jnp.split(
        p, [CC_WIDTH, c2, c2 + SB_WIDTH, c2 + 2 * SB_WIDTH], axis=-1)
    g = a * jax.nn.sigmoid(gate)
    g = causal_dwconv(g, w_cc) + b_cc.astype(g.dtype)
    y_c = jax.nn.silu(layer_norm(g, ln_g, ln_b))
    heads = lambda t: t.reshape(b, s, SB_HEADS, HEAD_DIM).astype(jnp.float32)
    y_d = stick_breaking(heads(q), heads(k), heads(v)).reshape(b, s, SB_WIDTH).astype(h.dtype)
    return jnp.concatenate([y_c.astype(h.dtype), y_d], axis=-1) @ w_out


def conv_ffn(h, w_up, w_conv, b_conv, w_down):
    g, u = jnp.split(h @ w_up, 2, axis=-1)
    g = causal_dwconv(g, w_conv) + b_conv.astype(g.dtype)
    return (jax.nn.silu(g) * u) @ w_down


def setup_inputs(seed: int = 0) -> dict:
    key = jax.random.key(seed)
    ks = jax.random.split(key, 20)
    nrm = lambda k, shape, scale: jax.random.normal(k, shape, jnp.float32) * scale
    return {
        "x": nrm(ks[0], (BATCH, SEQ, D_MODEL), 1.0),
        "norm_g": 1.0 + nrm(ks[1], (DEPTH, 4, D_MODEL), 0.05),
        "rel_bias": nrm(ks[2], (REL_BUCKETS, DA_HEADS), 0.5),
        "w_in_even": nrm(ks[3], (N_EVEN, D_MODEL, EVEN_IN), D_MODEL ** -0.5),
        "w_out_even": nrm(ks[4], (N_EVEN, EVEN_MIX, D_MODEL), EVEN_MIX ** -0.5),
        "w_sc": nrm(ks[5], (N_EVEN, SC_TAPS, SC_WIDTH), SC_TAPS ** -0.5),
        "w_in_odd": nrm(ks[6], (N_ODD, D_MODEL, ODD_IN), D_MODEL ** -0.5),
        "w_out_odd": nrm(ks[7], (N_ODD, ODD_MIX, D_MODEL), ODD_MIX ** -0.5),
        "w_cc": nrm(ks[8], (N_ODD, CC_TAPS, CC_WIDTH), CC_TAPS ** -0.5),
        "b_cc": nrm(ks[9], (N_ODD, CC_WIDTH), 0.02),
        "ln_cc_g": 1.0 + nrm(ks[10], (N_ODD, CC_WIDTH), 0.05),
        "ln_cc_b": nrm(ks[11], (N_ODD, CC_WIDTH), 0.02),
        "w_up": nrm(ks[12], (DEPTH, D_MODEL, 2 * D_FF), D_MODEL ** -0.5),
        "w_ffn_conv": nrm(ks[13], (DEPTH, FFN_TAPS, D_FF), FFN_TAPS ** -0.5),
        "b_ffn_conv": nrm(ks[14], (DEPTH, D_FF), 0.02),
        "w_down": nrm(ks[15], (DEPTH, D_FF, D_MODEL), D_FF ** -0.5),
    }


def reference(x, norm_g, rel_bias, w_in_even, w_out_even, w_sc, w_in_odd, w_out_odd,
              w_cc, b_cc, ln_cc_g, ln_cc_b, w_up, w_ffn_conv, b_ffn_conv, w_down):
    for layer in range(DEPTH):
        h = rms_norm(x, norm_g[layer, 0])
        if layer % 2 == 0:
            i = layer // 2
            mix = even_mixer(h, w_in_even[i], w_out_even[i], w_sc[i], rel_bias)
        else:
            i = layer // 2
            mix = odd_mixer(h, w_in_odd[i], w_out_odd[i], w_cc[i], b_cc[i], ln_cc_g[i], ln_cc_b[i])
        x = x + rms_norm(mix, norm_g[layer, 1])
        h = rms_norm(x, norm_g[layer, 2])
        x = x + rms_norm(conv_ffn(h, w_up[layer], w_ffn_conv[layer], b_ffn_conv[layer], w_down[layer]),
                         norm_g[layer, 3])
    return x
```

```python
import math
from contextlib import ExitStack

import numpy as np
import concourse.bass as bass
import concourse.mybir as mybir
from concourse.bass_utils import run_bass_kernel_spmd

F32 = mybir.dt.float32
BF16 = mybir.dt.bfloat16
AF = mybir.ActivationFunctionType
ALU = mybir.AluOpType

S = 4096
D = 1024
NT = 8
TT = 512
DFF = 2816
NFC = 22
EPS = 1e-6
DA_PAIRS = ((128, 1), (512, 4), (2048, 16))
NEG = -30000.0

GN0 = 0
WSC0 = 128
WCC0 = 152
BCC0 = 400
LNG0 = 408
LNB0 = 416
WFC0 = 424
BFC0 = 688
NV = 776
C_MASKD = 0
C_MASKS = 256
C_TRI = 2304
C_SU = 2432
C_ID = 2560
NC_ = 2688


def _is_ap(v):
    return hasattr(v, "tensor") and hasattr(v, "offset") and hasattr(v, "ap")


def _box(ap):
    t = ap.tensor
    shape = list(t.shape)
    row = 1
    for s in shape[1:]:
        row *= s
    off = ap.offset
    r0 = off // row
    c0 = off % row
    rext = 0
    cext = 0
    for step, cnt in ap.ap:
        if cnt <= 1:
            continue
        if step != 0 and step % row == 0:
            rext += (step // row) * (cnt - 1)
        else:
            cext += step * (cnt - 1)
    return (t.name, r0, r0 + rext + 1, c0, c0 + cext + 1)


def _ovl(a, b):
    return a[1] < b[2] and b[1] < a[2] and a[3] < b[4] and b[3] < a[4]


def _contains(a, b):
    return a[1] <= b[1] and b[2] <= a[2] and a[3] <= b[3] and b[4] <= a[4]


class Op:
    __slots__ = ("eng", "emit", "deps", "is_dma", "sem", "val", "signal")

    def __init__(self, eng, emit):
        self.eng = eng
        self.emit = emit
        self.deps = []
        self.is_dma = False
        self.sem = None
        self.val = None
        self.signal = False


ENGS = ("pe", "act", "dve", "pool", "sp")


class Prog:
    def __init__(self, nc, es, n_dsem=60):
        self.nc = nc
        self.esem = {e: es.enter_context(nc.semaphore("s_" + e)) for e in ENGS}
        self.ecnt = {e: 0 for e in ENGS}
        self.bar = es.enter_context(nc.semaphore("s_bar"))
        self.barcnt = 0
        self.dpool = [[es.enter_context(nc.semaphore("d%d" % i)), 0] for i in range(n_dsem)]
        self.n_ops = 0
        self.n_waits = 0
        self._reset()

    def _reset(self):
        self.ops = {e: [] for e in ENGS}
        self.hist = {}
        self.dma_last = {}
        self.dkey2idx = {}
        self.all_ops = []

    def _track(self, op, reads, writes):
        deps = []
        rb_ = [_box(a) for a in reads]
        wb_ = [_box(a) for a in writes]
        for b in rb_:
            h = self.hist.setdefault(b[0], {"w": [], "r": []})
            for wb, wop in h["w"]:
                if _ovl(wb, b):
                    deps.append(wop)
        for b in wb_:
            h = self.hist.setdefault(b[0], {"w": [], "r": []})
            for wb, wop in h["w"]:
                if _ovl(wb, b):
                    deps.append(wop)
            for rb, rop in h["r"]:
                if _ovl(rb, b):
                    deps.append(rop)
        for b in rb_:
            h = self.hist[b[0]]
            if len(h["r"]) > 64:
                h["r"] = [(rb, rop) for rb, rop in h["r"] if not (rop.eng == op.eng and not rop.is_dma and not op.is_dma and _contains(b, rb))]
            h["r"].append((b, op))
        for b in wb_:
            h = self.hist[b[0]]
            h["w"] = [(wb, wop) for wb, wop in h["w"] if not _contains(b, wb)]
            h["r"] = [(rb, rop) for rb, rop in h["r"] if not _contains(b, rb) or rop is op]
            h["w"].append((b, op))
        seen = set()
        for d in deps:
            if d is op or id(d) in seen:
                continue
            if d.eng == "pe" and op.eng == "pe" and not d.is_dma and not op.is_dma:
                continue
            seen.add(id(d))
            op.deps.append(d)

    def I(self, eng, method, extra_reads=(), extra_writes=(), **kw):
        reads = list(extra_reads)
        writes = list(extra_writes)
        for k, v in kw.items():
            if _is_ap(v):
                if k in ("out", "accum_out", "ap"):
                    writes.append(v)
                else:
                    reads.append(v)

        def emit(e, method=method, kw=kw):
            return getattr(e, method)(**kw)

        op = Op(eng, emit)
        self._track(op, reads, writes)
        self.ops[eng].append(op)
        self.all_ops.append(op)
        return op

    def dma(self, eng, out, in_, key=None):
        o_dram = "dram" in str(out.tensor.space).lower()
        sb = in_ if o_dram else out
        if key is None:
            key = sb.tensor.name

        def emit(e, out=out, in_=in_):
            return e.dma_start(out=out, in_=in_)

        op = Op(eng, emit)
        op.is_dma = True
        self._track(op, [in_], [out])
        prev = self.dma_last.get(key)
        if prev is not None and all(prev is not d for d in op.deps):
            op.deps.append(prev)
        self.dma_last[key] = op
        idx = self.dkey2idx.setdefault(key, len(self.dkey2idx))
        ent = self.dpool[idx]
        ent[1] += 16
        op.sem = ent[0]
        op.val = ent[1]
        self.ops[eng].append(op)
        self.all_ops.append(op)
        return op

    def flush(self):
        nc = self.nc
        for op in self.all_ops:
            for d in op.deps:
                d.signal = True
        for e in ENGS:
            for op in reversed(self.ops[e]):
                if not op.is_dma:
                    op.signal = True
                    break
        for e in ENGS:
            for op in self.ops[e]:
                if (not op.is_dma) and op.signal:
                    self.ecnt[e] += 1
                    op.sem = self.esem[e]
                    op.val = self.ecnt[e]
        self.barcnt += 1
        barval = self.barcnt
        used_dsems = [self.dpool[i] for i in range(len(self.dkey2idx))]

        def run(e_name, eh):
            waited = {}
            for op in self.ops[e_name]:
                need = {}
                for d in op.deps:
                    k = id(d.sem)
                    if waited.get(k, 0) >= d.val:
                        continue
                    if k not in need or need[k][1] < d.val:
                        need[k] = (d.sem, d.val)
                for k, (s, v) in need.items():
                    eh.wait_ge(s, v)
                    waited[k] = v
                    self.n_waits += 1
                ins = op.emit(eh)
                if op.is_dma:
                    ins.then_inc(op.sem, 16)
                elif op.signal:
                    ins.then_inc(op.sem, 1)
                self.n_ops += 1
            if e_name == "sp":
                for e in ENGS:
                    if e != "sp" and self.ecnt[e] > 0:
                        eh.wait_ge(self.esem[e], self.ecnt[e])
                for sem, cnt in used_dsems:
                    if cnt > 0:
                        eh.wait_ge(sem, cnt)
                eh.sem_inc(self.bar, 1)
            else:
                eh.wait_ge(self.bar, barval)

        with nc.Block() as block:

            @block.sync
            def _(eh):
                run("sp", eh)

            @block.tensor
            def _(eh):
                run("pe", eh)

            @block.scalar
            def _(eh):
                run("act", eh)

            @block.vector
            def _(eh):
                run("dve", eh)

            @block.gpsimd
            def _(eh):
                run("pool", eh)

        self._reset()


class Ctx:
    pass


def _vcol(vec_sb, c):
    return vec_sb[:, c:c + 1]


def build_program(layers=(0, 1, 2, 3), debug=False, stop_after=None):
    nc = bass.Bass("TRN2", target_bir_lowering=False)

    def din(name, shape, dt=F32):
        return nc.dram_tensor(name, shape, dt, kind="ExternalInput").ap()

    scratch_kind = "ExternalOutput" if debug else "Internal"

    def dscr(name, shape, dt):
        return nc.dram_tensor(name, shape, dt, kind=scratch_kind).ap()

    xT = din("xT", [D, S])
    vecs = din("vecs", [128, NV])
    consts = din("consts", [128, NC_])
    biasT = din("biasT", [128, 6144])
    w_in_even = din("w_in_even", [2, 1024, 3072])
    w_out_even = din("w_out_even", [2, 1024, 1024])
    w_in_odd = din("w_in_odd", [2, 1024, 2560])
    w_out_odd = din("w_out_odd", [2, 1024, 1024])
    w_up = din("w_up", [4, 1024, 5632])
    w_down = din("w_down", [4, 2816, 1024])
    yT = nc.dram_tensor("yT", [D, S], F32, kind="ExternalOutput").ap()

    XA = dscr("XA", [D, S], F32)
    XB = dscr("XB", [D, S], F32)
    QT = dscr("QT", [512, S], BF16)
    KT = dscr("KT", [512, S], BF16)
    VN = dscr("VN", [S, 512], BF16)
    MIX = dscr("MIX", [D, S], BF16)
    AT = dscr("AT", [DFF, S], BF16)

    with ExitStack() as es0:
        P = Prog(nc, es0)
        G = Ctx()
        G.nc = nc
        G.P = P
        sb0 = lambda n, s, d: es0.enter_context(nc.sbuf_tensor(n, s, d))
        G.PS = es0.enter_context(nc.psum_tensor("PS", [128, 4096], F32))
        G.vec = sb0("vec_sb", [128, NV], F32)
        G.ones_f = sb0("ones_f", [128, 128], F32)
        G.ones_b = sb0("ones_b", [128, 64], BF16)
        G.eps = sb0("eps_t", [128, 1], F32)
        G.one = sb0("one_t", [128, 1], F32)
        G.tri = sb0("tri_b", [128, 128], BF16)
        G.su = sb0("su_b", [128, 128], BF16)
        G.ident = sb0("ident_b", [128, 128], BF16)
        G.consts = consts
        G.biasT = biasT

        P.dma("sp", out=G.vec[:, :], in_=vecs[:, :])
        P.dma("pool", out=G.tri[:, :], in_=consts[:, C_TRI:C_TRI + 128])
        P.dma("pool", out=G.su[:, :], in_=consts[:, C_SU:C_SU + 128])
        P.dma("pool", out=G.ident[:, :], in_=consts[:, C_ID:C_ID + 128])
        P.I("dve", "memset", ap=G.ones_f[:, :], constant=1.0)
        P.I("dve", "memset", ap=G.ones_b[:, :], constant=1.0)
        P.I("dve", "memset", ap=G.eps[:, :], constant=EPS)
        P.I("dve", "memset", ap=G.one[:, :], constant=1.0)
        P.flush()

        n_layers = len(layers)
        for li, l in enumerate(layers):
            src = xT if li == 0 else XA
            last = li == n_layers - 1
            dst = yT if last else XA
            even = (l % 2 == 0)
            i2 = l // 2
            if even:
                w_in = w_in_even[i2]
                w_out = w_out_even[i2]
            else:
                w_in = w_in_odd[i2]
                w_out = w_out_odd[i2]
            phase1(G, l, src, w_in, QT, KT, VN, MIX)
            if stop_after == (l, 1):
                break
            if even:
                phase2_even(G, l, QT, KT, VN, MIX)
            else:
                phase2_odd(G, l, QT, KT, VN, MIX)
            if stop_after == (l, 2):
                break
            phase3(G, l, src, XB, MIX, AT, w_out, w_up[l])
            if stop_after == (l, 3):
                break
            phase4(G, l, XB, dst, AT, w_down[l])
        G.n_ops = P.n_ops
        G.n_waits = P.n_waits
    return nc


def bank(G, i):
    return G.PS[:, i * 512:(i + 1) * 512]


class Rot:
    def __init__(self, items):
        self.items = list(items)
        self.i = 0

    def next(self):
        v = self.items[self.i % len(self.items)]
        self.i += 1
        return v


def act(P, out, in_, func, bias=None, scale=None):
    kw = dict(out=out, in_=in_, func=func)
    if bias is not None:
        kw["bias"] = bias
    if scale is not None:
        kw["scale"] = scale
    return P.I("act", "activation", **kw)


def mm(P, out, lhsT, rhs, start, stop):
    return P.I("pe", "matmul", out=out, lhsT=lhsT, rhs=rhs, start=start, stop=stop)


def rms_rstd(G, sb_alloc, xs, nchunks, rstd, sqrot, lnv, ps_bank, inv_n):
    P = G.P
    for c in range(nchunks):
        sq = sqrot.next()
        act(P, sq[:, :], xs[:, c, :], AF.Square)
        mm(P, ps_bank, G.ones_f[:, :], sq[:, :], c == 0, c == nchunks - 1)
    act(P, lnv[:, :], ps_bank, AF.Ln, bias=G.eps[:, 0:1], scale=inv_n)
    act(P, rstd[:, :], lnv[:, :], AF.Exp, scale=-0.5)


def load_w(G, wsb, w_ap, nk, eng="pool"):
    wr = w_ap.rearrange("(c p) n -> p c n", p=128)
    for c in range(nk):
        G.P.dma(eng, out=wsb[:, c, :], in_=wr[:, c, :], key=wsb.name + "_%d" % (c % 4))


def phase1(G, l, src, w_in, QT, KT, VN, MIX):
    nc, P, vec = G.nc, G.P, G.vec
    even = (l % 2 == 0)
    i2 = l // 2
    NIN = 3072 if even else 2560
    srcr = src.rearrange("(c p) t -> p c t", p=128)
    with ExitStack() as es:
        sb = lambda n, s, d: es.enter_context(nc.sbuf_tensor("p1_%d_" % l + n, s, d))
        W = sb("W", [128, 8, NIN], BF16)
        load_w(G, W, w_in, 8)
        xbuf = [sb("x%d" % i, [128, 8, TT], F32) for i in range(2)]
        hbuf = [sb("h%d" % i, [128, 8, TT], BF16) for i in range(2)]
        sqrot = Rot([sb("sq%d" % i, [128, TT], F32) for i in range(2)])
        lnv = sb("lnv", [128, TT], F32)
        rstd = sb("rstd", [128, TT], F32)
        qkrot = Rot([sb("qk%d" % i, [128, TT], BF16) for i in range(4)])
        vrot = Rot([sb("v%d" % i, [128, 512], BF16) for i in range(2)])
        yrot = Rot([sb("y%d" % i, [128, TT], BF16) for i in range(2)])
        brot = Rot([1, 2, 3, 4, 5, 6, 7])
        if even:
            gcrot = Rot([sb("gc%d" % i, [128, TT], F32) for i in range(2)])
            Z = [sb("Z%d" % i, [128, TT + 2], F32) for i in range(4)]
            trot = Rot([sb("t%d" % i, [128, TT], F32) for i in range(2)])
            for i in range(4):
                P.I("pool", "memset", ap=Z[i][:, 0:2], constant=0.0)
        else:
            sgrot = Rot([sb("sg%d" % i, [128, TT], F32) for i in range(2)])
            GC = [sb("GC%d" % i, [128, TT + 30], BF16) for i in range(4)]
            diag = sb("diag", [128, 124, 128], BF16)
            Y = sb("Y", [128, 4, TT], F32)
            mean = sb("mean", [128, TT], F32)
            msq = sb("msq", [128, TT], F32)
            var = sb("var", [128, TT], F32)
            lnr = sb("lnr", [128, TT], F32)
            rs2 = sb("rs2", [128, TT], F32)
            ynrot = Rot([sb("yn%d" % i, [128, TT], F32) for i in range(2)])
            for i in range(4):
                P.I("pool", "memset", ap=GC[i][:, 0:30], constant=0.0)
                for j in range(31):
                    P.I("dve", "tensor_scalar", out=diag[:, i * 31 + j, :], in0=G.ident[:, :],
                        scalar1=_vcol(vec, WCC0 + (i2 * 31 + j) * 4 + i), scalar2=None, op0=ALU.mult)

        def proj(h, oc):
            b = bank(G, brot.next())
            for c in range(8):
                mm(P, b, W[:, c, oc * 128:(oc + 1) * 128], h[:, c, :], c == 0, c == 7)
            return b

        P.dma("sp", out=xbuf[0][:, :, :], in_=srcr[:, :, 0:TT])
        for tt in range(NT):
            t0 = tt * TT
            xs = xbuf[tt % 2]
            h = hbuf[tt % 2]
            if tt + 1 < NT:
                P.dma("sp", out=xbuf[(tt + 1) % 2][:, :, :], in_=srcr[:, :, t0 + TT:t0 + 2 * TT])
            rms_rstd(G, sb, xs, 8, rstd, sqrot, lnv, bank(G, 0), 1.0 / D)
            for c in range(8):
                P.I("dve", "scalar_tensor_tensor", out=h[:, c, :], in0=xs[:, c, :],
                    scalar=_vcol(vec, GN0 + (l * 4 + 0) * 8 + c), in1=rstd[:, :], op0=ALU.mult, op1=ALU.mult)
            if even:
                qoc, koc, vcol = 12, 16, 2560
                for i in range(4):
                    pc = proj(h, 4 + i)
                    gcb = gcrot.next()
                    act(P, gcb[:, :], pc, AF.Copy)
                    px = proj(h, 8 + i)
                    if tt > 0:
                        P.I("pool", "tensor_copy", out=Z[i][:, 0:2], in_=Z[i][:, TT:TT + 2])
                    P.I("dve", "tensor_tensor", out=Z[i][:, 2:TT + 2], in0=px, in1=gcb[:, :], op=ALU.mult)
                    pb = proj(h, i)
                    t = trot.next()
                    wc = lambda j: _vcol(vec, WSC0 + (i2 * 3 + j) * 4 + i)
                    P.I("dve", "tensor_scalar", out=t[:, :], in0=Z[i][:, 0:TT], scalar1=wc(0), scalar2=None, op0=ALU.mult)
                    P.I("dve", "scalar_tensor_tensor", out=t[:, :], in0=Z[i][:, 1:TT + 1], scalar=wc(1), in1=t[:, :], op0=ALU.mult, op1=ALU.add)
                    P.I("dve", "scalar_tensor_tensor", out=t[:, :], in0=Z[i][:, 2:TT + 2], scalar=wc(2), in1=t[:, :], op0=ALU.mult, op1=ALU.add)
                    ya = yrot.next()
                    P.I("dve", "tensor_tensor", out=ya[:, :], in0=pb, in1=t[:, :], op=ALU.mult)
                    P.dma("sp", out=MIX[i * 128:(i + 1) * 128, t0:t0 + TT], in_=ya[:, :])
            else:
                qoc, koc, vcol = 8, 12, 2048
                for i in range(4):
                    pg = proj(h, 4 + i)
                    sg = sgrot.next()
                    act(P, sg[:, :], pg, AF.Sigmoid)
                    pa = proj(h, i)
                    if tt > 0:
                        P.I("pool", "tensor_copy", out=GC[i][:, 0:30], in_=GC[i][:, TT:TT + 30])
                    P.I("dve", "tensor_tensor", out=GC[i][:, 30:TT + 30], in0=pa, in1=sg[:, :], op=ALU.mult)
                    pcv = bank(G, brot.next())
                    for j in range(31):
                        mm(P, pcv, diag[:, i * 31 + j, :], GC[i][:, j:j + TT], j == 0, j == 30)
                    act(P, Y[:, i, :], pcv, AF.Identity, bias=_vcol(vec, BCC0 + i2 * 4 + i), scale=1.0)
                sbk = bank(G, brot.next())
                sqk = bank(G, brot.next())
                for i in range(4):
                    mm(P, sbk, G.ones_f[:, :], Y[:, i, :], i == 0, i == 3)
                for i in range(4):
                    sq = sqrot.next()
                    act(P, sq[:, :], Y[:, i, :], AF.Square)
                    mm(P, sqk, G.ones_f[:, :], sq[:, :], i == 0, i == 3)
                act(P, mean[:, :], sbk, AF.Copy, scale=1.0 / 512)
                P.I("dve", "tensor_tensor", out=msq[:, :], in0=mean[:, :], in1=mean[:, :], op=ALU.mult)
                P.I("dve", "scalar_tensor_tensor", out=var[:, :], in0=sqk, scalar=1.0 / 512, in1=msq[:, :], op0=ALU.mult, op1=ALU.subtract)
                act(P, lnr[:, :], var[:, :], AF.Ln, bias=G.eps[:, 0:1], scale=1.0)
                act(P, rs2[:, :], lnr[:, :], AF.Exp, scale=-0.5)
                for i in range(4):
                    yn = ynrot.next()
                    P.I("dve", "tensor_tensor", out=yn[:, :], in0=Y[:, i, :], in1=mean[:, :], op=ALU.subtract)
                    P.I("dve", "tensor_tensor", out=yn[:, :], in0=yn[:, :], in1=rs2[:, :], op=ALU.mult)
                    yc = yrot.next()
                    act(P, yc[:, :], yn[:, :], AF.Silu, bias=_vcol(vec, LNB0 + i2 * 4 + i), scale=_vcol(vec, LNG0 + i2 * 4 + i))
                    P.dma("sp", out=MIX[i * 128:(i + 1) * 128, t0:t0 + TT], in_=yc[:, :])
            for i in range(4):
                pq = proj(h, qoc + i)
                qs = qkrot.next()
                act(P, qs[:, :], pq, AF.Copy, scale=0.125)
                P.dma("sp", out=QT[i * 128:(i + 1) * 128, t0:t0 + TT], in_=qs[:, :])
                pk = proj(h, koc + i)
                ks = qkrot.next()
                P.I("dve", "tensor_copy", out=ks[:, :], in_=pk)
                P.dma("sp", out=KT[i * 128:(i + 1) * 128, t0:t0 + TT], in_=ks[:, :])
            for sub in range(4):
                b = bank(G, brot.next())
                for c in range(8):
                    mm(P, b, h[:, c, sub * 128:(sub + 1) * 128], W[:, c, vcol:vcol + 512], c == 0, c == 7)
                vs = vrot.next()
                act(P, vs[:, :], b, AF.Copy)
                P.dma("sp", out=VN[t0 + sub * 128:t0 + (sub + 1) * 128, :], in_=vs[:, :])
        P.flush()


def phase2_even(G, l, QT, KT, VN, MIX):
    nc, P = G.nc, G.P
    with ExitStack() as es:
        sb = lambda n, s, d: es.enter_context(nc.sbuf_tensor("p2e_%d_" % l + n, s, d))
        VD = [sb("VD%d" % g, [128, 32, 512], BF16) for g in range(3)]
        bias = sb("bias", [128, 6144], F32)
        maskd = sb("maskd", [128, 256], F32)
        qsb = [sb("q%d" % i, [128, S], BF16) for i in range(1)]
        ksb = [sb("k%d" % i, [128, S], BF16) for i in range(1)]
        acc = sb("acc", [128, 2, S], F32)
        ysb = sb("ysb", [128, S], BF16)
        sbf = [sb("sbf%d" % i, [128, 1024], F32) for i in range(2)]
        Ab = [sb("A%d" % i, [128, 1024], BF16) for i in range(4)]
        for g, (win, d) in enumerate(DA_PAIRS):
            vr = VN.rearrange("(blk p r) f -> p blk r f", p=128, r=d)
            for blk in range(32 // d):
                P.dma("sp" if blk % 2 == 0 else "pool", out=VD[g][:, blk * d:(blk + 1) * d, :], in_=vr[:, blk, :, :],
                      key="VD%d_%d" % (g, blk % 4))
        P.dma("sp", out=bias[:, :], in_=G.biasT[:, :])
        P.dma("sp", out=maskd[:, :], in_=G.consts[:, C_MASKD:C_MASKD + 256])
        for gh in range(24):
            P.I("pool", "tensor_tensor", out=bias[:, gh * 256:(gh + 1) * 256], in0=bias[:, gh * 256:(gh + 1) * 256], in1=maskd[:, :], op=ALU.add)
        srot = Rot([0, 1])
        orot = Rot([0, 1])
        arot = Rot([0, 1, 2, 3])
        fr = Rot([0, 1])
        for hp in range(4):
            q = qsb[0]
            k = ksb[0]
            P.dma("sp", out=q[:, :], in_=QT[hp * 128:(hp + 1) * 128, :])
            P.dma("sp", out=k[:, :], in_=KT[hp * 128:(hp + 1) * 128, :])
            for g, (win, d) in enumerate(DA_PAIRS):
                groups = []
                if d == 1:
                    for gi in range(8):
                        groups.append(([((gi * 4 + bi) * 128, 0) for bi in range(4)], ("c", gi * 512)))
                elif d == 4:
                    for gi in range(8):
                        groups.append(([(gi * 128, r) for r in range(4)], ("s4", gi * 512)))
                else:
                    for sg in range(2):
                        for rq in range(4):
                            groups.append(([(sg * 128, rq * 4 + bi) for bi in range(4)], ("s16", sg * 2048, rq)))
                for blocks, aview in groups:
                    oi = orot.next()
                    o_bank = 4 + oi * 2
                    den_bank = o_bank + 1
                    A_e = []
                    for e in range(2):
                        hh = hp * 2 + e
                        hr = slice(e * 64, (e + 1) * 64)
                        sbase = srot.next() * 1024
                        Sps = G.PS[:, sbase:sbase + 1024]
                        for bi, (m0, r) in enumerate(blocks):
                            qs0 = m0 * d + r
                            for jt in range(2):
                                mk0 = m0 - 128 + jt * 128
                                if mk0 < 0:
                                    continue
                                ks0 = mk0 * d + r
                                c0 = sbase + (bi * 2 + jt) * 128
                                mm(P, G.PS[:, c0:c0 + 128], k[hr, ks0:ks0 + 127 * d + 1:d], q[hr, qs0:qs0 + 127 * d + 1:d], True, True)
                        sf = sbf[fr.next()]
                        bsl = bias[:, (g * 8 + hh) * 256:(g * 8 + hh + 1) * 256].unsqueeze(1).broadcast_to([128, 4, 256])
                        P.I("dve", "tensor_tensor", out=sf[:, :].rearrange("p (b f) -> p b f", b=4),
                            in0=Sps.rearrange("p (b f) -> p b f", b=4), in1=bsl, op=ALU.add)
                        A = Ab[arot.next()]
                        act(P, A[:, :], sf[:, :], AF.Exp)
                        A_e.append(A)
                    for e in range(2):
                        hh = hp * 2 + e
                        hr = slice(e * 64, (e + 1) * 64)
                        A = A_e[e]
                        for bi, (m0, r) in enumerate(blocks):
                            first = True
                            for jt in range(2):
                                mk0 = m0 - 128 + jt * 128
                                if mk0 < 0:
                                    continue
                                tile = (mk0 // 128) * d + r
                                asl = A[:, (bi * 2 + jt) * 128:(bi * 2 + jt) * 128 + 128]
                                co = o_bank * 512 + bi * 128
                                cd = den_bank * 512 + bi * 128
                                mm(P, G.PS[hr, co:co + 128], VD[g][:, tile, hh * 64:(hh + 1) * 64], asl, first, jt == 1)
                                mm(P, G.PS[hr, cd:cd + 128], G.ones_b[:, :], asl, first, jt == 1)
                                first = False
                    for w_, bk in ((0, o_bank), (1, den_bank)):
                        src_ps = bank(G, bk)
                        if aview[0] == "c":
                            c0 = aview[1]
                            if w_ == 0:
                                act(P, acc[:, w_, c0:c0 + 512], src_ps, AF.Copy)
                            else:
                                P.I("dve", "tensor_copy", out=acc[:, w_, c0:c0 + 512], in_=src_ps)
                        elif aview[0] == "s4":
                            c0 = aview[1]
                            av = acc[:, w_, c0:c0 + 512].rearrange("p (i r) -> p r i", r=4)
                            P.I("dve", "tensor_tensor", out=av, in0=src_ps.rearrange("p (r i) -> p r i", r=4), in1=av, op=ALU.add)
                        else:
                            c0, rq = aview[1], aview[2]
                            av = acc[:, w_, c0:c0 + 2048].rearrange("p (i r) -> p r i", r=16)[:, rq * 4:(rq + 1) * 4, :]
                            P.I("dve", "tensor_tensor", out=av, in0=src_ps.rearrange("p (r i) -> p r i", r=4), in1=av, op=ALU.add)
            for qh in range(4):
                cs = slice(qh * 1024, (qh + 1) * 1024)
                P.I("dve", "reciprocal", out=acc[:, 1, cs], in_=acc[:, 1, cs])
                P.I("pool", "tensor_tensor", out=ysb[:, cs], in0=acc[:, 0, cs], in1=acc[:, 1, cs], op=ALU.mult)
            P.dma("sp", out=MIX[512 + hp * 128:512 + (hp + 1) * 128, :], in_=ysb[:, :])
        P.flush()


def phase2_odd(G, l, QT, KT, VN, MIX):
    nc, P = G.nc, G.P
    with ExitStack() as es:
        sb = lambda n, s, d: es.enter_context(nc.sbuf_tensor("p2o_%d_" % l + n, s, d))
        V = sb("V", [128, 32, 512], BF16)
        masks = sb("masks", [128, 4, 512], BF16)
        qsb = [sb("q%d" % i, [128, S], BF16) for i in range(2)]
        ksb = [sb("k%d" % i, [128, S], BF16) for i in range(2)]
        ysb = [sb("ysb%d" % i, [128, S], BF16) for i in range(2)]
        NSL = 4
        E = [sb("E%d" % i, [128, 512], F32) for i in range(NSL)]
        SP = [[sb("SP%d_%d" % (i, j), [128, 512], BF16) for j in range(2)] for i in range(NSL)]
        Gt = [sb("G%d" % i, [128, 512], F32) for i in range(NSL)]
        At = [sb("At%d" % i, [128, 512], BF16) for i in range(NSL)]
        vr = VN.rearrange("(t p) f -> p t f", p=128)
        for qq in range(4):
            P.dma("sp", out=V[:, qq * 8:(qq + 1) * 8, :], in_=vr[:, qq * 8:(qq + 1) * 8, :], key="Vo_%d" % qq)
        P.dma("pool", out=masks[:, :, :], in_=G.consts[:, C_MASKS:C_MASKS + 2048].rearrange("p (o f) -> p o f", o=4))
        P.dma("sp", out=qsb[0][:, :], in_=QT[0:128, :])
        P.dma("sp", out=ksb[0][:, :], in_=KT[0:128, :])
        zrot = Rot([0, 1])
        for hp in range(4):
            q = qsb[hp % 2]
            k = ksb[hp % 2]
            ys = ysb[hp % 2]
            if hp + 1 < 4:
                P.dma("sp", out=qsb[(hp + 1) % 2][:, :], in_=QT[(hp + 1) * 128:(hp + 2) * 128, :])
                P.dma("sp", out=ksb[(hp + 1) % 2][:, :], in_=KT[(hp + 1) * 128:(hp + 2) * 128, :])
            pending = {e: [qt for qt in range(7, -1, -1)] for e in range(2)}
            slots = [None] * NSL

            def start_chain(s):
                e = s % 2
                if not pending[e]:
                    slots[s] = None
                    return
                qt = pending[e].pop(0)
                slots[s] = dict(e=e, qt=qt, kb=4 * qt + 3, step=0, par=0)

            for s in range(NSL):
                start_chain(s)
            while any(sl is not None for sl in slots):
                live = [s for s in range(NSL) if slots[s] is not None]
                for s in live:
                    sl = slots[s]
                    e, qt, kb = sl["e"], sl["qt"], sl["kb"]
                    hr = slice(e * 64, (e + 1) * 64)
                    zb = bank(G, zrot.next())
                    mm(P, zb, k[hr, kb * 128:(kb + 1) * 128], q[hr, qt * 512:(qt + 1) * 512], True, True)
                    act(P, E[s][:, :], zb, AF.Exp)
                    spc = SP[s][sl["par"]]
                    act(P, spc[:, :], E[s][:, :], AF.Ln, bias=G.one[:, 0:1], scale=1.0)
                    o = kb - 4 * qt
                    if o >= 0:
                        P.I("pool", "tensor_tensor", out=spc[:, :], in0=spc[:, :], in1=masks[:, o, :], op=ALU.mult)
                        P.I("pool", "tensor_tensor", out=E[s][:, :], in0=E[s][:, :], in1=masks[:, o, :], op=ALU.mult)
                for s in live:
                    sl = slots[s]
                    pb = bank(G, 2 + s)
                    spc = SP[s][sl["par"]]
                    spp = SP[s][1 - sl["par"]]
                    if sl["step"] == 0:
                        mm(P, pb, G.tri[:, :], spc[:, :], True, True)
                    else:
                        mm(P, pb, G.su[:, :], spp[:, :], False, False)
                        mm(P, pb, G.tri[:, :], spc[:, :], False, True)
                    act(P, Gt[s][:, :], pb, AF.Exp, scale=-1.0)
                    P.I("dve", "tensor_tensor", out=At[s][:, :], in0=E[s][:, :], in1=Gt[s][:, :], op=ALU.mult)
                for s in live:
                    sl = slots[s]
                    e, qt, kb = sl["e"], sl["qt"], sl["kb"]
                    hh = hp * 2 + e
                    hr = slice(e * 64, (e + 1) * 64)
                    ob = G.PS[hr, (6 + s // 2) * 512:(7 + s // 2) * 512]
                    mm(P, ob, V[:, kb, hh * 64:(hh + 1) * 64], At[s][:, :], sl["step"] == 0, kb == 0)
                    if kb == 0:
                        act(P, ys[hr, qt * 512:(qt + 1) * 512], ob, AF.Copy)
                        start_chain(s)
                    else:
                        sl["kb"] -= 1
                        sl["step"] += 1
                        sl["par"] = 1 - sl["par"]
            P.dma("sp", out=MIX[512 + hp * 128:512 + (hp + 1) * 128, :], in_=ys[:, :])
        P.flush()


def post_norm_residual(G, l, j, M, xs, rstd, sqrot, lnv, ps_bank, tmprot):
    P, vec = G.P, G.vec
    rms_rstd(G, None, M, 8, rstd, sqrot, lnv, ps_bank, 1.0 / D)
    for c in range(8):
        tmp = tmprot.next()
        P.I("dve", "scalar_tensor_tensor", out=tmp[:, :], in0=M[:, c, :], scalar=_vcol(vec, GN0 + (l * 4 + j) * 8 + c),
            in1=rstd[:, :], op0=ALU.mult, op1=ALU.mult)
        P.I("pool", "tensor_tensor", out=xs[:, c, :], in0=xs[:, c, :], in1=tmp[:, :], op=ALU.add)


def phase3(G, l, src, XB, MIX, AT, w_out, w_up):
    nc, P, vec = G.nc, G.P, G.vec
    srcr = src.rearrange("(c p) t -> p c t", p=128)
    xbr = XB.rearrange("(c p) t -> p c t", p=128)
    mixr = MIX.rearrange("(c p) t -> p c t", p=128)
    with ExitStack() as es:
        sb = lambda n, s, d: es.enter_context(nc.sbuf_tensor("p3_%d_" % l + n, s, d))
        WO = sb("WO", [128, 8, 1024], BF16)
        WU = sb("WU", [128, 8, 2 * DFF], BF16)
        load_w(G, WO, w_out, 8)
        load_w(G, WU, w_up, 8)
        xs = sb("x", [128, 8, TT], F32)
        mbuf = [sb("mix%d" % i, [128, 8, TT], BF16) for i in range(2)]
        M = sb("M", [128, 8, TT], F32)
        h2 = sb("h2", [128, 8, TT], BF16)
        sqrot = Rot([sb("sq%d" % i, [128, TT], F32) for i in range(2)])
        tmprot = Rot([sb("tmp%d" % i, [128, TT], F32) for i in range(2)])
        lnv = sb("lnv", [128, TT], F32)
        rstd = sb("rstd", [128, TT], F32)
        Gb = [sb("Gb%d" % i, [128, TT + 2], F32) for i in range(2)]
        H = sb("H", [128, NFC, 2], F32)
        trot = Rot([sb("t%d" % i, [128, TT], F32) for i in range(2)])
        srot = Rot([sb("s%d" % i, [128, TT], F32) for i in range(2)])
        arot = Rot([sb("a%d" % i, [128, TT], BF16) for i in range(4)])
        brot = Rot([1, 2, 3, 4, 5, 6, 7])
        P.I("pool", "memset", ap=H[:, :, :], constant=0.0)
        P.dma("sp", out=mbuf[0][:, :, :], in_=mixr[:, :, 0:TT])
        for tt in range(NT):
            t0 = tt * TT
            P.dma("sp", out=xs[:, :, :], in_=srcr[:, :, t0:t0 + TT])
            mx = mbuf[tt % 2]
            if tt + 1 < NT:
                P.dma("sp", out=mbuf[(tt + 1) % 2][:, :, :], in_=mixr[:, :, t0 + TT:t0 + 2 * TT])
            for oc in range(8):
                b = bank(G, brot.next())
                for c in range(8):
                    mm(P, b, WO[:, c, oc * 128:(oc + 1) * 128], mx[:, c, :], c == 0, c == 7)
                act(P, M[:, oc, :], b, AF.Copy)
            post_norm_residual(G, l, 1, M, xs, rstd, sqrot, lnv, bank(G, 0), tmprot)
            P.dma("sp", out=xbr[:, :, t0:t0 + TT], in_=xs[:, :, :])
            rms_rstd(G, None, xs, 8, rstd, sqrot, lnv, bank(G, 0), 1.0 / D)
            for c in range(8):
                P.I("dve", "scalar_tensor_tensor", out=h2[:, c, :], in0=xs[:, c, :],
                    scalar=_vcol(vec, GN0 + (l * 4 + 2) * 8 + c), in1=rstd[:, :], op0=ALU.mult, op1=ALU.mult)
            for fc in range(NFC):
                pg = bank(G, brot.next())
                for c in range(8):
                    mm(P, pg, WU[:, c, fc * 128:(fc + 1) * 128], h2[:, c, :], c == 0, c == 7)
                pu = bank(G, brot.next())
                for c in range(8):
                    mm(P, pu, WU[:, c, DFF + fc * 128:DFF + (fc + 1) * 128], h2[:, c, :], c == 0, c == 7)
                gb = Gb[fc % 2]
                P.I("pool", "tensor_copy", out=gb[:, 0:2], in_=H[:, fc, :])
                act(P, gb[:, 2:TT + 2], pg, AF.Copy)
                P.I("pool", "tensor_copy", out=H[:, fc, :], in_=gb[:, TT:TT + 2])
                t = trot.next()
                wc = lambda j: _vcol(vec, WFC0 + (l * 3 + j) * NFC + fc)
                P.I("dve", "tensor_scalar", out=t[:, :], in0=gb[:, 0:TT], scalar1=wc(0), scalar2=_vcol(vec, BFC0 + l * NFC + fc),
                    op0=ALU.mult, op1=ALU.add)
                P.I("dve", "scalar_tensor_tensor", out=t[:, :], in0=gb[:, 1:TT + 1], scalar=wc(1), in1=t[:, :], op0=ALU.mult, op1=ALU.add)
                P.I("dve", "scalar_tensor_tensor", out=t[:, :], in0=gb[:, 2:TT + 2], scalar=wc(2), in1=t[:, :], op0=ALU.mult, op1=ALU.add)
                sv = srot.next()
                act(P, sv[:, :], t[:, :], AF.Silu)
                a = arot.next()
                P.I("dve", "tensor_tensor", out=a[:, :], in0=pu, in1=sv[:, :], op=ALU.mult)
                P.dma("sp", out=AT[fc * 128:(fc + 1) * 128, t0:t0 + TT], in_=a[:, :])
        P.flush()


def phase4(G, l, XB, dst, AT, w_down):
    nc, P, vec = G.nc, G.P, G.vec
    xbr = XB.rearrange("(c p) t -> p c t", p=128)
    dstr = dst.rearrange("(c p) t -> p c t", p=128)
    atr = AT.rearrange("(c p) t -> p c t", p=128)
    with ExitStack() as es:
        sb = lambda n, s, d: es.enter_context(nc.sbuf_tensor("p4_%d_" % l + n, s, d))
        WD = sb("WD", [128, NFC, 1024], BF16)
        load_w(G, WD, w_down, NFC)
        xbuf = [sb("x%d" % i, [128, 8, TT], F32) for i in range(2)]
        abuf = [sb("a%d" % i, [128, NFC, TT], BF16) for i in range(2)]
        M = sb("M", [128, 8, TT], F32)
        sqrot = Rot([sb("sq%d" % i, [128, TT], F32) for i in range(2)])
        tmprot = Rot([sb("tmp%d" % i, [128, TT], F32) for i in range(2)])
        lnv = sb("lnv", [128, TT], F32)
        rstd = sb("rstd", [128, TT], F32)
        brot = Rot([1, 2, 3, 4, 5, 6, 7])
        P.dma("sp", out=abuf[0][:, :, :], in_=atr[:, :, 0:TT])
        P.dma("sp", out=xbuf[0][:, :, :], in_=xbr[:, :, 0:TT])
        for tt in range(NT):
            t0 = tt * TT
            xs = xbuf[tt % 2]
            a = abuf[tt % 2]
            if tt + 1 < NT:
                P.dma("sp", out=abuf[(tt + 1) % 2][:, :, :], in_=atr[:, :, t0 + TT:t0 + 2 * TT])
                P.dma("sp", out=xbuf[(tt + 1) % 2][:, :, :], in_=xbr[:, :, t0 + TT:t0 + 2 * TT])
            for oc in range(8):
                b = bank(G, brot.next())
                for c in range(NFC):
                    mm(P, b, WD[:, c, oc * 128:(oc + 1) * 128], a[:, c, :], c == 0, c == NFC - 1)
                act(P, M[:, oc, :], b, AF.Copy)
            post_norm_residual(G, l, 3, M, xs, rstd, sqrot, lnv, bank(G, 0), tmprot)
            P.dma("sp", out=dstr[:, :, t0:t0 + TT], in_=xs[:, :, :])
        P.flush()


def _t5_bucket(dist):
    max_exact = 16
    d = np.maximum(dist, 1).astype(np.float32)
    large = max_exact + (np.log(d / np.float32(max_exact)) / np.float32(math.log(2048 / 16)) * np.float32(16)).astype(np.int32)
    large = np.minimum(large, 31)
    return np.where(dist < max_exact, dist, large)


def _host_tables(norm_g, rel_bias, w_sc, w_cc, b_cc, ln_cc_g, ln_cc_b, w_ffn_conv, b_ffn_conv):
    vecs = np.zeros((128, NV), np.float32)

    def put(col0, arr):
        a = np.asarray(arr, np.float32)
        lead = a.shape[:-1]
        C = a.shape[-1] // 128
        a = a.reshape(lead + (C, 128))
        a = np.moveaxis(a, -1, 0).reshape(128, -1)
        vecs[:, col0:col0 + a.shape[1]] = a

    put(GN0, norm_g)
    put(WSC0, w_sc)
    put(WCC0, w_cc)
    put(BCC0, b_cc)
    put(LNG0, ln_cc_g)
    put(LNB0, ln_cc_b)
    put(WFC0, w_ffn_conv)
    put(BFC0, b_ffn_conv)

    consts = np.zeros((128, NC_), np.float32)
    p = np.arange(128)[:, None]
    i = np.arange(128)[None, :]
    consts[:, C_MASKD:C_MASKD + 128] = np.where(i <= p, 0.0, NEG)
    consts[:, C_MASKD + 128:C_MASKD + 256] = np.where(i >= p, 0.0, NEG)
    iq = np.arange(512)[None, :]
    for o in range(4):
        consts[:, C_MASKS + o * 512:C_MASKS + (o + 1) * 512] = (o * 128 + p < iq).astype(np.float32)
    consts[:, C_TRI:C_TRI + 128] = (p >= i).astype(np.float32)
    consts[:, C_SU:C_SU + 128] = (p < i).astype(np.float32)
    consts[:, C_ID:C_ID + 128] = (p == i).astype(np.float32)

    rb = np.asarray(rel_bias, np.float32)
    biasT = np.zeros((128, 3, 8, 2, 128), np.float32)
    for g, (win, d) in enumerate(DA_PAIRS):
        for jt in range(2):
            rel = 128 + i - (jt * 128 + p)
            relc = np.clip(rel, 0, 128)
            bk = _t5_bucket(relc * d)
            biasT[:, g, :, jt, :] = np.transpose(rb[bk], (0, 2, 1))
    return vecs, consts, biasT.reshape(128, 6144)


_NC_CACHE = {}


def kernel(x, norm_g, rel_bias, w_in_even, w_out_even, w_sc, w_in_odd, w_out_odd,
           w_cc, b_cc, ln_cc_g, ln_cc_b, w_up, w_ffn_conv, b_ffn_conv, w_down):
    x = np.asarray(x, np.float32)
    B = x.shape[0]
    vecs, consts, biasT = _host_tables(norm_g, rel_bias, w_sc, w_cc, b_cc, ln_cc_g, ln_cc_b, w_ffn_conv, b_ffn_conv)
    if "nc" not in _NC_CACHE:
        _NC_CACHE["nc"] = build_program()
    nc = _NC_CACHE["nc"]
    shared = dict(
        vecs=vecs, consts=consts, biasT=biasT,
        w_in_even=np.ascontiguousarray(w_in_even, np.float32), w_out_even=np.ascontiguousarray(w_out_even, np.float32),
        w_in_odd=np.ascontiguousarray(w_in_odd, np.float32), w_out_odd=np.ascontiguousarray(w_out_odd, np.float32),
        w_up=np.ascontiguousarray(w_up, np.float32), w_down=np.ascontiguousarray(w_down, np.float32),
    )
    in_maps = []
    for b in range(B):
        m = dict(shared)
        m["xT"] = np.ascontiguousarray(x[b].T)
        in_maps.append(m)
    res = run_bass_kernel_spmd(nc, in_maps, core_ids=list(range(B)))
    out = np.stack([np.ascontiguousarray(r["yT"].T) for r in res.results], axis=0)
    return out.astype(np.float32)
```

```python
import math
from contextlib import ExitStack

import numpy as np
import concourse.bass as bass
import concourse.mybir as mybir
from concourse.bass_utils import run_bass_kernel_spmd

F32 = mybir.dt.float32
BF16 = mybir.dt.bfloat16
AF = mybir.ActivationFunctionType
ALU = mybir.AluOpType

S = 4096
D = 1024
NT = 8
TT = 512
DFF = 2816
NFC = 22
EPS = 1e-6
DA_PAIRS = ((128, 1), (512, 4), (2048, 16))
NEG = -30000.0

GN0 = 0
WSC0 = 128
WCC0 = 152
BCC0 = 400
LNG0 = 408
LNB0 = 416
WFC0 = 424
BFC0 = 688
NV = 776
C_MASKD = 0
C_MASKS = 256
C_TRI = 2304
C_SU = 2432
C_ID = 2560
NC_ = 2688


def _is_ap(v):
    return hasattr(v, "tensor") and hasattr(v, "offset") and hasattr(v, "ap")


def _box(ap):
    t = ap.tensor
    shape = list(t.shape)
    row = 1
    for s in shape[1:]:
        row *= s
    off = ap.offset
    r0 = off // row
    c0 = off % row
    rext = 0
    cext = 0
    for step, cnt in ap.ap:
        if cnt <= 1:
            continue
        if step != 0 and step % row == 0:
            rext += (step // row) * (cnt - 1)
        else:
            cext += step * (cnt - 1)
    return (t.name, r0, r0 + rext + 1, c0, c0 + cext + 1)


def _ovl(a, b):
    return a[1] < b[2] and b[1] < a[2] and a[3] < b[4] and b[3] < a[4]


def _contains(a, b):
    return a[1] <= b[1] and b[2] <= a[2] and a[3] <= b[3] and b[4] <= a[4]


class Op:
    __slots__ = ("eng", "emit", "deps", "is_dma", "sem", "val", "signal")

    def __init__(self, eng, emit):
        self.eng = eng
        self.emit = emit
        self.deps = []
        self.is_dma = False
        self.sem = None
        self.val = None
        self.signal = False


ENGS = ("pe", "act", "dve", "pool", "sp")


class Prog:
    def __init__(self, nc, es, n_dsem=60):
        self.nc = nc
        self.esem = {e: es.enter_context(nc.semaphore("s_" + e)) for e in ENGS}
        self.ecnt = {e: 0 for e in ENGS}
        self.bar = es.enter_context(nc.semaphore("s_bar"))
        self.barcnt = 0
        self.dpool = [[es.enter_context(nc.semaphore("d%d" % i)), 0] for i in range(n_dsem)]
        self.n_ops = 0
        self.n_waits = 0
        self._reset()

    def _reset(self):
        self.ops = {e: [] for e in ENGS}
        self.hist = {}
        self.dma_last = {}
        self.dkey2idx = {}
        self.all_ops = []

    def _track(self, op, reads, writes):
        deps = []
        rb_ = [_box(a) for a in reads]
        wb_ = [_box(a) for a in writes]
        for b in rb_:
            h = self.hist.setdefault(b[0], {"w": [], "r": []})
            for wb, wop in h["w"]:
                if _ovl(wb, b):
                    deps.append(wop)
        for b in wb_:
            h = self.hist.setdefault(b[0], {"w": [], "r": []})
            for wb, wop in h["w"]:
                if _ovl(wb, b):
                    deps.append(wop)
            for rb, rop in h["r"]:
                if _ovl(rb, b):
                    deps.append(rop)
        for b in rb_:
            h = self.hist[b[0]]
            if len(h["r"]) > 64:
                h["r"] = [(rb, rop) for rb, rop in h["r"] if not (rop.eng == op.eng and not rop.is_dma and not op.is_dma and _contains(b, rb))]
            h["r"].append((b, op))
        for b in wb_:
            h = self.hist[b[0]]
            h["w"] = [(wb, wop) for wb, wop in h["w"] if not _contains(b, wb)]
            h["r"] = [(rb, rop) for rb, rop in h["r"] if not _contains(b, rb) or rop is op]
            h["w"].append((b, op))
        seen = set()
        for d in deps:
            if d is op or id(d) in seen:
                continue
            if d.eng == "pe" and op.eng == "pe" and not d.is_dma and not op.is_dma:
                continue
            seen.add(id(d))
            op.deps.append(d)

    def I(self, eng, method, extra_reads=(), extra_writes=(), **kw):
        reads = list(extra_reads)
        writes = list(extra_writes)
        for k, v in kw.items():
            if _is_ap(v):
                if k in ("out", "accum_out", "ap"):
                    writes.append(v)
                else:
                    reads.append(v)

        def emit(e, method=method, kw=kw):
            return getattr(e, method)(**kw)

        op = Op(eng, emit)
        self._track(op, reads, writes)
        self.ops[eng].append(op)
        self.all_ops.append(op)
        return op

    def dma(self, eng, out, in_, key=None):
        o_dram = "dram" in str(out.tensor.space).lower()
        sb = in_ if o_dram else out
        if key is None:
            key = sb.tensor.name

        def emit(e, out=out, in_=in_):
            return e.dma_start(out=out, in_=in_)

        op = Op(eng, emit)
        op.is_dma = True
        self._track(op, [in_], [out])
        prev = self.dma_last.get(key)
        if prev is not None and all(prev is not d for d in op.deps):
            op.deps.append(prev)
        self.dma_last[key] = op
        idx = self.dkey2idx.setdefault(key, len(self.dkey2idx))
        ent = self.dpool[idx]
        ent[1] += 16
        op.sem = ent[0]
        op.val = ent[1]
        self.ops[eng].append(op)
        self.all_ops.append(op)
        return op

    def flush(self):
        nc = self.nc
        for op in self.all_ops:
            for d in op.deps:
                d.signal = True
        for e in ENGS:
            for op in reversed(self.ops[e]):
                if not op.is_dma:
                    op.signal = True
                    break
        for e in ENGS:
            for op in self.ops[e]:
                if (not op.is_dma) and op.signal:
                    self.ecnt[e] += 1
                    op.sem = self.esem[e]
                    op.val = self.ecnt[e]
        self.barcnt += 1
        barval = self.barcnt
        used_dsems = [self.dpool[i] for i in range(len(self.dkey2idx))]

        def run(e_name, eh):
            waited = {}
            for op in self.ops[e_name]:
                need = {}
                for d in op.deps:
                    k = id(d.sem)
                    if waited.get(k, 0) >= d.val:
                        continue
                    if k not in need or need[k][1] < d.val:
                        need[k] = (d.sem, d.val)
                for k, (s, v) in need.items():
                    eh.wait_ge(s, v)
                    waited[k] = v
                    self.n_waits += 1
                ins = op.emit(eh)
                if op.is_dma:
                    ins.then_inc(op.sem, 16)
                elif op.signal:
                    ins.then_inc(op.sem, 1)
                self.n_ops += 1
            if e_name == "sp":
                for e in ENGS:
                    if e != "sp" and self.ecnt[e] > 0:
                        eh.wait_ge(self.esem[e], self.ecnt[e])
                for sem, cnt in used_dsems:
                    if cnt > 0:
                        eh.wait_ge(sem, cnt)
                eh.sem_inc(self.bar, 1)
            else:
                eh.wait_ge(self.bar, barval)

        with nc.Block() as block:

            @block.sync
            def _(eh):
                run("sp", eh)

            @block.tensor
            def _(eh):
                run("pe", eh)

            @block.scalar
            def _(eh):
                run("act", eh)

            @block.vector
            def _(eh):
                run("dve", eh)

            @block.gpsimd
            def _(eh):
                run("pool", eh)

        self._reset()


class Ctx:
    pass


def _vcol(vec_sb, c):
    return vec_sb[:, c:c + 1]


def build_program(layers=(0, 1, 2, 3), debug=False, stop_after=None):
    nc = bass.Bass("TRN2", target_bir_lowering=False)

    def din(name, shape, dt=F32):
        return nc.dram_tensor(name, shape, dt, kind="ExternalInput").ap()

    scratch_kind = "ExternalOutput" if debug else "Internal"

    def dscr(name, shape, dt):
        return nc.dram_tensor(name, shape, dt, kind=scratch_kind).ap()

    xT = din("xT", [D, S])
    vecs = din("vecs", [128, NV])
    consts = din("consts", [128, NC_])
    biasT = din("biasT", [128, 6144])
    w_in_even = din("w_in_even", [2, 1024, 3072])
    w_out_even = din("w_out_even", [2, 1024, 1024])
    w_in_odd = din("w_in_odd", [2, 1024, 2560])
    w_out_odd = din("w_out_odd", [2, 1024, 1024])
    w_up = din("w_up", [4, 1024, 5632])
    w_down = din("w_down", [4, 2816, 1024])
    yT = nc.dram_tensor("yT", [D, S], F32, kind="ExternalOutput").ap()

    XA = dscr("XA", [D, S], F32)
    XB = dscr("XB", [D, S], F32)
    QT = dscr("QT", [512, S], BF16)
    KT = dscr("KT", [512, S], BF16)
    VN = dscr("VN", [S, 512], BF16)
    MIX = dscr("MIX", [D, S], BF16)
    AT = dscr("AT", [DFF, S], BF16)

    with ExitStack() as es0:
        P = Prog(nc, es0)
        G = Ctx()
        G.nc = nc
        G.P = P
        sb0 = lambda n, s, d: es0.enter_context(nc.sbuf_tensor(n, s, d))
        G.PS = es0.enter_context(nc.psum_tensor("PS", [128, 4096], F32))
        G.vec = sb0("vec_sb", [128, NV], F32)
        G.ones_f = sb0("ones_f", [128, 128], F32)
        G.ones_b = sb0("ones_b", [128, 64], BF16)
        G.ones_bb = sb0("ones_bb", [128, 128], BF16)
        G.eps = sb0("eps_t", [128, 1], F32)
        G.one = sb0("one_t", [128, 1], F32)
        G.tri = sb0("tri_b", [128, 128], BF16)
        G.su = sb0("su_b", [128, 128], BF16)
        G.ident = sb0("ident_b", [128, 128], BF16)
        G.consts = consts
        G.biasT = biasT

        P.dma("sp", out=G.vec[:, :], in_=vecs[:, :])
        P.dma("pool", out=G.tri[:, :], in_=consts[:, C_TRI:C_TRI + 128])
        P.dma("pool", out=G.su[:, :], in_=consts[:, C_SU:C_SU + 128])
        P.dma("pool", out=G.ident[:, :], in_=consts[:, C_ID:C_ID + 128])
        P.I("dve", "memset", ap=G.ones_f[:, :], constant=1.0)
        P.I("dve", "memset", ap=G.ones_b[:, :], constant=1.0)
        P.I("dve", "memset", ap=G.ones_bb[:, :], constant=1.0)
        P.I("dve", "memset", ap=G.eps[:, :], constant=EPS)
        P.I("dve", "memset", ap=G.one[:, :], constant=1.0)
        P.flush()

        n_layers = len(layers)
        for li, l in enumerate(layers):
            src = xT if li == 0 else XA
            last = li == n_layers - 1
            dst = yT if last else XA
            even = (l % 2 == 0)
            i2 = l // 2
            if even:
                w_in = w_in_even[i2]
                w_out = w_out_even[i2]
            else:
                w_in = w_in_odd[i2]
                w_out = w_out_odd[i2]
            phase1(G, l, src, w_in, QT, KT, VN, MIX)
            if stop_after == (l, 1):
                break
            if even:
                phase2_even(G, l, QT, KT, VN, MIX)
            else:
                phase2_odd(G, l, QT, KT, VN, MIX)
            if stop_after == (l, 2):
                break
            phase3(G, l, src, XB, MIX, AT, w_out, w_up[l])
            if stop_after == (l, 3):
                break
            phase4(G, l, XB, dst, AT, w_down[l])
        G.n_ops = P.n_ops
        G.n_waits = P.n_waits
    return nc


def bank(G, i):
    return G.PS[:, i * 512:(i + 1) * 512]


class Rot:
    def __init__(self, items):
        self.items = list(items)
        self.i = 0

    def next(self):
        v = self.items[self.i % len(self.items)]
        self.i += 1
        return v


def act(P, out, in_, func, bias=None, scale=None):
    kw = dict(out=out, in_=in_, func=func)
    if bias is not None:
        kw["bias"] = bias
    if scale is not None:
        kw["scale"] = scale
    return P.I("act", "activation", **kw)


def mm(P, out, lhsT, rhs, start, stop):
    return P.I("pe", "matmul", out=out, lhsT=lhsT, rhs=rhs, start=start, stop=stop)


def rms_rstd(G, sb_alloc, xs, nchunks, rstd, sqrot, lnv, ps_bank, inv_n):
    P = G.P
    for c in range(nchunks):
        sq = sqrot.next()
        act(P, sq[:, :], xs[:, c, :], AF.Square)
        mm(P, ps_bank, G.ones_bb[:, :], sq[:, :], c == 0, c == nchunks - 1)
    act(P, lnv[:, :], ps_bank, AF.Ln, bias=G.eps[:, 0:1], scale=inv_n)
    act(P, rstd[:, :], lnv[:, :], AF.Exp, scale=-0.5)


def load_w(G, wsb, w_ap, nk, eng="pool"):
    wr = w_ap.rearrange("(c p) n -> p c n", p=128)
    for c in range(nk):
        G.P.dma(eng, out=wsb[:, c, :], in_=wr[:, c, :], key=wsb.name + "_%d" % (c % 4))


def phase1(G, l, src, w_in, QT, KT, VN, MIX):
    nc, P, vec = G.nc, G.P, G.vec
    even = (l % 2 == 0)
    i2 = l // 2
    NIN = 3072 if even else 2560
    srcr = src.rearrange("(c p) t -> p c t", p=128)
    with ExitStack() as es:
        sb = lambda n, s, d: es.enter_context(nc.sbuf_tensor("p1_%d_" % l + n, s, d))
        W = sb("W", [128, 8, NIN], BF16)
        load_w(G, W, w_in, 8)
        xbuf = [sb("x%d" % i, [128, 8, TT], F32) for i in range(2)]
        hbuf = [sb("h%d" % i, [128, 8, TT], BF16) for i in range(2)]
        sqrot = Rot([sb("sq%d" % i, [128, TT], BF16) for i in range(2)])
        sqfrot = Rot([sb("sqf%d" % i, [128, TT], F32) for i in range(2)])
        lnv = sb("lnv", [128, TT], F32)
        rstd = sb("rstd", [128, TT], F32)
        qkrot = Rot([sb("qk%d" % i, [128, TT], BF16) for i in range(4)])
        vrot = Rot([sb("v%d" % i, [128, 512], BF16) for i in range(2)])
        yrot = Rot([sb("y%d" % i, [128, TT], BF16) for i in range(2)])
        brot = Rot([1, 2, 3, 4, 5, 6, 7])
        if even:
            gcrot = Rot([sb("gc%d" % i, [128, TT], F32) for i in range(2)])
            Z = [sb("Z%d" % i, [128, TT + 2], F32) for i in range(4)]
            trot = Rot([sb("t%d" % i, [128, TT], F32) for i in range(2)])
            for i in range(4):
                P.I("pool", "memset", ap=Z[i][:, 0:2], constant=0.0)
        else:
            sgrot = Rot([sb("sg%d" % i, [128, TT], F32) for i in range(2)])
            GC = [sb("GC%d" % i, [128, TT + 30], BF16) for i in range(4)]
            diag = sb("diag", [128, 124, 128], BF16)
            Y = sb("Y", [128, 4, TT], F32)
            mean = sb("mean", [128, TT], F32)
            msq = sb("msq", [128, TT], F32)
            var = sb("var", [128, TT], F32)
            lnr = sb("lnr", [128, TT], F32)
            rs2 = sb("rs2", [128, TT], F32)
            ynrot = Rot([sb("yn%d" % i, [128, TT], F32) for i in range(2)])
            for i in range(4):
                P.I("pool", "memset", ap=GC[i][:, 0:30], constant=0.0)
                for j in range(31):
                    P.I("dve", "tensor_scalar", out=diag[:, i * 31 + j, :], in0=G.ident[:, :],
                        scalar1=_vcol(vec, WCC0 + (i2 * 31 + j) * 4 + i), scalar2=None, op0=ALU.mult)

        def proj(h, oc):
            b = bank(G, brot.next())
            for c in range(8):
                mm(P, b, W[:, c, oc * 128:(oc + 1) * 128], h[:, c, :], c == 0, c == 7)
            return b

        P.dma("sp", out=xbuf[0][:, :, :], in_=srcr[:, :, 0:TT])
        for tt in range(NT):
            t0 = tt * TT
            xs = xbuf[tt % 2]
            h = hbuf[tt % 2]
            if tt + 1 < NT:
                P.dma("sp", out=xbuf[(tt + 1) % 2][:, :, :], in_=srcr[:, :, t0 + TT:t0 + 2 * TT])
            rms_rstd(G, sb, xs, 8, rstd, sqrot, lnv, bank(G, 0), 1.0 / D)
            for c in range(8):
                P.I("dve", "scalar_tensor_tensor", out=h[:, c, :], in0=xs[:, c, :],
                    scalar=_vcol(vec, GN0 + (l * 4 + 0) * 8 + c), in1=rstd[:, :], op0=ALU.mult, op1=ALU.mult)
            if even:
                qoc, koc, vcol = 12, 16, 2560
                for i in range(4):
                    pc = proj(h, 4 + i)
                    gcb = gcrot.next()
                    act(P, gcb[:, :], pc, AF.Copy)
                    px = proj(h, 8 + i)
                    if tt > 0:
                        P.I("pool", "tensor_copy", out=Z[i][:, 0:2], in_=Z[i][:, TT:TT + 2])
                    P.I("dve", "tensor_tensor", out=Z[i][:, 2:TT + 2], in0=px, in1=gcb[:, :], op=ALU.mult)
                    pb = proj(h, i)
                    t = trot.next()
                    wc = lambda j: _vcol(vec, WSC0 + (i2 * 3 + j) * 4 + i)
                    P.I("dve", "tensor_scalar", out=t[:, :], in0=Z[i][:, 0:TT], scalar1=wc(0), scalar2=None, op0=ALU.mult)
                    P.I("dve", "scalar_tensor_tensor", out=t[:, :], in0=Z[i][:, 1:TT + 1], scalar=wc(1), in1=t[:, :], op0=ALU.mult, op1=ALU.add)
                    P.I("dve", "scalar_tensor_tensor", out=t[:, :], in0=Z[i][:, 2:TT + 2], scalar=wc(2), in1=t[:, :], op0=ALU.mult, op1=ALU.add)
                    ya = yrot.next()
                    P.I("dve", "tensor_tensor", out=ya[:, :], in0=pb, in1=t[:, :], op=ALU.mult)
                    P.dma("sp", out=MIX[i * 128:(i + 1) * 128, t0:t0 + TT], in_=ya[:, :])
            else:
                qoc, koc, vcol = 8, 12, 2048
                for i in range(4):
                    pg = proj(h, 4 + i)
                    sg = sgrot.next()
                    act(P, sg[:, :], pg, AF.Sigmoid)
                    pa = proj(h, i)
                    if tt > 0:
                        P.I("pool", "tensor_copy", out=GC[i][:, 0:30], in_=GC[i][:, TT:TT + 30])
                    P.I("dve", "tensor_tensor", out=GC[i][:, 30:TT + 30], in0=pa, in1=sg[:, :], op=ALU.mult)
                    pcv = bank(G, brot.next())
                    for j in range(31):
                        mm(P, pcv, diag[:, i * 31 + j, :], GC[i][:, j:j + TT], j == 0, j == 30)
                    act(P, Y[:, i, :], pcv, AF.Identity, bias=_vcol(vec, BCC0 + i2 * 4 + i), scale=1.0)
                sbk = bank(G, brot.next())
                sqk = bank(G, brot.next())
                for i in range(4):
                    mm(P, sbk, G.ones_f[:, :], Y[:, i, :], i == 0, i == 3)
                for i in range(4):
                    sq = sqfrot.next()
                    act(P, sq[:, :], Y[:, i, :], AF.Square)
                    mm(P, sqk, G.ones_f[:, :], sq[:, :], i == 0, i == 3)
                act(P, mean[:, :], sbk, AF.Copy, scale=1.0 / 512)
                P.I("dve", "tensor_tensor", out=msq[:, :], in0=mean[:, :], in1=mean[:, :], op=ALU.mult)
                P.I("dve", "scalar_tensor_tensor", out=var[:, :], in0=sqk, scalar=1.0 / 512, in1=msq[:, :], op0=ALU.mult, op1=ALU.subtract)
                act(P, lnr[:, :], var[:, :], AF.Ln, bias=G.eps[:, 0:1], scale=1.0)
                act(P, rs2[:, :], lnr[:, :], AF.Exp, scale=-0.5)
                for i in range(4):
                    yn = ynrot.next()
                    P.I("dve", "tensor_tensor", out=yn[:, :], in0=Y[:, i, :], in1=mean[:, :], op=ALU.subtract)
                    P.I("dve", "tensor_tensor", out=yn[:, :], in0=yn[:, :], in1=rs2[:, :], op=ALU.mult)
                    yc = yrot.next()
                    act(P, yc[:, :], yn[:, :], AF.Silu, bias=_vcol(vec, LNB0 + i2 * 4 + i), scale=_vcol(vec, LNG0 + i2 * 4 + i))
                    P.dma("sp", out=MIX[i * 128:(i + 1) * 128, t0:t0 + TT], in_=yc[:, :])
            for i in range(4):
                pq = proj(h, qoc + i)
                qs = qkrot.next()
                act(P, qs[:, :], pq, AF.Copy, scale=0.125)
                P.dma("sp", out=QT[i * 128:(i + 1) * 128, t0:t0 + TT], in_=qs[:, :])
                pk = proj(h, koc + i)
                ks = qkrot.next()
                P.I("dve", "tensor_copy", out=ks[:, :], in_=pk)
                P.dma("sp", out=KT[i * 128:(i + 1) * 128, t0:t0 + TT], in_=ks[:, :])
            for sub in range(4):
                b = bank(G, brot.next())
                for c in range(8):
                    mm(P, b, h[:, c, sub * 128:(sub + 1) * 128], W[:, c, vcol:vcol + 512], c == 0, c == 7)
                vs = vrot.next()
                act(P, vs[:, :], b, AF.Copy)
                P.dma("sp", out=VN[t0 + sub * 128:t0 + (sub + 1) * 128, :], in_=vs[:, :])
        P.flush()


def phase2_even(G, l, QT, KT, VN, MIX):
    nc, P = G.nc, G.P
    with ExitStack() as es:
        sb = lambda n, s, d: es.enter_context(nc.sbuf_tensor("p2e_%d_" % l + n, s, d))
        VD = [sb("VD%d" % g, [128, 32, 512], BF16) for g in range(3)]
        bias = sb("bias", [128, 6144], F32)
        maskd = sb("maskd", [128, 256], F32)
        qsb = [sb("q%d" % i, [128, S], BF16) for i in range(1)]
        ksb = [sb("k%d" % i, [128, S], BF16) for i in range(1)]
        acc = sb("acc", [128, 2, S], F32)
        ysb = sb("ysb", [128, S], BF16)
        sbf = [sb("sbf%d" % i, [128, 1024], F32) for i in range(2)]
        Ab = [sb("A%d" % i, [128, 1024], BF16) for i in range(4)]
        for g, (win, d) in enumerate(DA_PAIRS):
            vr = VN.rearrange("(blk p r) f -> p blk r f", p=128, r=d)
            for blk in range(32 // d):
                P.dma("sp" if blk % 2 == 0 else "pool", out=VD[g][:, blk * d:(blk + 1) * d, :], in_=vr[:, blk, :, :],
                      key="VD%d_%d" % (g, blk % 4))
        P.dma("sp", out=bias[:, :], in_=G.biasT[:, :])
        P.dma("sp", out=maskd[:, :], in_=G.consts[:, C_MASKD:C_MASKD + 256])
        for gh in range(24):
            P.I("pool", "tensor_tensor", out=bias[:, gh * 256:(gh + 1) * 256], in0=bias[:, gh * 256:(gh + 1) * 256], in1=maskd[:, :], op=ALU.add)
        srot = Rot([0, 1])
        arot = Rot([0, 1, 2, 3])
        fr = Rot([0, 1])
        for hp in range(4):
            q = qsb[0]
            k = ksb[0]
            P.dma("sp", out=q[:, :], in_=QT[hp * 128:(hp + 1) * 128, :])
            P.dma("sp", out=k[:, :], in_=KT[hp * 128:(hp + 1) * 128, :])
            units = []
            for g, (win, d) in enumerate(DA_PAIRS):
                if d == 1:
                    for gi in range(8):
                        units.append(dict(g=g, d=d, blocks=[((gi * 4 + bi) * 128, 0) for bi in range(4)], aview=("c", gi * 512)))
                elif d == 4:
                    for gi in range(8):
                        units.append(dict(g=g, d=d, blocks=[(gi * 128, r) for r in range(4)], aview=("s4", gi * 512)))
                else:
                    for sg in range(2):
                        for rq in range(4):
                            units.append(dict(g=g, d=d, blocks=[(sg * 128, rq * 4 + bi) for bi in range(4)], aview=("s16", sg * 2048, rq)))
            for ui, u in enumerate(units):
                u["o_bank"] = 4 + (ui % 2) * 2

            def stageA(u):
                g, d, blocks = u["g"], u["d"], u["blocks"]
                u["A"] = []
                for e in range(2):
                    hh = hp * 2 + e
                    hr = slice(e * 64, (e + 1) * 64)
                    sbase = srot.next() * 1024
                    Sps = G.PS[:, sbase:sbase + 1024]
                    for bi, (m0, r) in enumerate(blocks):
                        qs0 = m0 * d + r
                        for jt in range(2):
                            mk0 = m0 - 128 + jt * 128
                            if mk0 < 0:
                                continue
                            ks0 = mk0 * d + r
                            c0 = sbase + (bi * 2 + jt) * 128
                            mm(P, G.PS[:, c0:c0 + 128], k[hr, ks0:ks0 + 127 * d + 1:d], q[hr, qs0:qs0 + 127 * d + 1:d], True, True)
                    sf = sbf[fr.next()]
                    bsl = bias[:, (g * 8 + hh) * 256:(g * 8 + hh + 1) * 256].unsqueeze(1).broadcast_to([128, 4, 256])
                    P.I("dve", "tensor_tensor", out=sf[:, :].rearrange("p (b f) -> p b f", b=4),
                        in0=Sps.rearrange("p (b f) -> p b f", b=4), in1=bsl, op=ALU.add)
                    A = Ab[arot.next()]
                    act(P, A[:, :], sf[:, :], AF.Exp)
                    u["A"].append(A)

            def stageB(u):
                g, d, blocks = u["g"], u["d"], u["blocks"]
                o_bank = u["o_bank"]
                den_bank = o_bank + 1
                for e in range(2):
                    hh = hp * 2 + e
                    hr = slice(e * 64, (e + 1) * 64)
                    A = u["A"][e]
                    for bi, (m0, r) in enumerate(blocks):
                        first = True
                        for jt in range(2):
                            mk0 = m0 - 128 + jt * 128
                            if mk0 < 0:
                                continue
                            tile = (mk0 // 128) * d + r
                            asl = A[:, (bi * 2 + jt) * 128:(bi * 2 + jt) * 128 + 128]
                            co = o_bank * 512 + bi * 128
                            cd = den_bank * 512 + bi * 128
                            mm(P, G.PS[hr, co:co + 128], VD[g][:, tile, hh * 64:(hh + 1) * 64], asl, first, jt == 1)
                            mm(P, G.PS[hr, cd:cd + 128], G.ones_b[:, :], asl, first, jt == 1)
                            first = False

            def stageC(u):
                aview = u["aview"]
                o_bank = u["o_bank"]
                for w_, bk in ((0, o_bank), (1, o_bank + 1)):
                    src_ps = bank(G, bk)
                    if aview[0] == "c":
                        c0 = aview[1]
                        if w_ == 0:
                            act(P, acc[:, w_, c0:c0 + 512], src_ps, AF.Copy)
                        else:
                            P.I("dve", "tensor_copy", out=acc[:, w_, c0:c0 + 512], in_=src_ps)
                    elif aview[0] == "s4":
                        c0 = aview[1]
                        av = acc[:, w_, c0:c0 + 512].rearrange("p (i r) -> p r i", r=4)
                        P.I("dve", "tensor_tensor", out=av, in0=src_ps.rearrange("p (r i) -> p r i", r=4), in1=av, op=ALU.add)
                    else:
                        c0, rq = aview[1], aview[2]
                        av = acc[:, w_, c0:c0 + 2048].rearrange("p (i r) -> p r i", r=16)[:, rq * 4:(rq + 1) * 4, :]
                        P.I("dve", "tensor_tensor", out=av, in0=src_ps.rearrange("p (r i) -> p r i", r=4), in1=av, op=ALU.add)

            nu = len(units)
            for it in range(nu + 2):
                if it < nu:
                    stageA(units[it])
                if 0 <= it - 1 < nu:
                    stageB(units[it - 1])
                if 0 <= it - 2 < nu:
                    stageC(units[it - 2])
            for qh in range(4):
                cs = slice(qh * 1024, (qh + 1) * 1024)
                act(P, acc[:, 1, cs], acc[:, 1, cs], AF.Ln)
                act(P, acc[:, 1, cs], acc[:, 1, cs], AF.Exp, scale=-1.0)
                P.I("pool", "tensor_tensor", out=ysb[:, cs], in0=acc[:, 0, cs], in1=acc[:, 1, cs], op=ALU.mult)
            P.dma("sp", out=MIX[512 + hp * 128:512 + (hp + 1) * 128, :], in_=ysb[:, :])
        P.flush()


def phase2_odd(G, l, QT, KT, VN, MIX):
    nc, P = G.nc, G.P
    with ExitStack() as es:
        sb = lambda n, s, d: es.enter_context(nc.sbuf_tensor("p2o_%d_" % l + n, s, d))
        V = sb("V", [128, 32, 512], BF16)
        masks = sb("masks", [128, 4, 512], BF16)
        qsb = [sb("q%d" % i, [128, S], BF16) for i in range(2)]
        ksb = [sb("k%d" % i, [128, S], BF16) for i in range(2)]
        ysb = [sb("ysb%d" % i, [128, S], BF16) for i in range(2)]
        NSL = 4
        E = [[sb("E%d_%d" % (i, j), [128, 512], F32) for j in range(2)] for i in range(NSL)]
        SP = [[sb("SP%d_%d" % (i, j), [128, 512], BF16) for j in range(3)] for i in range(NSL)]
        Gt = [[sb("G%d_%d" % (i, j), [128, 512], F32) for j in range(2)] for i in range(NSL)]
        At = [[sb("At%d_%d" % (i, j), [128, 512], BF16) for j in range(2)] for i in range(NSL)]
        vr = VN.rearrange("(t p) f -> p t f", p=128)
        for qq in range(4):
            P.dma("sp", out=V[:, qq * 8:(qq + 1) * 8, :], in_=vr[:, qq * 8:(qq + 1) * 8, :], key="Vo_%d" % qq)
        P.dma("pool", out=masks[:, :, :], in_=G.consts[:, C_MASKS:C_MASKS + 2048].rearrange("p (o f) -> p o f", o=4))
        P.dma("sp", out=qsb[0][:, :], in_=QT[0:128, :])
        P.dma("sp", out=ksb[0][:, :], in_=KT[0:128, :])
        zrot = Rot([0, 1])
        slot_qts = [[7, 4, 3, 0], [7, 4, 3, 0], [6, 5, 2, 1], [6, 5, 2, 1]]
        for hp in range(4):
            q = qsb[hp % 2]
            k = ksb[hp % 2]
            ys = ysb[hp % 2]
            if hp + 1 < 4:
                P.dma("sp", out=qsb[(hp + 1) % 2][:, :], in_=QT[(hp + 1) * 128:(hp + 2) * 128, :])
                P.dma("sp", out=ksb[(hp + 1) % 2][:, :], in_=KT[(hp + 1) * 128:(hp + 2) * 128, :])
            streams = []
            for s_ in range(NSL):
                e = s_ % 2
                st = []
                for qt in slot_qts[s_]:
                    n = 4 * qt + 4
                    for j in range(n):
                        st.append(dict(e=e, qt=qt, kb=4 * qt + 3 - j, first=(j == 0), last=(j == n - 1), idx=len(st)))
                streams.append(st)
            nit = max(len(st) for st in streams) + 2

            def stageA(s_, stp):
                e, qt, kb, idx = stp["e"], stp["qt"], stp["kb"], stp["idx"]
                hr = slice(e * 64, (e + 1) * 64)
                zb = bank(G, zrot.next())
                mm(P, zb, k[hr, kb * 128:(kb + 1) * 128], q[hr, qt * 512:(qt + 1) * 512], True, True)
                Ec = E[s_][idx % 2]
                act(P, Ec[:, :], zb, AF.Exp)
                spc = SP[s_][idx % 3]
                act(P, spc[:, :], Ec[:, :], AF.Ln, bias=G.one[:, 0:1], scale=1.0)
                o = kb - 4 * qt
                if o >= 0:
                    P.I("pool", "tensor_tensor", out=spc[:, :], in0=spc[:, :], in1=masks[:, o, :], op=ALU.mult)
                    P.I("pool", "tensor_tensor", out=Ec[:, :], in0=Ec[:, :], in1=masks[:, o, :], op=ALU.mult)

            def stageB(s_, stp):
                idx = stp["idx"]
                pb = bank(G, 2 + s_)
                spc = SP[s_][idx % 3]
                spp = SP[s_][(idx - 1) % 3]
                if stp["first"]:
                    mm(P, pb, G.tri[:, :], spc[:, :], True, True)
                else:
                    mm(P, pb, G.su[:, :], spp[:, :], False, False)
                    mm(P, pb, G.tri[:, :], spc[:, :], False, True)
                Gc = Gt[s_][idx % 2]
                act(P, Gc[:, :], pb, AF.Exp, scale=-1.0)
                P.I("dve", "tensor_tensor", out=At[s_][idx % 2][:, :], in0=E[s_][idx % 2][:, :], in1=Gc[:, :], op=ALU.mult)

            def stageC(s_, stp):
                e, qt, kb, idx = stp["e"], stp["qt"], stp["kb"], stp["idx"]
                hh = hp * 2 + e
                hr = slice(e * 64, (e + 1) * 64)
                ob = G.PS[hr, (6 + s_ // 2) * 512:(7 + s_ // 2) * 512]
                mm(P, ob, V[:, kb, hh * 64:(hh + 1) * 64], At[s_][idx % 2][:, :], stp["first"], stp["last"])
                if stp["last"]:
                    P.I("dve", "tensor_copy", out=ys[hr, qt * 512:(qt + 1) * 512], in_=ob)

            for it in range(nit):
                for s_ in range(NSL):
                    if it < len(streams[s_]):
                        stageA(s_, streams[s_][it])
                for s_ in range(NSL):
                    if 0 <= it - 1 < len(streams[s_]):
                        stageB(s_, streams[s_][it - 1])
                for s_ in range(NSL):
                    if 0 <= it - 2 < len(streams[s_]):
                        stageC(s_, streams[s_][it - 2])
            P.dma("sp", out=MIX[512 + hp * 128:512 + (hp + 1) * 128, :], in_=ys[:, :])
        P.flush()


def post_norm_residual(G, l, j, M, xs, rstd, sqrot, lnv, ps_bank, tmprot):
    P, vec = G.P, G.vec
    rms_rstd(G, None, M, 8, rstd, sqrot, lnv, ps_bank, 1.0 / D)
    for c in range(8):
        tmp = tmprot.next()
        P.I("dve", "scalar_tensor_tensor", out=tmp[:, :], in0=M[:, c, :], scalar=_vcol(vec, GN0 + (l * 4 + j) * 8 + c),
            in1=rstd[:, :], op0=ALU.mult, op1=ALU.mult)
        P.I("pool", "tensor_tensor", out=xs[:, c, :], in0=xs[:, c, :], in1=tmp[:, :], op=ALU.add)


def phase3(G, l, src, XB, MIX, AT, w_out, w_up):
    nc, P, vec = G.nc, G.P, G.vec
    srcr = src.rearrange("(c p) t -> p c t", p=128)
    xbr = XB.rearrange("(c p) t -> p c t", p=128)
    mixr = MIX.rearrange("(c p) t -> p c t", p=128)
    with ExitStack() as es:
        sb = lambda n, s, d: es.enter_context(nc.sbuf_tensor("p3_%d_" % l + n, s, d))
        WO = sb("WO", [128, 8, 1024], BF16)
        WU = sb("WU", [128, 8, 2 * DFF], BF16)
        load_w(G, WO, w_out, 8)
        load_w(G, WU, w_up, 8)
        xs = sb("x", [128, 8, TT], F32)
        mbuf = [sb("mix%d" % i, [128, 8, TT], BF16) for i in range(2)]
        M = sb("M", [128, 8, TT], F32)
        h2 = sb("h2", [128, 8, TT], BF16)
        sqrot = Rot([sb("sq%d" % i, [128, TT], BF16) for i in range(2)])
        tmprot = Rot([sb("tmp%d" % i, [128, TT], F32) for i in range(2)])
        lnv = sb("lnv", [128, TT], F32)
        rstd = sb("rstd", [128, TT], F32)
        Gb = [sb("Gb%d" % i, [128, TT + 2], F32) for i in range(2)]
        H = sb("H", [128, NFC, 2], F32)
        trot = Rot([sb("t%d" % i, [128, TT], F32) for i in range(2)])
        srot = Rot([sb("s%d" % i, [128, TT], F32) for i in range(2)])
        arot = Rot([sb("a%d" % i, [128, TT], BF16) for i in range(4)])
        brot = Rot([1, 2, 3, 4, 5, 6, 7])
        P.I("pool", "memset", ap=H[:, :, :], constant=0.0)
        P.dma("sp", out=mbuf[0][:, :, :], in_=mixr[:, :, 0:TT])
        for tt in range(NT):
            t0 = tt * TT
            P.dma("sp", out=xs[:, :, :], in_=srcr[:, :, t0:t0 + TT])
            mx = mbuf[tt % 2]
            if tt + 1 < NT:
                P.dma("sp", out=mbuf[(tt + 1) % 2][:, :, :], in_=mixr[:, :, t0 + TT:t0 + 2 * TT])
            for oc in range(8):
                b = bank(G, brot.next())
                for c in range(8):
                    mm(P, b, WO[:, c, oc * 128:(oc + 1) * 128], mx[:, c, :], c == 0, c == 7)
                act(P, M[:, oc, :], b, AF.Copy)
            post_norm_residual(G, l, 1, M, xs, rstd, sqrot, lnv, bank(G, 0), tmprot)
            P.dma("sp", out=xbr[:, :, t0:t0 + TT], in_=xs[:, :, :])
            rms_rstd(G, None, xs, 8, rstd, sqrot, lnv, bank(G, 0), 1.0 / D)
            for c in range(8):
                P.I("dve", "scalar_tensor_tensor", out=h2[:, c, :], in0=xs[:, c, :],
                    scalar=_vcol(vec, GN0 + (l * 4 + 2) * 8 + c), in1=rstd[:, :], op0=ALU.mult, op1=ALU.mult)
            for fc in range(NFC):
                pg = bank(G, brot.next())
                for c in range(8):
                    mm(P, pg, WU[:, c, fc * 128:(fc + 1) * 128], h2[:, c, :], c == 0, c == 7)
                pu = bank(G, brot.next())
                for c in range(8):
                    mm(P, pu, WU[:, c, DFF + fc * 128:DFF + (fc + 1) * 128], h2[:, c, :], c == 0, c == 7)
                gb = Gb[fc % 2]
                P.I("pool", "tensor_copy", out=gb[:, 0:2], in_=H[:, fc, :])
                act(P, gb[:, 2:TT + 2], pg, AF.Copy)
                P.I("pool", "tensor_copy", out=H[:, fc, :], in_=gb[:, TT:TT + 2])
                t = trot.next()
                wc = lambda j: _vcol(vec, WFC0 + (l * 3 + j) * NFC + fc)
                P.I("dve", "tensor_scalar", out=t[:, :], in0=gb[:, 0:TT], scalar1=wc(0), scalar2=_vcol(vec, BFC0 + l * NFC + fc),
                    op0=ALU.mult, op1=ALU.add)
                P.I("dve", "scalar_tensor_tensor", out=t[:, :], in0=gb[:, 1:TT + 1], scalar=wc(1), in1=t[:, :], op0=ALU.mult, op1=ALU.add)
                P.I("dve", "scalar_tensor_tensor", out=t[:, :], in0=gb[:, 2:TT + 2], scalar=wc(2), in1=t[:, :], op0=ALU.mult, op1=ALU.add)
                sv = srot.next()
                act(P, sv[:, :], t[:, :], AF.Silu)
                a = arot.next()
                P.I("dve", "tensor_tensor", out=a[:, :], in0=pu, in1=sv[:, :], op=ALU.mult)
                P.dma("sp", out=AT[fc * 128:(fc + 1) * 128, t0:t0 + TT], in_=a[:, :])
        P.flush()


def phase4(G, l, XB, dst, AT, w_down):
    nc, P, vec = G.nc, G.P, G.vec
    xbr = XB.rearrange("(c p) t -> p c t", p=128)
    dstr = dst.rearrange("(c p) t -> p c t", p=128)
    atr = AT.rearrange("(c p) t -> p c t", p=128)
    with ExitStack() as es:
        sb = lambda n, s, d: es.enter_context(nc.sbuf_tensor("p4_%d_" % l + n, s, d))
        WD = sb("WD", [128, NFC, 1024], BF16)
        load_w(G, WD, w_down, NFC)
        xbuf = [sb("x%d" % i, [128, 8, TT], F32) for i in range(2)]
        abuf = [sb("a%d" % i, [128, NFC, TT], BF16) for i in range(2)]
        M = sb("M", [128, 8, TT], F32)
        sqrot = Rot([sb("sq%d" % i, [128, TT], BF16) for i in range(2)])
        tmprot = Rot([sb("tmp%d" % i, [128, TT], F32) for i in range(2)])
        lnv = sb("lnv", [128, TT], F32)
        rstd = sb("rstd", [128, TT], F32)
        brot = Rot([1, 2, 3, 4, 5, 6, 7])
        P.dma("sp", out=abuf[0][:, :, :], in_=atr[:, :, 0:TT])
        P.dma("sp", out=xbuf[0][:, :, :], in_=xbr[:, :, 0:TT])
        for tt in range(NT):
            t0 = tt * TT
            xs = xbuf[tt % 2]
            a = abuf[tt % 2]
            if tt + 1 < NT:
                P.dma("sp", out=abuf[(tt + 1) % 2][:, :, :], in_=atr[:, :, t0 + TT:t0 + 2 * TT])
                P.dma("sp", out=xbuf[(tt + 1) % 2][:, :, :], in_=xbr[:, :, t0 + TT:t0 + 2 * TT])
            for oc in range(8):
                b = bank(G, brot.next())
                for c in range(NFC):
                    mm(P, b, WD[:, c, oc * 128:(oc + 1) * 128], a[:, c, :], c == 0, c == NFC - 1)
                act(P, M[:, oc, :], b, AF.Copy)
            post_norm_residual(G, l, 3, M, xs, rstd, sqrot, lnv, bank(G, 0), tmprot)
            P.dma("sp", out=dstr[:, :, t0:t0 + TT], in_=xs[:, :, :])
        P.flush()


def _t5_bucket(dist):
    max_exact = 16
    d = np.maximum(dist, 1).astype(np.float32)
    large = max_exact + (np.log(d / np.float32(max_exact)) / np.float32(math.log(2048 / 16)) * np.float32(16)).astype(np.int32)
    large = np.minimum(large, 31)
    return np.where(dist < max_exact, dist, large)


def _host_tables(norm_g, rel_bias, w_sc, w_cc, b_cc, ln_cc_g, ln_cc_b, w_ffn_conv, b_ffn_conv):
    vecs = np.zeros((128, NV), np.float32)

    def put(col0, arr):
        a = np.asarray(arr, np.float32)
        lead = a.shape[:-1]
        C = a.shape[-1] // 128
        a = a.reshape(lead + (C, 128))
        a = np.moveaxis(a, -1, 0).reshape(128, -1)
        vecs[:, col0:col0 + a.shape[1]] = a

    put(GN0, norm_g)
    put(WSC0, w_sc)
    put(WCC0, w_cc)
    put(BCC0, b_cc)
    put(LNG0, ln_cc_g)
    put(LNB0, ln_cc_b)
    put(WFC0, w_ffn_conv)
    put(BFC0, b_ffn_conv)

    consts = np.zeros((128, NC_), np.float32)
    p = np.arange(128)[:, None]
    i = np.arange(128)[None, :]
    consts[:, C_MASKD:C_MASKD + 128] = np.where(i <= p, 0.0, NEG)
    consts[:, C_MASKD + 128:C_MASKD + 256] = np.where(i >= p, 0.0, NEG)
    iq = np.arange(512)[None, :]
    for o in range(4):
        consts[:, C_MASKS + o * 512:C_MASKS + (o + 1) * 512] = (o * 128 + p < iq).astype(np.float32)
    consts[:, C_TRI:C_TRI + 128] = (p >= i).astype(np.float32)
    consts[:, C_SU:C_SU + 128] = (p < i).astype(np.float32)
    consts[:, C_ID:C_ID + 128] = (p == i).astype(np.float32)

    rb = np.asarray(rel_bias, np.float32)
    biasT = np.zeros((128, 3, 8, 2, 128), np.float32)
    for g, (win, d) in enumerate(DA_PAIRS):
        for jt in range(2):
            rel = 128 + i - (jt * 128 + p)
            relc = np.clip(rel, 0, 128)
            bk = _t5_bucket(relc * d)
            biasT[:, g, :, jt, :] = np.transpose(rb[bk], (0, 2, 1))
    return vecs, consts, biasT.reshape(128, 6144)


_NC_CACHE = {}


def kernel(x, norm_g, rel_bias, w_in_even, w_out_even, w_sc, w_in_odd, w_out_odd,
           w_cc, b_cc, ln_cc_g, ln_cc_b, w_up, w_ffn_conv, b_ffn_conv, w_down):
    x = np.asarray(x, np.float32)
    B = x.shape[0]
    vecs, consts, biasT = _host_tables(norm_g, rel_bias, w_sc, w_cc, b_cc, ln_cc_g, ln_cc_b, w_ffn_conv, b_ffn_conv)
    if "nc" not in _NC_CACHE:
        _NC_CACHE["nc"] = build_program()
    nc = _NC_CACHE["nc"]
    shared = dict(
        vecs=vecs, consts=consts, biasT=biasT,
        w_in_even=np.ascontiguousarray(w_in_even, np.float32), w_out_even=np.ascontiguousarray(w_out_even, np.float32),
        w_in_odd=np.ascontiguousarray(w_in_odd, np.float32), w_out_odd=np.ascontiguousarray(w_out_odd, np.float32),
        w_up=np.ascontiguousarray(w_up, np.float32), w_down=np.ascontiguousarray(w_down, np.float32),
    )
    in_maps = []
    for b in range(B):
        m = dict(shared)
        m["xT"] = np.ascontiguousarray(x[b].T)
        in_maps.append(m)
    res = run_bass_kernel_spmd(nc, in_maps, core_ids=list(range(B)))
    out = np.stack([np.ascontiguousarray(r["yT"].T) for r in res.results], axis=0)
    return out.astype(np.float32)
```

```python
import math
from contextlib import ExitStack

import numpy as np
import concourse.bass as bass
import concourse.mybir as mybir
from concourse.bass_utils import run_bass_kernel_spmd

F32 = mybir.dt.float32
BF16 = mybir.dt.bfloat16
AF = mybir.ActivationFunctionType
ALU = mybir.AluOpType

S = 4096
D = 1024
NT = 8
TT = 512
DFF = 2816
NFC = 22
EPS = 1e-6
DA_PAIRS = ((128, 1), (512, 4), (2048, 16))
NEG = -30000.0

GN0 = 0
WSC0 = 128
WCC0 = 152
BCC0 = 400
LNG0 = 408
LNB0 = 416
WFC0 = 424
BFC0 = 688
NV = 776
C_MASKD = 0
C_MASKS = 256
C_TRI = 2304
C_SU = 2432
C_ID = 2560
NC_ = 2688


def _is_ap(v):
    return hasattr(v, "tensor") and hasattr(v, "offset") and hasattr(v, "ap")


def _box(ap):
    t = ap.tensor
    shape = list(t.shape)
    row = 1
    for s in shape[1:]:
        row *= s
    off = ap.offset
    r0 = off // row
    c0 = off % row
    rext = 0
    cext = 0
    for step, cnt in ap.ap:
        if cnt <= 1:
            continue
        if step != 0 and step % row == 0:
            rext += (step // row) * (cnt - 1)
        else:
            cext += step * (cnt - 1)
    return (t.name, r0, r0 + rext + 1, c0, c0 + cext + 1)


def _ovl(a, b):
    return a[1] < b[2] and b[1] < a[2] and a[3] < b[4] and b[3] < a[4]


def _contains(a, b):
    return a[1] <= b[1] and b[2] <= a[2] and a[3] <= b[3] and b[4] <= a[4]


class Op:
    __slots__ = ("eng", "emit", "deps", "is_dma", "sem", "val", "signal")

    def __init__(self, eng, emit):
        self.eng = eng
        self.emit = emit
        self.deps = []
        self.is_dma = False
        self.sem = None
        self.val = None
        self.signal = False


ENGS = ("pe", "act", "dve", "pool", "sp")


class Prog:
    def __init__(self, nc, es, n_dsem=60):
        self.nc = nc
        self.esem = {e: es.enter_context(nc.semaphore("s_" + e)) for e in ENGS}
        self.ecnt = {e: 0 for e in ENGS}
        self.bar = es.enter_context(nc.semaphore("s_bar"))
        self.barcnt = 0
        self.dpool = [[es.enter_context(nc.semaphore("d%d" % i)), 0] for i in range(n_dsem)]
        self.n_ops = 0
        self.n_waits = 0
        self._reset()

    def _reset(self):
        self.ops = {e: [] for e in ENGS}
        self.hist = {}
        self.dma_last = {}
        self.dkey2idx = {}
        self.all_ops = []

    def _track(self, op, reads, writes):
        deps = []
        rb_ = [_box(a) for a in reads]
        wb_ = [_box(a) for a in writes]
        for b in rb_:
            h = self.hist.setdefault(b[0], {"w": [], "r": []})
            for wb, wop in h["w"]:
                if _ovl(wb, b):
                    deps.append(wop)
        for b in wb_:
            h = self.hist.setdefault(b[0], {"w": [], "r": []})
            for wb, wop in h["w"]:
                if _ovl(wb, b):
                    deps.append(wop)
            for rb, rop in h["r"]:
                if _ovl(rb, b):
                    deps.append(rop)
        for b in rb_:
            h = self.hist[b[0]]
            if len(h["r"]) > 64:
                h["r"] = [(rb, rop) for rb, rop in h["r"] if not (rop.eng == op.eng and not rop.is_dma and not op.is_dma and _contains(b, rb))]
            h["r"].append((b, op))
        for b in wb_:
            h = self.hist[b[0]]
            h["w"] = [(wb, wop) for wb, wop in h["w"] if not _contains(b, wb)]
            h["r"] = [(rb, rop) for rb, rop in h["r"] if not _contains(b, rb) or rop is op]
            h["w"].append((b, op))
        seen = set()
        for d in deps:
            if d is op or id(d) in seen:
                continue
            if d.eng == "pe" and op.eng == "pe" and not d.is_dma and not op.is_dma:
                continue
            seen.add(id(d))
            op.deps.append(d)

    def I(self, eng, method, extra_reads=(), extra_writes=(), **kw):
        reads = list(extra_reads)
        writes = list(extra_writes)
        for k, v in kw.items():
            if _is_ap(v):
                if k in ("out", "accum_out", "ap"):
                    writes.append(v)
                else:
                    reads.append(v)

        def emit(e, method=method, kw=kw):
            return getattr(e, method)(**kw)

        op = Op(eng, emit)
        self._track(op, reads, writes)
        self.ops[eng].append(op)
        self.all_ops.append(op)
        return op

    def dma(self, eng, out, in_, key=None):
        o_dram = "dram" in str(out.tensor.space).lower()
        sb = in_ if o_dram else out
        if key is None:
            key = sb.tensor.name

        def emit(e, out=out, in_=in_):
            return e.dma_start(out=out, in_=in_)

        op = Op(eng, emit)
        op.is_dma = True
        self._track(op, [in_], [out])
        prev = self.dma_last.get(key)
        if prev is not None and all(prev is not d for d in op.deps):
            op.deps.append(prev)
        self.dma_last[key] = op
        idx = self.dkey2idx.setdefault(key, len(self.dkey2idx))
        ent = self.dpool[idx]
        ent[1] += 16
        op.sem = ent[0]
        op.val = ent[1]
        self.ops[eng].append(op)
        self.all_ops.append(op)
        return op

    def flush(self):
        nc = self.nc
        for op in self.all_ops:
            for d in op.deps:
                d.signal = True
        for e in ENGS:
            for op in reversed(self.ops[e]):
                if not op.is_dma:
                    op.signal = True
                    break
        for e in ENGS:
            for op in self.ops[e]:
                if (not op.is_dma) and op.signal:
                    self.ecnt[e] += 1
                    op.sem = self.esem[e]
                    op.val = self.ecnt[e]
        self.barcnt += 1
        barval = self.barcnt
        used_dsems = [self.dpool[i] for i in range(len(self.dkey2idx))]

        def run(e_name, eh):
            waited = {}
            for op in self.ops[e_name]:
                need = {}
                for d in op.deps:
                    k = id(d.sem)
                    if waited.get(k, 0) >= d.val:
                        continue
                    if k not in need or need[k][1] < d.val:
                        need[k] = (d.sem, d.val)
                for k, (s, v) in need.items():
                    eh.wait_ge(s, v)
                    waited[k] = v
                    self.n_waits += 1
                ins = op.emit(eh)
                if op.is_dma:
                    ins.then_inc(op.sem, 16)
                elif op.signal:
                    ins.then_inc(op.sem, 1)
                self.n_ops += 1
            if e_name == "sp":
                for e in ENGS:
                    if e != "sp" and self.ecnt[e] > 0:
                        eh.wait_ge(self.esem[e], self.ecnt[e])
                for sem, cnt in used_dsems:
                    if cnt > 0:
                        eh.wait_ge(sem, cnt)
                eh.sem_inc(self.bar, 1)
            else:
                eh.wait_ge(self.bar, barval)

        with nc.Block() as block:

            @block.sync
            def _(eh):
                run("sp", eh)

            @block.tensor
            def _(eh):
                run("pe", eh)

            @block.scalar
            def _(eh):
                run("act", eh)

            @block.vector
            def _(eh):
                run("dve", eh)

            @block.gpsimd
            def _(eh):
                run("pool", eh)

        self._reset()


class Ctx:
    pass


def _vcol(vec_sb, c):
    return vec_sb[:, c:c + 1]


def build_program(layers=(0, 1, 2, 3), debug=False, stop_after=None):
    nc = bass.Bass("TRN2", target_bir_lowering=False)

    def din(name, shape, dt=F32):
        return nc.dram_tensor(name, shape, dt, kind="ExternalInput").ap()

    scratch_kind = "ExternalOutput" if debug else "Internal"

    def dscr(name, shape, dt):
        return nc.dram_tensor(name, shape, dt, kind=scratch_kind).ap()

    xT = din("xT", [D, S])
    vecs = din("vecs", [128, NV])
    consts = din("consts", [128, NC_])
    biasT = din("biasT", [128, 6144])
    w_in_even = din("w_in_even", [2, 1024, 3072])
    w_out_even = din("w_out_even", [2, 1024, 1024])
    w_in_odd = din("w_in_odd", [2, 1024, 2560])
    w_out_odd = din("w_out_odd", [2, 1024, 1024])
    w_up = din("w_up", [4, 1024, 5632])
    w_down = din("w_down", [4, 2816, 1024])
    yT = nc.dram_tensor("yT", [D, S], F32, kind="ExternalOutput").ap()

    XA = dscr("XA", [D, S], F32)
    XB = dscr("XB", [D, S], F32)
    QT = dscr("QT", [512, S], BF16)
    KT = dscr("KT", [512, S], BF16)
    VN = dscr("VN", [S, 512], BF16)
    MIX = dscr("MIX", [D, S], BF16)
    AT = dscr("AT", [DFF, S], BF16)

    with ExitStack() as es0:
        P = Prog(nc, es0)
        G = Ctx()
        G.nc = nc
        G.P = P
        sb0 = lambda n, s, d: es0.enter_context(nc.sbuf_tensor(n, s, d))
        G.PS = es0.enter_context(nc.psum_tensor("PS", [128, 4096], F32))
        G.vec = sb0("vec_sb", [128, NV], F32)
        G.ones_f = sb0("ones_f", [128, 128], F32)
        G.ones_b = sb0("ones_b", [128, 64], BF16)
        G.ones_bb = sb0("ones_bb", [128, 128], BF16)
        G.eps = sb0("eps_t", [128, 1], F32)
        G.one = sb0("one_t", [128, 1], F32)
        G.tri = sb0("tri_b", [128, 128], BF16)
        G.su = sb0("su_b", [128, 128], BF16)
        G.ident = sb0("ident_b", [128, 128], BF16)
        G.consts = consts
        G.biasT = biasT

        P.dma("sp", out=G.vec[:, :], in_=vecs[:, :])
        P.dma("pool", out=G.tri[:, :], in_=consts[:, C_TRI:C_TRI + 128])
        P.dma("pool", out=G.su[:, :], in_=consts[:, C_SU:C_SU + 128])
        P.dma("pool", out=G.ident[:, :], in_=consts[:, C_ID:C_ID + 128])
        P.I("dve", "memset", ap=G.ones_f[:, :], constant=1.0)
        P.I("dve", "memset", ap=G.ones_b[:, :], constant=1.0)
        P.I("dve", "memset", ap=G.ones_bb[:, :], constant=1.0)
        P.I("dve", "memset", ap=G.eps[:, :], constant=EPS)
        P.I("dve", "memset", ap=G.one[:, :], constant=1.0)
        P.flush()

        n_layers = len(layers)
        for li, l in enumerate(layers):
            src = xT if li == 0 else XA
            last = li == n_layers - 1
            dst = yT if last else XA
            even = (l % 2 == 0)
            i2 = l // 2
            if even:
                w_in = w_in_even[i2]
                w_out = w_out_even[i2]
            else:
                w_in = w_in_odd[i2]
                w_out = w_out_odd[i2]
            phase1(G, l, src, w_in, QT, KT, VN, MIX)
            if stop_after == (l, 1):
                break
            if even:
                phase2_even(G, l, QT, KT, VN, MIX)
            else:
                phase2_odd(G, l, QT, KT, VN, MIX)
            if stop_after == (l, 2):
                break
            phase3(G, l, src, XB, MIX, AT, w_out, w_up[l])
            if stop_after == (l, 3):
                break
            phase4(G, l, XB, dst, AT, w_down[l])
        G.n_ops = P.n_ops
        G.n_waits = P.n_waits
    return nc


def bank(G, i):
    return G.PS[:, i * 512:(i + 1) * 512]


class Rot:
    def __init__(self, items):
        self.items = list(items)
        self.i = 0

    def next(self):
        v = self.items[self.i % len(self.items)]
        self.i += 1
        return v


def act(P, out, in_, func, bias=None, scale=None):
    kw = dict(out=out, in_=in_, func=func)
    if bias is not None:
        kw["bias"] = bias
    if scale is not None:
        kw["scale"] = scale
    return P.I("act", "activation", **kw)


def mm(P, out, lhsT, rhs, start, stop):
    return P.I("pe", "matmul", out=out, lhsT=lhsT, rhs=rhs, start=start, stop=stop)


def rms_rstd(G, sb_alloc, xs, nchunks, rstd, sqrot, lnv, ps_bank, inv_n):
    P = G.P
    for c in range(nchunks):
        sq = sqrot.next()
        act(P, sq[:, :], xs[:, c, :], AF.Square)
        mm(P, ps_bank, G.ones_bb[:, :], sq[:, :], c == 0, c == nchunks - 1)
    act(P, lnv[:, :], ps_bank, AF.Ln, bias=G.eps[:, 0:1], scale=inv_n)
    act(P, rstd[:, :], lnv[:, :], AF.Exp, scale=-0.5)


def load_w(G, wsb, w_ap, nk, eng="pool"):
    wr = w_ap.rearrange("(c p) n -> p c n", p=128)
    for c in range(nk):
        G.P.dma(eng, out=wsb[:, c, :], in_=wr[:, c, :], key=wsb.name + "_%d" % (c % 4))


def phase1(G, l, src, w_in, QT, KT, VN, MIX):
    nc, P, vec = G.nc, G.P, G.vec
    even = (l % 2 == 0)
    i2 = l // 2
    NIN = 3072 if even else 2560
    srcr = src.rearrange("(c p) t -> p c t", p=128)
    with ExitStack() as es:
        sb = lambda n, s, d: es.enter_context(nc.sbuf_tensor("p1_%d_" % l + n, s, d))
        W = sb("W", [128, 8, NIN], BF16)
        load_w(G, W, w_in, 8)
        xbuf = [sb("x%d" % i, [128, 8, TT], F32) for i in range(2)]
        hbuf = [sb("h%d" % i, [128, 8, TT], BF16) for i in range(2)]
        sqrot = Rot([sb("sq%d" % i, [128, TT], BF16) for i in range(2)])
        sqfrot = Rot([sb("sqf%d" % i, [128, TT], F32) for i in range(2)])
        lnv = sb("lnv", [128, TT], F32)
        rstd = sb("rstd", [128, TT], F32)
        qkrot = Rot([sb("qk%d" % i, [128, TT], BF16) for i in range(4)])
        vrot = Rot([sb("v%d" % i, [128, 512], BF16) for i in range(2)])
        yrot = Rot([sb("y%d" % i, [128, TT], BF16) for i in range(2)])
        brot = Rot([1, 2, 3, 4, 5, 6, 7])
        if even:
            gcrot = Rot([sb("gc%d" % i, [128, TT], F32) for i in range(2)])
            Z = [sb("Z%d" % i, [128, TT + 2], F32) for i in range(4)]
            trot = Rot([sb("t%d" % i, [128, TT], F32) for i in range(2)])
            for i in range(4):
                P.I("pool", "memset", ap=Z[i][:, 0:2], constant=0.0)
        else:
            sgrot = Rot([sb("sg%d" % i, [128, TT], F32) for i in range(2)])
            GC = [sb("GC%d" % i, [128, TT + 30], BF16) for i in range(4)]
            diag = sb("diag", [128, 124, 128], BF16)
            Y = sb("Y", [128, 4, TT], F32)
            mean = sb("mean", [128, TT], F32)
            msq = sb("msq", [128, TT], F32)
            var = sb("var", [128, TT], F32)
            lnr = sb("lnr", [128, TT], F32)
            rs2 = sb("rs2", [128, TT], F32)
            ynrot = Rot([sb("yn%d" % i, [128, TT], F32) for i in range(2)])
            for i in range(4):
                P.I("pool", "memset", ap=GC[i][:, 0:30], constant=0.0)
                for j in range(31):
                    P.I("dve", "tensor_scalar", out=diag[:, i * 31 + j, :], in0=G.ident[:, :],
                        scalar1=_vcol(vec, WCC0 + (i2 * 31 + j) * 4 + i), scalar2=None, op0=ALU.mult)

        def proj(h, oc):
            b = bank(G, brot.next())
            for c in range(8):
                mm(P, b, W[:, c, oc * 128:(oc + 1) * 128], h[:, c, :], c == 0, c == 7)
            return b

        P.dma("sp", out=xbuf[0][:, :, :], in_=srcr[:, :, 0:TT])
        for tt in range(NT):
            t0 = tt * TT
            xs = xbuf[tt % 2]
            h = hbuf[tt % 2]
            if tt + 1 < NT:
                P.dma("sp", out=xbuf[(tt + 1) % 2][:, :, :], in_=srcr[:, :, t0 + TT:t0 + 2 * TT])
            rms_rstd(G, sb, xs, 8, rstd, sqrot, lnv, bank(G, 0), 1.0 / D)
            for c in range(8):
                P.I("dve", "scalar_tensor_tensor", out=h[:, c, :], in0=xs[:, c, :],
                    scalar=_vcol(vec, GN0 + (l * 4 + 0) * 8 + c), in1=rstd[:, :], op0=ALU.mult, op1=ALU.mult)
            if even:
                qoc, koc, vcol = 12, 16, 2560
                for i in range(4):
                    pc = proj(h, 4 + i)
                    gcb = gcrot.next()
                    act(P, gcb[:, :], pc, AF.Copy)
                    px = proj(h, 8 + i)
                    if tt > 0:
                        P.I("pool", "tensor_copy", out=Z[i][:, 0:2], in_=Z[i][:, TT:TT + 2])
                    P.I("dve", "tensor_tensor", out=Z[i][:, 2:TT + 2], in0=px, in1=gcb[:, :], op=ALU.mult)
                    pb = proj(h, i)
                    t = trot.next()
                    wc = lambda j: _vcol(vec, WSC0 + (i2 * 3 + j) * 4 + i)
                    P.I("dve", "tensor_scalar", out=t[:, :], in0=Z[i][:, 0:TT], scalar1=wc(0), scalar2=None, op0=ALU.mult)
                    P.I("dve", "scalar_tensor_tensor", out=t[:, :], in0=Z[i][:, 1:TT + 1], scalar=wc(1), in1=t[:, :], op0=ALU.mult, op1=ALU.add)
                    P.I("dve", "scalar_tensor_tensor", out=t[:, :], in0=Z[i][:, 2:TT + 2], scalar=wc(2), in1=t[:, :], op0=ALU.mult, op1=ALU.add)
                    ya = yrot.next()
                    P.I("dve", "tensor_tensor", out=ya[:, :], in0=pb, in1=t[:, :], op=ALU.mult)
                    P.dma("sp", out=MIX[i * 128:(i + 1) * 128, t0:t0 + TT], in_=ya[:, :])
            else:
                qoc, koc, vcol = 8, 12, 2048
                for i in range(4):
                    pg = proj(h, 4 + i)
                    sg = sgrot.next()
                    act(P, sg[:, :], pg, AF.Sigmoid)
                    pa = proj(h, i)
                    if tt > 0:
                        P.I("pool", "tensor_copy", out=GC[i][:, 0:30], in_=GC[i][:, TT:TT + 30])
                    P.I("dve", "tensor_tensor", out=GC[i][:, 30:TT + 30], in0=pa, in1=sg[:, :], op=ALU.mult)
                    pcv = bank(G, brot.next())
                    for j in range(31):
                        mm(P, pcv, diag[:, i * 31 + j, :], GC[i][:, j:j + TT], j == 0, j == 30)
                    act(P, Y[:, i, :], pcv, AF.Identity, bias=_vcol(vec, BCC0 + i2 * 4 + i), scale=1.0)
                sbk = bank(G, brot.next())
                sqk = bank(G, brot.next())
                for i in range(4):
                    mm(P, sbk, G.ones_f[:, :], Y[:, i, :], i == 0, i == 3)
                for i in range(4):
                    sq = sqfrot.next()
                    act(P, sq[:, :], Y[:, i, :], AF.Square)
                    mm(P, sqk, G.ones_f[:, :], sq[:, :], i == 0, i == 3)
                act(P, mean[:, :], sbk, AF.Copy, scale=1.0 / 512)
                P.I("dve", "tensor_tensor", out=msq[:, :], in0=mean[:, :], in1=mean[:, :], op=ALU.mult)
                P.I("dve", "scalar_tensor_tensor", out=var[:, :], in0=sqk, scalar=1.0 / 512, in1=msq[:, :], op0=ALU.mult, op1=ALU.subtract)
                act(P, lnr[:, :], var[:, :], AF.Ln, bias=G.eps[:, 0:1], scale=1.0)
                act(P, rs2[:, :], lnr[:, :], AF.Exp, scale=-0.5)
                for i in range(4):
                    yn = ynrot.next()
                    P.I("dve", "tensor_tensor", out=yn[:, :], in0=Y[:, i, :], in1=mean[:, :], op=ALU.subtract)
                    P.I("dve", "tensor_tensor", out=yn[:, :], in0=yn[:, :], in1=rs2[:, :], op=ALU.mult)
                    yc = yrot.next()
                    act(P, yc[:, :], yn[:, :], AF.Silu, bias=_vcol(vec, LNB0 + i2 * 4 + i), scale=_vcol(vec, LNG0 + i2 * 4 + i))
                    P.dma("sp", out=MIX[i * 128:(i + 1) * 128, t0:t0 + TT], in_=yc[:, :])
            for i in range(4):
                pq = proj(h, qoc + i)
                qs = qkrot.next()
                act(P, qs[:, :], pq, AF.Copy, scale=0.125)
                P.dma("sp", out=QT[i * 128:(i + 1) * 128, t0:t0 + TT], in_=qs[:, :])
                pk = proj(h, koc + i)
                ks = qkrot.next()
                P.I("dve", "tensor_copy", out=ks[:, :], in_=pk)
                P.dma("sp", out=KT[i * 128:(i + 1) * 128, t0:t0 + TT], in_=ks[:, :])
            for sub in range(4):
                b = bank(G, brot.next())
                for c in range(8):
                    mm(P, b, h[:, c, sub * 128:(sub + 1) * 128], W[:, c, vcol:vcol + 512], c == 0, c == 7)
                vs = vrot.next()
                act(P, vs[:, :], b, AF.Copy)
                P.dma("sp", out=VN[t0 + sub * 128:t0 + (sub + 1) * 128, :], in_=vs[:, :])
        P.flush()


def phase2_even(G, l, QT, KT, VN, MIX):
    nc, P = G.nc, G.P
    with ExitStack() as es:
        sb = lambda n, s, d: es.enter_context(nc.sbuf_tensor("p2e_%d_" % l + n, s, d))
        VD = [sb("VD%d" % g, [128, 32, 512], BF16) for g in range(3)]
        bias = sb("bias", [128, 6144], F32)
        maskd = sb("maskd", [128, 256], F32)
        qsb = [sb("q%d" % i, [128, S], BF16) for i in range(1)]
        ksb = [sb("k%d" % i, [128, S], BF16) for i in range(1)]
        acc = sb("acc", [128, 2, S], F32)
        ysb = sb("ysb", [128, S], BF16)
        sbf = [sb("sbf%d" % i, [128, 1024], F32) for i in range(2)]
        Ab = [sb("A%d" % i, [128, 1024], BF16) for i in range(4)]
        for g, (win, d) in enumerate(DA_PAIRS):
            vr = VN.rearrange("(blk p r) f -> p blk r f", p=128, r=d)
            for blk in range(32 // d):
                P.dma("sp" if blk % 2 == 0 else "pool", out=VD[g][:, blk * d:(blk + 1) * d, :], in_=vr[:, blk, :, :],
                      key="VD%d_%d" % (g, blk % 4))
        P.dma("sp", out=bias[:, :], in_=G.biasT[:, :])
        P.dma("sp", out=maskd[:, :], in_=G.consts[:, C_MASKD:C_MASKD + 256])
        for gh in range(24):
            P.I("pool", "tensor_tensor", out=bias[:, gh * 256:(gh + 1) * 256], in0=bias[:, gh * 256:(gh + 1) * 256], in1=maskd[:, :], op=ALU.add)
        srot = Rot([0, 1])
        arot = Rot([0, 1, 2, 3])
        fr = Rot([0, 1])
        for hp in range(4):
            q = qsb[0]
            k = ksb[0]
            P.dma("sp", out=q[:, :], in_=QT[hp * 128:(hp + 1) * 128, :])
            P.dma("sp", out=k[:, :], in_=KT[hp * 128:(hp + 1) * 128, :])
            units = []
            for g, (win, d) in enumerate(DA_PAIRS):
                if d == 1:
                    for gi in range(8):
                        units.append(dict(g=g, d=d, blocks=[((gi * 4 + bi) * 128, 0) for bi in range(4)], aview=("c", gi * 512)))
                elif d == 4:
                    for gi in range(8):
                        units.append(dict(g=g, d=d, blocks=[(gi * 128, r) for r in range(4)], aview=("s4", gi * 512)))
                else:
                    for sg in range(2):
                        for rq in range(4):
                            units.append(dict(g=g, d=d, blocks=[(sg * 128, rq * 4 + bi) for bi in range(4)], aview=("s16", sg * 2048, rq)))
            for ui, u in enumerate(units):
                u["o_bank"] = 4 + (ui % 2) * 2

            def stageA(u):
                g, d, blocks = u["g"], u["d"], u["blocks"]
                u["A"] = []
                for e in range(2):
                    hh = hp * 2 + e
                    hr = slice(e * 64, (e + 1) * 64)
                    sbase = srot.next() * 1024
                    Sps = G.PS[:, sbase:sbase + 1024]
                    for bi, (m0, r) in enumerate(blocks):
                        qs0 = m0 * d + r
                        for jt in range(2):
                            mk0 = m0 - 128 + jt * 128
                            if mk0 < 0:
                                continue
                            ks0 = mk0 * d + r
                            c0 = sbase + (bi * 2 + jt) * 128
                            mm(P, G.PS[:, c0:c0 + 128], k[hr, ks0:ks0 + 127 * d + 1:d], q[hr, qs0:qs0 + 127 * d + 1:d], True, True)
                    sf = sbf[fr.next()]
                    bsl = bias[:, (g * 8 + hh) * 256:(g * 8 + hh + 1) * 256].unsqueeze(1).broadcast_to([128, 4, 256])
                    P.I("dve", "tensor_tensor", out=sf[:, :].rearrange("p (b f) -> p b f", b=4),
                        in0=Sps.rearrange("p (b f) -> p b f", b=4), in1=bsl, op=ALU.add)
                    A = Ab[arot.next()]
                    act(P, A[:, :], sf[:, :], AF.Exp)
                    u["A"].append(A)

            def stageB(u):
                g, d, blocks = u["g"], u["d"], u["blocks"]
                o_bank = u["o_bank"]
                den_bank = o_bank + 1
                for e in range(2):
                    hh = hp * 2 + e
                    hr = slice(e * 64, (e + 1) * 64)
                    A = u["A"][e]
                    for bi, (m0, r) in enumerate(blocks):
                        first = True
                        for jt in range(2):
                            mk0 = m0 - 128 + jt * 128
                            if mk0 < 0:
                                continue
                            tile = (mk0 // 128) * d + r
                            asl = A[:, (bi * 2 + jt) * 128:(bi * 2 + jt) * 128 + 128]
                            co = o_bank * 512 + bi * 128
                            cd = den_bank * 512 + bi * 128
                            mm(P, G.PS[hr, co:co + 128], VD[g][:, tile, hh * 64:(hh + 1) * 64], asl, first, jt == 1)
                            mm(P, G.PS[hr, cd:cd + 128], G.ones_b[:, :], asl, first, jt == 1)
                            first = False

            def stageC(u):
                aview = u["aview"]
                o_bank = u["o_bank"]
                for w_, bk in ((0, o_bank), (1, o_bank + 1)):
                    src_ps = bank(G, bk)
                    if aview[0] == "c":
                        c0 = aview[1]
                        if w_ == 0:
                            act(P, acc[:, w_, c0:c0 + 512], src_ps, AF.Copy)
                        else:
                            P.I("dve", "tensor_copy", out=acc[:, w_, c0:c0 + 512], in_=src_ps)
                    elif aview[0] == "s4":
                        c0 = aview[1]
                        av = acc[:, w_, c0:c0 + 512].rearrange("p (i r) -> p r i", r=4)
                        P.I("dve", "tensor_tensor", out=av, in0=src_ps.rearrange("p (r i) -> p r i", r=4), in1=av, op=ALU.add)
                    else:
                        c0, rq = aview[1], aview[2]
                        av = acc[:, w_, c0:c0 + 2048].rearrange("p (i r) -> p r i", r=16)[:, rq * 4:(rq + 1) * 4, :]
                        P.I("dve", "tensor_tensor", out=av, in0=src_ps.rearrange("p (r i) -> p r i", r=4), in1=av, op=ALU.add)

            nu = len(units)
            for it in range(nu + 2):
                if it < nu:
                    stageA(units[it])
                if 0 <= it - 1 < nu:
                    stageB(units[it - 1])
                if 0 <= it - 2 < nu:
                    stageC(units[it - 2])
            for qh in range(4):
                cs = slice(qh * 1024, (qh + 1) * 1024)
                act(P, acc[:, 1, cs], acc[:, 1, cs], AF.Ln)
                act(P, acc[:, 1, cs], acc[:, 1, cs], AF.Exp, scale=-1.0)
                P.I("pool", "tensor_tensor", out=ysb[:, cs], in0=acc[:, 0, cs], in1=acc[:, 1, cs], op=ALU.mult)
            P.dma("sp", out=MIX[512 + hp * 128:512 + (hp + 1) * 128, :], in_=ysb[:, :])
        P.flush()


def phase2_odd(G, l, QT, KT, VN, MIX):
    nc, P = G.nc, G.P
    with ExitStack() as es:
        sb = lambda n, s, d: es.enter_context(nc.sbuf_tensor("p2o_%d_" % l + n, s, d))
        V = sb("V", [128, 32, 512], BF16)
        masks = sb("masks", [128, 4, 512], BF16)
        qsb = [sb("q%d" % i, [128, S], BF16) for i in range(2)]
        ksb = [sb("k%d" % i, [128, S], BF16) for i in range(2)]
        ysb = [sb("ysb%d" % i, [128, S], BF16) for i in range(2)]
        NSL = 4
        E = [[sb("E%d_%d" % (i, j), [128, 512], F32) for j in range(2)] for i in range(NSL)]
        SP = [[sb("SP%d_%d" % (i, j), [128, 512], BF16) for j in range(3)] for i in range(NSL)]
        Gt = [[sb("G%d_%d" % (i, j), [128, 512], F32) for j in range(2)] for i in range(NSL)]
        At = [[sb("At%d_%d" % (i, j), [128, 512], BF16) for j in range(2)] for i in range(NSL)]
        vr = VN.rearrange("(t p) f -> p t f", p=128)
        for qq in range(4):
            P.dma("sp", out=V[:, qq * 8:(qq + 1) * 8, :], in_=vr[:, qq * 8:(qq + 1) * 8, :], key="Vo_%d" % qq)
        P.dma("pool", out=masks[:, :, :], in_=G.consts[:, C_MASKS:C_MASKS + 2048].rearrange("p (o f) -> p o f", o=4))
        P.dma("sp", out=qsb[0][:, :], in_=QT[0:128, :])
        P.dma("sp", out=ksb[0][:, :], in_=KT[0:128, :])
        zrot = Rot([0, 1])
        slot_qts = [[7, 4, 3, 0], [7, 4, 3, 0], [6, 5, 2, 1], [6, 5, 2, 1]]
        for hp in range(4):
            q = qsb[hp % 2]
            k = ksb[hp % 2]
            ys = ysb[hp % 2]
            if hp + 1 < 4:
                P.dma("sp", out=qsb[(hp + 1) % 2][:, :], in_=QT[(hp + 1) * 128:(hp + 2) * 128, :])
                P.dma("sp", out=ksb[(hp + 1) % 2][:, :], in_=KT[(hp + 1) * 128:(hp + 2) * 128, :])
            streams = []
            for s_ in range(NSL):
                e = s_ % 2
                st = []
                for qt in slot_qts[s_]:
                    n = 4 * qt + 4
                    for j in range(n):
                        st.append(dict(e=e, qt=qt, kb=4 * qt + 3 - j, first=(j == 0), last=(j == n - 1), idx=len(st)))
                streams.append(st)
            nit = max(len(st) for st in streams) + 2

            def stageA(s_, stp):
                e, qt, kb, idx = stp["e"], stp["qt"], stp["kb"], stp["idx"]
                hr = slice(e * 64, (e + 1) * 64)
                zb = bank(G, zrot.next())
                mm(P, zb, k[hr, kb * 128:(kb + 1) * 128], q[hr, qt * 512:(qt + 1) * 512], True, True)
                Ec = E[s_][idx % 2]
                act(P, Ec[:, :], zb, AF.Exp)
                spc = SP[s_][idx % 3]
                act(P, spc[:, :], Ec[:, :], AF.Ln, bias=G.one[:, 0:1], scale=1.0)
                o = kb - 4 * qt
                if o >= 0:
                    P.I("pool", "tensor_tensor", out=spc[:, :], in0=spc[:, :], in1=masks[:, o, :], op=ALU.mult)
                    P.I("pool", "tensor_tensor", out=Ec[:, :], in0=Ec[:, :], in1=masks[:, o, :], op=ALU.mult)

            def stageB(s_, stp):
                idx = stp["idx"]
                pb = bank(G, 2 + s_)
                spc = SP[s_][idx % 3]
                spp = SP[s_][(idx - 1) % 3]
                if stp["first"]:
                    mm(P, pb, G.tri[:, :], spc[:, :], True, True)
                else:
                    mm(P, pb, G.su[:, :], spp[:, :], False, False)
                    mm(P, pb, G.tri[:, :], spc[:, :], False, True)
                Gc = Gt[s_][idx % 2]
                act(P, Gc[:, :], pb, AF.Exp, scale=-1.0)
                P.I("dve", "tensor_tensor", out=At[s_][idx % 2][:, :], in0=E[s_][idx % 2][:, :], in1=Gc[:, :], op=ALU.mult)

            def stageC(s_, stp):
                e, qt, kb, idx = stp["e"], stp["qt"], stp["kb"], stp["idx"]
                hh = hp * 2 + e
                hr = slice(e * 64, (e + 1) * 64)
                ob = G.PS[hr, (6 + s_ // 2) * 512:(7 + s_ // 2) * 512]
                mm(P, ob, V[:, kb, hh * 64:(hh + 1) * 64], At[s_][idx % 2][:, :], stp["first"], stp["last"])
                if stp["last"]:
                    P.I("dve", "tensor_copy", out=ys[hr, qt * 512:(qt + 1) * 512], in_=ob)

            for it in range(nit):
                for s_ in range(NSL):
                    if it < len(streams[s_]):
                        stageA(s_, streams[s_][it])
                for s_ in range(NSL):
                    if 0 <= it - 1 < len(streams[s_]):
                        stageB(s_, streams[s_][it - 1])
                for s_ in range(NSL):
                    if 0 <= it - 2 < len(streams[s_]):
                        stageC(s_, streams[s_][it - 2])
            P.dma("sp", out=MIX[512 + hp * 128:512 + (hp + 1) * 128, :], in_=ys[:, :])
        P.flush()


def post_norm_residual(G, l, j, M, xs, rstd, sqrot, lnv, ps_bank, tmprot):
    P, vec = G.P, G.vec
    rms_rstd(G, None, M, 8, rstd, sqrot, lnv, ps_bank, 1.0 / D)
    for c in range(8):
        tmp = tmprot.next()
        P.I("dve", "scalar_tensor_tensor", out=tmp[:, :], in0=M[:, c, :], scalar=_vcol(vec, GN0 + (l * 4 + j) * 8 + c),
            in1=rstd[:, :], op0=ALU.mult, op1=ALU.mult)
        P.I("pool", "tensor_tensor", out=xs[:, c, :], in0=xs[:, c, :], in1=tmp[:, :], op=ALU.add)


def interleave(main, side, offset=1, stride=1):
    k = 0
    for i, f in enumerate(main):
        f()
        while k < len(side) and i >= offset + k * stride:
            side[k]()
            k += 1
    while k < len(side):
        side[k]()
        k += 1


def interleave_events(main, events):
    ev = sorted(enumerate(events), key=lambda t: (t[1][0], t[0]))
    k = 0
    for i, f in enumerate(main):
        f()
        while k < len(ev) and ev[k][1][0] < i + 1:
            ev[k][1][1]()
            k += 1
    while k < len(ev):
        ev[k][1][1]()
        k += 1


def phase3(G, l, src, XB, MIX, AT, w_out, w_up):
    nc, P, vec = G.nc, G.P, G.vec
    srcr = src.rearrange("(c p) t -> p c t", p=128)
    xbr = XB.rearrange("(c p) t -> p c t", p=128)
    mixr = MIX.rearrange("(c p) t -> p c t", p=128)
    with ExitStack() as es:
        sb = lambda n, s, d: es.enter_context(nc.sbuf_tensor("p3_%d_" % l + n, s, d))
        WO = sb("WO", [128, 8, 1024], BF16)
        WU = sb("WU", [128, 8, 2 * DFF], BF16)
        load_w(G, WO, w_out, 8)
        load_w(G, WU, w_up, 8)
        xs = sb("x", [128, 8, TT], F32)
        mbuf = [sb("mix%d" % i, [128, 8, TT], BF16) for i in range(2)]
        M = sb("M", [128, 8, TT], F32)
        h2b = [sb("h2_%d" % i, [128, 8, TT], BF16) for i in range(2)]
        sqrot = Rot([sb("sq%d" % i, [128, TT], BF16) for i in range(6)])
        sq2rot = Rot([sb("sqb%d" % i, [128, TT], BF16) for i in range(6)])
        tmprot = Rot([sb("tmp%d" % i, [128, TT], F32) for i in range(2)])
        rstd = sb("rstd", [128, TT], F32)
        Gb = [sb("Gb%d" % i, [128, TT + 2], F32) for i in range(2)]
        H = sb("H", [128, NFC, 2], F32)
        trot = Rot([sb("t%d" % i, [128, TT], F32) for i in range(2)])
        srot = Rot([sb("s%d" % i, [128, TT], F32) for i in range(2)])
        arot = Rot([sb("a%d" % i, [128, TT], BF16) for i in range(3)])
        brot = Rot([2, 3, 4, 5, 6, 7])
        P.I("pool", "memset", ap=H[:, :, :], constant=0.0)
        P.dma("sp", out=mbuf[0][:, :, :], in_=mixr[:, :, 0:TT])

        OWN = {}

        def A_events(tt):
            t0 = tt * TT
            mx = mbuf[tt % 2]
            h2 = h2b[tt % 2]
            ev = []

            def a_load():
                P.dma("sp", out=xs[:, :, :], in_=srcr[:, :, t0:t0 + TT])
                if tt + 1 < NT:
                    P.dma("sp", out=mbuf[(tt + 1) % 2][:, :, :], in_=mixr[:, :, t0 + TT:t0 + 2 * TT])
            ev.append((0.0, a_load))
            sqs = {}

            def mk_oc(oc):
                def f():
                    b = bank(G, brot.next())
                    for c in range(8):
                        mm(P, b, WO[:, c, oc * 128:(oc + 1) * 128], mx[:, c, :], c == 0, c == 7)
                    act(P, M[:, oc, :], b, AF.Copy)
                    sq = sqrot.next()
                    act(P, sq[:, :], M[:, oc, :], AF.Square)
                    sqs[oc] = sq
                    OWN[sq.name] = ("a", tt, oc)
                return f

            def mk_st1(oc):
                def f():
                    assert OWN[sqs[oc].name] == ("a", tt, oc)
                    mm(P, bank(G, 0), G.ones_bb[:, :], sqs[oc][:, :], oc == 0, oc == 7)
                return f
            for oc in range(8):
                ev.append((0.1 + 0.5 * oc, mk_oc(oc)))
                ev.append((0.1 + 0.5 * oc + 2.6, mk_st1(oc)))

            def a_rstd1():
                act(P, rstd[:, :], bank(G, 0), AF.Ln, bias=G.eps[:, 0:1], scale=1.0 / D)
                act(P, rstd[:, :], rstd[:, :], AF.Exp, scale=-0.5)
            ev.append((6.4, a_rstd1))
            sq2 = {}

            def mk_res(c):
                def f():
                    tmp = tmprot.next()
                    P.I("dve", "scalar_tensor_tensor", out=tmp[:, :], in0=M[:, c, :], scalar=_vcol(vec, GN0 + (l * 4 + 1) * 8 + c),
                        in1=rstd[:, :], op0=ALU.mult, op1=ALU.mult)
                    P.I("dve", "tensor_tensor", out=xs[:, c, :], in0=xs[:, c, :], in1=tmp[:, :], op=ALU.add)
                    sq = sq2rot.next()
                    act(P, sq[:, :], xs[:, c, :], AF.Square)
                    sq2[c] = sq
                    OWN[sq.name] = ("b", tt, c)
                return f

            def mk_st2(c):
                def f():
                    assert OWN[sq2[c].name] == ("b", tt, c)
                    mm(P, bank(G, 1), G.ones_bb[:, :], sq2[c][:, :], c == 0, c == 7)
                return f
            for c in range(8):
                ev.append((7.5 + 0.45 * c, mk_res(c)))
                ev.append((7.5 + 0.45 * c + 2.5, mk_st2(c)))

            def a_rstd2():
                P.dma("sp", out=xbr[:, :, t0:t0 + TT], in_=xs[:, :, :])
                act(P, rstd[:, :], bank(G, 1), AF.Ln, bias=G.eps[:, 0:1], scale=1.0 / D)
                act(P, rstd[:, :], rstd[:, :], AF.Exp, scale=-0.5)
            ev.append((13.2, a_rstd2))

            def mk_h(c0):
                def f():
                    for c in (c0, c0 + 1):
                        P.I("dve", "scalar_tensor_tensor", out=h2[:, c, :], in0=xs[:, c, :],
                            scalar=_vcol(vec, GN0 + (l * 4 + 2) * 8 + c), in1=rstd[:, :], op0=ALU.mult, op1=ALU.mult)
                return f
            for j, c0 in enumerate(range(0, 8, 2)):
                ev.append((14.5 + 0.5 * j, mk_h(c0)))
            return ev

        def F_closures(tt):
            t0 = tt * TT
            h2 = h2b[tt % 2]
            cl = []

            def mk_fc(fc):
                def f():
                    pg = bank(G, brot.next())
                    for c in range(8):
                        mm(P, pg, WU[:, c, fc * 128:(fc + 1) * 128], h2[:, c, :], c == 0, c == 7)
                    pu = bank(G, brot.next())
                    for c in range(8):
                        mm(P, pu, WU[:, c, DFF + fc * 128:DFF + (fc + 1) * 128], h2[:, c, :], c == 0, c == 7)
                    gb = Gb[fc % 2]
                    P.I("pool", "tensor_copy", out=gb[:, 0:2], in_=H[:, fc, :])
                    act(P, gb[:, 2:TT + 2], pg, AF.Copy)
                    P.I("pool", "tensor_copy", out=H[:, fc, :], in_=gb[:, TT:TT + 2])
                    t = trot.next()
                    wc = lambda j: _vcol(vec, WFC0 + (l * 3 + j) * NFC + fc)
                    P.I("dve", "tensor_scalar", out=t[:, :], in0=gb[:, 0:TT], scalar1=wc(0), scalar2=_vcol(vec, BFC0 + l * NFC + fc),
                        op0=ALU.mult, op1=ALU.add)
                    P.I("dve", "scalar_tensor_tensor", out=t[:, :], in0=gb[:, 1:TT + 1], scalar=wc(1), in1=t[:, :], op0=ALU.mult, op1=ALU.add)
                    P.I("dve", "scalar_tensor_tensor", out=t[:, :], in0=gb[:, 2:TT + 2], scalar=wc(2), in1=t[:, :], op0=ALU.mult, op1=ALU.add)
                    sv = srot.next()
                    act(P, sv[:, :], t[:, :], AF.Silu)
                    a = arot.next()
                    P.I("dve", "tensor_tensor", out=a[:, :], in0=pu, in1=sv[:, :], op=ALU.mult)
                    P.dma("sp", out=AT[fc * 128:(fc + 1) * 128, t0:t0 + TT], in_=a[:, :])
                return f
            for fc in range(NFC):
                cl.append(mk_fc(fc))
            return cl

        interleave_events([], A_events(0))
        for tt in range(NT):
            side = A_events(tt + 1) if tt + 1 < NT else []
            interleave_events(F_closures(tt), side)
        P.flush()


def phase4(G, l, XB, dst, AT, w_down):
    nc, P, vec = G.nc, G.P, G.vec
    xbr = XB.rearrange("(c p) t -> p c t", p=128)
    dstr = dst.rearrange("(c p) t -> p c t", p=128)
    atr = AT.rearrange("(c p) t -> p c t", p=128)
    with ExitStack() as es:
        sb = lambda n, s, d: es.enter_context(nc.sbuf_tensor("p4_%d_" % l + n, s, d))
        WD = sb("WD", [128, NFC, 1024], BF16)
        load_w(G, WD, w_down, NFC)
        xbuf = [sb("x%d" % i, [128, 8, TT], F32) for i in range(2)]
        abuf = [sb("a%d" % i, [128, NFC, TT], BF16) for i in range(2)]
        Mb = [sb("M%d" % i, [128, 8, TT], F32) for i in range(2)]
        sqrot = Rot([sb("sq%d" % i, [128, TT], BF16) for i in range(8)])
        tmprot = Rot([sb("tmp%d" % i, [128, TT], F32) for i in range(2)])
        lnv = sb("lnv", [128, TT], F32)
        rstd = sb("rstd", [128, TT], F32)
        brot = Rot([1, 2, 3, 4, 5, 6, 7])
        P.dma("sp", out=abuf[0][:, :, :], in_=atr[:, :, 0:TT])
        P.dma("sp", out=xbuf[0][:, :, :], in_=xbr[:, :, 0:TT])
        P.dma("sp", out=xbuf[1][:, :, :], in_=xbr[:, :, TT:2 * TT])

        def D_closures(tt):
            t0 = tt * TT
            a = abuf[tt % 2]
            M = Mb[tt % 2]
            cl = []

            def d_load():
                if tt + 1 < NT:
                    P.dma("sp", out=abuf[(tt + 1) % 2][:, :, :], in_=atr[:, :, t0 + TT:t0 + 2 * TT])
            cl.append(d_load)

            def mk_oc(oc):
                def f():
                    b = bank(G, brot.next())
                    for c in range(NFC):
                        mm(P, b, WD[:, c, oc * 128:(oc + 1) * 128], a[:, c, :], c == 0, c == NFC - 1)
                    act(P, M[:, oc, :], b, AF.Copy)
                return f
            for oc in range(8):
                cl.append(mk_oc(oc))
            return cl

        OWN = {}

        def N_events(tt):
            t0 = tt * TT
            xs = xbuf[tt % 2]
            M = Mb[tt % 2]
            ev = []
            sqs = {}

            def n_sq(c0):
                def f():
                    for c in range(c0, c0 + 4):
                        sq = sqrot.next()
                        act(P, sq[:, :], M[:, c, :], AF.Square)
                        sqs[c] = sq
                        OWN[sq.name] = (tt, c)
                return f

            def n_mm(c0):
                def f():
                    for c in range(c0, c0 + 4):
                        assert OWN[sqs[c].name] == (tt, c)
                        mm(P, bank(G, 0), G.ones_bb[:, :], sqs[c][:, :], c == 0, c == 7)
                return f

            def n_rstd():
                act(P, lnv[:, :], bank(G, 0), AF.Ln, bias=G.eps[:, 0:1], scale=1.0 / D)
                act(P, rstd[:, :], lnv[:, :], AF.Exp, scale=-0.5)
            ev.append((0.0, n_sq(0)))
            ev.append((0.5, n_sq(4)))
            ev.append((2.2, n_mm(0)))
            ev.append((2.6, n_mm(4)))
            ev.append((3.0, n_rstd))

            def n_res(c0):
                def f():
                    for c in range(c0, c0 + 4):
                        tmp = tmprot.next()
                        P.I("dve", "scalar_tensor_tensor", out=tmp[:, :], in0=M[:, c, :], scalar=_vcol(vec, GN0 + (l * 4 + 3) * 8 + c),
                            in1=rstd[:, :], op0=ALU.mult, op1=ALU.mult)
                        P.I("pool", "tensor_tensor", out=xs[:, c, :], in0=xs[:, c, :], in1=tmp[:, :], op=ALU.add)
                    if c0 == 4:
                        P.dma("sp", out=dstr[:, :, t0:t0 + TT], in_=xs[:, :, :])
                return f
            ev.append((4.0, n_res(0)))
            ev.append((5.0, n_res(4)))
            return ev

        for f in D_closures(0):
            f()
        for tt in range(NT):
            main = D_closures(tt + 1) if tt + 1 < NT else []
            interleave_events(main, N_events(tt))
            if tt + 2 < NT:
                P.dma("sp", out=xbuf[tt % 2][:, :, :], in_=xbr[:, :, (tt + 2) * TT:(tt + 3) * TT])
        P.flush()


def _t5_bucket(dist):
    max_exact = 16
    d = np.maximum(dist, 1).astype(np.float32)
    large = max_exact + (np.log(d / np.float32(max_exact)) / np.float32(math.log(2048 / 16)) * np.float32(16)).astype(np.int32)
    large = np.minimum(large, 31)
    return np.where(dist < max_exact, dist, large)


def _host_tables(norm_g, rel_bias, w_sc, w_cc, b_cc, ln_cc_g, ln_cc_b, w_ffn_conv, b_ffn_conv):
    vecs = np.zeros((128, NV), np.float32)

    def put(col0, arr):
        a = np.asarray(arr, np.float32)
        lead = a.shape[:-1]
        C = a.shape[-1] // 128
        a = a.reshape(lead + (C, 128))
        a = np.moveaxis(a, -1, 0).reshape(128, -1)
        vecs[:, col0:col0 + a.shape[1]] = a

    put(GN0, norm_g)
    put(WSC0, w_sc)
    put(WCC0, w_cc)
    put(BCC0, b_cc)
    put(LNG0, ln_cc_g)
    put(LNB0, ln_cc_b)
    put(WFC0, w_ffn_conv)
    put(BFC0, b_ffn_conv)

    consts = np.zeros((128, NC_), np.float32)
    p = np.arange(128)[:, None]
    i = np.arange(128)[None, :]
    consts[:, C_MASKD:C_MASKD + 128] = np.where(i <= p, 0.0, NEG)
    consts[:, C_MASKD + 128:C_MASKD + 256] = np.where(i >= p, 0.0, NEG)
    iq = np.arange(512)[None, :]
    for o in range(4):
        consts[:, C_MASKS + o * 512:C_MASKS + (o + 1) * 512] = (o * 128 + p < iq).astype(np.float32)
    consts[:, C_TRI:C_TRI + 128] = (p >= i).astype(np.float32)
    consts[:, C_SU:C_SU + 128] = (p < i).astype(np.float32)
    consts[:, C_ID:C_ID + 128] = (p == i).astype(np.float32)

    rb = np.asarray(rel_bias, np.float32)
    biasT = np.zeros((128, 3, 8, 2, 128), np.float32)
    for g, (win, d) in enumerate(DA_PAIRS):
        for jt in range(2):
            rel = 128 + i - (jt * 128 + p)
            relc = np.clip(rel, 0, 128)
            bk = _t5_bucket(relc * d)
            biasT[:, g, :, jt, :] = np.transpose(rb[bk], (0, 2, 1))
    return vecs, consts, biasT.reshape(128, 6144)


_NC_CACHE = {}


def kernel(x, norm_g, rel_bias, w_in_even, w_out_even, w_sc, w_in_odd, w_out_odd,
           w_cc, b_cc, ln_cc_g, ln_cc_b, w_up, w_ffn_conv, b_ffn_conv, w_down):
    x = np.asarray(x, np.float32)
    B = x.shape[0]
    vecs, consts, biasT = _host_tables(norm_g, rel_bias, w_sc, w_cc, b_cc, ln_cc_g, ln_cc_b, w_ffn_conv, b_ffn_conv)
    if "nc" not in _NC_CACHE:
        _NC_CACHE["nc"] = build_program()
    nc = _NC_CACHE["nc"]
    shared = dict(
        vecs=vecs, consts=consts, biasT=biasT,
        w_in_even=np.ascontiguousarray(w_in_even, np.float32), w_out_even=np.ascontiguousarray(w_out_even, np.float32),
        w_in_odd=np.ascontiguousarray(w_in_odd, np.float32), w_out_odd=np.ascontiguousarray(w_out_odd, np.float32),
        w_up=np.ascontiguousarray(w_up, np.float32), w_down=np.ascontiguousarray(w_down, np.float32),
    )
    in_maps = []
    for b in range(B):
        m = dict(shared)
        m["xT"] = np.ascontiguousarray(x[b].T)
        in_maps.append(m)
    res = run_bass_kernel_spmd(nc, in_maps, core_ids=list(range(B)))
    out = np.stack([np.ascontiguousarray(r["yT"].T) for r in res.results], axis=0)
    return out.astype(np.float32)
```

```python
import math
from contextlib import ExitStack

import numpy as np
import concourse.bass as bass
import concourse.mybir as mybir
from concourse.bass_utils import run_bass_kernel_spmd

F32 = mybir.dt.float32
BF16 = mybir.dt.bfloat16
AF = mybir.ActivationFunctionType
ALU = mybir.AluOpType

S = 4096
D = 1024
NT = 8
TT = 512
DFF = 2816
NFC = 22
EPS = 1e-6
DA_PAIRS = ((128, 1), (512, 4), (2048, 16))
NEG = -30000.0

GN0 = 0
WSC0 = 128
WCC0 = 152
BCC0 = 400
LNG0 = 408
LNB0 = 416
WFC0 = 424
BFC0 = 688
NV = 776
C_MASKD = 0
C_MASKS = 256
C_TRI = 2304
C_SU = 2432
C_ID = 2560
NC_ = 2688


def _is_ap(v):
    return hasattr(v, "tensor") and hasattr(v, "offset") and hasattr(v, "ap")


def _box(ap):
    t = ap.tensor
    shape = list(t.shape)
    nd = len(shape)
    strides = [1] * nd
    for k in range(nd - 2, -1, -1):
        strides[k] = strides[k + 1] * shape[k + 1]
    off = ap.offset
    lo = []
    for k in range(nd):
        lo.append(off // strides[k])
        off = off % strides[k]
    ext = [0] * nd
    for step, cnt in ap.ap:
        if cnt <= 1 or step == 0:
            continue
        k = 0
        while k < nd - 1 and step < strides[k]:
            k += 1
        ext[k] += (step // strides[k]) * (cnt - 1)
        rem = step % strides[k]
        if rem != 0:
            for kk in range(k + 1, nd):
                ext[kk] = shape[kk]
    return (t.name, tuple((lo[k], lo[k] + ext[k] + 1) for k in range(nd)))


def _ovl(a, b):
    for (al, ah), (bl, bh) in zip(a[1], b[1]):
        if not (al < bh and bl < ah):
            return False
    return True


def _contains(a, b):
    for (al, ah), (bl, bh) in zip(a[1], b[1]):
        if not (al <= bl and bh <= ah):
            return False
    return True


class Op:
    __slots__ = ("eng", "emit", "deps", "is_dma", "sem", "val", "signal")

    def __init__(self, eng, emit):
        self.eng = eng
        self.emit = emit
        self.deps = []
        self.is_dma = False
        self.sem = None
        self.val = None
        self.signal = False


ENGS = ("pe", "act", "dve", "pool", "sp")


class Prog:
    def __init__(self, nc, es, n_dsem=60):
        self.nc = nc
        self.esem = {e: es.enter_context(nc.semaphore("s_" + e)) for e in ENGS}
        self.ecnt = {e: 0 for e in ENGS}
        self.bar = es.enter_context(nc.semaphore("s_bar"))
        self.barcnt = 0
        self.dpool = [[es.enter_context(nc.semaphore("d%d" % i)), 0] for i in range(n_dsem)]
        self.n_ops = 0
        self.n_waits = 0
        self._reset()

    def _reset(self):
        self.ops = {e: [] for e in ENGS}
        self.hist = {}
        self.dma_last = {}
        self.dkey2idx = {}
        self.all_ops = []

    def _track(self, op, reads, writes):
        deps = []
        rb_ = [_box(a) for a in reads]
        wb_ = [_box(a) for a in writes]
        for b in rb_:
            h = self.hist.setdefault(b[0], {"w": [], "r": []})
            for wb, wop in h["w"]:
                if _ovl(wb, b):
                    deps.append(wop)
        for b in wb_:
            h = self.hist.setdefault(b[0], {"w": [], "r": []})
            for wb, wop in h["w"]:
                if _ovl(wb, b):
                    deps.append(wop)
            for rb, rop in h["r"]:
                if _ovl(rb, b):
                    deps.append(rop)
        for b in rb_:
            h = self.hist[b[0]]
            if len(h["r"]) > 64:
                h["r"] = [(rb, rop) for rb, rop in h["r"] if not (rop.eng == op.eng and not rop.is_dma and not op.is_dma and _contains(b, rb))]
            h["r"].append((b, op))
        for b in wb_:
            h = self.hist[b[0]]
            h["w"] = [(wb, wop) for wb, wop in h["w"] if not _contains(b, wb)]
            h["r"] = [(rb, rop) for rb, rop in h["r"] if not _contains(b, rb) or rop is op]
            h["w"].append((b, op))
        seen = set()
        for d in deps:
            if d is op or id(d) in seen:
                continue
            if d.eng == "pe" and op.eng == "pe" and not d.is_dma and not op.is_dma:
                continue
            seen.add(id(d))
            op.deps.append(d)

    def I(self, eng, method, extra_reads=(), extra_writes=(), **kw):
        reads = list(extra_reads)
        writes = list(extra_writes)
        for k, v in kw.items():
            if _is_ap(v):
                if k in ("out", "accum_out", "ap"):
                    writes.append(v)
                else:
                    reads.append(v)

        def emit(e, method=method, kw=kw):
            return getattr(e, method)(**kw)

        op = Op(eng, emit)
        self._track(op, reads, writes)
        self.ops[eng].append(op)
        self.all_ops.append(op)
        return op

    def dma(self, eng, out, in_, key=None):
        o_dram = "dram" in str(out.tensor.space).lower()
        sb = in_ if o_dram else out
        if key is None:
            key = sb.tensor.name

        def emit(e, out=out, in_=in_):
            return e.dma_start(out=out, in_=in_)

        op = Op(eng, emit)
        op.is_dma = True
        self._track(op, [in_], [out])
        prev = self.dma_last.get(key)
        if prev is not None and all(prev is not d for d in op.deps):
            op.deps.append(prev)
        self.dma_last[key] = op
        idx = self.dkey2idx.setdefault(key, len(self.dkey2idx))
        ent = self.dpool[idx]
        ent[1] += 16
        op.sem = ent[0]
        op.val = ent[1]
        self.ops[eng].append(op)
        self.all_ops.append(op)
        return op

    def flush(self):
        nc = self.nc
        for op in self.all_ops:
            for d in op.deps:
                d.signal = True
        for e in ENGS:
            for op in reversed(self.ops[e]):
                if not op.is_dma:
                    op.signal = True
                    break
        for e in ENGS:
            for op in self.ops[e]:
                if (not op.is_dma) and op.signal:
                    self.ecnt[e] += 1
                    op.sem = self.esem[e]
                    op.val = self.ecnt[e]
        self.barcnt += 1
        barval = self.barcnt
        used_dsems = [self.dpool[i] for i in range(len(self.dkey2idx))]

        def run(e_name, eh):
            waited = {}
            for op in self.ops[e_name]:
                need = {}
                for d in op.deps:
                    k = id(d.sem)
                    if waited.get(k, 0) >= d.val:
                        continue
                    if k not in need or need[k][1] < d.val:
                        need[k] = (d.sem, d.val)
                for k, (s, v) in need.items():
                    eh.wait_ge(s, v)
                    waited[k] = v
                    self.n_waits += 1
                ins = op.emit(eh)
                if op.is_dma:
                    ins.then_inc(op.sem, 16)
                elif op.signal:
                    ins.then_inc(op.sem, 1)
                self.n_ops += 1
            if e_name == "sp":
                for e in ENGS:
                    if e != "sp" and self.ecnt[e] > 0:
                        eh.wait_ge(self.esem[e], self.ecnt[e])
                for sem, cnt in used_dsems:
                    if cnt > 0:
                        eh.wait_ge(sem, cnt)
                eh.sem_inc(self.bar, 1)
            else:
                eh.wait_ge(self.bar, barval)

        with nc.Block() as block:

            @block.sync
            def _(eh):
                run("sp", eh)

            @block.tensor
            def _(eh):
                run("pe", eh)

            @block.scalar
            def _(eh):
                run("act", eh)

            @block.vector
            def _(eh):
                run("dve", eh)

            @block.gpsimd
            def _(eh):
                run("pool", eh)

        self._reset()


class Ctx:
    pass


def _vcol(vec_sb, c):
    return vec_sb[:, c:c + 1]


def build_program(layers=(0, 1, 2, 3), debug=False, stop_after=None):
    nc = bass.Bass("TRN2", target_bir_lowering=False)

    def din(name, shape, dt=F32):
        return nc.dram_tensor(name, shape, dt, kind="ExternalInput").ap()

    scratch_kind = "ExternalOutput" if debug else "Internal"

    def dscr(name, shape, dt):
        return nc.dram_tensor(name, shape, dt, kind=scratch_kind).ap()

    xT = din("xT", [D, S])
    vecs = din("vecs", [128, NV])
    consts = din("consts", [128, NC_])
    biasT = din("biasT", [128, 6144])
    w_in_even = din("w_in_even", [2, 1024, 3072])
    w_out_even = din("w_out_even", [2, 1024, 1024])
    w_in_odd = din("w_in_odd", [2, 1024, 2560])
    w_out_odd = din("w_out_odd", [2, 1024, 1024])
    w_up = din("w_up", [4, 1024, 5632])
    w_down = din("w_down", [4, 2816, 1024])
    yT = nc.dram_tensor("yT", [D, S], F32, kind="ExternalOutput").ap()

    XA = dscr("XA", [D, S], F32)
    XB = dscr("XB", [D, S], F32)
    QT = dscr("QT", [512, S], BF16)
    KT = dscr("KT", [512, S], BF16)
    VN = dscr("VN", [S, 512], BF16)
    MIX = dscr("MIX", [D, S], BF16)
    AT = dscr("AT", [DFF, S], BF16)

    with ExitStack() as es0:
        P = Prog(nc, es0)
        G = Ctx()
        G.nc = nc
        G.P = P
        sb0 = lambda n, s, d: es0.enter_context(nc.sbuf_tensor(n, s, d))
        G.PS = es0.enter_context(nc.psum_tensor("PS", [128, 4096], F32))
        G.vec = sb0("vec_sb", [128, NV], F32)
        G.ones_f = sb0("ones_f", [128, 128], F32)
        G.ones_b = sb0("ones_b", [128, 64], BF16)
        G.ones_bb = sb0("ones_bb", [128, 128], BF16)
        G.eps = sb0("eps_t", [128, 1], F32)
        G.one = sb0("one_t", [128, 1], F32)
        G.tri = sb0("tri_b", [128, 128], BF16)
        G.su = sb0("su_b", [128, 128], BF16)
        G.ident = sb0("ident_b", [128, 128], BF16)
        G.consts = consts
        G.biasT = biasT

        P.dma("sp", out=G.vec[:, :], in_=vecs[:, :])
        P.dma("pool", out=G.tri[:, :], in_=consts[:, C_TRI:C_TRI + 128])
        P.dma("pool", out=G.su[:, :], in_=consts[:, C_SU:C_SU + 128])
        P.dma("pool", out=G.ident[:, :], in_=consts[:, C_ID:C_ID + 128])
        P.I("dve", "memset", ap=G.ones_f[:, :], constant=1.0)
        P.I("dve", "memset", ap=G.ones_b[:, :], constant=1.0)
        P.I("dve", "memset", ap=G.ones_bb[:, :], constant=1.0)
        P.I("dve", "memset", ap=G.eps[:, :], constant=EPS)
        P.I("dve", "memset", ap=G.one[:, :], constant=1.0)
        P.flush()

        n_layers = len(layers)
        for li, l in enumerate(layers):
            src = xT if li == 0 else XA
            last = li == n_layers - 1
            dst = yT if last else XA
            even = (l % 2 == 0)
            i2 = l // 2
            if even:
                w_in = w_in_even[i2]
                w_out = w_out_even[i2]
            else:
                w_in = w_in_odd[i2]
                w_out = w_out_odd[i2]
            phase1(G, l, src, w_in, QT, KT, VN, MIX)
            if stop_after == (l, 1):
                break
            if even:
                phase2_even(G, l, QT, KT, VN, MIX)
            else:
                phase2_odd(G, l, QT, KT, VN, MIX)
            if stop_after == (l, 2):
                break
            phase3(G, l, src, XB, MIX, AT, w_out, w_up[l])
            if stop_after == (l, 3):
                break
            phase4(G, l, XB, dst, AT, w_down[l])
        G.n_ops = P.n_ops
        G.n_waits = P.n_waits
    return nc


def bank(G, i):
    return G.PS[:, i * 512:(i + 1) * 512]


class Rot:
    def __init__(self, items):
        self.items = list(items)
        self.i = 0

    def next(self):
        v = self.items[self.i % len(self.items)]
        self.i += 1
        return v


def act(P, out, in_, func, bias=None, scale=None):
    kw = dict(out=out, in_=in_, func=func)
    if bias is not None:
        kw["bias"] = bias
    if scale is not None:
        kw["scale"] = scale
    return P.I("act", "activation", **kw)


def mm(P, out, lhsT, rhs, start, stop):
    return P.I("pe", "matmul", out=out, lhsT=lhsT, rhs=rhs, start=start, stop=stop)


def rms_rstd(G, sb_alloc, xs, nchunks, rstd, sqrot, lnv, ps_bank, inv_n):
    P = G.P
    for c in range(nchunks):
        sq = sqrot.next()
        act(P, sq[:, :], xs[:, c, :], AF.Square)
        mm(P, ps_bank, G.ones_bb[:, :], sq[:, :], c == 0, c == nchunks - 1)
    act(P, lnv[:, :], ps_bank, AF.Ln, bias=EPS, scale=inv_n)
    act(P, rstd[:, :], lnv[:, :], AF.Exp, scale=-0.5)


def load_w(G, wsb, w_ap, nk, groups, eng="pool"):
    wr = w_ap.rearrange("(c p) n -> p c n", p=128)
    for gi, (c0, c1) in enumerate(groups):
        G.P.dma(eng, out=wsb[:, :, c0:c1], in_=wr[:, :, c0:c1], key=wsb.name + "_%d" % (gi % 6))


def phase1(G, l, src, w_in, QT, KT, VN, MIX):
    nc, P, vec = G.nc, G.P, G.vec
    even = (l % 2 == 0)
    i2 = l // 2
    NIN = 3072 if even else 2560
    srcr = src.rearrange("(c p) t -> p c t", p=128)
    with ExitStack() as es:
        sb = lambda n, s, d: es.enter_context(nc.sbuf_tensor("p1_%d_" % l + n, s, d))
        W = sb("W", [128, 8, NIN], BF16)
        if even:
            grp = [(512, 768), (1024, 1280), (0, 256), (768, 1024), (1280, 1536), (256, 512)]
            grp += [(c, c + 256) for c in range(1536, 3072, 256)]
        else:
            grp = [(512, 768), (0, 256), (768, 1024), (256, 512)]
            grp += [(c, c + 256) for c in range(1024, 2560, 256)]
        load_w(G, W, w_in, 8, grp)
        xbuf = [sb("x%d" % i, [128, 8, TT], F32) for i in range(2)]
        hbuf = [sb("h%d" % i, [128, 8, TT], BF16) for i in range(2)]
        sqrot = Rot([sb("sq%d" % i, [128, TT], BF16) for i in range(2)])
        sqfrot = Rot([sb("sqf%d" % i, [128, TT], F32) for i in range(2)])
        lnv = sb("lnv", [128, TT], F32)
        rstd = sb("rstd", [128, TT], F32)
        qkrot = Rot([sb("qk%d" % i, [128, TT], BF16) for i in range(4)])
        vrot = Rot([sb("v%d" % i, [128, 512], BF16) for i in range(2)])
        yrot = Rot([sb("y%d" % i, [128, TT], BF16) for i in range(2)])
        brot = Rot([1, 2, 3, 4, 5, 6, 7])
        if even:
            gcrot = Rot([sb("gc%d" % i, [128, TT], F32) for i in range(2)])
            Z = [sb("Z%d" % i, [128, TT + 2], F32) for i in range(4)]
            trot = Rot([sb("t%d" % i, [128, TT], F32) for i in range(2)])
            for i in range(4):
                P.I("pool", "memset", ap=Z[i][:, 0:2], constant=0.0)
        else:
            sgrot = Rot([sb("sg%d" % i, [128, TT], F32) for i in range(2)])
            GC = [sb("GC%d" % i, [128, TT + 30], BF16) for i in range(4)]
            diag = sb("diag", [128, 124, 128], BF16)
            Y = sb("Y", [128, 4, TT], F32)
            mean = sb("mean", [128, TT], F32)
            msq = sb("msq", [128, TT], F32)
            var = sb("var", [128, TT], F32)
            lnr = sb("lnr", [128, TT], F32)
            rs2 = sb("rs2", [128, TT], F32)
            ynrot = Rot([sb("yn%d" % i, [128, TT], F32) for i in range(2)])
            for i in range(4):
                P.I("pool", "memset", ap=GC[i][:, 0:30], constant=0.0)
                for j in range(31):
                    P.I("dve", "tensor_scalar", out=diag[:, i * 31 + j, :], in0=G.ident[:, :],
                        scalar1=_vcol(vec, WCC0 + (i2 * 31 + j) * 4 + i), scalar2=None, op0=ALU.mult)

        def proj(h, oc):
            b = bank(G, brot.next())
            for c in range(8):
                mm(P, b, W[:, c, oc * 128:(oc + 1) * 128], h[:, c, :], c == 0, c == 7)
            return b

        P.dma("sp", out=xbuf[0][:, :, :], in_=srcr[:, :, 0:TT])
        for tt in range(NT):
            t0 = tt * TT
            xs = xbuf[tt % 2]
            h = hbuf[tt % 2]
            if tt + 1 < NT:
                P.dma("sp", out=xbuf[(tt + 1) % 2][:, :, :], in_=srcr[:, :, t0 + TT:t0 + 2 * TT])
            rms_rstd(G, sb, xs, 8, rstd, sqrot, lnv, bank(G, 0), 1.0 / D)
            for c in range(8):
                P.I("dve", "scalar_tensor_tensor", out=h[:, c, :], in0=xs[:, c, :],
                    scalar=_vcol(vec, GN0 + (l * 4 + 0) * 8 + c), in1=rstd[:, :], op0=ALU.mult, op1=ALU.mult)
            if even:
                qoc, koc, vcol = 12, 16, 2560
                for i in range(4):
                    pc = proj(h, 4 + i)
                    gcb = gcrot.next()
                    act(P, gcb[:, :], pc, AF.Copy)
                    px = proj(h, 8 + i)
                    if tt > 0:
                        P.I("pool", "tensor_copy", out=Z[i][:, 0:2], in_=Z[i][:, TT:TT + 2])
                    P.I("dve", "tensor_tensor", out=Z[i][:, 2:TT + 2], in0=px, in1=gcb[:, :], op=ALU.mult)
                    pb = proj(h, i)
                    t = trot.next()
                    wc = lambda j: _vcol(vec, WSC0 + (i2 * 3 + j) * 4 + i)
                    P.I("dve", "tensor_scalar", out=t[:, :], in0=Z[i][:, 0:TT], scalar1=wc(0), scalar2=None, op0=ALU.mult)
                    P.I("dve", "scalar_tensor_tensor", out=t[:, :], in0=Z[i][:, 1:TT + 1], scalar=wc(1), in1=t[:, :], op0=ALU.mult, op1=ALU.add)
                    P.I("dve", "scalar_tensor_tensor", out=t[:, :], in0=Z[i][:, 2:TT + 2], scalar=wc(2), in1=t[:, :], op0=ALU.mult, op1=ALU.add)
                    ya = yrot.next()
                    P.I("dve", "tensor_tensor", out=ya[:, :], in0=pb, in1=t[:, :], op=ALU.mult)
                    P.dma("sp", out=MIX[i * 128:(i + 1) * 128, t0:t0 + TT], in_=ya[:, :])
            else:
                qoc, koc, vcol = 8, 12, 2048
                for i in range(4):
                    pg = proj(h, 4 + i)
                    sg = sgrot.next()
                    act(P, sg[:, :], pg, AF.Sigmoid)
                    pa = proj(h, i)
                    if tt > 0:
                        P.I("pool", "tensor_copy", out=GC[i][:, 0:30], in_=GC[i][:, TT:TT + 30])
                    P.I("dve", "tensor_tensor", out=GC[i][:, 30:TT + 30], in0=pa, in1=sg[:, :], op=ALU.mult)
                    pcv = bank(G, brot.next())
                    for j in range(31):
                        mm(P, pcv, diag[:, i * 31 + j, :], GC[i][:, j:j + TT], j == 0, j == 30)
                    act(P, Y[:, i, :], pcv, AF.Identity, bias=_vcol(vec, BCC0 + i2 * 4 + i), scale=1.0)
                sbk = bank(G, brot.next())
                sqk = bank(G, brot.next())
                for i in range(4):
                    mm(P, sbk, G.ones_f[:, :], Y[:, i, :], i == 0, i == 3)
                for i in range(4):
                    sq = sqfrot.next()
                    act(P, sq[:, :], Y[:, i, :], AF.Square)
                    mm(P, sqk, G.ones_f[:, :], sq[:, :], i == 0, i == 3)
                act(P, mean[:, :], sbk, AF.Copy, scale=1.0 / 512)
                P.I("dve", "tensor_tensor", out=msq[:, :], in0=mean[:, :], in1=mean[:, :], op=ALU.mult)
                P.I("dve", "scalar_tensor_tensor", out=var[:, :], in0=sqk, scalar=1.0 / 512, in1=msq[:, :], op0=ALU.mult, op1=ALU.subtract)
                act(P, lnr[:, :], var[:, :], AF.Ln, bias=EPS, scale=1.0)
                act(P, rs2[:, :], lnr[:, :], AF.Exp, scale=-0.5)
                for i in range(4):
                    yn = ynrot.next()
                    P.I("dve", "tensor_tensor", out=yn[:, :], in0=Y[:, i, :], in1=mean[:, :], op=ALU.subtract)
                    P.I("dve", "tensor_tensor", out=yn[:, :], in0=yn[:, :], in1=rs2[:, :], op=ALU.mult)
                    yc = yrot.next()
                    act(P, yc[:, :], yn[:, :], AF.Silu, bias=_vcol(vec, LNB0 + i2 * 4 + i), scale=_vcol(vec, LNG0 + i2 * 4 + i))
                    P.dma("sp", out=MIX[i * 128:(i + 1) * 128, t0:t0 + TT], in_=yc[:, :])
            for i in range(4):
                pq = proj(h, qoc + i)
                qs = qkrot.next()
                act(P, qs[:, :], pq, AF.Copy, scale=0.125)
                P.dma("sp", out=QT[i * 128:(i + 1) * 128, t0:t0 + TT], in_=qs[:, :])
                pk = proj(h, koc + i)
                ks = qkrot.next()
                P.I("dve", "tensor_copy", out=ks[:, :], in_=pk)
                P.dma("sp", out=KT[i * 128:(i + 1) * 128, t0:t0 + TT], in_=ks[:, :])
            for sub in range(4):
                b = bank(G, brot.next())
                for c in range(8):
                    mm(P, b, h[:, c, sub * 128:(sub + 1) * 128], W[:, c, vcol:vcol + 512], c == 0, c == 7)
                vs = vrot.next()
                act(P, vs[:, :], b, AF.Copy)
                P.dma("sp", out=VN[t0 + sub * 128:t0 + (sub + 1) * 128, :], in_=vs[:, :])
        P.flush()


def phase2_even(G, l, QT, KT, VN, MIX):
    nc, P = G.nc, G.P
    with ExitStack() as es:
        sb = lambda n, s, d: es.enter_context(nc.sbuf_tensor("p2e_%d_" % l + n, s, d))
        VD = [sb("VD%d" % g, [128, 32, 512], BF16) for g in range(3)]
        bias = sb("bias", [128, 6144], F32)
        maskd = sb("maskd", [128, 256], F32)
        qsb = [sb("q%d" % i, [128, S], BF16) for i in range(1)]
        ksb = [sb("k%d" % i, [128, S], BF16) for i in range(1)]
        acc = sb("acc", [128, 2, S], F32)
        ysb = sb("ysb", [128, S], BF16)
        sbf = [sb("sbf%d" % i, [128, 1024], F32) for i in range(2)]
        Ab = [sb("A%d" % i, [128, 1024], BF16) for i in range(4)]
        for g, (win, d) in enumerate(DA_PAIRS):
            vr = VN.rearrange("(blk p r) f -> p blk r f", p=128, r=d)
            for blk in range(32 // d):
                P.dma("sp" if blk % 2 == 0 else "pool", out=VD[g][:, blk * d:(blk + 1) * d, :], in_=vr[:, blk, :, :],
                      key="VD%d_%d" % (g, blk % 4))
        P.dma("sp", out=bias[:, :], in_=G.biasT[:, :])
        P.dma("sp", out=maskd[:, :], in_=G.consts[:, C_MASKD:C_MASKD + 256])
        for gh in range(24):
            P.I("pool", "tensor_tensor", out=bias[:, gh * 256:(gh + 1) * 256], in0=bias[:, gh * 256:(gh + 1) * 256], in1=maskd[:, :], op=ALU.add)
        srot = Rot([0, 1])
        arot = Rot([0, 1, 2, 3])
        fr = Rot([0, 1])
        for hp in range(4):
            q = qsb[0]
            k = ksb[0]
            P.dma("sp", out=q[:, :], in_=QT[hp * 128:(hp + 1) * 128, :])
            P.dma("sp", out=k[:, :], in_=KT[hp * 128:(hp + 1) * 128, :])
            units = []
            for g, (win, d) in enumerate(DA_PAIRS):
                if d == 1:
                    for gi in range(8):
                        units.append(dict(g=g, d=d, blocks=[((gi * 4 + bi) * 128, 0) for bi in range(4)], aview=("c", gi * 512)))
                elif d == 4:
                    for gi in range(8):
                        units.append(dict(g=g, d=d, blocks=[(gi * 128, r) for r in range(4)], aview=("s4", gi * 512)))
                else:
                    for sg in range(2):
                        for rq in range(4):
                            units.append(dict(g=g, d=d, blocks=[(sg * 128, rq * 4 + bi) for bi in range(4)], aview=("s16", sg * 2048, rq)))
            for ui, u in enumerate(units):
                u["o_bank"] = 4 + (ui % 2) * 2

            def stageA(u):
                g, d, blocks = u["g"], u["d"], u["blocks"]
                u["A"] = []
                for e in range(2):
                    hh = hp * 2 + e
                    hr = slice(e * 64, (e + 1) * 64)
                    sbase = srot.next() * 1024
                    Sps = G.PS[:, sbase:sbase + 1024]
                    for bi, (m0, r) in enumerate(blocks):
                        qs0 = m0 * d + r
                        for jt in range(2):
                            mk0 = m0 - 128 + jt * 128
                            if mk0 < 0:
                                continue
                            ks0 = mk0 * d + r
                            c0 = sbase + (bi * 2 + jt) * 128
                            mm(P, G.PS[:, c0:c0 + 128], k[hr, ks0:ks0 + 127 * d + 1:d], q[hr, qs0:qs0 + 127 * d + 1:d], True, True)
                    sf = sbf[fr.next()]
                    bsl = bias[:, (g * 8 + hh) * 256:(g * 8 + hh + 1) * 256].unsqueeze(1).broadcast_to([128, 4, 256])
                    P.I("dve", "tensor_tensor", out=sf[:, :].rearrange("p (b f) -> p b f", b=4),
                        in0=Sps.rearrange("p (b f) -> p b f", b=4), in1=bsl, op=ALU.add)
                    A = Ab[arot.next()]
                    act(P, A[:, :], sf[:, :], AF.Exp)
                    u["A"].append(A)

            def stageB(u):
                g, d, blocks = u["g"], u["d"], u["blocks"]
                o_bank = u["o_bank"]
                den_bank = o_bank + 1
                for e in range(2):
                    hh = hp * 2 + e
                    hr = slice(e * 64, (e + 1) * 64)
                    A = u["A"][e]
                    for bi, (m0, r) in enumerate(blocks):
                        first = True
                        for jt in range(2):
                            mk0 = m0 - 128 + jt * 128
                            if mk0 < 0:
                                continue
                            tile = (mk0 // 128) * d + r
                            asl = A[:, (bi * 2 + jt) * 128:(bi * 2 + jt) * 128 + 128]
                            co = o_bank * 512 + bi * 128
                            cd = den_bank * 512 + bi * 128
                            mm(P, G.PS[hr, co:co + 128], VD[g][:, tile, hh * 64:(hh + 1) * 64], asl, first, jt == 1)
                            mm(P, G.PS[hr, cd:cd + 128], G.ones_b[:, :], asl, first, jt == 1)
                            first = False

            def stageC(u):
                aview = u["aview"]
                o_bank = u["o_bank"]
                for w_, bk in ((0, o_bank), (1, o_bank + 1)):
                    src_ps = bank(G, bk)
                    if aview[0] == "c":
                        c0 = aview[1]
                        if w_ == 0:
                            act(P, acc[:, w_, c0:c0 + 512], src_ps, AF.Copy)
                        else:
                            P.I("dve", "tensor_copy", out=acc[:, w_, c0:c0 + 512], in_=src_ps)
                    elif aview[0] == "s4":
                        c0 = aview[1]
                        av = acc[:, w_, c0:c0 + 512].rearrange("p (i r) -> p r i", r=4)
                        P.I("dve", "tensor_tensor", out=av, in0=src_ps.rearrange("p (r i) -> p r i", r=4), in1=av, op=ALU.add)
                    else:
                        c0, rq = aview[1], aview[2]
                        av = acc[:, w_, c0:c0 + 2048].rearrange("p (i r) -> p r i", r=16)[:, rq * 4:(rq + 1) * 4, :]
                        P.I("dve", "tensor_tensor", out=av, in0=src_ps.rearrange("p (r i) -> p r i", r=4), in1=av, op=ALU.add)

            nu = len(units)
            for it in range(nu + 2):
                if it < nu:
                    stageA(units[it])
                if 0 <= it - 1 < nu:
                    stageB(units[it - 1])
                if 0 <= it - 2 < nu:
                    stageC(units[it - 2])
            for qh in range(4):
                cs = slice(qh * 1024, (qh + 1) * 1024)
                act(P, acc[:, 1, cs], acc[:, 1, cs], AF.Ln)
                act(P, acc[:, 1, cs], acc[:, 1, cs], AF.Exp, scale=-1.0)
                P.I("pool", "tensor_tensor", out=ysb[:, cs], in0=acc[:, 0, cs], in1=acc[:, 1, cs], op=ALU.mult)
            P.dma("sp", out=MIX[512 + hp * 128:512 + (hp + 1) * 128, :], in_=ysb[:, :])
        P.flush()


def phase2_odd(G, l, QT, KT, VN, MIX):
    nc, P = G.nc, G.P
    with ExitStack() as es:
        sb = lambda n, s, d: es.enter_context(nc.sbuf_tensor("p2o_%d_" % l + n, s, d))
        V = sb("V", [128, 32, 512], BF16)
        masks = sb("masks", [128, 4, 512], BF16)
        qsb = [sb("q%d" % i, [128, S], BF16) for i in range(2)]
        ksb = [sb("k%d" % i, [128, S], BF16) for i in range(2)]
        ysb = [sb("ysb%d" % i, [128, S], BF16) for i in range(2)]
        NSL = 2
        E = [[sb("E%d_%d" % (i, j), [128, 2, 512], F32) for j in range(3)] for i in range(NSL)]
        SP = [[sb("SP%d_%d" % (i, j), [128, 2, 512], BF16) for j in range(3)] for i in range(NSL)]
        Gt = [[sb("G%d_%d" % (i, j), [128, 2, 512], F32) for j in range(2)] for i in range(NSL)]
        At = [[sb("At%d_%d" % (i, j), [128, 2, 512], BF16) for j in range(2)] for i in range(NSL)]
        vr = VN.rearrange("(t p) f -> p t f", p=128)
        for qq in range(4):
            P.dma("sp", out=V[:, qq * 8:(qq + 1) * 8, :], in_=vr[:, qq * 8:(qq + 1) * 8, :], key="Vo_%d" % qq)
        P.dma("pool", out=masks[:, :, :], in_=G.consts[:, C_MASKS:C_MASKS + 2048].rearrange("p (o f) -> p o f", o=4))
        P.dma("sp", out=qsb[0][:, :], in_=QT[0:128, :])
        P.dma("sp", out=ksb[0][:, :], in_=KT[0:128, :])
        slot_qts = [[7, 4, 3, 0], [6, 5, 2, 1]]
        for hp in range(4):
            q = qsb[hp % 2]
            k = ksb[hp % 2]
            ys = ysb[hp % 2]
            if hp + 1 < 4:
                P.dma("sp", out=qsb[(hp + 1) % 2][:, :], in_=QT[(hp + 1) * 128:(hp + 2) * 128, :])
                P.dma("sp", out=ksb[(hp + 1) % 2][:, :], in_=KT[(hp + 1) * 128:(hp + 2) * 128, :])
            streams = []
            for s_ in range(NSL):
                st = []
                for qt in slot_qts[s_]:
                    n = 4 * qt + 4
                    for j in range(n):
                        st.append(dict(qt=qt, kb=4 * qt + 3 - j, first=(j == 0), last=(j == n - 1), idx=len(st)))
                streams.append(st)
            nit = max(len(st) for st in streams) + 2

            def stageA(s_, stp):
                qt, kb, idx = stp["qt"], stp["kb"], stp["idx"]
                o = kb - 4 * qt
                for e in range(2):
                    hr = slice(e * 64, (e + 1) * 64)
                    mm(P, bank(G, e), k[hr, kb * 128:(kb + 1) * 128], q[hr, qt * 512:(qt + 1) * 512], True, o < 0)
                    if o >= 0:
                        mm(P, bank(G, e), G.ident[:, :], masks[:, o, :], False, True)
                Ec = E[s_][idx % 3]
                act(P, Ec[:, :, :], G.PS[:, 0:1024].rearrange("p (e f) -> p e f", e=2), AF.Exp)
                spc = SP[s_][idx % 3]
                act(P, spc[:, :, :], Ec[:, :, :], AF.Ln, bias=1.0, scale=1.0)

            def stageB(s_, stp):
                idx = stp["idx"]
                spc = SP[s_][idx % 3]
                spp = SP[s_][(idx - 1) % 3]
                for e in range(2):
                    pb = bank(G, 2 + 2 * s_ + e)
                    if stp["first"]:
                        mm(P, pb, G.tri[:, :], spc[:, e, :], True, True)
                    else:
                        mm(P, pb, G.su[:, :], spp[:, e, :], False, False)
                        mm(P, pb, G.tri[:, :], spc[:, e, :], False, True)
                Gc = Gt[s_][idx % 2]
                c0 = (2 + 2 * s_) * 512
                act(P, Gc[:, :, :], G.PS[:, c0:c0 + 1024].rearrange("p (e f) -> p e f", e=2), AF.Exp, scale=-1.0)
                P.I("dve", "tensor_tensor", out=At[s_][idx % 2][:, :, :], in0=E[s_][idx % 3][:, :, :], in1=Gc[:, :, :], op=ALU.mult)

            def stageC(s_, stp):
                qt, kb, idx = stp["qt"], stp["kb"], stp["idx"]
                for e in range(2):
                    hh = hp * 2 + e
                    hr = slice(e * 64, (e + 1) * 64)
                    ob = G.PS[hr, (6 + s_) * 512:(7 + s_) * 512]
                    mm(P, ob, V[:, kb, hh * 64:(hh + 1) * 64], At[s_][idx % 2][:, e, :], stp["first"], stp["last"])
                if stp["last"]:
                    P.I("dve", "tensor_copy", out=ys[:, qt * 512:(qt + 1) * 512], in_=bank(G, 6 + s_))

            for it in range(nit):
                for s_ in range(NSL):
                    if it < len(streams[s_]):
                        stageA(s_, streams[s_][it])
                for s_ in range(NSL):
                    if 0 <= it - 1 < len(streams[s_]):
                        stageB(s_, streams[s_][it - 1])
                for s_ in range(NSL):
                    if 0 <= it - 2 < len(streams[s_]):
                        stageC(s_, streams[s_][it - 2])
            P.dma("sp", out=MIX[512 + hp * 128:512 + (hp + 1) * 128, :], in_=ys[:, :])
        P.flush()


def post_norm_residual(G, l, j, M, xs, rstd, sqrot, lnv, ps_bank, tmprot):
    P, vec = G.P, G.vec
    rms_rstd(G, None, M, 8, rstd, sqrot, lnv, ps_bank, 1.0 / D)
    for c in range(8):
        tmp = tmprot.next()
        P.I("dve", "scalar_tensor_tensor", out=tmp[:, :], in0=M[:, c, :], scalar=_vcol(vec, GN0 + (l * 4 + j) * 8 + c),
            in1=rstd[:, :], op0=ALU.mult, op1=ALU.mult)
        P.I("pool", "tensor_tensor", out=xs[:, c, :], in0=xs[:, c, :], in1=tmp[:, :], op=ALU.add)


def interleave(main, side, offset=1, stride=1):
    k = 0
    for i, f in enumerate(main):
        f()
        while k < len(side) and i >= offset + k * stride:
            side[k]()
            k += 1
    while k < len(side):
        side[k]()
        k += 1


def interleave_events(main, events):
    ev = sorted(enumerate(events), key=lambda t: (t[1][0], t[0]))
    k = 0
    for i, f in enumerate(main):
        f()
        while k < len(ev) and ev[k][1][0] < i + 1:
            ev[k][1][1]()
            k += 1
    while k < len(ev):
        ev[k][1][1]()
        k += 1


def phase3(G, l, src, XB, MIX, AT, w_out, w_up):
    nc, P, vec = G.nc, G.P, G.vec
    srcr = src.rearrange("(c p) t -> p c t", p=128)
    xbr = XB.rearrange("(c p) t -> p c t", p=128)
    mixr = MIX.rearrange("(c p) t -> p c t", p=128)
    with ExitStack() as es:
        sb = lambda n, s, d: es.enter_context(nc.sbuf_tensor("p3_%d_" % l + n, s, d))
        WO = sb("WO", [128, 8, 1024], BF16)
        WU = sb("WU", [128, 8, 2 * DFF], BF16)
        load_w(G, WO, w_out, 8, [(c, c + 256) for c in range(0, 1024, 256)])
        grp = []
        for c in range(0, DFF, 256):
            grp.append((c, c + 256))
            grp.append((DFF + c, DFF + c + 256))
        load_w(G, WU, w_up, 8, grp)
        xs = sb("x", [128, 8, TT], F32)
        mbuf = [sb("mix%d" % i, [128, 8, TT], BF16) for i in range(2)]
        M = sb("M", [128, 8, TT], F32)
        h2b = [sb("h2_%d" % i, [128, 8, TT], BF16) for i in range(2)]
        sqrot = Rot([sb("sq%d" % i, [128, TT], BF16) for i in range(6)])
        sq2rot = Rot([sb("sqb%d" % i, [128, TT], BF16) for i in range(6)])
        tmprot = Rot([sb("tmp%d" % i, [128, TT], F32) for i in range(2)])
        rstd = sb("rstd", [128, TT], F32)
        Gb = [sb("Gb%d" % i, [128, TT + 2], F32) for i in range(2)]
        H = sb("H", [128, NFC, 2], F32)
        trot = Rot([sb("t%d" % i, [128, TT], F32) for i in range(2)])
        srot = Rot([sb("s%d" % i, [128, TT], F32) for i in range(2)])
        arot = Rot([sb("a%d" % i, [128, TT], BF16) for i in range(3)])
        brot = Rot([2, 3, 4, 5, 6, 7])
        P.I("pool", "memset", ap=H[:, :, :], constant=0.0)
        P.dma("sp", out=mbuf[0][:, :, :], in_=mixr[:, :, 0:TT])

        OWN = {}

        def A_events(tt):
            t0 = tt * TT
            mx = mbuf[tt % 2]
            h2 = h2b[tt % 2]
            ev = []

            def a_load():
                P.dma("sp", out=xs[:, :, :], in_=srcr[:, :, t0:t0 + TT])
                if tt + 1 < NT:
                    P.dma("sp", out=mbuf[(tt + 1) % 2][:, :, :], in_=mixr[:, :, t0 + TT:t0 + 2 * TT])
            ev.append((0.0, a_load))
            sqs = {}

            def mk_oc(oc):
                def f():
                    b = bank(G, brot.next())
                    for c in range(8):
                        mm(P, b, WO[:, c, oc * 128:(oc + 1) * 128], mx[:, c, :], c == 0, c == 7)
                    act(P, M[:, oc, :], b, AF.Copy)
                    sq = sqrot.next()
                    act(P, sq[:, :], M[:, oc, :], AF.Square)
                    sqs[oc] = sq
                    OWN[sq.name] = ("a", tt, oc)
                return f

            def mk_st1(oc):
                def f():
                    assert OWN[sqs[oc].name] == ("a", tt, oc)
                    mm(P, bank(G, 0), G.ones_bb[:, :], sqs[oc][:, :], oc == 0, oc == 7)
                return f
            for oc in range(8):
                ev.append((0.1 + 0.5 * oc, mk_oc(oc)))
                ev.append((0.1 + 0.5 * oc + 2.6, mk_st1(oc)))

            def a_rstd1():
                act(P, rstd[:, :], bank(G, 0), AF.Ln, bias=EPS, scale=1.0 / D)
                act(P, rstd[:, :], rstd[:, :], AF.Exp, scale=-0.5)
            ev.append((6.4, a_rstd1))
            sq2 = {}

            def mk_res(c):
                def f():
                    tmp = tmprot.next()
                    P.I("dve", "scalar_tensor_tensor", out=tmp[:, :], in0=M[:, c, :], scalar=_vcol(vec, GN0 + (l * 4 + 1) * 8 + c),
                        in1=rstd[:, :], op0=ALU.mult, op1=ALU.mult)
                    P.I("dve", "tensor_tensor", out=xs[:, c, :], in0=xs[:, c, :], in1=tmp[:, :], op=ALU.add)
                    sq = sq2rot.next()
                    act(P, sq[:, :], xs[:, c, :], AF.Square)
                    sq2[c] = sq
                    OWN[sq.name] = ("b", tt, c)
                return f

            def mk_st2(c):
                def f():
                    assert OWN[sq2[c].name] == ("b", tt, c)
                    mm(P, bank(G, 1), G.ones_bb[:, :], sq2[c][:, :], c == 0, c == 7)
                return f
            for c in range(8):
                ev.append((7.5 + 0.45 * c, mk_res(c)))
                ev.append((7.5 + 0.45 * c + 2.5, mk_st2(c)))

            def a_rstd2():
                P.dma("sp", out=xbr[:, :, t0:t0 + TT], in_=xs[:, :, :])
                act(P, rstd[:, :], bank(G, 1), AF.Ln, bias=EPS, scale=1.0 / D)
                act(P, rstd[:, :], rstd[:, :], AF.Exp, scale=-0.5)
            ev.append((13.2, a_rstd2))

            def mk_h(c0):
                def f():
                    for c in (c0, c0 + 1):
                        P.I("dve", "scalar_tensor_tensor", out=h2[:, c, :], in0=xs[:, c, :],
                            scalar=_vcol(vec, GN0 + (l * 4 + 2) * 8 + c), in1=rstd[:, :], op0=ALU.mult, op1=ALU.mult)
                return f
            for j, c0 in enumerate(range(0, 8, 2)):
                ev.append((14.5 + 0.5 * j, mk_h(c0)))
            return ev

        def F_closures(tt):
            t0 = tt * TT
            h2 = h2b[tt % 2]
            cl = []

            def mk_fc(fc):
                def f():
                    pg = bank(G, brot.next())
                    for c in range(8):
                        mm(P, pg, WU[:, c, fc * 128:(fc + 1) * 128], h2[:, c, :], c == 0, c == 7)
                    pu = bank(G, brot.next())
                    for c in range(8):
                        mm(P, pu, WU[:, c, DFF + fc * 128:DFF + (fc + 1) * 128], h2[:, c, :], c == 0, c == 7)
                    gb = Gb[fc % 2]
                    P.I("pool", "tensor_copy", out=gb[:, 0:2], in_=H[:, fc, :])
                    act(P, gb[:, 2:TT + 2], pg, AF.Copy)
                    P.I("pool", "tensor_copy", out=H[:, fc, :], in_=gb[:, TT:TT + 2])
                    t = trot.next()
                    wc = lambda j: _vcol(vec, WFC0 + (l * 3 + j) * NFC + fc)
                    act(P, t[:, :], pg, AF.Identity, bias=_vcol(vec, BFC0 + l * NFC + fc), scale=wc(2))
                    P.I("dve", "scalar_tensor_tensor", out=t[:, :], in0=gb[:, 0:TT], scalar=wc(0), in1=t[:, :], op0=ALU.mult, op1=ALU.add)
                    P.I("dve", "scalar_tensor_tensor", out=t[:, :], in0=gb[:, 1:TT + 1], scalar=wc(1), in1=t[:, :], op0=ALU.mult, op1=ALU.add)
                    sv = srot.next()
                    act(P, sv[:, :], t[:, :], AF.Silu)
                    a = arot.next()
                    P.I("dve", "tensor_tensor", out=a[:, :], in0=pu, in1=sv[:, :], op=ALU.mult)
                    P.dma("sp", out=AT[fc * 128:(fc + 1) * 128, t0:t0 + TT], in_=a[:, :])
                return f
            for fc in range(NFC):
                cl.append(mk_fc(fc))
            return cl

        interleave_events([], A_events(0))
        for tt in range(NT):
            side = A_events(tt + 1) if tt + 1 < NT else []
            interleave_events(F_closures(tt), side)
        P.flush()


def phase4(G, l, XB, dst, AT, w_down):
    nc, P, vec = G.nc, G.P, G.vec
    xbr = XB.rearrange("(c p) t -> p c t", p=128)
    dstr = dst.rearrange("(c p) t -> p c t", p=128)
    atr = AT.rearrange("(c p) t -> p c t", p=128)
    with ExitStack() as es:
        sb = lambda n, s, d: es.enter_context(nc.sbuf_tensor("p4_%d_" % l + n, s, d))
        WD = sb("WD", [128, NFC, 1024], BF16)
        load_w(G, WD, w_down, NFC, [(c, c + 256) for c in range(0, 1024, 256)])
        xbuf = [sb("x%d" % i, [128, 8, TT], F32) for i in range(2)]
        abuf = [sb("a%d" % i, [128, NFC, TT], BF16) for i in range(2)]
        Mb = [sb("M%d" % i, [128, 8, TT], F32) for i in range(2)]
        sqrot = Rot([sb("sq%d" % i, [128, TT], BF16) for i in range(8)])
        tmprot = Rot([sb("tmp%d" % i, [128, TT], F32) for i in range(2)])
        lnv = sb("lnv", [128, TT], F32)
        rstd = sb("rstd", [128, TT], F32)
        brot = Rot([1, 2, 3, 4, 5, 6, 7])
        P.dma("sp", out=abuf[0][:, :, :], in_=atr[:, :, 0:TT])
        P.dma("sp", out=xbuf[0][:, :, :], in_=xbr[:, :, 0:TT])
        P.dma("sp", out=xbuf[1][:, :, :], in_=xbr[:, :, TT:2 * TT])

        def D_closures(tt):
            t0 = tt * TT
            a = abuf[tt % 2]
            M = Mb[tt % 2]
            cl = []

            def d_load():
                if tt + 1 < NT:
                    P.dma("sp", out=abuf[(tt + 1) % 2][:, :, :], in_=atr[:, :, t0 + TT:t0 + 2 * TT])
            cl.append(d_load)

            def mk_oc(oc):
                def f():
                    b = bank(G, brot.next())
                    for c in range(NFC):
                        mm(P, b, WD[:, c, oc * 128:(oc + 1) * 128], a[:, c, :], c == 0, c == NFC - 1)
                    act(P, M[:, oc, :], b, AF.Copy)
                return f
            for oc in range(8):
                cl.append(mk_oc(oc))
            return cl

        OWN = {}

        def N_events(tt):
            t0 = tt * TT
            xs = xbuf[tt % 2]
            M = Mb[tt % 2]
            ev = []
            sqs = {}

            def n_sq(c0):
                def f():
                    for c in range(c0, c0 + 4):
                        sq = sqrot.next()
                        act(P, sq[:, :], M[:, c, :], AF.Square)
                        sqs[c] = sq
                        OWN[sq.name] = (tt, c)
                return f

            def n_mm(c0):
                def f():
                    for c in range(c0, c0 + 4):
                        assert OWN[sqs[c].name] == (tt, c)
                        mm(P, bank(G, 0), G.ones_bb[:, :], sqs[c][:, :], c == 0, c == 7)
                return f

            def n_rstd():
                act(P, lnv[:, :], bank(G, 0), AF.Ln, bias=EPS, scale=1.0 / D)
                act(P, rstd[:, :], lnv[:, :], AF.Exp, scale=-0.5)
            ev.append((0.0, n_sq(0)))
            ev.append((0.5, n_sq(4)))
            ev.append((2.2, n_mm(0)))
            ev.append((2.6, n_mm(4)))
            ev.append((3.0, n_rstd))

            def n_res(c0):
                def f():
                    for c in range(c0, c0 + 4):
                        tmp = tmprot.next()
                        P.I("dve", "scalar_tensor_tensor", out=tmp[:, :], in0=M[:, c, :], scalar=_vcol(vec, GN0 + (l * 4 + 3) * 8 + c),
                            in1=rstd[:, :], op0=ALU.mult, op1=ALU.mult)
                        P.I("pool", "tensor_tensor", out=xs[:, c, :], in0=xs[:, c, :], in1=tmp[:, :], op=ALU.add)
                    if c0 == 4:
                        P.dma("sp", out=dstr[:, :, t0:t0 + TT], in_=xs[:, :, :])
                return f
            ev.append((4.0, n_res(0)))
            ev.append((5.0, n_res(4)))
            return ev

        for f in D_closures(0):
            f()
        for tt in range(NT):
            main = D_closures(tt + 1) if tt + 1 < NT else []
            interleave_events(main, N_events(tt))
            if tt + 2 < NT:
                P.dma("sp", out=xbuf[tt % 2][:, :, :], in_=xbr[:, :, (tt + 2) * TT:(tt + 3) * TT])
        P.flush()


def _t5_bucket(dist):
    max_exact = 16
    d = np.maximum(dist, 1).astype(np.float32)
    large = max_exact + (np.log(d / np.float32(max_exact)) / np.float32(math.log(2048 / 16)) * np.float32(16)).astype(np.int32)
    large = np.minimum(large, 31)
    return np.where(dist < max_exact, dist, large)


def _host_tables(norm_g, rel_bias, w_sc, w_cc, b_cc, ln_cc_g, ln_cc_b, w_ffn_conv, b_ffn_conv):
    vecs = np.zeros((128, NV), np.float32)

    def put(col0, arr):
        a = np.asarray(arr, np.float32)
        lead = a.shape[:-1]
        C = a.shape[-1] // 128
        a = a.reshape(lead + (C, 128))
        a = np.moveaxis(a, -1, 0).reshape(128, -1)
        vecs[:, col0:col0 + a.shape[1]] = a

    put(GN0, norm_g)
    put(WSC0, w_sc)
    put(WCC0, w_cc)
    put(BCC0, b_cc)
    put(LNG0, ln_cc_g)
    put(LNB0, ln_cc_b)
    put(WFC0, w_ffn_conv)
    put(BFC0, b_ffn_conv)

    consts = np.zeros((128, NC_), np.float32)
    p = np.arange(128)[:, None]
    i = np.arange(128)[None, :]
    consts[:, C_MASKD:C_MASKD + 128] = np.where(i <= p, 0.0, NEG)
    consts[:, C_MASKD + 128:C_MASKD + 256] = np.where(i >= p, 0.0, NEG)
    iq = np.arange(512)[None, :]
    for o in range(4):
        consts[:, C_MASKS + o * 512:C_MASKS + (o + 1) * 512] = np.where(o * 128 + p < iq, 0.0, NEG)
    consts[:, C_TRI:C_TRI + 128] = (p >= i).astype(np.float32)
    consts[:, C_SU:C_SU + 128] = (p < i).astype(np.float32)
    consts[:, C_ID:C_ID + 128] = (p == i).astype(np.float32)

    rb = np.asarray(rel_bias, np.float32)
    biasT = np.zeros((128, 3, 8, 2, 128), np.float32)
    for g, (win, d) in enumerate(DA_PAIRS):
        for jt in range(2):
            rel = 128 + i - (jt * 128 + p)
            relc = np.clip(rel, 0, 128)
            bk = _t5_bucket(relc * d)
            biasT[:, g, :, jt, :] = np.transpose(rb[bk], (0, 2, 1))
    return vecs, consts, biasT.reshape(128, 6144)


_NC_CACHE = {}


def kernel(x, norm_g, rel_bias, w_in_even, w_out_even, w_sc, w_in_odd, w_out_odd,
           w_cc, b_cc, ln_cc_g, ln_cc_b, w_up, w_ffn_conv, b_ffn_conv, w_down):
    x = np.asarray(x, np.float32)
    B = x.shape[0]
    vecs, consts, biasT = _host_tables(norm_g, rel_bias, w_sc, w_cc, b_cc, ln_cc_g, ln_cc_b, w_ffn_conv, b_ffn_conv)
    if "nc" not in _NC_CACHE:
        _NC_CACHE["nc"] = build_program()
    nc = _NC_CACHE["nc"]
    shared = dict(
        vecs=vecs, consts=consts, biasT=biasT,
        w_in_even=np.ascontiguousarray(w_in_even, np.float32), w_out_even=np.ascontiguousarray(w_out_even, np.float32),
        w_in_odd=np.ascontiguousarray(w_in_odd, np.float32), w_out_odd=np.ascontiguousarray(w_out_odd, np.float32),
        w_up=np.ascontiguousarray(w_up, np.float32), w_down=np.ascontiguousarray(w_down, np.float32),
    )
    in_maps = []
    for b in range(B):
        m = dict(shared)
        m["xT"] = np.ascontiguousarray(x[b].T)
        in_maps.append(m)
    res = run_bass_kernel_spmd(nc, in_maps, core_ids=list(range(B)))
    out = np.stack([np.ascontiguousarray(r["yT"].T) for r in res.results], axis=0)
    return out.astype(np.float32)
```

```python
import math
from contextlib import ExitStack

import numpy as np
import concourse.bass as bass
import concourse.mybir as mybir
from concourse.bass_utils import run_bass_kernel_spmd

F32 = mybir.dt.float32
BF16 = mybir.dt.bfloat16
AF = mybir.ActivationFunctionType
ALU = mybir.AluOpType

S = 4096
D = 1024
NT = 8
TT = 512
DFF = 2816
NFC = 22
EPS = 1e-6
DA_PAIRS = ((128, 1), (512, 4), (2048, 16))
NEG = -30000.0

GN0 = 0
WSC0 = 128
WCC0 = 152
BCC0 = 400
LNG0 = 408
LNB0 = 416
WFC0 = 424
BFC0 = 688
NV = 776
C_MASKD = 0
C_MASKS = 256
C_TRI = 2304
C_SU = 2432
C_ID = 2560
NC_ = 2688


def _is_ap(v):
    return hasattr(v, "tensor") and hasattr(v, "offset") and hasattr(v, "ap")


def _box(ap):
    t = ap.tensor
    shape = list(t.shape)
    nd = len(shape)
    strides = [1] * nd
    for k in range(nd - 2, -1, -1):
        strides[k] = strides[k + 1] * shape[k + 1]
    off = ap.offset
    lo = []
    for k in range(nd):
        lo.append(off // strides[k])
        off = off % strides[k]
    ext = [0] * nd
    for step, cnt in ap.ap:
        if cnt <= 1 or step == 0:
            continue
        k = 0
        while k < nd - 1 and step < strides[k]:
            k += 1
        ext[k] += (step // strides[k]) * (cnt - 1)
        rem = step % strides[k]
        if rem != 0:
            for kk in range(k + 1, nd):
                ext[kk] = shape[kk]
    return (t.name, tuple((lo[k], lo[k] + ext[k] + 1) for k in range(nd)))


def _ovl(a, b):
    for (al, ah), (bl, bh) in zip(a[1], b[1]):
        if not (al < bh and bl < ah):
            return False
    return True


def _contains(a, b):
    for (al, ah), (bl, bh) in zip(a[1], b[1]):
        if not (al <= bl and bh <= ah):
            return False
    return True


class Op:
    __slots__ = ("eng", "emit", "deps", "is_dma", "sem", "val", "signal")

    def __init__(self, eng, emit):
        self.eng = eng
        self.emit = emit
        self.deps = []
        self.is_dma = False
        self.sem = None
        self.val = None
        self.signal = False


ENGS = ("pe", "act", "dve", "pool", "sp")


class Prog:
    def __init__(self, nc, es, n_dsem=60):
        self.nc = nc
        self.esem = {e: es.enter_context(nc.semaphore("s_" + e)) for e in ENGS}
        self.ecnt = {e: 0 for e in ENGS}
        self.bar = es.enter_context(nc.semaphore("s_bar"))
        self.barcnt = 0
        self.dpool = [[es.enter_context(nc.semaphore("d%d" % i)), 0] for i in range(n_dsem)]
        self.n_ops = 0
        self.n_waits = 0
        self._reset()

    def _reset(self):
        self.ops = {e: [] for e in ENGS}
        self.hist = {}
        self.dma_last = {}
        self.dkey2idx = {}
        self.all_ops = []

    def _track(self, op, reads, writes):
        deps = []
        rb_ = [_box(a) for a in reads]
        wb_ = [_box(a) for a in writes]
        for b in rb_:
            h = self.hist.setdefault(b[0], {"w": [], "r": []})
            for wb, wop in h["w"]:
                if _ovl(wb, b):
                    deps.append(wop)
        for b in wb_:
            h = self.hist.setdefault(b[0], {"w": [], "r": []})
            for wb, wop in h["w"]:
                if _ovl(wb, b):
                    deps.append(wop)
            for rb, rop in h["r"]:
                if _ovl(rb, b):
                    deps.append(rop)
        for b in rb_:
            h = self.hist[b[0]]
            if len(h["r"]) > 64:
                h["r"] = [(rb, rop) for rb, rop in h["r"] if not (rop.eng == op.eng and not rop.is_dma and not op.is_dma and _contains(b, rb))]
            h["r"].append((b, op))
        for b in wb_:
            h = self.hist[b[0]]
            h["w"] = [(wb, wop) for wb, wop in h["w"] if not _contains(b, wb)]
            h["r"] = [(rb, rop) for rb, rop in h["r"] if not _contains(b, rb) or rop is op]
            h["w"].append((b, op))
        seen = set()
        for d in deps:
            if d is op or id(d) in seen:
                continue
            if d.eng == "pe" and op.eng == "pe" and not d.is_dma and not op.is_dma:
                continue
            seen.add(id(d))
            op.deps.append(d)

    def I(self, eng, method, extra_reads=(), extra_writes=(), **kw):
        reads = list(extra_reads)
        writes = list(extra_writes)
        for k, v in kw.items():
            if _is_ap(v):
                if k in ("out", "accum_out", "ap"):
                    writes.append(v)
                else:
                    reads.append(v)

        def emit(e, method=method, kw=kw):
            return getattr(e, method)(**kw)

        op = Op(eng, emit)
        self._track(op, reads, writes)
        self.ops[eng].append(op)
        self.all_ops.append(op)
        return op

    def dma(self, eng, out, in_, key=None):
        o_dram = "dram" in str(out.tensor.space).lower()
        sb = in_ if o_dram else out
        if key is None:
            key = sb.tensor.name

        def emit(e, out=out, in_=in_):
            return e.dma_start(out=out, in_=in_)

        op = Op(eng, emit)
        op.is_dma = True
        self._track(op, [in_], [out])
        prev = self.dma_last.get(key)
        if prev is not None and all(prev is not d for d in op.deps):
            op.deps.append(prev)
        self.dma_last[key] = op
        idx = self.dkey2idx.setdefault(key, len(self.dkey2idx))
        ent = self.dpool[idx]
        ent[1] += 16
        op.sem = ent[0]
        op.val = ent[1]
        self.ops[eng].append(op)
        self.all_ops.append(op)
        return op

    def flush(self):
        nc = self.nc
        for op in self.all_ops:
            for d in op.deps:
                d.signal = True
        for e in ENGS:
            for op in reversed(self.ops[e]):
                if not op.is_dma:
                    op.signal = True
                    break
        for e in ENGS:
            for op in self.ops[e]:
                if (not op.is_dma) and op.signal:
                    self.ecnt[e] += 1
                    op.sem = self.esem[e]
                    op.val = self.ecnt[e]
        self.barcnt += 1
        barval = self.barcnt
        used_dsems = [self.dpool[i] for i in range(len(self.dkey2idx))]

        def run(e_name, eh):
            waited = {}
            for op in self.ops[e_name]:
                need = {}
                for d in op.deps:
                    k = id(d.sem)
                    if waited.get(k, 0) >= d.val:
                        continue
                    if k not in need or need[k][1] < d.val:
                        need[k] = (d.sem, d.val)
                for k, (s, v) in need.items():
                    eh.wait_ge(s, v)
                    waited[k] = v
                    self.n_waits += 1
                ins = op.emit(eh)
                if op.is_dma:
                    ins.then_inc(op.sem, 16)
                elif op.signal:
                    ins.then_inc(op.sem, 1)
                self.n_ops += 1
            if e_name == "sp":
                for e in ENGS:
                    if e != "sp" and self.ecnt[e] > 0:
                        eh.wait_ge(self.esem[e], self.ecnt[e])
                for sem, cnt in used_dsems:
                    if cnt > 0:
                        eh.wait_ge(sem, cnt)
                eh.sem_inc(self.bar, 1)
            else:
                eh.wait_ge(self.bar, barval)

        with nc.Block() as block:

            @block.sync
            def _(eh):
                run("sp", eh)

            @block.tensor
            def _(eh):
                run("pe", eh)

            @block.scalar
            def _(eh):
                run("act", eh)

            @block.vector
            def _(eh):
                run("dve", eh)

            @block.gpsimd
            def _(eh):
                run("pool", eh)

        self._reset()


class Ctx:
    pass


def _vcol(vec_sb, c):
    return vec_sb[:, c:c + 1]


def build_program(layers=(0, 1, 2, 3), debug=False, stop_after=None):
    nc = bass.Bass("TRN2", target_bir_lowering=False)

    def din(name, shape, dt=F32):
        return nc.dram_tensor(name, shape, dt, kind="ExternalInput").ap()

    scratch_kind = "ExternalOutput" if debug else "Internal"

    def dscr(name, shape, dt):
        return nc.dram_tensor(name, shape, dt, kind=scratch_kind).ap()

    xT = din("xT", [D, S])
    vecs = din("vecs", [128, NV])
    consts = din("consts", [128, NC_])
    biasT = din("biasT", [128, 6144])
    w_in_even = din("w_in_even", [2, 1024, 3072])
    w_out_even = din("w_out_even", [2, 1024, 1024])
    w_in_odd = din("w_in_odd", [2, 1024, 2560])
    w_out_odd = din("w_out_odd", [2, 1024, 1024])
    w_up = din("w_up", [4, 1024, 5632])
    w_down = din("w_down", [4, 2816, 1024])
    yT = nc.dram_tensor("yT", [D, S], F32, kind="ExternalOutput").ap()

    XA = dscr("XA", [D, S], F32)
    XB = dscr("XB", [D, S], F32)
    QT = dscr("QT", [512, S], BF16)
    KT = dscr("KT", [512, S], BF16)
    VN = dscr("VN", [S, 512], BF16)
    MIX = dscr("MIX", [D, S], BF16)
    AT = dscr("AT", [DFF, S], BF16)

    with ExitStack() as es0:
        P = Prog(nc, es0)
        G = Ctx()
        G.nc = nc
        G.P = P
        sb0 = lambda n, s, d: es0.enter_context(nc.sbuf_tensor(n, s, d))
        G.PS = es0.enter_context(nc.psum_tensor("PS", [128, 4096], F32))
        G.vec = sb0("vec_sb", [128, NV], F32)
        G.ones_f = sb0("ones_f", [128, 128], F32)
        G.ones_b = sb0("ones_b", [128, 64], BF16)
        G.ones_bb = sb0("ones_bb", [128, 128], BF16)
        G.eps = sb0("eps_t", [128, 1], F32)
        G.one = sb0("one_t", [128, 1], F32)
        G.tri = sb0("tri_b", [128, 128], BF16)
        G.su = sb0("su_b", [128, 128], BF16)
        G.ident = sb0("ident_b", [128, 128], BF16)
        G.consts = consts
        G.biasT = biasT

        P.dma("sp", out=G.vec[:, :], in_=vecs[:, :])
        P.dma("pool", out=G.tri[:, :], in_=consts[:, C_TRI:C_TRI + 128])
        P.dma("pool", out=G.su[:, :], in_=consts[:, C_SU:C_SU + 128])
        P.dma("pool", out=G.ident[:, :], in_=consts[:, C_ID:C_ID + 128])
        P.I("dve", "memset", ap=G.ones_f[:, :], constant=1.0)
        P.I("dve", "memset", ap=G.ones_b[:, :], constant=1.0)
        P.I("dve", "memset", ap=G.ones_bb[:, :], constant=1.0)
        P.I("dve", "memset", ap=G.eps[:, :], constant=EPS)
        P.I("dve", "memset", ap=G.one[:, :], constant=1.0)
        P.flush()

        n_layers = len(layers)
        for li, l in enumerate(layers):
            src = xT if li == 0 else XA
            last = li == n_layers - 1
            dst = yT if last else XA
            even = (l % 2 == 0)
            i2 = l // 2
            if even:
                w_in = w_in_even[i2]
                w_out = w_out_even[i2]
            else:
                w_in = w_in_odd[i2]
                w_out = w_out_odd[i2]
            phase1(G, l, src, w_in, QT, KT, VN, MIX)
            if stop_after == (l, 1):
                break
            if even:
                phase2_even(G, l, QT, KT, VN, MIX)
            else:
                phase2_odd(G, l, QT, KT, VN, MIX)
            if stop_after == (l, 2):
                break
            phase3(G, l, src, XB, MIX, AT, w_out, w_up[l])
            if stop_after == (l, 3):
                break
            phase4(G, l, XB, dst, AT, w_down[l])
        G.n_ops = P.n_ops
        G.n_waits = P.n_waits
    return nc


def bank(G, i):
    return G.PS[:, i * 512:(i + 1) * 512]


class Rot:
    def __init__(self, items):
        self.items = list(items)
        self.i = 0

    def next(self):
        v = self.items[self.i % len(self.items)]
        self.i += 1
        return v


def act(P, out, in_, func, bias=None, scale=None):
    kw = dict(out=out, in_=in_, func=func)
    if bias is not None:
        kw["bias"] = bias
    if scale is not None:
        kw["scale"] = scale
    return P.I("act", "activation", **kw)


def mm(P, out, lhsT, rhs, start, stop):
    return P.I("pe", "matmul", out=out, lhsT=lhsT, rhs=rhs, start=start, stop=stop)


def rms_rstd(G, sb_alloc, xs, nchunks, rstd, sqrot, lnv, ps_bank, inv_n):
    P = G.P
    for c in range(nchunks):
        sq = sqrot.next()
        act(P, sq[:, :], xs[:, c, :], AF.Square)
        mm(P, ps_bank, G.ones_bb[:, :], sq[:, :], c == 0, c == nchunks - 1)
    act(P, lnv[:, :], ps_bank, AF.Ln, bias=EPS, scale=inv_n)
    act(P, rstd[:, :], lnv[:, :], AF.Exp, scale=-0.5)


def load_w(G, wsb, w_ap, nk, groups, eng="pool"):
    wr = w_ap.rearrange("(c p) n -> p c n", p=128)
    for gi, (c0, c1) in enumerate(groups):
        G.P.dma(eng, out=wsb[:, :, c0:c1], in_=wr[:, :, c0:c1], key=wsb.name + "_%d" % (gi % 6))


def interleave_events(main, events):
    ev = sorted(enumerate(events), key=lambda t: (t[1][0], t[0]))
    k = 0
    for i, f in enumerate(main):
        f()
        while k < len(ev) and ev[k][1][0] < i + 1:
            ev[k][1][1]()
            k += 1
    while k < len(ev):
        ev[k][1][1]()
        k += 1


def phase1(G, l, src, w_in, QT, KT, VN, MIX):
    nc, P, vec = G.nc, G.P, G.vec
    even = (l % 2 == 0)
    i2 = l // 2
    NIN = 3072 if even else 2560
    srcr = src.rearrange("(c p) t -> p c t", p=128)
    with ExitStack() as es:
        sb = lambda n, s, d: es.enter_context(nc.sbuf_tensor("p1_%d_" % l + n, s, d))
        W = sb("W", [128, 8, NIN], BF16)
        if even:
            grp = [(512, 768), (1024, 1280), (0, 256), (768, 1024), (1280, 1536), (256, 512)]
            grp += [(c, c + 256) for c in range(1536, 3072, 256)]
        else:
            grp = [(512, 768), (0, 256), (768, 1024), (256, 512)]
            grp += [(c, c + 256) for c in range(1024, 2560, 256)]
        load_w(G, W, w_in, 8, grp)
        xbuf = [sb("x%d" % i, [128, 8, TT], F32) for i in range(2)]
        hbuf = [sb("h%d" % i, [128, 8, TT], BF16) for i in range(2)]
        sqrot = Rot([sb("sq%d" % i, [128, TT], BF16) for i in range(7)])
        sqfrot = Rot([sb("sqf%d" % i, [128, TT], F32) for i in range(2)])
        lnv = sb("lnv", [128, TT], F32)
        rstd = sb("rstd", [128, TT], F32)
        qkrot = Rot([sb("qk%d" % i, [128, TT], BF16) for i in range(4)])
        vrot = Rot([sb("v%d" % i, [128, 512], BF16) for i in range(2)])
        yrot = Rot([sb("y%d" % i, [128, TT], BF16) for i in range(2)])
        brot = Rot([1, 2, 3, 4, 5, 6, 7])
        if even:
            gcrot = Rot([sb("gc%d" % i, [128, TT], F32) for i in range(2)])
            Z = [sb("Z%d" % i, [128, TT + 2], F32) for i in range(4)]
            trot = Rot([sb("t%d" % i, [128, TT], F32) for i in range(2)])
            for i in range(4):
                P.I("pool", "memset", ap=Z[i][:, 0:2], constant=0.0)
        else:
            sgrot = Rot([sb("sg%d" % i, [128, TT], F32) for i in range(2)])
            GC = [sb("GC%d" % i, [128, TT + 30], BF16) for i in range(4)]
            diag = sb("diag", [128, 124, 128], BF16)
            Y = sb("Y", [128, 4, TT], F32)
            mean = sb("mean", [128, TT], F32)
            msq = sb("msq", [128, TT], F32)
            var = sb("var", [128, TT], F32)
            lnr = sb("lnr", [128, TT], F32)
            rs2 = sb("rs2", [128, TT], F32)
            ynrot = Rot([sb("yn%d" % i, [128, TT], F32) for i in range(2)])
            for i in range(4):
                P.I("pool", "memset", ap=GC[i][:, 0:30], constant=0.0)
                for j in range(31):
                    P.I("dve", "tensor_scalar", out=diag[:, i * 31 + j, :], in0=G.ident[:, :],
                        scalar1=_vcol(vec, WCC0 + (i2 * 31 + j) * 4 + i), scalar2=None, op0=ALU.mult)

        def proj(h, oc):
            b = bank(G, brot.next())
            for c in range(8):
                mm(P, b, W[:, c, oc * 128:(oc + 1) * 128], h[:, c, :], c == 0, c == 7)
            return b

        OWN = {}

        def norm_events(tt):
            xs = xbuf[tt % 2]
            h = hbuf[tt % 2]
            ev = []
            sqs = {}

            def mk_sq(c):
                def f():
                    sq = sqrot.next()
                    act(P, sq[:, :], xs[:, c, :], AF.Square)
                    sqs[c] = sq
                    OWN[sq.name] = (tt, c)
                return f

            def mk_st(c):
                def f():
                    assert OWN[sqs[c].name] == (tt, c)
                    mm(P, bank(G, 0), G.ones_bb[:, :], sqs[c][:, :], c == 0, c == 7)
                return f
            for c in range(8):
                ev.append((0.0 + 0.5 * c, mk_sq(c)))
                ev.append((0.0 + 0.5 * c + 2.5, mk_st(c)))

            def rs():
                act(P, rstd[:, :], bank(G, 0), AF.Ln, bias=EPS, scale=1.0 / D)
                act(P, rstd[:, :], rstd[:, :], AF.Exp, scale=-0.5)
            ev.append((6.2, rs))

            def mk_h(c0):
                def f():
                    for c in (c0, c0 + 1):
                        P.I("dve", "scalar_tensor_tensor", out=h[:, c, :], in0=xs[:, c, :],
                            scalar=_vcol(vec, GN0 + (l * 4 + 0) * 8 + c), in1=rstd[:, :], op0=ALU.mult, op1=ALU.mult)
                return f
            for j, c0 in enumerate(range(0, 8, 2)):
                ev.append((7.2 + 1.0 * j, mk_h(c0)))
            return ev

        def tile_main(tt):
            t0 = tt * TT
            h = hbuf[tt % 2]
            cl = []
            if even:
                qoc, koc, vcol = 12, 16, 2560

                def mk_conv(i):
                    def f():
                        pc = proj(h, 4 + i)
                        gcb = gcrot.next()
                        act(P, gcb[:, :], pc, AF.Copy)
                        px = proj(h, 8 + i)
                        if tt > 0:
                            P.I("pool", "tensor_copy", out=Z[i][:, 0:2], in_=Z[i][:, TT:TT + 2])
                        P.I("dve", "tensor_tensor", out=Z[i][:, 2:TT + 2], in0=px, in1=gcb[:, :], op=ALU.mult)
                        pb = proj(h, i)
                        t = trot.next()
                        wc = lambda j: _vcol(vec, WSC0 + (i2 * 3 + j) * 4 + i)
                        P.I("dve", "tensor_scalar", out=t[:, :], in0=Z[i][:, 0:TT], scalar1=wc(0), scalar2=None, op0=ALU.mult)
                        P.I("dve", "scalar_tensor_tensor", out=t[:, :], in0=Z[i][:, 1:TT + 1], scalar=wc(1), in1=t[:, :], op0=ALU.mult, op1=ALU.add)
                        P.I("dve", "scalar_tensor_tensor", out=t[:, :], in0=Z[i][:, 2:TT + 2], scalar=wc(2), in1=t[:, :], op0=ALU.mult, op1=ALU.add)
                        ya = yrot.next()
                        P.I("dve", "tensor_tensor", out=ya[:, :], in0=pb, in1=t[:, :], op=ALU.mult)
                        P.dma("sp", out=MIX[i * 128:(i + 1) * 128, t0:t0 + TT], in_=ya[:, :])
                    return f
                for i in range(4):
                    cl.append(mk_conv(i))
            else:
                qoc, koc, vcol = 8, 12, 2048

                def mk_cconv(i):
                    def f():
                        pg = proj(h, 4 + i)
                        sg = sgrot.next()
                        act(P, sg[:, :], pg, AF.Sigmoid)
                        pa = proj(h, i)
                        if tt > 0:
                            P.I("pool", "tensor_copy", out=GC[i][:, 0:30], in_=GC[i][:, TT:TT + 30])
                        P.I("dve", "tensor_tensor", out=GC[i][:, 30:TT + 30], in0=pa, in1=sg[:, :], op=ALU.mult)
                        pcv = bank(G, brot.next())
                        for j in range(31):
                            mm(P, pcv, diag[:, i * 31 + j, :], GC[i][:, j:j + TT], j == 0, j == 30)
                        act(P, Y[:, i, :], pcv, AF.Identity, bias=_vcol(vec, BCC0 + i2 * 4 + i), scale=1.0)
                    return f
                for i in range(4):
                    cl.append(mk_cconv(i))

                def ln_part():
                    sbk = bank(G, brot.next())
                    sqk = bank(G, brot.next())
                    for i in range(4):
                        mm(P, sbk, G.ones_f[:, :], Y[:, i, :], i == 0, i == 3)
                    for i in range(4):
                        sq = sqfrot.next()
                        act(P, sq[:, :], Y[:, i, :], AF.Square)
                        mm(P, sqk, G.ones_f[:, :], sq[:, :], i == 0, i == 3)
                    act(P, mean[:, :], sbk, AF.Copy, scale=1.0 / 512)
                    P.I("dve", "tensor_tensor", out=msq[:, :], in0=mean[:, :], in1=mean[:, :], op=ALU.mult)
                    P.I("dve", "scalar_tensor_tensor", out=var[:, :], in0=sqk, scalar=1.0 / 512, in1=msq[:, :], op0=ALU.mult, op1=ALU.subtract)
                    act(P, lnr[:, :], var[:, :], AF.Ln, bias=EPS, scale=1.0)
                    act(P, rs2[:, :], lnr[:, :], AF.Exp, scale=-0.5)
                    for i in range(4):
                        yn = ynrot.next()
                        P.I("dve", "tensor_tensor", out=yn[:, :], in0=Y[:, i, :], in1=mean[:, :], op=ALU.subtract)
                        P.I("dve", "tensor_tensor", out=yn[:, :], in0=yn[:, :], in1=rs2[:, :], op=ALU.mult)
                        yc = yrot.next()
                        act(P, yc[:, :], yn[:, :], AF.Silu, bias=_vcol(vec, LNB0 + i2 * 4 + i), scale=_vcol(vec, LNG0 + i2 * 4 + i))
                        P.dma("sp", out=MIX[i * 128:(i + 1) * 128, t0:t0 + TT], in_=yc[:, :])

            def mk_qk(i):
                def f():
                    pq = proj(h, qoc + i)
                    qs = qkrot.next()
                    act(P, qs[:, :], pq, AF.Copy, scale=0.125)
                    P.dma("sp", out=QT[i * 128:(i + 1) * 128, t0:t0 + TT], in_=qs[:, :])
                    pk = proj(h, koc + i)
                    ks = qkrot.next()
                    P.I("dve", "tensor_copy", out=ks[:, :], in_=pk)
                    P.dma("sp", out=KT[i * 128:(i + 1) * 128, t0:t0 + TT], in_=ks[:, :])
                return f

            def mk_v(sub):
                def f():
                    b = bank(G, brot.next())
                    for c in range(8):
                        mm(P, b, h[:, c, sub * 128:(sub + 1) * 128], W[:, c, vcol:vcol + 512], c == 0, c == 7)
                    vs = vrot.next()
                    act(P, vs[:, :], b, AF.Copy)
                    P.dma("sp", out=VN[t0 + sub * 128:t0 + (sub + 1) * 128, :], in_=vs[:, :])
                return f
            if not even:
                cl.append(mk_qk(0))
                cl.append(mk_qk(1))
                cl.append(ln_part)
                cl.append(mk_qk(2))
                cl.append(mk_qk(3))
            else:
                for i in range(4):
                    cl.append(mk_qk(i))
            for sub in range(4):
                cl.append(mk_v(sub))
            return cl

        P.dma("sp", out=xbuf[0][:, :, :], in_=srcr[:, :, 0:TT])
        P.dma("sp", out=xbuf[1][:, :, :], in_=srcr[:, :, TT:2 * TT])
        interleave_events([], norm_events(0))
        for tt in range(NT):
            interleave_events(tile_main(tt), norm_events(tt + 1) if tt + 1 < NT else [])
            if tt + 2 < NT:
                P.dma("sp", out=xbuf[tt % 2][:, :, :], in_=srcr[:, :, (tt + 2) * TT:(tt + 3) * TT])
        P.flush()


def phase2_even(G, l, QT, KT, VN, MIX):
    nc, P = G.nc, G.P
    with ExitStack() as es:
        sb = lambda n, s, d: es.enter_context(nc.sbuf_tensor("p2e_%d_" % l + n, s, d))
        VD = [sb("VD%d" % g, [128, 32, 512], BF16) for g in range(3)]
        bias = sb("bias", [128, 6144], F32)
        maskd = sb("maskd", [128, 256], F32)
        qsb = [sb("q%d" % i, [128, S], BF16) for i in range(1)]
        ksb = [sb("k%d" % i, [128, S], BF16) for i in range(1)]
        acc = sb("acc", [128, 2, S], F32)
        ysb = sb("ysb", [128, S], BF16)
        sbf = [sb("sbf%d" % i, [128, 1024], F32) for i in range(2)]
        Ab = [sb("A%d" % i, [128, 1024], BF16) for i in range(4)]
        for g, (win, d) in enumerate(DA_PAIRS):
            vr = VN.rearrange("(blk p r) f -> p blk r f", p=128, r=d)
            for blk in range(32 // d):
                P.dma("sp" if blk % 2 == 0 else "pool", out=VD[g][:, blk * d:(blk + 1) * d, :], in_=vr[:, blk, :, :],
                      key="VD%d_%d" % (g, blk % 4))
        P.dma("sp", out=bias[:, :], in_=G.biasT[:, :])
        P.dma("sp", out=maskd[:, :], in_=G.consts[:, C_MASKD:C_MASKD + 256])
        for gh in range(24):
            P.I("pool", "tensor_tensor", out=bias[:, gh * 256:(gh + 1) * 256], in0=bias[:, gh * 256:(gh + 1) * 256], in1=maskd[:, :], op=ALU.add)
        srot = Rot([0, 1])
        arot = Rot([0, 1, 2, 3])
        fr = Rot([0, 1])
        for hp in range(4):
            q = qsb[0]
            k = ksb[0]
            P.dma("sp", out=q[:, :], in_=QT[hp * 128:(hp + 1) * 128, :])
            P.dma("sp", out=k[:, :], in_=KT[hp * 128:(hp + 1) * 128, :])
            units = []
            for g, (win, d) in enumerate(DA_PAIRS):
                if d == 1:
                    for gi in range(8):
                        units.append(dict(g=g, d=d, blocks=[((gi * 4 + bi) * 128, 0) for bi in range(4)], aview=("c", gi * 512)))
                elif d == 4:
                    for gi in range(8):
                        units.append(dict(g=g, d=d, blocks=[(gi * 128, r) for r in range(4)], aview=("s4", gi * 512)))
                else:
                    for sg in range(2):
                        for rq in range(4):
                            units.append(dict(g=g, d=d, blocks=[(sg * 128, rq * 4 + bi) for bi in range(4)], aview=("s16", sg * 2048, rq)))
            for ui, u in enumerate(units):
                u["o_bank"] = 4 + (ui % 2) * 2

            def stageA(u):
                g, d, blocks = u["g"], u["d"], u["blocks"]
                u["A"] = []
                for e in range(2):
                    hh = hp * 2 + e
                    hr = slice(e * 64, (e + 1) * 64)
                    sbase = srot.next() * 1024
                    Sps = G.PS[:, sbase:sbase + 1024]
                    for bi, (m0, r) in enumerate(blocks):
                        qs0 = m0 * d + r
                        for jt in range(2):
                            mk0 = m0 - 128 + jt * 128
                            if mk0 < 0:
                                continue
                            ks0 = mk0 * d + r
                            c0 = sbase + (bi * 2 + jt) * 128
                            mm(P, G.PS[:, c0:c0 + 128], k[hr, ks0:ks0 + 127 * d + 1:d], q[hr, qs0:qs0 + 127 * d + 1:d], True, True)
                    sf = sbf[fr.next()]
                    bsl = bias[:, (g * 8 + hh) * 256:(g * 8 + hh + 1) * 256].unsqueeze(1).broadcast_to([128, 4, 256])
                    P.I("dve", "tensor_tensor", out=sf[:, :].rearrange("p (b f) -> p b f", b=4),
                        in0=Sps.rearrange("p (b f) -> p b f", b=4), in1=bsl, op=ALU.add)
                    A = Ab[arot.next()]
                    act(P, A[:, :], sf[:, :], AF.Exp)
                    u["A"].append(A)

            def stageB(u):
                g, d, blocks = u["g"], u["d"], u["blocks"]
                o_bank = u["o_bank"]
                den_bank = o_bank + 1
                for e in range(2):
                    hh = hp * 2 + e
                    hr = slice(e * 64, (e + 1) * 64)
                    A = u["A"][e]
                    for bi, (m0, r) in enumerate(blocks):
                        first = True
                        for jt in range(2):
                            mk0 = m0 - 128 + jt * 128
                            if mk0 < 0:
                                continue
                            tile = (mk0 // 128) * d + r
                            asl = A[:, (bi * 2 + jt) * 128:(bi * 2 + jt) * 128 + 128]
                            co = o_bank * 512 + bi * 128
                            cd = den_bank * 512 + bi * 128
                            mm(P, G.PS[hr, co:co + 128], VD[g][:, tile, hh * 64:(hh + 1) * 64], asl, first, jt == 1)
                            mm(P, G.PS[hr, cd:cd + 128], G.ones_b[:, :], asl, first, jt == 1)
                            first = False

            def stageC(u):
                aview = u["aview"]
                o_bank = u["o_bank"]
                for w_, bk in ((0, o_bank), (1, o_bank + 1)):
                    src_ps = bank(G, bk)
                    if aview[0] == "c":
                        c0 = aview[1]
                        if w_ == 0:
                            act(P, acc[:, w_, c0:c0 + 512], src_ps, AF.Copy)
                        else:
                            P.I("dve", "tensor_copy", out=acc[:, w_, c0:c0 + 512], in_=src_ps)
                    elif aview[0] == "s4":
                        c0 = aview[1]
                        av = acc[:, w_, c0:c0 + 512].rearrange("p (i r) -> p r i", r=4)
                        P.I("dve", "tensor_tensor", out=av, in0=src_ps.rearrange("p (r i) -> p r i", r=4), in1=av, op=ALU.add)
                    else:
                        c0, rq = aview[1], aview[2]
                        av = acc[:, w_, c0:c0 + 2048].rearrange("p (i r) -> p r i", r=16)[:, rq * 4:(rq + 1) * 4, :]
                        P.I("dve", "tensor_tensor", out=av, in0=src_ps.rearrange("p (r i) -> p r i", r=4), in1=av, op=ALU.add)

            nu = len(units)
            for it in range(nu + 2):
                if it < nu:
                    stageA(units[it])
                if 0 <= it - 1 < nu:
                    stageB(units[it - 1])
                if 0 <= it - 2 < nu:
                    stageC(units[it - 2])
            for qh in range(4):
                cs = slice(qh * 1024, (qh + 1) * 1024)
                act(P, acc[:, 1, cs], acc[:, 1, cs], AF.Ln)
                act(P, acc[:, 1, cs], acc[:, 1, cs], AF.Exp, scale=-1.0)
                P.I("pool", "tensor_tensor", out=ysb[:, cs], in0=acc[:, 0, cs], in1=acc[:, 1, cs], op=ALU.mult)
            P.dma("sp", out=MIX[512 + hp * 128:512 + (hp + 1) * 128, :], in_=ysb[:, :])
        P.flush()


def phase2_odd(G, l, QT, KT, VN, MIX):
    nc, P = G.nc, G.P
    with ExitStack() as es:
        sb = lambda n, s, d: es.enter_context(nc.sbuf_tensor("p2o_%d_" % l + n, s, d))
        V = sb("V", [128, 32, 512], BF16)
        masks = sb("masks", [128, 4, 512], BF16)
        qsb = [sb("q%d" % i, [128, S], BF16) for i in range(2)]
        ksb = [sb("k%d" % i, [128, S], BF16) for i in range(2)]
        ysb = [sb("ysb%d" % i, [128, S], BF16) for i in range(2)]
        NSL = 2
        E = [[sb("E%d_%d" % (i, j), [128, 2, 512], F32) for j in range(3)] for i in range(NSL)]
        SP = [[sb("SP%d_%d" % (i, j), [128, 2, 512], BF16) for j in range(3)] for i in range(NSL)]
        Gt = [[sb("G%d_%d" % (i, j), [128, 2, 512], F32) for j in range(2)] for i in range(NSL)]
        At = [[sb("At%d_%d" % (i, j), [128, 2, 512], BF16) for j in range(2)] for i in range(NSL)]
        vr = VN.rearrange("(t p) f -> p t f", p=128)
        for qq in range(4):
            P.dma("sp", out=V[:, qq * 8:(qq + 1) * 8, :], in_=vr[:, qq * 8:(qq + 1) * 8, :], key="Vo_%d" % qq)
        P.dma("pool", out=masks[:, :, :], in_=G.consts[:, C_MASKS:C_MASKS + 2048].rearrange("p (o f) -> p o f", o=4))
        P.dma("sp", out=qsb[0][:, :], in_=QT[0:128, :])
        P.dma("sp", out=ksb[0][:, :], in_=KT[0:128, :])
        slot_qts = [[7, 4, 3, 0], [6, 5, 2, 1]]
        for hp in range(4):
            q = qsb[hp % 2]
            k = ksb[hp % 2]
            ys = ysb[hp % 2]
            if hp + 1 < 4:
                P.dma("sp", out=qsb[(hp + 1) % 2][:, :], in_=QT[(hp + 1) * 128:(hp + 2) * 128, :])
                P.dma("sp", out=ksb[(hp + 1) % 2][:, :], in_=KT[(hp + 1) * 128:(hp + 2) * 128, :])
            streams = []
            for s_ in range(NSL):
                st = []
                for qt in slot_qts[s_]:
                    n = 4 * qt + 4
                    for j in range(n):
                        st.append(dict(qt=qt, kb=4 * qt + 3 - j, first=(j == 0), last=(j == n - 1), idx=len(st)))
                streams.append(st)
            nit = max(len(st) for st in streams) + 2

            def stageA(s_, stp):
                qt, kb, idx = stp["qt"], stp["kb"], stp["idx"]
                o = kb - 4 * qt
                for e in range(2):
                    hr = slice(e * 64, (e + 1) * 64)
                    mm(P, bank(G, e), k[hr, kb * 128:(kb + 1) * 128], q[hr, qt * 512:(qt + 1) * 512], True, o < 0)
                    if o >= 0:
                        mm(P, bank(G, e), G.ident[:, :], masks[:, o, :], False, True)
                Ec = E[s_][idx % 3]
                act(P, Ec[:, :, :], G.PS[:, 0:1024].rearrange("p (e f) -> p e f", e=2), AF.Exp)
                spc = SP[s_][idx % 3]
                act(P, spc[:, :, :], Ec[:, :, :], AF.Ln, bias=1.0, scale=1.0)

            def stageB(s_, stp):
                idx = stp["idx"]
                spc = SP[s_][idx % 3]
                spp = SP[s_][(idx - 1) % 3]
                for e in range(2):
                    pb = bank(G, 2 + 2 * s_ + e)
                    if stp["first"]:
                        mm(P, pb, G.tri[:, :], spc[:, e, :], True, True)
                    else:
                        mm(P, pb, G.su[:, :], spp[:, e, :], False, False)
                        mm(P, pb, G.tri[:, :], spc[:, e, :], False, True)
                Gc = Gt[s_][idx % 2]
                c0 = (2 + 2 * s_) * 512
                act(P, Gc[:, :, :], G.PS[:, c0:c0 + 1024].rearrange("p (e f) -> p e f", e=2), AF.Exp, scale=-1.0)
                P.I("dve", "tensor_tensor", out=At[s_][idx % 2][:, :, :], in0=E[s_][idx % 3][:, :, :], in1=Gc[:, :, :], op=ALU.mult)

            def stageC(s_, stp):
                qt, kb, idx = stp["qt"], stp["kb"], stp["idx"]
                for e in range(2):
                    hh = hp * 2 + e
                    hr = slice(e * 64, (e + 1) * 64)
                    ob = G.PS[hr, (6 + s_) * 512:(7 + s_) * 512]
                    mm(P, ob, V[:, kb, hh * 64:(hh + 1) * 64], At[s_][idx % 2][:, e, :], stp["first"], stp["last"])
                if stp["last"]:
                    P.I("dve", "tensor_copy", out=ys[:, qt * 512:(qt + 1) * 512], in_=bank(G, 6 + s_))

            for it in range(nit):
                for s_ in range(NSL):
                    if it < len(streams[s_]):
                        stageA(s_, streams[s_][it])
                for s_ in range(NSL):
                    if 0 <= it - 1 < len(streams[s_]):
                        stageB(s_, streams[s_][it - 1])
                for s_ in range(NSL):
                    if 0 <= it - 2 < len(streams[s_]):
                        stageC(s_, streams[s_][it - 2])
            P.dma("sp", out=MIX[512 + hp * 128:512 + (hp + 1) * 128, :], in_=ys[:, :])
        P.flush()


def post_norm_residual(G, l, j, M, xs, rstd, sqrot, lnv, ps_bank, tmprot):
    P, vec = G.P, G.vec
    rms_rstd(G, None, M, 8, rstd, sqrot, lnv, ps_bank, 1.0 / D)
    for c in range(8):
        tmp = tmprot.next()
        P.I("dve", "scalar_tensor_tensor", out=tmp[:, :], in0=M[:, c, :], scalar=_vcol(vec, GN0 + (l * 4 + j) * 8 + c),
            in1=rstd[:, :], op0=ALU.mult, op1=ALU.mult)
        P.I("pool", "tensor_tensor", out=xs[:, c, :], in0=xs[:, c, :], in1=tmp[:, :], op=ALU.add)


def interleave(main, side, offset=1, stride=1):
    k = 0
    for i, f in enumerate(main):
        f()
        while k < len(side) and i >= offset + k * stride:
            side[k]()
            k += 1
    while k < len(side):
        side[k]()
        k += 1


def phase3(G, l, src, XB, MIX, AT, w_out, w_up):
    nc, P, vec = G.nc, G.P, G.vec
    srcr = src.rearrange("(c p) t -> p c t", p=128)
    xbr = XB.rearrange("(c p) t -> p c t", p=128)
    mixr = MIX.rearrange("(c p) t -> p c t", p=128)
    with ExitStack() as es:
        sb = lambda n, s, d: es.enter_context(nc.sbuf_tensor("p3_%d_" % l + n, s, d))
        WO = sb("WO", [128, 8, 1024], BF16)
        WU = sb("WU", [128, 8, 2 * DFF], BF16)
        load_w(G, WO, w_out, 8, [(c, c + 256) for c in range(0, 1024, 256)])
        grp = []
        for c in range(0, DFF, 256):
            grp.append((c, c + 256))
            grp.append((DFF + c, DFF + c + 256))
        load_w(G, WU, w_up, 8, grp)
        xs = sb("x", [128, 8, TT], F32)
        mbuf = [sb("mix%d" % i, [128, 8, TT], BF16) for i in range(2)]
        M = sb("M", [128, 8, TT], F32)
        h2b = [sb("h2_%d" % i, [128, 8, TT], BF16) for i in range(2)]
        sqrot = Rot([sb("sq%d" % i, [128, TT], BF16) for i in range(6)])
        sq2rot = Rot([sb("sqb%d" % i, [128, TT], BF16) for i in range(6)])
        tmprot = Rot([sb("tmp%d" % i, [128, TT], F32) for i in range(2)])
        rstd = sb("rstd", [128, TT], F32)
        Gb = [sb("Gb%d" % i, [128, TT + 2], F32) for i in range(2)]
        H = sb("H", [128, NFC, 2], F32)
        trot = Rot([sb("t%d" % i, [128, TT], F32) for i in range(2)])
        srot = Rot([sb("s%d" % i, [128, TT], F32) for i in range(2)])
        arot = Rot([sb("a%d" % i, [128, TT], BF16) for i in range(3)])
        brot = Rot([2, 3, 4, 5, 6, 7])
        P.I("pool", "memset", ap=H[:, :, :], constant=0.0)
        P.dma("sp", out=mbuf[0][:, :, :], in_=mixr[:, :, 0:TT])

        OWN = {}

        def A_events(tt):
            t0 = tt * TT
            mx = mbuf[tt % 2]
            h2 = h2b[tt % 2]
            ev = []

            def a_load():
                P.dma("sp", out=xs[:, :, :], in_=srcr[:, :, t0:t0 + TT])
                if tt + 1 < NT:
                    P.dma("sp", out=mbuf[(tt + 1) % 2][:, :, :], in_=mixr[:, :, t0 + TT:t0 + 2 * TT])
            ev.append((0.0, a_load))
            sqs = {}

            def mk_oc(oc):
                def f():
                    b = bank(G, brot.next())
                    for c in range(8):
                        mm(P, b, WO[:, c, oc * 128:(oc + 1) * 128], mx[:, c, :], c == 0, c == 7)
                    act(P, M[:, oc, :], b, AF.Copy)
                    sq = sqrot.next()
                    act(P, sq[:, :], M[:, oc, :], AF.Square)
                    sqs[oc] = sq
                    OWN[sq.name] = ("a", tt, oc)
                return f

            def mk_st1(oc):
                def f():
                    assert OWN[sqs[oc].name] == ("a", tt, oc)
                    mm(P, bank(G, 0), G.ones_bb[:, :], sqs[oc][:, :], oc == 0, oc == 7)
                return f
            for oc in range(8):
                ev.append((0.1 + 0.5 * oc, mk_oc(oc)))
                ev.append((0.1 + 0.5 * oc + 2.6, mk_st1(oc)))

            def a_rstd1():
                act(P, rstd[:, :], bank(G, 0), AF.Ln, bias=EPS, scale=1.0 / D)
                act(P, rstd[:, :], rstd[:, :], AF.Exp, scale=-0.5)
            ev.append((6.4, a_rstd1))
            sq2 = {}

            def mk_res(c):
                def f():
                    tmp = tmprot.next()
                    P.I("dve", "scalar_tensor_tensor", out=tmp[:, :], in0=M[:, c, :], scalar=_vcol(vec, GN0 + (l * 4 + 1) * 8 + c),
                        in1=rstd[:, :], op0=ALU.mult, op1=ALU.mult)
                    P.I("dve", "tensor_tensor", out=xs[:, c, :], in0=xs[:, c, :], in1=tmp[:, :], op=ALU.add)
                    sq = sq2rot.next()
                    act(P, sq[:, :], xs[:, c, :], AF.Square)
                    sq2[c] = sq
                    OWN[sq.name] = ("b", tt, c)
                return f

            def mk_st2(c):
                def f():
                    assert OWN[sq2[c].name] == ("b", tt, c)
                    mm(P, bank(G, 1), G.ones_bb[:, :], sq2[c][:, :], c == 0, c == 7)
                return f
            for c in range(8):
                ev.append((7.5 + 0.45 * c, mk_res(c)))
                ev.append((7.5 + 0.45 * c + 2.5, mk_st2(c)))

            def a_rstd2():
                P.dma("sp", out=xbr[:, :, t0:t0 + TT], in_=xs[:, :, :])
                act(P, rstd[:, :], bank(G, 1), AF.Ln, bias=EPS, scale=1.0 / D)
                act(P, rstd[:, :], rstd[:, :], AF.Exp, scale=-0.5)
            ev.append((13.2, a_rstd2))

            def mk_h(c0):
                def f():
                    for c in (c0, c0 + 1):
                        P.I("dve", "scalar_tensor_tensor", out=h2[:, c, :], in0=xs[:, c, :],
                            scalar=_vcol(vec, GN0 + (l * 4 + 2) * 8 + c), in1=rstd[:, :], op0=ALU.mult, op1=ALU.mult)
                return f
            for j, c0 in enumerate(range(0, 8, 2)):
                ev.append((14.5 + 0.5 * j, mk_h(c0)))
            return ev

        def F_closures(tt):
            t0 = tt * TT
            h2 = h2b[tt % 2]
            cl = []

            def mk_fc(fc):
                def f():
                    pg = bank(G, brot.next())
                    for c in range(8):
                        mm(P, pg, WU[:, c, fc * 128:(fc + 1) * 128], h2[:, c, :], c == 0, c == 7)
                    pu = bank(G, brot.next())
                    for c in range(8):
                        mm(P, pu, WU[:, c, DFF + fc * 128:DFF + (fc + 1) * 128], h2[:, c, :], c == 0, c == 7)
                    gb = Gb[fc % 2]
                    P.I("pool", "tensor_copy", out=gb[:, 0:2], in_=H[:, fc, :])
                    act(P, gb[:, 2:TT + 2], pg, AF.Copy)
                    P.I("pool", "tensor_copy", out=H[:, fc, :], in_=gb[:, TT:TT + 2])
                    t = trot.next()
                    wc = lambda j: _vcol(vec, WFC0 + (l * 3 + j) * NFC + fc)
                    act(P, t[:, :], pg, AF.Identity, bias=_vcol(vec, BFC0 + l * NFC + fc), scale=wc(2))
                    P.I("dve", "scalar_tensor_tensor", out=t[:, :], in0=gb[:, 0:TT], scalar=wc(0), in1=t[:, :], op0=ALU.mult, op1=ALU.add)
                    P.I("dve", "scalar_tensor_tensor", out=t[:, :], in0=gb[:, 1:TT + 1], scalar=wc(1), in1=t[:, :], op0=ALU.mult, op1=ALU.add)
                    sv = srot.next()
                    act(P, sv[:, :], t[:, :], AF.Silu)
                    a = arot.next()
                    P.I("dve", "tensor_tensor", out=a[:, :], in0=pu, in1=sv[:, :], op=ALU.mult)
                    P.dma("sp", out=AT[fc * 128:(fc + 1) * 128, t0:t0 + TT], in_=a[:, :])
                return f
            for fc in range(NFC):
                cl.append(mk_fc(fc))
            return cl

        interleave_events([], A_events(0))
        for tt in range(NT):
            side = A_events(tt + 1) if tt + 1 < NT else []
            interleave_events(F_closures(tt), side)
        P.flush()


def phase4(G, l, XB, dst, AT, w_down):
    nc, P, vec = G.nc, G.P, G.vec
    xbr = XB.rearrange("(c p) t -> p c t", p=128)
    dstr = dst.rearrange("(c p) t -> p c t", p=128)
    atr = AT.rearrange("(c p) t -> p c t", p=128)
    with ExitStack() as es:
        sb = lambda n, s, d: es.enter_context(nc.sbuf_tensor("p4_%d_" % l + n, s, d))
        WD = sb("WD", [128, NFC, 1024], BF16)
        load_w(G, WD, w_down, NFC, [(c, c + 256) for c in range(0, 1024, 256)])
        xbuf = [sb("x%d" % i, [128, 8, TT], F32) for i in range(2)]
        abuf = [sb("a%d" % i, [128, NFC, TT], BF16) for i in range(2)]
        Mb = [sb("M%d" % i, [128, 8, TT], F32) for i in range(2)]
        sqrot = Rot([sb("sq%d" % i, [128, TT], BF16) for i in range(8)])
        tmprot = Rot([sb("tmp%d" % i, [128, TT], F32) for i in range(2)])
        lnv = sb("lnv", [128, TT], F32)
        rstd = sb("rstd", [128, TT], F32)
        brot = Rot([1, 2, 3, 4, 5, 6, 7])
        P.dma("sp", out=abuf[0][:, :, :], in_=atr[:, :, 0:TT])
        P.dma("sp", out=xbuf[0][:, :, :], in_=xbr[:, :, 0:TT])
        P.dma("sp", out=xbuf[1][:, :, :], in_=xbr[:, :, TT:2 * TT])

        def D_closures(tt):
            t0 = tt * TT
            a = abuf[tt % 2]
            M = Mb[tt % 2]
            cl = []

            def d_load():
                if tt + 1 < NT:
                    P.dma("sp", out=abuf[(tt + 1) % 2][:, :, :], in_=atr[:, :, t0 + TT:t0 + 2 * TT])
            cl.append(d_load)

            def mk_oc(oc):
                def f():
                    b = bank(G, brot.next())
                    for c in range(NFC):
                        mm(P, b, WD[:, c, oc * 128:(oc + 1) * 128], a[:, c, :], c == 0, c == NFC - 1)
                    act(P, M[:, oc, :], b, AF.Copy)
                return f
            for oc in range(8):
                cl.append(mk_oc(oc))
            return cl

        OWN = {}

        def N_events(tt):
            t0 = tt * TT
            xs = xbuf[tt % 2]
            M = Mb[tt % 2]
            ev = []
            sqs = {}

            def n_sq(c0):
                def f():
                    for c in range(c0, c0 + 4):
                        sq = sqrot.next()
                        act(P, sq[:, :], M[:, c, :], AF.Square)
                        sqs[c] = sq
                        OWN[sq.name] = (tt, c)
                return f

            def n_mm(c0):
                def f():
                    for c in range(c0, c0 + 4):
                        assert OWN[sqs[c].name] == (tt, c)
                        mm(P, bank(G, 0), G.ones_bb[:, :], sqs[c][:, :], c == 0, c == 7)
                return f

            def n_rstd():
                act(P, lnv[:, :], bank(G, 0), AF.Ln, bias=EPS, scale=1.0 / D)
                act(P, rstd[:, :], lnv[:, :], AF.Exp, scale=-0.5)
            ev.append((0.0, n_sq(0)))
            ev.append((0.5, n_sq(4)))
            ev.append((2.2, n_mm(0)))
            ev.append((2.6, n_mm(4)))
            ev.append((3.0, n_rstd))

            def n_res(c0):
                def f():
                    for c in range(c0, c0 + 4):
                        tmp = tmprot.next()
                        P.I("dve", "scalar_tensor_tensor", out=tmp[:, :], in0=M[:, c, :], scalar=_vcol(vec, GN0 + (l * 4 + 3) * 8 + c),
                            in1=rstd[:, :], op0=ALU.mult, op1=ALU.mult)
                        P.I("pool", "tensor_tensor", out=xs[:, c, :], in0=xs[:, c, :], in1=tmp[:, :], op=ALU.add)
                    if c0 == 4:
                        P.dma("sp", out=dstr[:, :, t0:t0 + TT], in_=xs[:, :, :])
                return f
            ev.append((4.0, n_res(0)))
            ev.append((5.0, n_res(4)))
            return ev

        for f in D_closures(0):
            f()
        for tt in range(NT):
            main = D_closures(tt + 1) if tt + 1 < NT else []
            interleave_events(main, N_events(tt))
            if tt + 2 < NT:
                P.dma("sp", out=xbuf[tt % 2][:, :, :], in_=xbr[:, :, (tt + 2) * TT:(tt + 3) * TT])
        P.flush()


def _t5_bucket(dist):
    max_exact = 16
    d = np.maximum(dist, 1).astype(np.float32)
    large = max_exact + (np.log(d / np.float32(max_exact)) / np.float32(math.log(2048 / 16)) * np.float32(16)).astype(np.int32)
    large = np.minimum(large, 31)
    return np.where(dist < max_exact, dist, large)


def _host_tables(norm_g, rel_bias, w_sc, w_cc, b_cc, ln_cc_g, ln_cc_b, w_ffn_conv, b_ffn_conv):
    vecs = np.zeros((128, NV), np.float32)

    def put(col0, arr):
        a = np.asarray(arr, np.float32)
        lead = a.shape[:-1]
        C = a.shape[-1] // 128
        a = a.reshape(lead + (C, 128))
        a = np.moveaxis(a, -1, 0).reshape(128, -1)
        vecs[:, col0:col0 + a.shape[1]] = a

    put(GN0, norm_g)
    put(WSC0, w_sc)
    put(WCC0, w_cc)
    put(BCC0, b_cc)
    put(LNG0, ln_cc_g)
    put(LNB0, ln_cc_b)
    put(WFC0, w_ffn_conv)
    put(BFC0, b_ffn_conv)

    consts = np.zeros((128, NC_), np.float32)
    p = np.arange(128)[:, None]
    i = np.arange(128)[None, :]
    consts[:, C_MASKD:C_MASKD + 128] = np.where(i <= p, 0.0, NEG)
    consts[:, C_MASKD + 128:C_MASKD + 256] = np.where(i >= p, 0.0, NEG)
    iq = np.arange(512)[None, :]
    for o in range(4):
        consts[:, C_MASKS + o * 512:C_MASKS + (o + 1) * 512] = np.where(o * 128 + p < iq, 0.0, NEG)
    consts[:, C_TRI:C_TRI + 128] = (p >= i).astype(np.float32)
    consts[:, C_SU:C_SU + 128] = (p < i).astype(np.float32)
    consts[:, C_ID:C_ID + 128] = (p == i).astype(np.float32)

    rb = np.asarray(rel_bias, np.float32)
    biasT = np.zeros((128, 3, 8, 2, 128), np.float32)
    for g, (win, d) in enumerate(DA_PAIRS):
        for jt in range(2):
            rel = 128 + i - (jt * 128 + p)
            relc = np.clip(rel, 0, 128)
            bk = _t5_bucket(relc * d)
            biasT[:, g, :, jt, :] = np.transpose(rb[bk], (0, 2, 1))
    return vecs, consts, biasT.reshape(128, 6144)


_NC_CACHE = {}


def kernel(x, norm_g, rel_bias, w_in_even, w_out_even, w_sc, w_in_odd, w_out_odd,
           w_cc, b_cc, ln_cc_g, ln_cc_b, w_up, w_ffn_conv, b_ffn_conv, w_down):
    x = np.asarray(x, np.float32)
    B = x.shape[0]
    vecs, consts, biasT = _host_tables(norm_g, rel_bias, w_sc, w_cc, b_cc, ln_cc_g, ln_cc_b, w_ffn_conv, b_ffn_conv)
    if "nc" not in _NC_CACHE:
        _NC_CACHE["nc"] = build_program()
    nc = _NC_CACHE["nc"]
    shared = dict(
        vecs=vecs, consts=consts, biasT=biasT,
        w_in_even=np.ascontiguousarray(w_in_even, np.float32), w_out_even=np.ascontiguousarray(w_out_even, np.float32),
        w_in_odd=np.ascontiguousarray(w_in_odd, np.float32), w_out_odd=np.ascontiguousarray(w_out_odd, np.float32),
        w_up=np.ascontiguousarray(w_up, np.float32), w_down=np.ascontiguousarray(w_down, np.float32),
    )
    in_maps = []
    for b in range(B):
        m = dict(shared)
        m["xT"] = np.ascontiguousarray(x[b].T)
        in_maps.append(m)
    res = run_bass_kernel_spmd(nc, in_maps, core_ids=list(range(B)))
    out = np.stack([np.ascontiguousarray(r["yT"].T) for r in res.results], axis=0)
    return out.astype(np.float32)
```

```python
import math
from contextlib import ExitStack

import numpy as np
import concourse.bass as bass
import concourse.mybir as mybir
from concourse.bass_utils import run_bass_kernel_spmd

F32 = mybir.dt.float32
BF16 = mybir.dt.bfloat16
AF = mybir.ActivationFunctionType
ALU = mybir.AluOpType

S = 4096
D = 1024
NT = 8
TT = 512
DFF = 2816
NFC = 22
EPS = 1e-6
DA_PAIRS = ((128, 1), (512, 4), (2048, 16))
NEG = -30000.0

GN0 = 0
WSC0 = 128
WCC0 = 152
BCC0 = 400
LNG0 = 408
LNB0 = 416
WFC0 = 424
BFC0 = 688
NV = 776
C_MASKD = 0
C_MASKS = 256
C_TRI = 2304
C_SU = 2432
C_ID = 2560
NC_ = 2688


def _is_ap(v):
    return hasattr(v, "tensor") and hasattr(v, "offset") and hasattr(v, "ap")


def _box(ap):
    t = ap.tensor
    shape = list(t.shape)
    nd = len(shape)
    strides = [1] * nd
    for k in range(nd - 2, -1, -1):
        strides[k] = strides[k + 1] * shape[k + 1]
    off = ap.offset
    lo = []
    for k in range(nd):
        lo.append(off // strides[k])
        off = off % strides[k]
    ext = [0] * nd
    for step, cnt in ap.ap:
        if cnt <= 1 or step == 0:
            continue
        k = 0
        while k < nd - 1 and step < strides[k]:
            k += 1
        ext[k] += (step // strides[k]) * (cnt - 1)
        rem = step % strides[k]
        if rem != 0:
            for kk in range(k + 1, nd):
                ext[kk] = shape[kk]
    return (t.name, tuple((lo[k], lo[k] + ext[k] + 1) for k in range(nd)))


def _ovl(a, b):
    for (al, ah), (bl, bh) in zip(a[1], b[1]):
        if not (al < bh and bl < ah):
            return False
    return True


def _contains(a, b):
    for (al, ah), (bl, bh) in zip(a[1], b[1]):
        if not (al <= bl and bh <= ah):
            return False
    return True


class Op:
    __slots__ = ("eng", "emit", "deps", "is_dma", "sem", "val", "signal")

    def __init__(self, eng, emit):
        self.eng = eng
        self.emit = emit
        self.deps = []
        self.is_dma = False
        self.sem = None
        self.val = None
        self.signal = False


ENGS = ("pe", "act", "dve", "pool", "sp")


class Prog:
    def __init__(self, nc, es, n_dsem=60):
        self.nc = nc
        self.esem = {e: es.enter_context(nc.semaphore("s_" + e)) for e in ENGS}
        self.ecnt = {e: 0 for e in ENGS}
        self.bar = es.enter_context(nc.semaphore("s_bar"))
        self.barcnt = 0
        self.dpool = [[es.enter_context(nc.semaphore("d%d" % i)), 0] for i in range(n_dsem)]
        self.n_ops = 0
        self.n_waits = 0
        self._reset()

    def _reset(self):
        self.ops = {e: [] for e in ENGS}
        self.hist = {}
        self.dma_last = {}
        self.dkey2idx = {}
        self.all_ops = []

    def _track(self, op, reads, writes):
        deps = []
        rb_ = [_box(a) for a in reads]
        wb_ = [_box(a) for a in writes]
        for b in rb_:
            h = self.hist.setdefault(b[0], {"w": [], "r": []})
            for wb, wop in h["w"]:
                if _ovl(wb, b):
                    deps.append(wop)
        for b in wb_:
            h = self.hist.setdefault(b[0], {"w": [], "r": []})
            for wb, wop in h["w"]:
                if _ovl(wb, b):
                    deps.append(wop)
            for rb, rop in h["r"]:
                if _ovl(rb, b):
                    deps.append(rop)
        for b in rb_:
            h = self.hist[b[0]]
            if len(h["r"]) > 64:
                h["r"] = [(rb, rop) for rb, rop in h["r"] if not (rop.eng == op.eng and not rop.is_dma and not op.is_dma and _contains(b, rb))]
            h["r"].append((b, op))
        for b in wb_:
            h = self.hist[b[0]]
            h["w"] = [(wb, wop) for wb, wop in h["w"] if not _contains(b, wb)]
            h["r"] = [(rb, rop) for rb, rop in h["r"] if not _contains(b, rb) or rop is op]
            h["w"].append((b, op))
        seen = set()
        for d in deps:
            if d is op or id(d) in seen:
                continue
            if d.eng == "pe" and op.eng == "pe" and not d.is_dma and not op.is_dma:
                continue
            seen.add(id(d))
            op.deps.append(d)

    def I(self, eng, method, extra_reads=(), extra_writes=(), **kw):
        reads = list(extra_reads)
        writes = list(extra_writes)
        for k, v in kw.items():
            if _is_ap(v):
                if k in ("out", "accum_out", "ap"):
                    writes.append(v)
                else:
                    reads.append(v)

        def emit(e, method=method, kw=kw):
            return getattr(e, method)(**kw)

        op = Op(eng, emit)
        self._track(op, reads, writes)
        self.ops[eng].append(op)
        self.all_ops.append(op)
        return op

    def dma(self, eng, out, in_, key=None):
        o_dram = "dram" in str(out.tensor.space).lower()
        sb = in_ if o_dram else out
        if key is None:
            key = sb.tensor.name

        def emit(e, out=out, in_=in_):
            return e.dma_start(out=out, in_=in_)

        op = Op(eng, emit)
        op.is_dma = True
        self._track(op, [in_], [out])
        prev = self.dma_last.get(key)
        if prev is not None and all(prev is not d for d in op.deps):
            op.deps.append(prev)
        self.dma_last[key] = op
        idx = self.dkey2idx.setdefault(key, len(self.dkey2idx))
        ent = self.dpool[idx]
        ent[1] += 16
        op.sem = ent[0]
        op.val = ent[1]
        self.ops[eng].append(op)
        self.all_ops.append(op)
        return op

    def flush(self):
        nc = self.nc
        for op in self.all_ops:
            for d in op.deps:
                d.signal = True
        for e in ENGS:
            for op in reversed(self.ops[e]):
                if not op.is_dma:
                    op.signal = True
                    break
        for e in ENGS:
            for op in self.ops[e]:
                if (not op.is_dma) and op.signal:
                    self.ecnt[e] += 1
                    op.sem = self.esem[e]
                    op.val = self.ecnt[e]
        self.barcnt += 1
        barval = self.barcnt
        used_dsems = [self.dpool[i] for i in range(len(self.dkey2idx))]

        def run(e_name, eh):
            waited = {}
            for op in self.ops[e_name]:
                need = {}
                for d in op.deps:
                    k = id(d.sem)
                    if waited.get(k, 0) >= d.val:
                        continue
                    if k not in need or need[k][1] < d.val:
                        need[k] = (d.sem, d.val)
                for k, (s, v) in need.items():
                    eh.wait_ge(s, v)
                    waited[k] = v
                    self.n_waits += 1
                ins = op.emit(eh)
                if op.is_dma:
                    ins.then_inc(op.sem, 16)
                elif op.signal:
                    ins.then_inc(op.sem, 1)
                self.n_ops += 1
            if e_name == "sp":
                for e in ENGS:
                    if e != "sp" and self.ecnt[e] > 0:
                        eh.wait_ge(self.esem[e], self.ecnt[e])
                for sem, cnt in used_dsems:
                    if cnt > 0:
                        eh.wait_ge(sem, cnt)
                eh.sem_inc(self.bar, 1)
            else:
                eh.wait_ge(self.bar, barval)

        with nc.Block() as block:

            @block.sync
            def _(eh):
                run("sp", eh)

            @block.tensor
            def _(eh):
                run("pe", eh)

            @block.scalar
            def _(eh):
                run("act", eh)

            @block.vector
            def _(eh):
                run("dve", eh)

            @block.gpsimd
            def _(eh):
                run("pool", eh)

        self._reset()


class Ctx:
    pass


def _vcol(vec_sb, c):
    return vec_sb[:, c:c + 1]


def build_program(layers=(0, 1, 2, 3), debug=False, stop_after=None):
    nc = bass.Bass("TRN2", target_bir_lowering=False)

    def din(name, shape, dt=F32):
        return nc.dram_tensor(name, shape, dt, kind="ExternalInput").ap()

    scratch_kind = "ExternalOutput" if debug else "Internal"

    def dscr(name, shape, dt):
        return nc.dram_tensor(name, shape, dt, kind=scratch_kind).ap()

    xT = din("xT", [D, S])
    vecs = din("vecs", [128, NV])
    consts = din("consts", [128, NC_])
    biasT = din("biasT", [128, 6144])
    w_in_even = din("w_in_even", [2, 1024, 3072])
    w_out_even = din("w_out_even", [2, 1024, 1024])
    w_in_odd = din("w_in_odd", [2, 1024, 2560])
    w_out_odd = din("w_out_odd", [2, 1024, 1024])
    w_up = din("w_up", [4, 1024, 5632])
    w_down = din("w_down", [4, 2816, 1024])
    yT = nc.dram_tensor("yT", [D, S], F32, kind="ExternalOutput").ap()

    XA = dscr("XA", [D, S], F32)
    XB = dscr("XB", [D, S], F32)
    QT = dscr("QT", [512, S], BF16)
    KT = dscr("KT", [512, S], BF16)
    VN = dscr("VN", [S, 512], BF16)
    MIX = dscr("MIX", [D, S], BF16)
    AT = dscr("AT", [DFF, S], BF16)

    with ExitStack() as es0:
        P = Prog(nc, es0)
        G = Ctx()
        G.nc = nc
        G.P = P
        sb0 = lambda n, s, d: es0.enter_context(nc.sbuf_tensor(n, s, d))
        G.PS = es0.enter_context(nc.psum_tensor("PS", [128, 4096], F32))
        G.vec = sb0("vec_sb", [128, NV], F32)
        G.ones_f = sb0("ones_f", [128, 128], F32)
        G.ones_b = sb0("ones_b", [128, 64], BF16)
        G.ones_bb = sb0("ones_bb", [128, 128], BF16)
        G.eps = sb0("eps_t", [128, 1], F32)
        G.one = sb0("one_t", [128, 1], F32)
        G.tri = sb0("tri_b", [128, 128], BF16)
        G.su = sb0("su_b", [128, 128], BF16)
        G.ident = sb0("ident_b", [128, 128], BF16)
        G.consts = consts
        G.biasT = biasT

        P.dma("sp", out=G.vec[:, :], in_=vecs[:, :])
        P.dma("pool", out=G.tri[:, :], in_=consts[:, C_TRI:C_TRI + 128])
        P.dma("pool", out=G.su[:, :], in_=consts[:, C_SU:C_SU + 128])
        P.dma("pool", out=G.ident[:, :], in_=consts[:, C_ID:C_ID + 128])
        P.I("dve", "memset", ap=G.ones_f[:, :], constant=1.0)
        P.I("dve", "memset", ap=G.ones_b[:, :], constant=1.0)
        P.I("dve", "memset", ap=G.ones_bb[:, :], constant=1.0)
        P.I("dve", "memset", ap=G.eps[:, :], constant=EPS)
        P.I("dve", "memset", ap=G.one[:, :], constant=1.0)
        P.flush()

        n_layers = len(layers)
        for li, l in enumerate(layers):
            src = xT if li == 0 else XA
            last = li == n_layers - 1
            dst = yT if last else XA
            even = (l % 2 == 0)
            i2 = l // 2
            if even:
                w_in = w_in_even[i2]
                w_out = w_out_even[i2]
            else:
                w_in = w_in_odd[i2]
                w_out = w_out_odd[i2]
            phase1(G, l, src, w_in, QT, KT, VN, MIX)
            if stop_after == (l, 1):
                break
            if even:
                phase2_even(G, l, QT, KT, VN, MIX)
            else:
                phase2_odd(G, l, QT, KT, VN, MIX)
            if stop_after == (l, 2):
                break
            phase3(G, l, src, XB, MIX, AT, w_out, w_up[l])
            if stop_after == (l, 3):
                break
            phase4(G, l, XB, dst, AT, w_down[l])
        G.n_ops = P.n_ops
        G.n_waits = P.n_waits
    return nc


def bank(G, i):
    return G.PS[:, i * 512:(i + 1) * 512]


class Rot:
    def __init__(self, items):
        self.items = list(items)
        self.i = 0

    def next(self):
        v = self.items[self.i % len(self.items)]
        self.i += 1
        return v


def act(P, out, in_, func, bias=None, scale=None):
    kw = dict(out=out, in_=in_, func=func)
    if bias is not None:
        kw["bias"] = bias
    if scale is not None:
        kw["scale"] = scale
    return P.I("act", "activation", **kw)


def mm(P, out, lhsT, rhs, start, stop):
    return P.I("pe", "matmul", out=out, lhsT=lhsT, rhs=rhs, start=start, stop=stop)


def rms_rstd(G, sb_alloc, xs, nchunks, rstd, sqrot, lnv, ps_bank, inv_n):
    P = G.P
    for c in range(nchunks):
        sq = sqrot.next()
        act(P, sq[:, :], xs[:, c, :], AF.Square)
        mm(P, ps_bank, G.ones_bb[:, :], sq[:, :], c == 0, c == nchunks - 1)
    act(P, lnv[:, :], ps_bank, AF.Ln, bias=EPS, scale=inv_n)
    act(P, rstd[:, :], lnv[:, :], AF.Exp, scale=-0.5)


def load_w(G, wsb, w_ap, nk, groups, eng="pool"):
    wr = w_ap.rearrange("(c p) n -> p c n", p=128)
    for gi, (c0, c1) in enumerate(groups):
        G.P.dma(eng, out=wsb[:, :, c0:c1], in_=wr[:, :, c0:c1], key=wsb.name + "_%d" % (gi % 6))


def interleave_events(main, events):
    ev = sorted(enumerate(events), key=lambda t: (t[1][0], t[0]))
    k = 0
    for i, f in enumerate(main):
        f()
        while k < len(ev) and ev[k][1][0] < i + 1:
            ev[k][1][1]()
            k += 1
    while k < len(ev):
        ev[k][1][1]()
        k += 1


def phase1(G, l, src, w_in, QT, KT, VN, MIX):
    nc, P, vec = G.nc, G.P, G.vec
    even = (l % 2 == 0)
    i2 = l // 2
    NIN = 3072 if even else 2560
    srcr = src.rearrange("(c p) t -> p c t", p=128)
    with ExitStack() as es:
        sb = lambda n, s, d: es.enter_context(nc.sbuf_tensor("p1_%d_" % l + n, s, d))
        W = sb("W", [128, 8, NIN], BF16)
        if even:
            grp = [(512, 768), (1024, 1280), (0, 256), (768, 1024), (1280, 1536), (256, 512)]
            grp += [(c, c + 256) for c in range(1536, 3072, 256)]
        else:
            grp = [(512, 768), (0, 256), (768, 1024), (256, 512)]
            grp += [(c, c + 256) for c in range(1024, 2560, 256)]
        load_w(G, W, w_in, 8, grp)
        xbuf = [sb("x%d" % i, [128, 8, TT], F32) for i in range(2)]
        hbuf = [sb("h%d" % i, [128, 8, TT], BF16) for i in range(2)]
        sqrot = Rot([sb("sq%d" % i, [128, TT], BF16) for i in range(7)])
        sqfrot = Rot([sb("sqf%d" % i, [128, TT], F32) for i in range(2)])
        lnv = sb("lnv", [128, TT], F32)
        rstd = sb("rstd", [128, TT], F32)
        qkrot = Rot([sb("qk%d" % i, [128, TT], BF16) for i in range(4)])
        vrot = Rot([sb("v%d" % i, [128, 512], BF16) for i in range(2)])
        yrot = Rot([sb("y%d" % i, [128, TT], BF16) for i in range(2)])
        brot = Rot([1, 2, 3, 4, 5, 6, 7])
        if even:
            gcrot = Rot([sb("gc%d" % i, [128, TT], F32) for i in range(2)])
            Z = [sb("Z%d" % i, [128, TT + 2], F32) for i in range(4)]
            trot = Rot([sb("t%d" % i, [128, TT], F32) for i in range(2)])
            for i in range(4):
                P.I("pool", "memset", ap=Z[i][:, 0:2], constant=0.0)
        else:
            sgrot = Rot([sb("sg%d" % i, [128, TT], F32) for i in range(2)])
            GC = [sb("GC%d" % i, [128, TT + 30], BF16) for i in range(4)]
            diag = sb("diag", [128, 124, 128], BF16)
            Y = sb("Y", [128, 4, TT], F32)
            mean = sb("mean", [128, TT], F32)
            msq = sb("msq", [128, TT], F32)
            var = sb("var", [128, TT], F32)
            lnr = sb("lnr", [128, TT], F32)
            rs2 = sb("rs2", [128, TT], F32)
            ynrot = Rot([sb("yn%d" % i, [128, TT], F32) for i in range(2)])
            for i in range(4):
                P.I("pool", "memset", ap=GC[i][:, 0:30], constant=0.0)
                for j in range(31):
                    P.I("dve", "tensor_scalar", out=diag[:, i * 31 + j, :], in0=G.ident[:, :],
                        scalar1=_vcol(vec, WCC0 + (i2 * 31 + j) * 4 + i), scalar2=None, op0=ALU.mult)

        def proj(h, oc):
            b = bank(G, brot.next())
            for c in range(8):
                mm(P, b, W[:, c, oc * 128:(oc + 1) * 128], h[:, c, :], c == 0, c == 7)
            return b

        OWN = {}

        def norm_events(tt):
            xs = xbuf[tt % 2]
            h = hbuf[tt % 2]
            ev = []
            sqs = {}

            def mk_sq(c):
                def f():
                    sq = sqrot.next()
                    act(P, sq[:, :], xs[:, c, :], AF.Square)
                    sqs[c] = sq
                    OWN[sq.name] = (tt, c)
                return f

            def mk_st(c):
                def f():
                    assert OWN[sqs[c].name] == (tt, c)
                    mm(P, bank(G, 0), G.ones_bb[:, :], sqs[c][:, :], c == 0, c == 7)
                return f
            for c in range(8):
                ev.append((0.0 + 0.5 * c, mk_sq(c)))
                ev.append((0.0 + 0.5 * c + 2.5, mk_st(c)))

            def rs():
                act(P, rstd[:, :], bank(G, 0), AF.Ln, bias=EPS, scale=1.0 / D)
                act(P, rstd[:, :], rstd[:, :], AF.Exp, scale=-0.5)
            ev.append((6.2, rs))

            def mk_h(c0):
                def f():
                    for c in (c0, c0 + 1):
                        P.I("dve", "scalar_tensor_tensor", out=h[:, c, :], in0=xs[:, c, :],
                            scalar=_vcol(vec, GN0 + (l * 4 + 0) * 8 + c), in1=rstd[:, :], op0=ALU.mult, op1=ALU.mult)
                return f
            for j, c0 in enumerate(range(0, 8, 2)):
                ev.append((7.2 + 1.0 * j, mk_h(c0)))
            return ev

        def tile_main(tt):
            t0 = tt * TT
            h = hbuf[tt % 2]
            cl = []
            if even:
                qoc, koc, vcol = 12, 16, 2560

                def mk_conv(i):
                    def f():
                        pc = proj(h, 4 + i)
                        gcb = gcrot.next()
                        act(P, gcb[:, :], pc, AF.Copy)
                        px = proj(h, 8 + i)
                        if tt > 0:
                            P.I("pool", "tensor_copy", out=Z[i][:, 0:2], in_=Z[i][:, TT:TT + 2])
                        P.I("dve", "tensor_tensor", out=Z[i][:, 2:TT + 2], in0=px, in1=gcb[:, :], op=ALU.mult)
                        pb = proj(h, i)
                        t = trot.next()
                        wc = lambda j: _vcol(vec, WSC0 + (i2 * 3 + j) * 4 + i)
                        P.I("dve", "tensor_scalar", out=t[:, :], in0=Z[i][:, 0:TT], scalar1=wc(0), scalar2=None, op0=ALU.mult)
                        P.I("dve", "scalar_tensor_tensor", out=t[:, :], in0=Z[i][:, 1:TT + 1], scalar=wc(1), in1=t[:, :], op0=ALU.mult, op1=ALU.add)
                        P.I("dve", "scalar_tensor_tensor", out=t[:, :], in0=Z[i][:, 2:TT + 2], scalar=wc(2), in1=t[:, :], op0=ALU.mult, op1=ALU.add)
                        ya = yrot.next()
                        P.I("dve", "tensor_tensor", out=ya[:, :], in0=pb, in1=t[:, :], op=ALU.mult)
                        P.dma("sp", out=MIX[i * 128:(i + 1) * 128, t0:t0 + TT], in_=ya[:, :])
                    return f
                for i in range(4):
                    cl.append(mk_conv(i))
            else:
                qoc, koc, vcol = 8, 12, 2048

                def mk_cconv(i):
                    def f():
                        pg = proj(h, 4 + i)
                        sg = sgrot.next()
                        act(P, sg[:, :], pg, AF.Sigmoid)
                        pa = proj(h, i)
                        if tt > 0:
                            P.I("pool", "tensor_copy", out=GC[i][:, 0:30], in_=GC[i][:, TT:TT + 30])
                        P.I("dve", "tensor_tensor", out=GC[i][:, 30:TT + 30], in0=pa, in1=sg[:, :], op=ALU.mult)
                        pcv = bank(G, brot.next())
                        for j in range(31):
                            mm(P, pcv, diag[:, i * 31 + j, :], GC[i][:, j:j + TT], j == 0, j == 30)
                        act(P, Y[:, i, :], pcv, AF.Identity, bias=_vcol(vec, BCC0 + i2 * 4 + i), scale=1.0)
                    return f
                for i in range(4):
                    cl.append(mk_cconv(i))

                def ln_part():
                    sbk = bank(G, brot.next())
                    sqk = bank(G, brot.next())
                    for i in range(4):
                        mm(P, sbk, G.ones_f[:, :], Y[:, i, :], i == 0, i == 3)
                    for i in range(4):
                        sq = sqfrot.next()
                        act(P, sq[:, :], Y[:, i, :], AF.Square)
                        mm(P, sqk, G.ones_f[:, :], sq[:, :], i == 0, i == 3)
                    act(P, mean[:, :], sbk, AF.Copy, scale=1.0 / 512)
                    P.I("dve", "tensor_tensor", out=msq[:, :], in0=mean[:, :], in1=mean[:, :], op=ALU.mult)
                    P.I("dve", "scalar_tensor_tensor", out=var[:, :], in0=sqk, scalar=1.0 / 512, in1=msq[:, :], op0=ALU.mult, op1=ALU.subtract)
                    act(P, lnr[:, :], var[:, :], AF.Ln, bias=EPS, scale=1.0)
                    act(P, rs2[:, :], lnr[:, :], AF.Exp, scale=-0.5)
                    for i in range(4):
                        yn = ynrot.next()
                        P.I("dve", "tensor_tensor", out=yn[:, :], in0=Y[:, i, :], in1=mean[:, :], op=ALU.subtract)
                        P.I("dve", "tensor_tensor", out=yn[:, :], in0=yn[:, :], in1=rs2[:, :], op=ALU.mult)
                        yc = yrot.next()
                        act(P, yc[:, :], yn[:, :], AF.Silu, bias=_vcol(vec, LNB0 + i2 * 4 + i), scale=_vcol(vec, LNG0 + i2 * 4 + i))
                        P.dma("sp", out=MIX[i * 128:(i + 1) * 128, t0:t0 + TT], in_=yc[:, :])

            def mk_qk(i):
                def f():
                    pq = proj(h, qoc + i)
                    qs = qkrot.next()
                    act(P, qs[:, :], pq, AF.Copy, scale=0.125)
                    P.dma("sp", out=QT[i * 128:(i + 1) * 128, t0:t0 + TT], in_=qs[:, :])
                    pk = proj(h, koc + i)
                    ks = qkrot.next()
                    P.I("dve", "tensor_copy", out=ks[:, :], in_=pk)
                    P.dma("sp", out=KT[i * 128:(i + 1) * 128, t0:t0 + TT], in_=ks[:, :])
                return f

            def mk_v(sub):
                def f():
                    b = bank(G, brot.next())
                    for c in range(8):
                        mm(P, b, h[:, c, sub * 128:(sub + 1) * 128], W[:, c, vcol:vcol + 512], c == 0, c == 7)
                    vs = vrot.next()
                    act(P, vs[:, :], b, AF.Copy)
                    P.dma("sp", out=VN[t0 + sub * 128:t0 + (sub + 1) * 128, :], in_=vs[:, :])
                return f
            if not even:
                cl.append(mk_qk(0))
                cl.append(mk_qk(1))
                cl.append(ln_part)
                cl.append(mk_qk(2))
                cl.append(mk_qk(3))
            else:
                for i in range(4):
                    cl.append(mk_qk(i))
            for sub in range(4):
                cl.append(mk_v(sub))
            return cl

        P.dma("sp", out=xbuf[0][:, :, :], in_=srcr[:, :, 0:TT])
        P.dma("sp", out=xbuf[1][:, :, :], in_=srcr[:, :, TT:2 * TT])
        interleave_events([], norm_events(0))
        for tt in range(NT):
            interleave_events(tile_main(tt), norm_events(tt + 1) if tt + 1 < NT else [])
            if tt + 2 < NT:
                P.dma("sp", out=xbuf[tt % 2][:, :, :], in_=srcr[:, :, (tt + 2) * TT:(tt + 3) * TT])
        P.flush()


def phase2_even(G, l, QT, KT, VN, MIX):
    nc, P = G.nc, G.P
    with ExitStack() as es:
        sb = lambda n, s, d: es.enter_context(nc.sbuf_tensor("p2e_%d_" % l + n, s, d))
        VD = [sb("VD%d" % g, [128, 32, 512], BF16) for g in range(3)]
        bias = sb("bias", [128, 6144], F32)
        maskd = sb("maskd", [128, 256], F32)
        qsb = [sb("q%d" % i, [128, S], BF16) for i in range(1)]
        ksb = [sb("k%d" % i, [128, S], BF16) for i in range(1)]
        acc = sb("acc", [128, 2, S], F32)
        ysb = sb("ysb", [128, S], BF16)
        sbf = [sb("sbf%d" % i, [128, 1024], F32) for i in range(2)]
        Ab = [sb("A%d" % i, [128, 1024], BF16) for i in range(4)]
        for g, (win, d) in enumerate(DA_PAIRS):
            vr = VN.rearrange("(blk p r) f -> p blk r f", p=128, r=d)
            for blk in range(32 // d):
                P.dma("sp" if blk % 2 == 0 else "pool", out=VD[g][:, blk * d:(blk + 1) * d, :], in_=vr[:, blk, :, :],
                      key="VD%d_%d" % (g, blk % 4))
        P.dma("sp", out=bias[:, :], in_=G.biasT[:, :])
        P.dma("sp", out=maskd[:, :], in_=G.consts[:, C_MASKD:C_MASKD + 256])
        for gh in range(24):
            P.I("pool", "tensor_tensor", out=bias[:, gh * 256:(gh + 1) * 256], in0=bias[:, gh * 256:(gh + 1) * 256], in1=maskd[:, :], op=ALU.add)
        srot = Rot([0, 1])
        arot = Rot([0, 1, 2, 3])
        fr = Rot([0, 1])
        for hp in range(4):
            q = qsb[0]
            k = ksb[0]
            P.dma("sp", out=q[:, :], in_=QT[hp * 128:(hp + 1) * 128, :])
            P.dma("sp", out=k[:, :], in_=KT[hp * 128:(hp + 1) * 128, :])
            units = []
            for g, (win, d) in enumerate(DA_PAIRS):
                if d == 1:
                    for gi in range(8):
                        units.append(dict(g=g, d=d, blocks=[((gi * 4 + bi) * 128, 0) for bi in range(4)], aview=("c", gi * 512)))
                elif d == 4:
                    for gi in range(8):
                        units.append(dict(g=g, d=d, blocks=[(gi * 128, r) for r in range(4)], aview=("s4", gi * 512)))
                else:
                    for sg in range(2):
                        for rq in range(4):
                            units.append(dict(g=g, d=d, blocks=[(sg * 128, rq * 4 + bi) for bi in range(4)], aview=("s16", sg * 2048, rq)))
            for ui, u in enumerate(units):
                u["o_bank"] = 4 + (ui % 2) * 2

            def stageA(u):
                g, d, blocks = u["g"], u["d"], u["blocks"]
                u["A"] = []
                for e in range(2):
                    hh = hp * 2 + e
                    hr = slice(e * 64, (e + 1) * 64)
                    sbase = srot.next() * 1024
                    Sps = G.PS[:, sbase:sbase + 1024]
                    for bi, (m0, r) in enumerate(blocks):
                        qs0 = m0 * d + r
                        for jt in range(2):
                            mk0 = m0 - 128 + jt * 128
                            if mk0 < 0:
                                continue
                            ks0 = mk0 * d + r
                            c0 = sbase + (bi * 2 + jt) * 128
                            mm(P, G.PS[:, c0:c0 + 128], k[hr, ks0:ks0 + 127 * d + 1:d], q[hr, qs0:qs0 + 127 * d + 1:d], True, True)
                    sf = sbf[fr.next()]
                    bsl = bias[:, (g * 8 + hh) * 256:(g * 8 + hh + 1) * 256].unsqueeze(1).broadcast_to([128, 4, 256])
                    P.I("dve", "tensor_tensor", out=sf[:, :].rearrange("p (b f) -> p b f", b=4),
                        in0=Sps.rearrange("p (b f) -> p b f", b=4), in1=bsl, op=ALU.add)
                    A = Ab[arot.next()]
                    act(P, A[:, :], sf[:, :], AF.Exp)
                    u["A"].append(A)

            def stageB(u):
                g, d, blocks = u["g"], u["d"], u["blocks"]
                o_bank = u["o_bank"]
                den_bank = o_bank + 1
                for e in range(2):
                    hh = hp * 2 + e
                    hr = slice(e * 64, (e + 1) * 64)
                    A = u["A"][e]
                    for bi, (m0, r) in enumerate(blocks):
                        first = True
                        for jt in range(2):
                            mk0 = m0 - 128 + jt * 128
                            if mk0 < 0:
                                continue
                            tile = (mk0 // 128) * d + r
                            asl = A[:, (bi * 2 + jt) * 128:(bi * 2 + jt) * 128 + 128]
                            co = o_bank * 512 + bi * 128
                            cd = den_bank * 512 + bi * 128
                            mm(P, G.PS[hr, co:co + 128], VD[g][:, tile, hh * 64:(hh + 1) * 64], asl, first, jt == 1)
                            mm(P, G.PS[hr, cd:cd + 128], G.ones_b[:, :], asl, first, jt == 1)
                            first = False

            def stageC(u):
                aview = u["aview"]
                o_bank = u["o_bank"]
                for w_, bk in ((0, o_bank), (1, o_bank + 1)):
                    src_ps = bank(G, bk)
                    if aview[0] == "c":
                        c0 = aview[1]
                        if w_ == 0:
                            act(P, acc[:, w_, c0:c0 + 512], src_ps, AF.Copy)
                        else:
                            P.I("dve", "tensor_copy", out=acc[:, w_, c0:c0 + 512], in_=src_ps)
                    elif aview[0] == "s4":
                        c0 = aview[1]
                        av = acc[:, w_, c0:c0 + 512].rearrange("p (i r) -> p r i", r=4)
                        P.I("dve", "tensor_tensor", out=av, in0=src_ps.rearrange("p (r i) -> p r i", r=4), in1=av, op=ALU.add)
                    else:
                        c0, rq = aview[1], aview[2]
                        av = acc[:, w_, c0:c0 + 2048].rearrange("p (i r) -> p r i", r=16)[:, rq * 4:(rq + 1) * 4, :]
                        P.I("dve", "tensor_tensor", out=av, in0=src_ps.rearrange("p (r i) -> p r i", r=4), in1=av, op=ALU.add)

            nu = len(units)
            for it in range(nu + 2):
                if it < nu:
                    stageA(units[it])
                if 0 <= it - 1 < nu:
                    stageB(units[it - 1])
                if 0 <= it - 2 < nu:
                    stageC(units[it - 2])
            for qh in range(4):
                cs = slice(qh * 1024, (qh + 1) * 1024)
                act(P, acc[:, 1, cs], acc[:, 1, cs], AF.Ln)
                act(P, acc[:, 1, cs], acc[:, 1, cs], AF.Exp, scale=-1.0)
                P.I("pool", "tensor_tensor", out=ysb[:, cs], in0=acc[:, 0, cs], in1=acc[:, 1, cs], op=ALU.mult)
            P.dma("sp", out=MIX[512 + hp * 128:512 + (hp + 1) * 128, :], in_=ysb[:, :])
        P.flush()


def phase2_odd(G, l, QT, KT, VN, MIX):
    nc, P = G.nc, G.P
    with ExitStack() as es:
        sb = lambda n, s, d: es.enter_context(nc.sbuf_tensor("p2o_%d_" % l + n, s, d))
        V = sb("V", [128, 32, 512], BF16)
        masks = sb("masks", [128, 4, 512], BF16)
        qsb = [sb("q%d" % i, [128, S], BF16) for i in range(2)]
        ksb = [sb("k%d" % i, [128, S], BF16) for i in range(2)]
        ysb = [sb("ysb%d" % i, [128, S], BF16) for i in range(2)]
        NSL = 2
        E = [[sb("E%d_%d" % (i, j), [128, 2, 512], F32) for j in range(3)] for i in range(NSL)]
        SP = [[sb("SP%d_%d" % (i, j), [128, 2, 512], BF16) for j in range(3)] for i in range(NSL)]
        Gt = [[sb("G%d_%d" % (i, j), [128, 2, 512], F32) for j in range(2)] for i in range(NSL)]
        At = [[sb("At%d_%d" % (i, j), [128, 2, 512], BF16) for j in range(2)] for i in range(NSL)]
        vr = VN.rearrange("(t p) f -> p t f", p=128)
        for qq in range(4):
            P.dma("sp", out=V[:, qq * 8:(qq + 1) * 8, :], in_=vr[:, qq * 8:(qq + 1) * 8, :], key="Vo_%d" % qq)
        P.dma("pool", out=masks[:, :, :], in_=G.consts[:, C_MASKS:C_MASKS + 2048].rearrange("p (o f) -> p o f", o=4))
        P.dma("sp", out=qsb[0][:, :], in_=QT[0:128, :])
        P.dma("sp", out=ksb[0][:, :], in_=KT[0:128, :])
        slot_qts = [[7, 4, 3, 0], [6, 5, 2, 1]]
        for hp in range(4):
            q = qsb[hp % 2]
            k = ksb[hp % 2]
            ys = ysb[hp % 2]
            if hp + 1 < 4:
                P.dma("sp", out=qsb[(hp + 1) % 2][:, :], in_=QT[(hp + 1) * 128:(hp + 2) * 128, :])
                P.dma("sp", out=ksb[(hp + 1) % 2][:, :], in_=KT[(hp + 1) * 128:(hp + 2) * 128, :])
            streams = []
            for s_ in range(NSL):
                st = []
                for qt in slot_qts[s_]:
                    n = 4 * qt + 4
                    for j in range(n):
                        st.append(dict(qt=qt, kb=4 * qt + 3 - j, first=(j == 0), last=(j == n - 1), idx=len(st)))
                streams.append(st)
            nit = max(len(st) for st in streams) + 2

            def stageA(s_, stp):
                qt, kb, idx = stp["qt"], stp["kb"], stp["idx"]
                o = kb - 4 * qt
                w0 = max(o, 0) * 128
                for e in range(2):
                    hr = slice(e * 64, (e + 1) * 64)
                    zc = e * 512
                    mm(P, G.PS[:, zc + w0:zc + 512], k[hr, kb * 128:(kb + 1) * 128], q[hr, qt * 512 + w0:(qt + 1) * 512], True, o < 0)
                    if o >= 0:
                        mm(P, G.PS[:, zc + w0:zc + 512], G.ident[:, :], masks[:, o, w0:512], False, True)
                Ec = E[s_][idx % 3]
                act(P, Ec[:, :, w0:512], G.PS[:, 0:1024].rearrange("p (e f) -> p e f", e=2)[:, :, w0:512], AF.Exp)
                spc = SP[s_][idx % 3]
                act(P, spc[:, :, w0:512], Ec[:, :, w0:512], AF.Ln, bias=1.0, scale=1.0)

            def stageB(s_, stp):
                qt, kb, idx = stp["qt"], stp["kb"], stp["idx"]
                o = kb - 4 * qt
                w0 = max(o, 0) * 128
                spc = SP[s_][idx % 3]
                spp = SP[s_][(idx - 1) % 3]
                for e in range(2):
                    pc = (2 + 2 * s_ + e) * 512
                    if o >= 0:
                        mm(P, G.PS[:, pc + w0:pc + w0 + 128], G.tri[:, :], spc[:, e, w0:w0 + 128], stp["first"], True)
                        if w0 + 128 < 512:
                            mm(P, G.PS[:, pc + w0 + 128:pc + 512], G.su[:, :], spp[:, e, w0 + 128:512], False, False)
                            mm(P, G.PS[:, pc + w0 + 128:pc + 512], G.tri[:, :], spc[:, e, w0 + 128:512], False, True)
                    else:
                        mm(P, G.PS[:, pc:pc + 512], G.su[:, :], spp[:, e, :], False, False)
                        mm(P, G.PS[:, pc:pc + 512], G.tri[:, :], spc[:, e, :], False, True)
                Gc = Gt[s_][idx % 2]
                c0 = (2 + 2 * s_) * 512
                act(P, Gc[:, :, w0:512], G.PS[:, c0:c0 + 1024].rearrange("p (e f) -> p e f", e=2)[:, :, w0:512], AF.Exp, scale=-1.0)
                P.I("dve", "tensor_tensor", out=At[s_][idx % 2][:, :, w0:512], in0=E[s_][idx % 3][:, :, w0:512], in1=Gc[:, :, w0:512], op=ALU.mult)

            def stageC(s_, stp):
                qt, kb, idx = stp["qt"], stp["kb"], stp["idx"]
                o = kb - 4 * qt
                w0 = max(o, 0) * 128
                for e in range(2):
                    hh = hp * 2 + e
                    hr = slice(e * 64, (e + 1) * 64)
                    oc0 = (6 + s_) * 512
                    vt = V[:, kb, hh * 64:(hh + 1) * 64]
                    at = At[s_][idx % 2]
                    if o >= 0:
                        mm(P, G.PS[hr, oc0 + w0:oc0 + w0 + 128], vt, at[:, e, w0:w0 + 128], stp["first"], False)
                        if w0 + 128 < 512:
                            mm(P, G.PS[hr, oc0 + w0 + 128:oc0 + 512], vt, at[:, e, w0 + 128:512], False, False)
                    else:
                        mm(P, G.PS[hr, oc0:oc0 + 512], vt, at[:, e, :], False, stp["last"])
                if stp["last"]:
                    P.I("dve", "tensor_copy", out=ys[:, qt * 512:(qt + 1) * 512], in_=bank(G, 6 + s_))

            for it in range(nit):
                for s_ in range(NSL):
                    if it < len(streams[s_]):
                        stageA(s_, streams[s_][it])
                for s_ in range(NSL):
                    if 0 <= it - 1 < len(streams[s_]):
                        stageB(s_, streams[s_][it - 1])
                for s_ in range(NSL):
                    if 0 <= it - 2 < len(streams[s_]):
                        stageC(s_, streams[s_][it - 2])
            P.dma("sp", out=MIX[512 + hp * 128:512 + (hp + 1) * 128, :], in_=ys[:, :])
        P.flush()


def post_norm_residual(G, l, j, M, xs, rstd, sqrot, lnv, ps_bank, tmprot):
    P, vec = G.P, G.vec
    rms_rstd(G, None, M, 8, rstd, sqrot, lnv, ps_bank, 1.0 / D)
    for c in range(8):
        tmp = tmprot.next()
        P.I("dve", "scalar_tensor_tensor", out=tmp[:, :], in0=M[:, c, :], scalar=_vcol(vec, GN0 + (l * 4 + j) * 8 + c),
            in1=rstd[:, :], op0=ALU.mult, op1=ALU.mult)
        P.I("pool", "tensor_tensor", out=xs[:, c, :], in0=xs[:, c, :], in1=tmp[:, :], op=ALU.add)


def interleave(main, side, offset=1, stride=1):
    k = 0
    for i, f in enumerate(main):
        f()
        while k < len(side) and i >= offset + k * stride:
            side[k]()
            k += 1
    while k < len(side):
        side[k]()
        k += 1


def phase3(G, l, src, XB, MIX, AT, w_out, w_up):
    nc, P, vec = G.nc, G.P, G.vec
    srcr = src.rearrange("(c p) t -> p c t", p=128)
    xbr = XB.rearrange("(c p) t -> p c t", p=128)
    mixr = MIX.rearrange("(c p) t -> p c t", p=128)
    with ExitStack() as es:
        sb = lambda n, s, d: es.enter_context(nc.sbuf_tensor("p3_%d_" % l + n, s, d))
        WO = sb("WO", [128, 8, 1024], BF16)
        WU = sb("WU", [128, 8, 2 * DFF], BF16)
        load_w(G, WO, w_out, 8, [(c, c + 256) for c in range(0, 1024, 256)])
        grp = []
        for c in range(0, DFF, 256):
            grp.append((c, c + 256))
            grp.append((DFF + c, DFF + c + 256))
        load_w(G, WU, w_up, 8, grp)
        xs = sb("x", [128, 8, TT], F32)
        mbuf = [sb("mix%d" % i, [128, 8, TT], BF16) for i in range(2)]
        M = sb("M", [128, 8, TT], F32)
        h2b = [sb("h2_%d" % i, [128, 8, TT], BF16) for i in range(2)]
        sqrot = Rot([sb("sq%d" % i, [128, TT], BF16) for i in range(6)])
        sq2rot = Rot([sb("sqb%d" % i, [128, TT], BF16) for i in range(6)])
        tmprot = Rot([sb("tmp%d" % i, [128, TT], F32) for i in range(2)])
        rstd = sb("rstd", [128, TT], F32)
        Gb = [sb("Gb%d" % i, [128, TT + 2], F32) for i in range(2)]
        H = sb("H", [128, NFC, 2], F32)
        trot = Rot([sb("t%d" % i, [128, TT], F32) for i in range(2)])
        srot = Rot([sb("s%d" % i, [128, TT], F32) for i in range(2)])
        arot = Rot([sb("a%d" % i, [128, TT], BF16) for i in range(3)])
        brot = Rot([2, 3, 4, 5, 6, 7])
        P.I("pool", "memset", ap=H[:, :, :], constant=0.0)
        P.dma("sp", out=mbuf[0][:, :, :], in_=mixr[:, :, 0:TT])

        OWN = {}

        def A_events(tt):
            t0 = tt * TT
            mx = mbuf[tt % 2]
            h2 = h2b[tt % 2]
            ev = []

            def a_load():
                P.dma("sp", out=xs[:, :, :], in_=srcr[:, :, t0:t0 + TT])
                if tt + 1 < NT:
                    P.dma("sp", out=mbuf[(tt + 1) % 2][:, :, :], in_=mixr[:, :, t0 + TT:t0 + 2 * TT])
            ev.append((0.0, a_load))
            sqs = {}

            def mk_oc(oc):
                def f():
                    b = bank(G, brot.next())
                    for c in range(8):
                        mm(P, b, WO[:, c, oc * 128:(oc + 1) * 128], mx[:, c, :], c == 0, c == 7)
                    act(P, M[:, oc, :], b, AF.Copy)
                    sq = sqrot.next()
                    act(P, sq[:, :], M[:, oc, :], AF.Square)
                    sqs[oc] = sq
                    OWN[sq.name] = ("a", tt, oc)
                return f

            def mk_st1(oc):
                def f():
                    assert OWN[sqs[oc].name] == ("a", tt, oc)
                    mm(P, bank(G, 0), G.ones_bb[:, :], sqs[oc][:, :], oc == 0, oc == 7)
                return f
            for oc in range(8):
                ev.append((0.1 + 0.6 * oc, mk_oc(oc)))
                ev.append((0.1 + 0.6 * oc + 3.0, mk_st1(oc)))

            def a_rstd1():
                act(P, rstd[:, :], bank(G, 0), AF.Ln, bias=EPS, scale=1.0 / D)
                act(P, rstd[:, :], rstd[:, :], AF.Exp, scale=-0.5)
            ev.append((7.5, a_rstd1))
            sq2 = {}

            def mk_res(c):
                def f():
                    tmp = tmprot.next()
                    P.I("dve", "scalar_tensor_tensor", out=tmp[:, :], in0=M[:, c, :], scalar=_vcol(vec, GN0 + (l * 4 + 1) * 8 + c),
                        in1=rstd[:, :], op0=ALU.mult, op1=ALU.mult)
                    P.I("dve", "tensor_tensor", out=xs[:, c, :], in0=xs[:, c, :], in1=tmp[:, :], op=ALU.add)
                    sq = sq2rot.next()
                    act(P, sq[:, :], xs[:, c, :], AF.Square)
                    sq2[c] = sq
                    OWN[sq.name] = ("b", tt, c)
                return f

            def mk_st2(c):
                def f():
                    assert OWN[sq2[c].name] == ("b", tt, c)
                    mm(P, bank(G, 1), G.ones_bb[:, :], sq2[c][:, :], c == 0, c == 7)
                return f
            for c in range(8):
                ev.append((8.3 + 0.7 * c, mk_res(c)))
                ev.append((8.3 + 0.7 * c + 3.2, mk_st2(c)))

            def a_rstd2():
                P.dma("sp", out=xbr[:, :, t0:t0 + TT], in_=xs[:, :, :])
                act(P, rstd[:, :], bank(G, 1), AF.Ln, bias=EPS, scale=1.0 / D)
                act(P, rstd[:, :], rstd[:, :], AF.Exp, scale=-0.5)
            ev.append((16.6, a_rstd2))

            def mk_h(c0):
                def f():
                    for c in (c0, c0 + 1):
                        P.I("dve", "scalar_tensor_tensor", out=h2[:, c, :], in0=xs[:, c, :],
                            scalar=_vcol(vec, GN0 + (l * 4 + 2) * 8 + c), in1=rstd[:, :], op0=ALU.mult, op1=ALU.mult)
                return f
            for j, c0 in enumerate(range(0, 8, 2)):
                ev.append((17.4 + 1.0 * j, mk_h(c0)))
            return ev

        def F_closures(tt):
            t0 = tt * TT
            h2 = h2b[tt % 2]
            cl = []

            def mk_fc(fc):
                def f():
                    pg = bank(G, brot.next())
                    for c in range(8):
                        mm(P, pg, WU[:, c, fc * 128:(fc + 1) * 128], h2[:, c, :], c == 0, c == 7)
                    pu = bank(G, brot.next())
                    for c in range(8):
                        mm(P, pu, WU[:, c, DFF + fc * 128:DFF + (fc + 1) * 128], h2[:, c, :], c == 0, c == 7)
                    gb = Gb[fc % 2]
                    P.I("pool", "tensor_copy", out=gb[:, 0:2], in_=H[:, fc, :])
                    act(P, gb[:, 2:TT + 2], pg, AF.Copy)
                    P.I("pool", "tensor_copy", out=H[:, fc, :], in_=gb[:, TT:TT + 2])
                    t = trot.next()
                    wc = lambda j: _vcol(vec, WFC0 + (l * 3 + j) * NFC + fc)
                    act(P, t[:, :], pg, AF.Identity, bias=_vcol(vec, BFC0 + l * NFC + fc), scale=wc(2))
                    P.I("dve", "scalar_tensor_tensor", out=t[:, :], in0=gb[:, 0:TT], scalar=wc(0), in1=t[:, :], op0=ALU.mult, op1=ALU.add)
                    P.I("dve", "scalar_tensor_tensor", out=t[:, :], in0=gb[:, 1:TT + 1], scalar=wc(1), in1=t[:, :], op0=ALU.mult, op1=ALU.add)
                    sv = srot.next()
                    act(P, sv[:, :], t[:, :], AF.Silu)
                    a = arot.next()
                    P.I("dve", "tensor_tensor", out=a[:, :], in0=pu, in1=sv[:, :], op=ALU.mult)
                    P.dma("sp", out=AT[fc * 128:(fc + 1) * 128, t0:t0 + TT], in_=a[:, :])
                return f
            for fc in range(NFC):
                cl.append(mk_fc(fc))
            return cl

        interleave_events([], A_events(0))
        for tt in range(NT):
            side = A_events(tt + 1) if tt + 1 < NT else []
            interleave_events(F_closures(tt), side)
        P.flush()


def phase4(G, l, XB, dst, AT, w_down):
    nc, P, vec = G.nc, G.P, G.vec
    xbr = XB.rearrange("(c p) t -> p c t", p=128)
    dstr = dst.rearrange("(c p) t -> p c t", p=128)
    atr = AT.rearrange("(c p) t -> p c t", p=128)
    with ExitStack() as es:
        sb = lambda n, s, d: es.enter_context(nc.sbuf_tensor("p4_%d_" % l + n, s, d))
        WD = sb("WD", [128, NFC, 1024], BF16)
        load_w(G, WD, w_down, NFC, [(c, c + 256) for c in range(0, 1024, 256)])
        xbuf = [sb("x%d" % i, [128, 8, TT], F32) for i in range(2)]
        abuf = [sb("a%d" % i, [128, NFC, TT], BF16) for i in range(2)]
        Mb = [sb("M%d" % i, [128, 8, TT], F32) for i in range(2)]
        sqrot = Rot([sb("sq%d" % i, [128, TT], BF16) for i in range(8)])
        tmprot = Rot([sb("tmp%d" % i, [128, TT], F32) for i in range(2)])
        lnv = sb("lnv", [128, TT], F32)
        rstd = sb("rstd", [128, TT], F32)
        brot = Rot([1, 2, 3, 4, 5, 6, 7])
        P.dma("sp", out=abuf[0][:, :, :], in_=atr[:, :, 0:TT])
        P.dma("sp", out=xbuf[0][:, :, :], in_=xbr[:, :, 0:TT])
        P.dma("sp", out=xbuf[1][:, :, :], in_=xbr[:, :, TT:2 * TT])

        def D_closures(tt):
            t0 = tt * TT
            a = abuf[tt % 2]
            M = Mb[tt % 2]
            cl = []

            def d_load():
                if tt + 1 < NT:
                    P.dma("sp", out=abuf[(tt + 1) % 2][:, :, :], in_=atr[:, :, t0 + TT:t0 + 2 * TT])
            cl.append(d_load)

            def mk_oc(oc):
                def f():
                    b = bank(G, brot.next())
                    for c in range(NFC):
                        mm(P, b, WD[:, c, oc * 128:(oc + 1) * 128], a[:, c, :], c == 0, c == NFC - 1)
                    act(P, M[:, oc, :], b, AF.Copy)
                return f
            for oc in range(8):
                cl.append(mk_oc(oc))
            return cl

        OWN = {}

        def N_events(tt):
            t0 = tt * TT
            xs = xbuf[tt % 2]
            M = Mb[tt % 2]
            ev = []
            sqs = {}

            def n_sq(c0):
                def f():
                    for c in range(c0, c0 + 4):
                        sq = sqrot.next()
                        act(P, sq[:, :], M[:, c, :], AF.Square)
                        sqs[c] = sq
                        OWN[sq.name] = (tt, c)
                return f

            def n_mm(c0):
                def f():
                    for c in range(c0, c0 + 4):
                        assert OWN[sqs[c].name] == (tt, c)
                        mm(P, bank(G, 0), G.ones_bb[:, :], sqs[c][:, :], c == 0, c == 7)
                return f

            def n_rstd():
                act(P, lnv[:, :], bank(G, 0), AF.Ln, bias=EPS, scale=1.0 / D)
                act(P, rstd[:, :], lnv[:, :], AF.Exp, scale=-0.5)
            ev.append((0.0, n_sq(0)))
            ev.append((0.5, n_sq(4)))
            ev.append((2.2, n_mm(0)))
            ev.append((2.6, n_mm(4)))
            ev.append((3.0, n_rstd))

            def n_res(c0):
                def f():
                    for c in range(c0, c0 + 4):
                        tmp = tmprot.next()
                        P.I("dve", "scalar_tensor_tensor", out=tmp[:, :], in0=M[:, c, :], scalar=_vcol(vec, GN0 + (l * 4 + 3) * 8 + c),
                            in1=rstd[:, :], op0=ALU.mult, op1=ALU.mult)
                        P.I("pool", "tensor_tensor", out=xs[:, c, :], in0=xs[:, c, :], in1=tmp[:, :], op=ALU.add)
                    if c0 == 4:
                        P.dma("sp", out=dstr[:, :, t0:t0 + TT], in_=xs[:, :, :])
                return f
            ev.append((4.0, n_res(0)))
            ev.append((5.0, n_res(4)))
            return ev

        for f in D_closures(0):
            f()
        for tt in range(NT):
            main = D_closures(tt + 1) if tt + 1 < NT else []
            interleave_events(main, N_events(tt))
            if tt + 2 < NT:
                P.dma("sp", out=xbuf[tt % 2][:, :, :], in_=xbr[:, :, (tt + 2) * TT:(tt + 3) * TT])
        P.flush()


def _t5_bucket(dist):
    max_exact = 16
    d = np.maximum(dist, 1).astype(np.float32)
    large = max_exact + (np.log(d / np.float32(max_exact)) / np.float32(math.log(2048 / 16)) * np.float32(16)).astype(np.int32)
    large = np.minimum(large, 31)
    return np.where(dist < max_exact, dist, large)


def _host_tables(norm_g, rel_bias, w_sc, w_cc, b_cc, ln_cc_g, ln_cc_b, w_ffn_conv, b_ffn_conv):
    vecs = np.zeros((128, NV), np.float32)

    def put(col0, arr):
        a = np.asarray(arr, np.float32)
        lead = a.shape[:-1]
        C = a.shape[-1] // 128
        a = a.reshape(lead + (C, 128))
        a = np.moveaxis(a, -1, 0).reshape(128, -1)
        vecs[:, col0:col0 + a.shape[1]] = a

    put(GN0, norm_g)
    put(WSC0, w_sc)
    put(WCC0, w_cc)
    put(BCC0, b_cc)
    put(LNG0, ln_cc_g)
    put(LNB0, ln_cc_b)
    put(WFC0, w_ffn_conv)
    put(BFC0, b_ffn_conv)

    consts = np.zeros((128, NC_), np.float32)
    p = np.arange(128)[:, None]
    i = np.arange(128)[None, :]
    consts[:, C_MASKD:C_MASKD + 128] = np.where(i <= p, 0.0, NEG)
    consts[:, C_MASKD + 128:C_MASKD + 256] = np.where(i >= p, 0.0, NEG)
    iq = np.arange(512)[None, :]
    for o in range(4):
        consts[:, C_MASKS + o * 512:C_MASKS + (o + 1) * 512] = np.where(o * 128 + p < iq, 0.0, NEG)
    consts[:, C_TRI:C_TRI + 128] = (p >= i).astype(np.float32)
    consts[:, C_SU:C_SU + 128] = (p < i).astype(np.float32)
    consts[:, C_ID:C_ID + 128] = (p == i).astype(np.float32)

    rb = np.asarray(rel_bias, np.float32)
    biasT = np.zeros((128, 3, 8, 2, 128), np.float32)
    for g, (win, d) in enumerate(DA_PAIRS):
        for jt in range(2):
            rel = 128 + i - (jt * 128 + p)
            relc = np.clip(rel, 0, 128)
            bk = _t5_bucket(relc * d)
            biasT[:, g, :, jt, :] = np.transpose(rb[bk], (0, 2, 1))
    return vecs, consts, biasT.reshape(128, 6144)


_NC_CACHE = {}


def kernel(x, norm_g, rel_bias, w_in_even, w_out_even, w_sc, w_in_odd, w_out_odd,
           w_cc, b_cc, ln_cc_g, ln_cc_b, w_up, w_ffn_conv, b_ffn_conv, w_down):
    x = np.asarray(x, np.float32)
    B = x.shape[0]
    vecs, consts, biasT = _host_tables(norm_g, rel_bias, w_sc, w_cc, b_cc, ln_cc_g, ln_cc_b, w_ffn_conv, b_ffn_conv)
    if "nc" not in _NC_CACHE:
        _NC_CACHE["nc"] = build_program()
    nc = _NC_CACHE["nc"]
    shared = dict(
        vecs=vecs, consts=consts, biasT=biasT,
        w_in_even=np.ascontiguousarray(w_in_even, np.float32), w_out_even=np.ascontiguousarray(w_out_even, np.float32),
        w_in_odd=np.ascontiguousarray(w_in_odd, np.float32), w_out_odd=np.ascontiguousarray(w_out_odd, np.float32),
        w_up=np.ascontiguousarray(w_up, np.float32), w_down=np.ascontiguousarray(w_down, np.float32),
    )
    in_maps = []
    for b in range(B):
        m = dict(shared)
        m["xT"] = np.ascontiguousarray(x[b].T)
        in_maps.append(m)
    res = run_bass_kernel_spmd(nc, in_maps, core_ids=list(range(B)))
    out = np.stack([np.ascontiguousarray(r["yT"].T) for r in res.results], axis=0)
    return out.astype(np.float32)
```

```python
import math
from contextlib import ExitStack

import numpy as np
import concourse.bass as bass
import concourse.mybir as mybir
from concourse.bass_utils import run_bass_kernel_spmd

F32 = mybir.dt.float32
BF16 = mybir.dt.bfloat16
AF = mybir.ActivationFunctionType
ALU = mybir.AluOpType

S = 4096
D = 1024
NT = 8
TT = 512
DFF = 2816
NFC = 22
EPS = 1e-6
DA_PAIRS = ((128, 1), (512, 4), (2048, 16))
NEG = -30000.0

GN0 = 0
WSC0 = 128
WCC0 = 152
BCC0 = 400
LNG0 = 408
LNB0 = 416
WFC0 = 424
BFC0 = 688
NV = 776
C_MASKD = 0
C_MASKS = 256
C_TRI = 2304
C_SU = 2432
C_ID = 2560
NC_ = 2688


def _is_ap(v):
    return hasattr(v, "tensor") and hasattr(v, "offset") and hasattr(v, "ap")


def _box(ap):
    t = ap.tensor
    shape = list(t.shape)
    nd = len(shape)
    strides = [1] * nd
    for k in range(nd - 2, -1, -1):
        strides[k] = strides[k + 1] * shape[k + 1]
    off = ap.offset
    lo = []
    for k in range(nd):
        lo.append(off // strides[k])
        off = off % strides[k]
    ext = [0] * nd
    for step, cnt in ap.ap:
        if cnt <= 1 or step == 0:
            continue
        k = 0
        while k < nd - 1 and step < strides[k]:
            k += 1
        ext[k] += (step // strides[k]) * (cnt - 1)
        rem = step % strides[k]
        if rem != 0:
            for kk in range(k + 1, nd):
                ext[kk] = shape[kk]
    return (t.name, tuple((lo[k], lo[k] + ext[k] + 1) for k in range(nd)))


def _ovl(a, b):
    for (al, ah), (bl, bh) in zip(a[1], b[1]):
        if not (al < bh and bl < ah):
            return False
    return True


def _contains(a, b):
    for (al, ah), (bl, bh) in zip(a[1], b[1]):
        if not (al <= bl and bh <= ah):
            return False
    return True


class Op:
    __slots__ = ("eng", "emit", "deps", "is_dma", "sem", "val", "signal")

    def __init__(self, eng, emit):
        self.eng = eng
        self.emit = emit
        self.deps = []
        self.is_dma = False
        self.sem = None
        self.val = None
        self.signal = False


ENGS = ("pe", "act", "dve", "pool", "sp")


class Prog:
    def __init__(self, nc, es, n_dsem=60):
        self.nc = nc
        self.esem = {e: es.enter_context(nc.semaphore("s_" + e)) for e in ENGS}
        self.ecnt = {e: 0 for e in ENGS}
        self.bar = es.enter_context(nc.semaphore("s_bar"))
        self.barcnt = 0
        self.dpool = [[es.enter_context(nc.semaphore("d%d" % i)), 0] for i in range(n_dsem)]
        self.n_ops = 0
        self.n_waits = 0
        self._reset()

    def _reset(self):
        self.ops = {e: [] for e in ENGS}
        self.hist = {}
        self.dma_last = {}
        self.dkey2idx = {}
        self.all_ops = []

    def _track(self, op, reads, writes):
        deps = []
        rb_ = [_box(a) for a in reads]
        wb_ = [_box(a) for a in writes]
        for b in rb_:
            h = self.hist.setdefault(b[0], {"w": [], "r": []})
            for wb, wop in h["w"]:
                if _ovl(wb, b):
                    deps.append(wop)
        for b in wb_:
            h = self.hist.setdefault(b[0], {"w": [], "r": []})
            for wb, wop in h["w"]:
                if _ovl(wb, b):
                    deps.append(wop)
            for rb, rop in h["r"]:
                if _ovl(rb, b):
                    deps.append(rop)
        for b in rb_:
            h = self.hist[b[0]]
            if len(h["r"]) > 64:
                h["r"] = [(rb, rop) for rb, rop in h["r"] if not (rop.eng == op.eng and not rop.is_dma and not op.is_dma and _contains(b, rb))]
            h["r"].append((b, op))
        for b in wb_:
            h = self.hist[b[0]]
            h["w"] = [(wb, wop) for wb, wop in h["w"] if not _contains(b, wb)]
            h["r"] = [(rb, rop) for rb, rop in h["r"] if not _contains(b, rb) or rop is op]
            h["w"].append((b, op))
        seen = set()
        for d in deps:
            if d is op or id(d) in seen:
                continue
            if d.eng == "pe" and op.eng == "pe" and not d.is_dma and not op.is_dma:
                continue
            seen.add(id(d))
            op.deps.append(d)

    def I(self, eng, method, extra_reads=(), extra_writes=(), **kw):
        reads = list(extra_reads)
        writes = list(extra_writes)
        for k, v in kw.items():
            if _is_ap(v):
                if k in ("out", "accum_out", "ap"):
                    writes.append(v)
                else:
                    reads.append(v)

        def emit(e, method=method, kw=kw):
            return getattr(e, method)(**kw)

        op = Op(eng, emit)
        self._track(op, reads, writes)
        self.ops[eng].append(op)
        self.all_ops.append(op)
        return op

    def dma(self, eng, out, in_, key=None):
        o_dram = "dram" in str(out.tensor.space).lower()
        sb = in_ if o_dram else out
        if key is None:
            key = sb.tensor.name

        def emit(e, out=out, in_=in_):
            return e.dma_start(out=out, in_=in_)

        op = Op(eng, emit)
        op.is_dma = True
        self._track(op, [in_], [out])
        prev = self.dma_last.get(key)
        if prev is not None and all(prev is not d for d in op.deps):
            op.deps.append(prev)
        self.dma_last[key] = op
        idx = self.dkey2idx.setdefault(key, len(self.dkey2idx))
        ent = self.dpool[idx]
        ent[1] += 16
        op.sem = ent[0]
        op.val = ent[1]
        self.ops[eng].append(op)
        self.all_ops.append(op)
        return op

    def flush(self):
        nc = self.nc
        for op in self.all_ops:
            for d in op.deps:
                d.signal = True
        for e in ENGS:
            for op in reversed(self.ops[e]):
                if not op.is_dma:
                    op.signal = True
                    break
        for e in ENGS:
            for op in self.ops[e]:
                if (not op.is_dma) and op.signal:
                    self.ecnt[e] += 1
                    op.sem = self.esem[e]
                    op.val = self.ecnt[e]
        self.barcnt += 1
        barval = self.barcnt
        used_dsems = [self.dpool[i] for i in range(len(self.dkey2idx))]

        def run(e_name, eh):
            waited = {}
            for op in self.ops[e_name]:
                need = {}
                for d in op.deps:
                    k = id(d.sem)
                    if waited.get(k, 0) >= d.val:
                        continue
                    if k not in need or need[k][1] < d.val:
                        need[k] = (d.sem, d.val)
                for k, (s, v) in need.items():
                    eh.wait_ge(s, v)
                    waited[k] = v
                    self.n_waits += 1
                ins = op.emit(eh)
                if op.is_dma:
                    ins.then_inc(op.sem, 16)
                elif op.signal:
                    ins.then_inc(op.sem, 1)
                self.n_ops += 1
            if e_name == "sp":
                for e in ENGS:
                    if e != "sp" and self.ecnt[e] > 0:
                        eh.wait_ge(self.esem[e], self.ecnt[e])
                for sem, cnt in used_dsems:
                    if cnt > 0:
                        eh.wait_ge(sem, cnt)
                eh.sem_inc(self.bar, 1)
            else:
                eh.wait_ge(self.bar, barval)

        with nc.Block() as block:

            @block.sync
            def _(eh):
                run("sp", eh)

            @block.tensor
            def _(eh):
                run("pe", eh)

            @block.scalar
            def _(eh):
                run("act", eh)

            @block.vector
            def _(eh):
                run("dve", eh)

            @block.gpsimd
            def _(eh):
                run("pool", eh)

        self._reset()


class Ctx:
    pass


def _vcol(vec_sb, c):
    return vec_sb[:, c:c + 1]


def build_program(layers=(0, 1, 2, 3), debug=False, stop_after=None):
    nc = bass.Bass("TRN2", target_bir_lowering=False)

    def din(name, shape, dt=F32):
        return nc.dram_tensor(name, shape, dt, kind="ExternalInput").ap()

    scratch_kind = "ExternalOutput" if debug else "Internal"

    def dscr(name, shape, dt):
        return nc.dram_tensor(name, shape, dt, kind=scratch_kind).ap()

    xT = din("xT", [D, S])
    vecs = din("vecs", [128, NV])
    consts = din("consts", [128, NC_])
    biasT = din("biasT", [128, 6144])
    w_in_even = din("w_in_even", [2, 1024, 3072])
    w_out_even = din("w_out_even", [2, 1024, 1024])
    w_in_odd = din("w_in_odd", [2, 1024, 2560])
    w_out_odd = din("w_out_odd", [2, 1024, 1024])
    w_up = din("w_up", [4, 1024, 5632])
    w_down = din("w_down", [4, 2816, 1024])
    yT = nc.dram_tensor("yT", [D, S], F32, kind="ExternalOutput").ap()

    XA = dscr("XA", [D, S], F32)
    XB = dscr("XB", [D, S], F32)
    QT = dscr("QT", [512, S], BF16)
    KT = dscr("KT", [512, S], BF16)
    VN = dscr("VN", [S, 512], BF16)
    MIX = dscr("MIX", [D, S], BF16)
    AT = dscr("AT", [DFF, S], BF16)

    with ExitStack() as es0:
        P = Prog(nc, es0)
        G = Ctx()
        G.nc = nc
        G.P = P
        sb0 = lambda n, s, d: es0.enter_context(nc.sbuf_tensor(n, s, d))
        G.PS = es0.enter_context(nc.psum_tensor("PS", [128, 4096], F32))
        G.vec = sb0("vec_sb", [128, NV], F32)
        G.ones_f = sb0("ones_f", [128, 128], F32)
        G.ones_b = sb0("ones_b", [128, 64], BF16)
        G.ones_bb = sb0("ones_bb", [128, 128], BF16)
        G.eps = sb0("eps_t", [128, 1], F32)
        G.one = sb0("one_t", [128, 1], F32)
        G.tri = sb0("tri_b", [128, 128], BF16)
        G.su = sb0("su_b", [128, 128], BF16)
        G.ident = sb0("ident_b", [128, 128], BF16)
        G.consts = consts
        G.biasT = biasT

        P.dma("sp", out=G.vec[:, :], in_=vecs[:, :])
        P.dma("pool", out=G.tri[:, :], in_=consts[:, C_TRI:C_TRI + 128])
        P.dma("pool", out=G.su[:, :], in_=consts[:, C_SU:C_SU + 128])
        P.dma("pool", out=G.ident[:, :], in_=consts[:, C_ID:C_ID + 128])
        P.I("dve", "memset", ap=G.ones_f[:, :], constant=1.0)
        P.I("dve", "memset", ap=G.ones_b[:, :], constant=1.0)
        P.I("dve", "memset", ap=G.ones_bb[:, :], constant=1.0)
        P.I("dve", "memset", ap=G.eps[:, :], constant=EPS)
        P.I("dve", "memset", ap=G.one[:, :], constant=1.0)
        P.flush()

        n_layers = len(layers)
        for li, l in enumerate(layers):
            src = xT if li == 0 else XA
            last = li == n_layers - 1
            dst = yT if last else XA
            even = (l % 2 == 0)
            i2 = l // 2
            if even:
                w_in = w_in_even[i2]
                w_out = w_out_even[i2]
            else:
                w_in = w_in_odd[i2]
                w_out = w_out_odd[i2]
            phase1(G, l, src, w_in, QT, KT, VN, MIX)
            if stop_after == (l, 1):
                break
            if even:
                phase2_even(G, l, QT, KT, VN, MIX)
            else:
                phase2_odd(G, l, QT, KT, VN, MIX)
            if stop_after == (l, 2):
                break
            phase3(G, l, src, XB, MIX, AT, w_out, w_up[l])
            if stop_after == (l, 3):
                break
            phase4(G, l, XB, dst, AT, w_down[l])
        G.n_ops = P.n_ops
        G.n_waits = P.n_waits
    return nc


def bank(G, i):
    return G.PS[:, i * 512:(i + 1) * 512]


class Rot:
    def __init__(self, items):
        self.items = list(items)
        self.i = 0

    def next(self):
        v = self.items[self.i % len(self.items)]
        self.i += 1
        return v


def act(P, out, in_, func, bias=None, scale=None):
    kw = dict(out=out, in_=in_, func=func)
    if bias is not None:
        kw["bias"] = bias
    if scale is not None:
        kw["scale"] = scale
    return P.I("act", "activation", **kw)


def mm(P, out, lhsT, rhs, start, stop):
    return P.I("pe", "matmul", out=out, lhsT=lhsT, rhs=rhs, start=start, stop=stop)


def rms_rstd(G, sb_alloc, xs, nchunks, rstd, sqrot, lnv, ps_bank, inv_n):
    P = G.P
    for c in range(nchunks):
        sq = sqrot.next()
        act(P, sq[:, :], xs[:, c, :], AF.Square)
        mm(P, ps_bank, G.ones_bb[:, :], sq[:, :], c == 0, c == nchunks - 1)
    act(P, lnv[:, :], ps_bank, AF.Ln, bias=EPS, scale=inv_n)
    act(P, rstd[:, :], lnv[:, :], AF.Exp, scale=-0.5)


def load_w(G, wsb, w_ap, nk, groups, eng="pool"):
    wr = w_ap.rearrange("(c p) n -> p c n", p=128)
    for gi, (c0, c1) in enumerate(groups):
        G.P.dma(eng, out=wsb[:, :, c0:c1], in_=wr[:, :, c0:c1], key=wsb.name + "_%d" % (gi % 6))


def interleave_events(main, events):
    ev = sorted(enumerate(events), key=lambda t: (t[1][0], t[0]))
    k = 0
    for i, f in enumerate(main):
        f()
        while k < len(ev) and ev[k][1][0] < i + 1:
            ev[k][1][1]()
            k += 1
    while k < len(ev):
        ev[k][1][1]()
        k += 1


def phase1(G, l, src, w_in, QT, KT, VN, MIX):
    nc, P, vec = G.nc, G.P, G.vec
    even = (l % 2 == 0)
    i2 = l // 2
    NIN = 3072 if even else 2560
    srcr = src.rearrange("(c p) t -> p c t", p=128)
    with ExitStack() as es:
        sb = lambda n, s, d: es.enter_context(nc.sbuf_tensor("p1_%d_" % l + n, s, d))
        W = sb("W", [128, 8, NIN], BF16)
        if even:
            grp = [(512, 768), (1024, 1280), (0, 256), (768, 1024), (1280, 1536), (256, 512)]
            grp += [(c, c + 256) for c in range(1536, 3072, 256)]
        else:
            grp = [(512, 768), (0, 256), (768, 1024), (256, 512)]
            grp += [(c, c + 256) for c in range(1024, 2560, 256)]
        load_w(G, W, w_in, 8, grp)
        xbuf = [sb("x%d" % i, [128, 8, TT], F32) for i in range(2)]
        hbuf = [sb("h%d" % i, [128, 8, TT], BF16) for i in range(2)]
        sqrot = Rot([sb("sq%d" % i, [128, TT], BF16) for i in range(7)])
        sqfrot = Rot([sb("sqf%d" % i, [128, TT], F32) for i in range(2)])
        lnv = sb("lnv", [128, TT], F32)
        rstd = sb("rstd", [128, TT], F32)
        qkrot = Rot([sb("qk%d" % i, [128, TT], BF16) for i in range(4)])
        vrot = Rot([sb("v%d" % i, [128, 512], BF16) for i in range(2)])
        yrot = Rot([sb("y%d" % i, [128, TT], BF16) for i in range(2)])
        brot = Rot([1, 2, 3, 4, 5, 6, 7])
        if even:
            gcrot = Rot([sb("gc%d" % i, [128, TT], F32) for i in range(2)])
            Z = [sb("Z%d" % i, [128, TT + 2], F32) for i in range(4)]
            trot = Rot([sb("t%d" % i, [128, TT], F32) for i in range(2)])
            for i in range(4):
                P.I("pool", "memset", ap=Z[i][:, 0:2], constant=0.0)
        else:
            sgrot = Rot([sb("sg%d" % i, [128, TT], F32) for i in range(2)])
            GC = [sb("GC%d" % i, [128, TT + 30], BF16) for i in range(4)]
            diag = sb("diag", [128, 124, 128], BF16)
            Y = sb("Y", [128, 4, TT], F32)
            mean = sb("mean", [128, TT], F32)
            msq = sb("msq", [128, TT], F32)
            var = sb("var", [128, TT], F32)
            lnr = sb("lnr", [128, TT], F32)
            rs2 = sb("rs2", [128, TT], F32)
            ynrot = Rot([sb("yn%d" % i, [128, TT], F32) for i in range(2)])
            for i in range(4):
                P.I("pool", "memset", ap=GC[i][:, 0:30], constant=0.0)
                for j in range(31):
                    P.I("dve", "tensor_scalar", out=diag[:, i * 31 + j, :], in0=G.ident[:, :],
                        scalar1=_vcol(vec, WCC0 + (i2 * 31 + j) * 4 + i), scalar2=None, op0=ALU.mult)

        def proj(h, oc):
            b = bank(G, brot.next())
            for c in range(8):
                mm(P, b, W[:, c, oc * 128:(oc + 1) * 128], h[:, c, :], c == 0, c == 7)
            return b

        OWN = {}

        def norm_events(tt):
            xs = xbuf[tt % 2]
            h = hbuf[tt % 2]
            ev = []
            sqs = {}

            def mk_sq(c):
                def f():
                    sq = sqrot.next()
                    act(P, sq[:, :], xs[:, c, :], AF.Square)
                    sqs[c] = sq
                    OWN[sq.name] = (tt, c)
                return f

            def mk_st(c):
                def f():
                    assert OWN[sqs[c].name] == (tt, c)
                    mm(P, bank(G, 0), G.ones_bb[:, :], sqs[c][:, :], c == 0, c == 7)
                return f
            for c in range(8):
                ev.append((0.0 + 0.5 * c, mk_sq(c)))
                ev.append((0.0 + 0.5 * c + 2.5, mk_st(c)))

            def rs():
                act(P, rstd[:, :], bank(G, 0), AF.Ln, bias=EPS, scale=1.0 / D)
                act(P, rstd[:, :], rstd[:, :], AF.Exp, scale=-0.5)
            ev.append((6.2, rs))

            def mk_h(c0):
                def f():
                    for c in (c0, c0 + 1):
                        P.I("dve", "scalar_tensor_tensor", out=h[:, c, :], in0=xs[:, c, :],
                            scalar=_vcol(vec, GN0 + (l * 4 + 0) * 8 + c), in1=rstd[:, :], op0=ALU.mult, op1=ALU.mult)
                return f
            for j, c0 in enumerate(range(0, 8, 2)):
                ev.append((7.2 + 1.0 * j, mk_h(c0)))
            return ev

        def tile_main(tt):
            t0 = tt * TT
            h = hbuf[tt % 2]
            cl = []
            if even:
                qoc, koc, vcol = 12, 16, 2560

                def mk_conv(i):
                    def f():
                        pc = proj(h, 4 + i)
                        gcb = gcrot.next()
                        act(P, gcb[:, :], pc, AF.Copy)
                        px = proj(h, 8 + i)
                        if tt > 0:
                            P.I("pool", "tensor_copy", out=Z[i][:, 0:2], in_=Z[i][:, TT:TT + 2])
                        P.I("dve", "tensor_tensor", out=Z[i][:, 2:TT + 2], in0=px, in1=gcb[:, :], op=ALU.mult)
                        pb = proj(h, i)
                        t = trot.next()
                        wc = lambda j: _vcol(vec, WSC0 + (i2 * 3 + j) * 4 + i)
                        P.I("dve", "tensor_scalar", out=t[:, :], in0=Z[i][:, 0:TT], scalar1=wc(0), scalar2=None, op0=ALU.mult)
                        P.I("dve", "scalar_tensor_tensor", out=t[:, :], in0=Z[i][:, 1:TT + 1], scalar=wc(1), in1=t[:, :], op0=ALU.mult, op1=ALU.add)
                        P.I("dve", "scalar_tensor_tensor", out=t[:, :], in0=Z[i][:, 2:TT + 2], scalar=wc(2), in1=t[:, :], op0=ALU.mult, op1=ALU.add)
                        ya = yrot.next()
                        P.I("dve", "tensor_tensor", out=ya[:, :], in0=pb, in1=t[:, :], op=ALU.mult)
                        P.dma("sp", out=MIX[i * 128:(i + 1) * 128, t0:t0 + TT], in_=ya[:, :])
                    return f
                for i in range(4):
                    cl.append(mk_conv(i))
            else:
                qoc, koc, vcol = 8, 12, 2048

                def mk_cconv(i):
                    def f():
                        pg = proj(h, 4 + i)
                        sg = sgrot.next()
                        act(P, sg[:, :], pg, AF.Sigmoid)
                        pa = proj(h, i)
                        if tt > 0:
                            P.I("pool", "tensor_copy", out=GC[i][:, 0:30], in_=GC[i][:, TT:TT + 30])
                        P.I("dve", "tensor_tensor", out=GC[i][:, 30:TT + 30], in0=pa, in1=sg[:, :], op=ALU.mult)
                        pcv = bank(G, brot.next())
                        for j in range(31):
                            mm(P, pcv, diag[:, i * 31 + j, :], GC[i][:, j:j + TT], j == 0, j == 30)
                        act(P, Y[:, i, :], pcv, AF.Identity, bias=_vcol(vec, BCC0 + i2 * 4 + i), scale=1.0)
                    return f
                for i in range(4):
                    cl.append(mk_cconv(i))

                def ln_part():
                    sbk = bank(G, brot.next())
                    sqk = bank(G, brot.next())
                    for i in range(4):
                        mm(P, sbk, G.ones_f[:, :], Y[:, i, :], i == 0, i == 3)
                    for i in range(4):
                        sq = sqfrot.next()
                        act(P, sq[:, :], Y[:, i, :], AF.Square)
                        mm(P, sqk, G.ones_f[:, :], sq[:, :], i == 0, i == 3)
                    act(P, mean[:, :], sbk, AF.Copy, scale=1.0 / 512)
                    P.I("dve", "tensor_tensor", out=msq[:, :], in0=mean[:, :], in1=mean[:, :], op=ALU.mult)
                    P.I("dve", "scalar_tensor_tensor", out=var[:, :], in0=sqk, scalar=1.0 / 512, in1=msq[:, :], op0=ALU.mult, op1=ALU.subtract)
                    act(P, lnr[:, :], var[:, :], AF.Ln, bias=EPS, scale=1.0)
                    act(P, rs2[:, :], lnr[:, :], AF.Exp, scale=-0.5)
                    for i in range(4):
                        yn = ynrot.next()
                        P.I("dve", "tensor_tensor", out=yn[:, :], in0=Y[:, i, :], in1=mean[:, :], op=ALU.subtract)
                        P.I("dve", "tensor_tensor", out=yn[:, :], in0=yn[:, :], in1=rs2[:, :], op=ALU.mult)
                        yc = yrot.next()
                        act(P, yc[:, :], yn[:, :], AF.Silu, bias=_vcol(vec, LNB0 + i2 * 4 + i), scale=_vcol(vec, LNG0 + i2 * 4 + i))
                        P.dma("sp", out=MIX[i * 128:(i + 1) * 128, t0:t0 + TT], in_=yc[:, :])

            def mk_qk(i):
                def f():
                    pq = proj(h, qoc + i)
                    qs = qkrot.next()
                    act(P, qs[:, :], pq, AF.Copy, scale=0.125)
                    P.dma("sp", out=QT[i * 128:(i + 1) * 128, t0:t0 + TT], in_=qs[:, :])
                    pk = proj(h, koc + i)
                    ks = qkrot.next()
                    P.I("dve", "tensor_copy", out=ks[:, :], in_=pk)
                    P.dma("sp", out=KT[i * 128:(i + 1) * 128, t0:t0 + TT], in_=ks[:, :])
                return f

            def mk_v(sub):
                def f():
                    b = bank(G, brot.next())
                    for c in range(8):
                        mm(P, b, h[:, c, sub * 128:(sub + 1) * 128], W[:, c, vcol:vcol + 512], c == 0, c == 7)
                    vs = vrot.next()
                    act(P, vs[:, :], b, AF.Copy)
                    P.dma("sp", out=VN[t0 + sub * 128:t0 + (sub + 1) * 128, :], in_=vs[:, :])
                return f
            if not even:
                cl.append(mk_qk(0))
                cl.append(mk_qk(1))
                cl.append(ln_part)
                cl.append(mk_qk(2))
                cl.append(mk_qk(3))
            else:
                for i in range(4):
                    cl.append(mk_qk(i))
            for sub in range(4):
                cl.append(mk_v(sub))
            return cl

        P.dma("sp", out=xbuf[0][:, :, :], in_=srcr[:, :, 0:TT])
        P.dma("sp", out=xbuf[1][:, :, :], in_=srcr[:, :, TT:2 * TT])
        interleave_events([], norm_events(0))
        for tt in range(NT):
            interleave_events(tile_main(tt), norm_events(tt + 1) if tt + 1 < NT else [])
            if tt + 2 < NT:
                P.dma("sp", out=xbuf[tt % 2][:, :, :], in_=srcr[:, :, (tt + 2) * TT:(tt + 3) * TT])
        P.flush()


def phase2_even(G, l, QT, KT, VN, MIX):
    nc, P = G.nc, G.P
    with ExitStack() as es:
        sb = lambda n, s, d: es.enter_context(nc.sbuf_tensor("p2e_%d_" % l + n, s, d))
        VD = [sb("VD%d" % g, [128, 32, 512], BF16) for g in range(3)]
        bias = sb("bias", [128, 6144], F32)
        maskd = sb("maskd", [128, 256], F32)
        qsb = [sb("q%d" % i, [128, S], BF16) for i in range(1)]
        ksb = [sb("k%d" % i, [128, S], BF16) for i in range(1)]
        acc = sb("acc", [128, 2, S], F32)
        ysb = sb("ysb", [128, S], BF16)
        sbf = [sb("sbf%d" % i, [128, 1024], F32) for i in range(2)]
        Ab = [sb("A%d" % i, [128, 1024], BF16) for i in range(4)]
        P.dma("sp", out=qsb[0][:, :], in_=QT[0:128, :])
        P.dma("sp", out=ksb[0][:, :], in_=KT[0:128, :])
        P.dma("sp", out=maskd[:, :], in_=G.consts[:, C_MASKD:C_MASKD + 256])
        for g in range(3):
            P.dma("sp", out=bias[:, g * 2048:(g + 1) * 2048], in_=G.biasT[:, g * 2048:(g + 1) * 2048], key="bias%d" % g)
        for gh in range(24):
            P.I("pool", "tensor_tensor", out=bias[:, gh * 256:(gh + 1) * 256], in0=bias[:, gh * 256:(gh + 1) * 256], in1=maskd[:, :], op=ALU.add)
        for g, (win, d) in enumerate(DA_PAIRS):
            vr = VN.rearrange("(blk p r) f -> p blk r f", p=128, r=d)
            for blk in range(32 // d):
                P.dma("sp", out=VD[g][:, blk * d:(blk + 1) * d, :], in_=vr[:, blk, :, :], key="VD%d_%d" % (g, blk % 4))
        srot = Rot([0, 1])
        arot = Rot([0, 1, 2, 3])
        fr = Rot([0, 1])
        for hp in range(4):
            q = qsb[0]
            k = ksb[0]
            if hp > 0:
                P.dma("sp", out=q[:, :], in_=QT[hp * 128:(hp + 1) * 128, :])
                P.dma("sp", out=k[:, :], in_=KT[hp * 128:(hp + 1) * 128, :])
            units = []
            for g, (win, d) in enumerate(DA_PAIRS):
                if d == 1:
                    for gi in range(8):
                        units.append(dict(g=g, d=d, blocks=[((gi * 4 + bi) * 128, 0) for bi in range(4)], aview=("c", gi * 512)))
                elif d == 4:
                    for gi in range(8):
                        units.append(dict(g=g, d=d, blocks=[(gi * 128, r) for r in range(4)], aview=("s4", gi * 512)))
                else:
                    for sg in range(2):
                        for rq in range(4):
                            units.append(dict(g=g, d=d, blocks=[(sg * 128, rq * 4 + bi) for bi in range(4)], aview=("s16", sg * 2048, rq)))
            for ui, u in enumerate(units):
                u["o_bank"] = 4 + (ui % 2) * 2

            def stageA(u):
                g, d, blocks = u["g"], u["d"], u["blocks"]
                u["A"] = []
                for e in range(2):
                    hh = hp * 2 + e
                    hr = slice(e * 64, (e + 1) * 64)
                    sbase = srot.next() * 1024
                    Sps = G.PS[:, sbase:sbase + 1024]
                    for bi, (m0, r) in enumerate(blocks):
                        qs0 = m0 * d + r
                        for jt in range(2):
                            mk0 = m0 - 128 + jt * 128
                            if mk0 < 0:
                                continue
                            ks0 = mk0 * d + r
                            c0 = sbase + (bi * 2 + jt) * 128
                            mm(P, G.PS[:, c0:c0 + 128], k[hr, ks0:ks0 + 127 * d + 1:d], q[hr, qs0:qs0 + 127 * d + 1:d], True, True)
                    sf = sbf[fr.next()]
                    bsl = bias[:, (g * 8 + hh) * 256:(g * 8 + hh + 1) * 256].unsqueeze(1).broadcast_to([128, 4, 256])
                    P.I("dve", "tensor_tensor", out=sf[:, :].rearrange("p (b f) -> p b f", b=4),
                        in0=Sps.rearrange("p (b f) -> p b f", b=4), in1=bsl, op=ALU.add)
                    A = Ab[arot.next()]
                    act(P, A[:, :], sf[:, :], AF.Exp)
                    u["A"].append(A)

            def stageB(u):
                g, d, blocks = u["g"], u["d"], u["blocks"]
                o_bank = u["o_bank"]
                den_bank = o_bank + 1
                for e in range(2):
                    hh = hp * 2 + e
                    hr = slice(e * 64, (e + 1) * 64)
                    A = u["A"][e]
                    for bi, (m0, r) in enumerate(blocks):
                        first = True
                        for jt in range(2):
                            mk0 = m0 - 128 + jt * 128
                            if mk0 < 0:
                                continue
                            tile = (mk0 // 128) * d + r
                            asl = A[:, (bi * 2 + jt) * 128:(bi * 2 + jt) * 128 + 128]
                            co = o_bank * 512 + bi * 128
                            cd = den_bank * 512 + bi * 128
                            mm(P, G.PS[hr, co:co + 128], VD[g][:, tile, hh * 64:(hh + 1) * 64], asl, first, jt == 1)
                            mm(P, G.PS[hr, cd:cd + 128], G.ones_b[:, :], asl, first, jt == 1)
                            first = False

            def stageC(u):
                aview = u["aview"]
                o_bank = u["o_bank"]
                for w_, bk in ((0, o_bank), (1, o_bank + 1)):
                    src_ps = bank(G, bk)
                    if aview[0] == "c":
                        c0 = aview[1]
                        act(P, acc[:, w_, c0:c0 + 512], src_ps, AF.Copy)
                    elif aview[0] == "s4":
                        c0 = aview[1]
                        av = acc[:, w_, c0:c0 + 512].rearrange("p (i r) -> p r i", r=4)
                        P.I("dve", "tensor_tensor", out=av, in0=src_ps.rearrange("p (r i) -> p r i", r=4), in1=av, op=ALU.add)
                    else:
                        c0, rq = aview[1], aview[2]
                        av = acc[:, w_, c0:c0 + 2048].rearrange("p (i r) -> p r i", r=16)[:, rq * 4:(rq + 1) * 4, :]
                        P.I("dve", "tensor_tensor", out=av, in0=src_ps.rearrange("p (r i) -> p r i", r=4), in1=av, op=ALU.add)

            nu = len(units)
            for it in range(nu + 2):
                if it < nu:
                    stageA(units[it])
                if 0 <= it - 1 < nu:
                    stageB(units[it - 1])
                if 0 <= it - 2 < nu:
                    stageC(units[it - 2])
            for qh in range(4):
                cs = slice(qh * 1024, (qh + 1) * 1024)
                act(P, acc[:, 1, cs], acc[:, 1, cs], AF.Ln)
                act(P, acc[:, 1, cs], acc[:, 1, cs], AF.Exp, scale=-1.0)
                P.I("pool", "tensor_tensor", out=ysb[:, cs], in0=acc[:, 0, cs], in1=acc[:, 1, cs], op=ALU.mult)
            P.dma("sp", out=MIX[512 + hp * 128:512 + (hp + 1) * 128, :], in_=ysb[:, :])
        P.flush()


def phase2_odd(G, l, QT, KT, VN, MIX):
    nc, P = G.nc, G.P
    with ExitStack() as es:
        sb = lambda n, s, d: es.enter_context(nc.sbuf_tensor("p2o_%d_" % l + n, s, d))
        V = sb("V", [128, 32, 512], BF16)
        masks = sb("masks", [128, 4, 512], BF16)
        qsb = [sb("q%d" % i, [128, S], BF16) for i in range(2)]
        ksb = [sb("k%d" % i, [128, S], BF16) for i in range(2)]
        ysb = [sb("ysb%d" % i, [128, S], BF16) for i in range(2)]
        NSL = 2
        E = [[sb("E%d_%d" % (i, j), [128, 2, 512], F32) for j in range(3)] for i in range(NSL)]
        SP = [[sb("SP%d_%d" % (i, j), [128, 2, 512], BF16) for j in range(3)] for i in range(NSL)]
        Gt = [[sb("G%d_%d" % (i, j), [128, 2, 512], F32) for j in range(2)] for i in range(NSL)]
        At = [[sb("At%d_%d" % (i, j), [128, 2, 512], BF16) for j in range(2)] for i in range(NSL)]
        vr = VN.rearrange("(t p) f -> p t f", p=128)
        P.dma("sp", out=qsb[0][:, :], in_=QT[0:128, :])
        P.dma("sp", out=ksb[0][:, :], in_=KT[0:128, :])
        P.dma("pool", out=masks[:, :, :], in_=G.consts[:, C_MASKS:C_MASKS + 2048].rearrange("p (o f) -> p o f", o=4))
        for qq in range(7, -1, -1):
            P.dma("sp", out=V[:, qq * 4:(qq + 1) * 4, :], in_=vr[:, qq * 4:(qq + 1) * 4, :], key="Vo_%d" % (qq % 4))
        slot_qts = [[7, 4, 3, 0], [6, 5, 2, 1]]
        for hp in range(4):
            q = qsb[hp % 2]
            k = ksb[hp % 2]
            ys = ysb[hp % 2]
            if hp + 1 < 4:
                P.dma("sp", out=qsb[(hp + 1) % 2][:, :], in_=QT[(hp + 1) * 128:(hp + 2) * 128, :])
                P.dma("sp", out=ksb[(hp + 1) % 2][:, :], in_=KT[(hp + 1) * 128:(hp + 2) * 128, :])
            streams = []
            for s_ in range(NSL):
                st = []
                for qt in slot_qts[s_]:
                    n = 4 * qt + 4
                    for j in range(n):
                        st.append(dict(qt=qt, kb=4 * qt + 3 - j, first=(j == 0), last=(j == n - 1), idx=len(st)))
                streams.append(st)
            nit = max(len(st) for st in streams) + 2

            def stageA(s_, stp):
                qt, kb, idx = stp["qt"], stp["kb"], stp["idx"]
                o = kb - 4 * qt
                w0 = max(o, 0) * 128
                for e in range(2):
                    hr = slice(e * 64, (e + 1) * 64)
                    zc = e * 512
                    mm(P, G.PS[:, zc + w0:zc + 512], k[hr, kb * 128:(kb + 1) * 128], q[hr, qt * 512 + w0:(qt + 1) * 512], True, o < 0)
                    if o >= 0:
                        mm(P, G.PS[:, zc + w0:zc + 512], G.ident[:, :], masks[:, o, w0:512], False, True)
                Ec = E[s_][idx % 3]
                act(P, Ec[:, :, w0:512], G.PS[:, 0:1024].rearrange("p (e f) -> p e f", e=2)[:, :, w0:512], AF.Exp)
                spc = SP[s_][idx % 3]
                act(P, spc[:, :, w0:512], Ec[:, :, w0:512], AF.Ln, bias=1.0, scale=1.0)

            def stageB(s_, stp):
                qt, kb, idx = stp["qt"], stp["kb"], stp["idx"]
                o = kb - 4 * qt
                w0 = max(o, 0) * 128
                spc = SP[s_][idx % 3]
                spp = SP[s_][(idx - 1) % 3]
                for e in range(2):
                    pc = (2 + 2 * s_ + e) * 512
                    if o >= 0:
                        mm(P, G.PS[:, pc + w0:pc + w0 + 128], G.tri[:, :], spc[:, e, w0:w0 + 128], stp["first"], True)
                        if w0 + 128 < 512:
                            mm(P, G.PS[:, pc + w0 + 128:pc + 512], G.su[:, :], spp[:, e, w0 + 128:512], False, False)
                            mm(P, G.PS[:, pc + w0 + 128:pc + 512], G.tri[:, :], spc[:, e, w0 + 128:512], False, True)
                    else:
                        mm(P, G.PS[:, pc:pc + 512], G.su[:, :], spp[:, e, :], False, False)
                        mm(P, G.PS[:, pc:pc + 512], G.tri[:, :], spc[:, e, :], False, True)
                Gc = Gt[s_][idx % 2]
                c0 = (2 + 2 * s_) * 512
                act(P, Gc[:, :, w0:512], G.PS[:, c0:c0 + 1024].rearrange("p (e f) -> p e f", e=2)[:, :, w0:512], AF.Exp, scale=-1.0)
                P.I("dve", "tensor_tensor", out=At[s_][idx % 2][:, :, w0:512], in0=E[s_][idx % 3][:, :, w0:512], in1=Gc[:, :, w0:512], op=ALU.mult)

            def stageC(s_, stp):
                qt, kb, idx = stp["qt"], stp["kb"], stp["idx"]
                o = kb - 4 * qt
                w0 = max(o, 0) * 128
                for e in range(2):
                    hh = hp * 2 + e
                    hr = slice(e * 64, (e + 1) * 64)
                    oc0 = (6 + s_) * 512
                    vt = V[:, kb, hh * 64:(hh + 1) * 64]
                    at = At[s_][idx % 2]
                    if o >= 0:
                        mm(P, G.PS[hr, oc0 + w0:oc0 + w0 + 128], vt, at[:, e, w0:w0 + 128], stp["first"], False)
                        if w0 + 128 < 512:
                            mm(P, G.PS[hr, oc0 + w0 + 128:oc0 + 512], vt, at[:, e, w0 + 128:512], False, False)
                    else:
                        mm(P, G.PS[hr, oc0:oc0 + 512], vt, at[:, e, :], False, stp["last"])
                if stp["last"]:
                    P.I("dve", "tensor_copy", out=ys[:, qt * 512:(qt + 1) * 512], in_=bank(G, 6 + s_))

            for it in range(nit):
                for s_ in range(NSL):
                    if it < len(streams[s_]):
                        stageA(s_, streams[s_][it])
                for s_ in range(NSL):
                    if 0 <= it - 1 < len(streams[s_]):
                        stageB(s_, streams[s_][it - 1])
                for s_ in range(NSL):
                    if 0 <= it - 2 < len(streams[s_]):
                        stageC(s_, streams[s_][it - 2])
            P.dma("sp", out=MIX[512 + hp * 128:512 + (hp + 1) * 128, :], in_=ys[:, :])
        P.flush()


def post_norm_residual(G, l, j, M, xs, rstd, sqrot, lnv, ps_bank, tmprot):
    P, vec = G.P, G.vec
    rms_rstd(G, None, M, 8, rstd, sqrot, lnv, ps_bank, 1.0 / D)
    for c in range(8):
        tmp = tmprot.next()
        P.I("dve", "scalar_tensor_tensor", out=tmp[:, :], in0=M[:, c, :], scalar=_vcol(vec, GN0 + (l * 4 + j) * 8 + c),
            in1=rstd[:, :], op0=ALU.mult, op1=ALU.mult)
        P.I("pool", "tensor_tensor", out=xs[:, c, :], in0=xs[:, c, :], in1=tmp[:, :], op=ALU.add)


def interleave(main, side, offset=1, stride=1):
    k = 0
    for i, f in enumerate(main):
        f()
        while k < len(side) and i >= offset + k * stride:
            side[k]()
            k += 1
    while k < len(side):
        side[k]()
        k += 1


def phase3(G, l, src, XB, MIX, AT, w_out, w_up):
    nc, P, vec = G.nc, G.P, G.vec
    srcr = src.rearrange("(c p) t -> p c t", p=128)
    xbr = XB.rearrange("(c p) t -> p c t", p=128)
    mixr = MIX.rearrange("(c p) t -> p c t", p=128)
    with ExitStack() as es:
        sb = lambda n, s, d: es.enter_context(nc.sbuf_tensor("p3_%d_" % l + n, s, d))
        WO = sb("WO", [128, 8, 1024], BF16)
        WU = sb("WU", [128, 8, 2 * DFF], BF16)
        load_w(G, WO, w_out, 8, [(c, c + 256) for c in range(0, 1024, 256)])
        grp = []
        for c in range(0, DFF, 256):
            grp.append((c, c + 256))
            grp.append((DFF + c, DFF + c + 256))
        load_w(G, WU, w_up, 8, grp)
        xs = sb("x", [128, 8, TT], F32)
        mbuf = [sb("mix%d" % i, [128, 8, TT], BF16) for i in range(2)]
        M = sb("M", [128, 8, TT], F32)
        h2b = [sb("h2_%d" % i, [128, 8, TT], BF16) for i in range(2)]
        sqrot = Rot([sb("sq%d" % i, [128, TT], BF16) for i in range(6)])
        sq2rot = Rot([sb("sqb%d" % i, [128, TT], BF16) for i in range(6)])
        tmprot = Rot([sb("tmp%d" % i, [128, TT], F32) for i in range(2)])
        rstd = sb("rstd", [128, TT], F32)
        Gb = [sb("Gb%d" % i, [128, TT + 2], F32) for i in range(2)]
        H = sb("H", [128, NFC, 2], F32)
        trot = Rot([sb("t%d" % i, [128, TT], F32) for i in range(2)])
        srot = Rot([sb("s%d" % i, [128, TT], F32) for i in range(2)])
        arot = Rot([sb("a%d" % i, [128, TT], BF16) for i in range(3)])
        brot = Rot([2, 3, 4, 5, 6, 7])
        P.I("pool", "memset", ap=H[:, :, :], constant=0.0)
        P.dma("sp", out=mbuf[0][:, :, :], in_=mixr[:, :, 0:TT])

        OWN = {}

        def A_events(tt):
            t0 = tt * TT
            mx = mbuf[tt % 2]
            h2 = h2b[tt % 2]
            ev = []

            def a_load():
                P.dma("sp", out=xs[:, :, :], in_=srcr[:, :, t0:t0 + TT])
                if tt + 1 < NT:
                    P.dma("sp", out=mbuf[(tt + 1) % 2][:, :, :], in_=mixr[:, :, t0 + TT:t0 + 2 * TT])
            ev.append((0.0, a_load))
            sqs = {}

            def mk_oc(oc):
                def f():
                    b = bank(G, brot.next())
                    for c in range(8):
                        mm(P, b, WO[:, c, oc * 128:(oc + 1) * 128], mx[:, c, :], c == 0, c == 7)
                    act(P, M[:, oc, :], b, AF.Copy)
                    sq = sqrot.next()
                    act(P, sq[:, :], M[:, oc, :], AF.Square)
                    sqs[oc] = sq
                    OWN[sq.name] = ("a", tt, oc)
                return f

            def mk_st1(oc):
                def f():
                    assert OWN[sqs[oc].name] == ("a", tt, oc)
                    mm(P, bank(G, 0), G.ones_bb[:, :], sqs[oc][:, :], oc == 0, oc == 7)
                return f
            for oc in range(8):
                ev.append((0.1 + 0.6 * oc, mk_oc(oc)))
                ev.append((0.1 + 0.6 * oc + 3.0, mk_st1(oc)))

            def a_rstd1():
                act(P, rstd[:, :], bank(G, 0), AF.Ln, bias=EPS, scale=1.0 / D)
                act(P, rstd[:, :], rstd[:, :], AF.Exp, scale=-0.5)
            ev.append((7.5, a_rstd1))
            sq2 = {}

            def mk_res(c):
                def f():
                    tmp = tmprot.next()
                    P.I("dve", "scalar_tensor_tensor", out=tmp[:, :], in0=M[:, c, :], scalar=_vcol(vec, GN0 + (l * 4 + 1) * 8 + c),
                        in1=rstd[:, :], op0=ALU.mult, op1=ALU.mult)
                    P.I("dve", "tensor_tensor", out=xs[:, c, :], in0=xs[:, c, :], in1=tmp[:, :], op=ALU.add)
                    sq = sq2rot.next()
                    act(P, sq[:, :], xs[:, c, :], AF.Square)
                    sq2[c] = sq
                    OWN[sq.name] = ("b", tt, c)
                return f

            def mk_st2(c):
                def f():
                    assert OWN[sq2[c].name] == ("b", tt, c)
                    mm(P, bank(G, 1), G.ones_bb[:, :], sq2[c][:, :], c == 0, c == 7)
                return f
            for c in range(8):
                ev.append((8.3 + 0.7 * c, mk_res(c)))
                ev.append((8.3 + 0.7 * c + 3.2, mk_st2(c)))

            def a_rstd2():
                P.dma("sp", out=xbr[:, :, t0:t0 + TT], in_=xs[:, :, :])
                act(P, rstd[:, :], bank(G, 1), AF.Ln, bias=EPS, scale=1.0 / D)
                act(P, rstd[:, :], rstd[:, :], AF.Exp, scale=-0.5)
            ev.append((16.6, a_rstd2))

            def mk_h(c0):
                def f():
                    for c in (c0, c0 + 1):
                        P.I("dve", "scalar_tensor_tensor", out=h2[:, c, :], in0=xs[:, c, :],
                            scalar=_vcol(vec, GN0 + (l * 4 + 2) * 8 + c), in1=rstd[:, :], op0=ALU.mult, op1=ALU.mult)
                return f
            for j, c0 in enumerate(range(0, 8, 2)):
                ev.append((17.4 + 1.0 * j, mk_h(c0)))
            return ev

        def F_closures(tt):
            t0 = tt * TT
            h2 = h2b[tt % 2]
            cl = []

            svs = {}

            def g_part(fc):
                pg = bank(G, brot.next())
                for c in range(8):
                    mm(P, pg, WU[:, c, fc * 128:(fc + 1) * 128], h2[:, c, :], c == 0, c == 7)
                gb = Gb[fc % 2]
                P.I("pool", "tensor_copy", out=gb[:, 0:2], in_=H[:, fc, :])
                act(P, gb[:, 2:TT + 2], pg, AF.Copy)
                P.I("pool", "tensor_copy", out=H[:, fc, :], in_=gb[:, TT:TT + 2])
                t = trot.next()
                wc = lambda j: _vcol(vec, WFC0 + (l * 3 + j) * NFC + fc)
                act(P, t[:, :], pg, AF.Identity, bias=_vcol(vec, BFC0 + l * NFC + fc), scale=wc(2))
                P.I("dve", "scalar_tensor_tensor", out=t[:, :], in0=gb[:, 0:TT], scalar=wc(0), in1=t[:, :], op0=ALU.mult, op1=ALU.add)
                P.I("dve", "scalar_tensor_tensor", out=t[:, :], in0=gb[:, 1:TT + 1], scalar=wc(1), in1=t[:, :], op0=ALU.mult, op1=ALU.add)
                sv = srot.next()
                act(P, sv[:, :], t[:, :], AF.Silu)
                svs[fc] = sv

            def u_part(fc):
                pu = bank(G, brot.next())
                for c in range(8):
                    mm(P, pu, WU[:, c, DFF + fc * 128:DFF + (fc + 1) * 128], h2[:, c, :], c == 0, c == 7)
                a = arot.next()
                P.I("dve", "tensor_tensor", out=a[:, :], in0=pu, in1=svs[fc][:, :], op=ALU.mult)
                P.dma("sp", out=AT[fc * 128:(fc + 1) * 128, t0:t0 + TT], in_=a[:, :])

            def mk_fc(fc):
                def f():
                    g_part(fc)
                    if fc > 0:
                        u_part(fc - 1)
                    if fc == NFC - 1:
                        u_part(fc)
                return f
            for fc in range(NFC):
                cl.append(mk_fc(fc))
            return cl

        interleave_events([], A_events(0))
        for tt in range(NT):
            side = A_events(tt + 1) if tt + 1 < NT else []
            interleave_events(F_closures(tt), side)
        P.flush()


def phase4(G, l, XB, dst, AT, w_down):
    nc, P, vec = G.nc, G.P, G.vec
    xbr = XB.rearrange("(c p) t -> p c t", p=128)
    dstr = dst.rearrange("(c p) t -> p c t", p=128)
    atr = AT.rearrange("(c p) t -> p c t", p=128)
    with ExitStack() as es:
        sb = lambda n, s, d: es.enter_context(nc.sbuf_tensor("p4_%d_" % l + n, s, d))
        WD = sb("WD", [128, NFC, 1024], BF16)
        load_w(G, WD, w_down, NFC, [(c, c + 256) for c in range(0, 1024, 256)])
        xbuf = [sb("x%d" % i, [128, 8, TT], F32) for i in range(2)]
        abuf = [sb("a%d" % i, [128, NFC, TT], BF16) for i in range(2)]
        Mb = [sb("M%d" % i, [128, 8, TT], F32) for i in range(2)]
        sqrot = Rot([sb("sq%d" % i, [128, TT], BF16) for i in range(8)])
        tmprot = Rot([sb("tmp%d" % i, [128, TT], F32) for i in range(2)])
        lnv = sb("lnv", [128, TT], F32)
        rstd = sb("rstd", [128, TT], F32)
        brot = Rot([1, 2, 3, 4, 5, 6, 7])
        P.dma("sp", out=abuf[0][:, :, :], in_=atr[:, :, 0:TT])
        P.dma("sp", out=xbuf[0][:, :, :], in_=xbr[:, :, 0:TT])
        P.dma("sp", out=xbuf[1][:, :, :], in_=xbr[:, :, TT:2 * TT])

        def D_closures(tt):
            t0 = tt * TT
            a = abuf[tt % 2]
            M = Mb[tt % 2]
            cl = []

            def d_load():
                if tt + 1 < NT:
                    P.dma("sp", out=abuf[(tt + 1) % 2][:, :, :], in_=atr[:, :, t0 + TT:t0 + 2 * TT])
            cl.append(d_load)

            def mk_oc(oc):
                def f():
                    b = bank(G, brot.next())
                    for c in range(NFC):
                        mm(P, b, WD[:, c, oc * 128:(oc + 1) * 128], a[:, c, :], c == 0, c == NFC - 1)
                    act(P, M[:, oc, :], b, AF.Copy)
                return f
            for oc in range(8):
                cl.append(mk_oc(oc))
            return cl

        OWN = {}

        def N_events(tt):
            t0 = tt * TT
            xs = xbuf[tt % 2]
            M = Mb[tt % 2]
            ev = []
            sqs = {}

            def n_sq(c0):
                def f():
                    for c in range(c0, c0 + 4):
                        sq = sqrot.next()
                        act(P, sq[:, :], M[:, c, :], AF.Square)
                        sqs[c] = sq
                        OWN[sq.name] = (tt, c)
                return f

            def n_mm(c0):
                def f():
                    for c in range(c0, c0 + 4):
                        assert OWN[sqs[c].name] == (tt, c)
                        mm(P, bank(G, 0), G.ones_bb[:, :], sqs[c][:, :], c == 0, c == 7)
                return f

            def n_rstd():
                act(P, lnv[:, :], bank(G, 0), AF.Ln, bias=EPS, scale=1.0 / D)
                act(P, rstd[:, :], lnv[:, :], AF.Exp, scale=-0.5)
            ev.append((0.0, n_sq(0)))
            ev.append((0.5, n_sq(4)))
            ev.append((2.2, n_mm(0)))
            ev.append((2.6, n_mm(4)))
            ev.append((3.0, n_rstd))

            def n_res(c0):
                def f():
                    for c in range(c0, c0 + 4):
                        tmp = tmprot.next()
                        P.I("dve", "scalar_tensor_tensor", out=tmp[:, :], in0=M[:, c, :], scalar=_vcol(vec, GN0 + (l * 4 + 3) * 8 + c),
                            in1=rstd[:, :], op0=ALU.mult, op1=ALU.mult)
                        P.I("pool", "tensor_tensor", out=xs[:, c, :], in0=xs[:, c, :], in1=tmp[:, :], op=ALU.add)
                    if c0 == 4:
                        P.dma("sp", out=dstr[:, :, t0:t0 + TT], in_=xs[:, :, :])
                return f
            ev.append((4.0, n_res(0)))
            ev.append((5.0, n_res(4)))
            return ev

        for f in D_closures(0):
            f()
        for tt in range(NT):
            main = D_closures(tt + 1) if tt + 1 < NT else []
            interleave_events(main, N_events(tt))
            if tt + 2 < NT:
                P.dma("sp", out=xbuf[tt % 2][:, :, :], in_=xbr[:, :, (tt + 2) * TT:(tt + 3) * TT])
        P.flush()


def _t5_bucket(dist):
    max_exact = 16
    d = np.maximum(dist, 1).astype(np.float32)
    large = max_exact + (np.log(d / np.float32(max_exact)) / np.float32(math.log(2048 / 16)) * np.float32(16)).astype(np.int32)
    large = np.minimum(large, 31)
    return np.where(dist < max_exact, dist, large)


def _host_tables(norm_g, rel_bias, w_sc, w_cc, b_cc, ln_cc_g, ln_cc_b, w_ffn_conv, b_ffn_conv):
    vecs = np.zeros((128, NV), np.float32)

    def put(col0, arr):
        a = np.asarray(arr, np.float32)
        lead = a.shape[:-1]
        C = a.shape[-1] // 128
        a = a.reshape(lead + (C, 128))
        a = np.moveaxis(a, -1, 0).reshape(128, -1)
        vecs[:, col0:col0 + a.shape[1]] = a

    put(GN0, norm_g)
    put(WSC0, w_sc)
    put(WCC0, w_cc)
    put(BCC0, b_cc)
    put(LNG0, ln_cc_g)
    put(LNB0, ln_cc_b)
    put(WFC0, w_ffn_conv)
    put(BFC0, b_ffn_conv)

    consts = np.zeros((128, NC_), np.float32)
    p = np.arange(128)[:, None]
    i = np.arange(128)[None, :]
    consts[:, C_MASKD:C_MASKD + 128] = np.where(i <= p, 0.0, NEG)
    consts[:, C_MASKD + 128:C_MASKD + 256] = np.where(i >= p, 0.0, NEG)
    iq = np.arange(512)[None, :]
    for o in range(4):
        consts[:, C_MASKS + o * 512:C_MASKS + (o + 1) * 512] = np.where(o * 128 + p < iq, 0.0, NEG)
    consts[:, C_TRI:C_TRI + 128] = (p >= i).astype(np.float32)
    consts[:, C_SU:C_SU + 128] = (p < i).astype(np.float32)
    consts[:, C_ID:C_ID + 128] = (p == i).astype(np.float32)

    rb = np.asarray(rel_bias, np.float32)
    biasT = np.zeros((128, 3, 8, 2, 128), np.float32)
    for g, (win, d) in enumerate(DA_PAIRS):
        for jt in range(2):
            rel = 128 + i - (jt * 128 + p)
            relc = np.clip(rel, 0, 128)
            bk = _t5_bucket(relc * d)
            biasT[:, g, :, jt, :] = np.transpose(rb[bk], (0, 2, 1))
    return vecs, consts, biasT.reshape(128, 6144)


_NC_CACHE = {}


def kernel(x, norm_g, rel_bias, w_in_even, w_out_even, w_sc, w_in_odd, w_out_odd,
           w_cc, b_cc, ln_cc_g, ln_cc_b, w_up, w_ffn_conv, b_ffn_conv, w_down):
    x = np.asarray(x, np.float32)
    B = x.shape[0]
    vecs, consts, biasT = _host_tables(norm_g, rel_bias, w_sc, w_cc, b_cc, ln_cc_g, ln_cc_b, w_ffn_conv, b_ffn_conv)
    if "nc" not in _NC_CACHE:
        _NC_CACHE["nc"] = build_program()
    nc = _NC_CACHE["nc"]
    shared = dict(
        vecs=vecs, consts=consts, biasT=biasT,
        w_in_even=np.ascontiguousarray(w_in_even, np.float32), w_out_even=np.ascontiguousarray(w_out_even, np.float32),
        w_in_odd=np.ascontiguousarray(w_in_odd, np.float32), w_out_odd=np.ascontiguousarray(w_out_odd, np.float32),
        w_up=np.ascontiguousarray(w_up, np.float32), w_down=np.ascontiguousarray(w_down, np.float32),
    )
    in_maps = []
    for b in range(B):
        m = dict(shared)
        m["xT"] = np.ascontiguousarray(x[b].T)
        in_maps.append(m)
    res = run_bass_kernel_spmd(nc, in_maps, core_ids=list(range(B)))
    out = np.stack([np.ascontiguousarray(r["yT"].T) for r in res.results], axis=0)
    return out.astype(np.float32)
```

```python
import math
from contextlib import ExitStack

import numpy as np
import concourse.bass as bass
import concourse.mybir as mybir
from concourse.bass_utils import run_bass_kernel_spmd

F32 = mybir.dt.float32
BF16 = mybir.dt.bfloat16
AF = mybir.ActivationFunctionType
ALU = mybir.AluOpType

S = 4096
D = 1024
NT = 8
TT = 512
DFF = 2816
NFC = 22
EPS = 1e-6
DA_PAIRS = ((128, 1), (512, 4), (2048, 16))
NEG = -30000.0

GN0 = 0
WSC0 = 128
WCC0 = 152
BCC0 = 400
LNG0 = 408
LNB0 = 416
WFC0 = 424
BFC0 = 688
NV = 776
C_MASKD = 0
C_MASKS = 256
C_TRI = 2304
C_SU = 2432
C_ID = 2560
NC_ = 2688


def _is_ap(v):
    return hasattr(v, "tensor") and hasattr(v, "offset") and hasattr(v, "ap")


def _box(ap):
    t = ap.tensor
    shape = list(t.shape)
    nd = len(shape)
    strides = [1] * nd
    for k in range(nd - 2, -1, -1):
        strides[k] = strides[k + 1] * shape[k + 1]
    off = ap.offset
    lo = []
    for k in range(nd):
        lo.append(off // strides[k])
        off = off % strides[k]
    ext = [0] * nd
    for step, cnt in ap.ap:
        if cnt <= 1 or step == 0:
            continue
        k = 0
        while k < nd - 1 and step < strides[k]:
            k += 1
        ext[k] += (step // strides[k]) * (cnt - 1)
        rem = step % strides[k]
        if rem != 0:
            for kk in range(k + 1, nd):
                ext[kk] = shape[kk]
    return (t.name, tuple((lo[k], lo[k] + ext[k] + 1) for k in range(nd)))


def _ovl(a, b):
    for (al, ah), (bl, bh) in zip(a[1], b[1]):
        if not (al < bh and bl < ah):
            return False
    return True


def _contains(a, b):
    for (al, ah), (bl, bh) in zip(a[1], b[1]):
        if not (al <= bl and bh <= ah):
            return False
    return True


class Op:
    __slots__ = ("eng", "emit", "deps", "is_dma", "sem", "val", "signal")

    def __init__(self, eng, emit):
        self.eng = eng
        self.emit = emit
        self.deps = []
        self.is_dma = False
        self.sem = None
        self.val = None
        self.signal = False


ENGS = ("pe", "act", "dve", "pool", "sp")


class Prog:
    def __init__(self, nc, es, n_dsem=60):
        self.nc = nc
        self.esem = {e: es.enter_context(nc.semaphore("s_" + e)) for e in ENGS}
        self.ecnt = {e: 0 for e in ENGS}
        self.bar = es.enter_context(nc.semaphore("s_bar"))
        self.barcnt = 0
        self.dpool = [[es.enter_context(nc.semaphore("d%d" % i)), 0] for i in range(n_dsem)]
        self.n_ops = 0
        self.n_waits = 0
        self._reset()

    def _reset(self):
        self.ops = {e: [] for e in ENGS}
        self.hist = {}
        self.dma_last = {}
        self.dkey2idx = {}
        self.all_ops = []

    def _track(self, op, reads, writes):
        deps = []
        rb_ = [_box(a) for a in reads]
        wb_ = [_box(a) for a in writes]
        for b in rb_:
            h = self.hist.setdefault(b[0], {"w": [], "r": []})
            for wb, wop in h["w"]:
                if _ovl(wb, b):
                    deps.append(wop)
        for b in wb_:
            h = self.hist.setdefault(b[0], {"w": [], "r": []})
            for wb, wop in h["w"]:
                if _ovl(wb, b):
                    deps.append(wop)
            for rb, rop in h["r"]:
                if _ovl(rb, b):
                    deps.append(rop)
        for b in rb_:
            h = self.hist[b[0]]
            if len(h["r"]) > 64:
                h["r"] = [(rb, rop) for rb, rop in h["r"] if not (rop.eng == op.eng and not rop.is_dma and not op.is_dma and _contains(b, rb))]
            h["r"].append((b, op))
        for b in wb_:
            h = self.hist[b[0]]
            h["w"] = [(wb, wop) for wb, wop in h["w"] if not _contains(b, wb)]
            h["r"] = [(rb, rop) for rb, rop in h["r"] if not _contains(b, rb) or rop is op]
            h["w"].append((b, op))
        seen = set()
        for d in deps:
            if d is op or id(d) in seen:
                continue
            if d.eng == "pe" and op.eng == "pe" and not d.is_dma and not op.is_dma:
                continue
            seen.add(id(d))
            op.deps.append(d)

    def I(self, eng, method, extra_reads=(), extra_writes=(), **kw):
        reads = list(extra_reads)
        writes = list(extra_writes)
        for k, v in kw.items():
            if _is_ap(v):
                if k in ("out", "accum_out", "ap"):
                    writes.append(v)
                else:
                    reads.append(v)

        def emit(e, method=method, kw=kw):
            return getattr(e, method)(**kw)

        op = Op(eng, emit)
        self._track(op, reads, writes)
        self.ops[eng].append(op)
        self.all_ops.append(op)
        return op

    def dma(self, eng, out, in_, key=None):
        o_dram = "dram" in str(out.tensor.space).lower()
        sb = in_ if o_dram else out
        if key is None:
            key = sb.tensor.name

        def emit(e, out=out, in_=in_):
            return e.dma_start(out=out, in_=in_)

        op = Op(eng, emit)
        op.is_dma = True
        self._track(op, [in_], [out])
        prev = self.dma_last.get(key)
        if prev is not None and all(prev is not d for d in op.deps):
            op.deps.append(prev)
        self.dma_last[key] = op
        idx = self.dkey2idx.setdefault(key, len(self.dkey2idx))
        ent = self.dpool[idx]
        ent[1] += 16
        op.sem = ent[0]
        op.val = ent[1]
        self.ops[eng].append(op)
        self.all_ops.append(op)
        return op

    def flush(self):
        nc = self.nc
        for op in self.all_ops:
            for d in op.deps:
                d.signal = True
        for e in ENGS:
            for op in reversed(self.ops[e]):
                if not op.is_dma:
                    op.signal = True
                    break
        for e in ENGS:
            for op in self.ops[e]:
                if (not op.is_dma) and op.signal:
                    self.ecnt[e] += 1
                    op.sem = self.esem[e]
                    op.val = self.ecnt[e]
        self.barcnt += 1
        barval = self.barcnt
        used_dsems = [self.dpool[i] for i in range(len(self.dkey2idx))]

        def run(e_name, eh):
            waited = {}
            for op in self.ops[e_name]:
                need = {}
                for d in op.deps:
                    k = id(d.sem)
                    if waited.get(k, 0) >= d.val:
                        continue
                    if k not in need or need[k][1] < d.val:
                        need[k] = (d.sem, d.val)
                for k, (s, v) in need.items():
                    eh.wait_ge(s, v)
                    waited[k] = v
                    self.n_waits += 1
                ins = op.emit(eh)
                if op.is_dma:
                    ins.then_inc(op.sem, 16)
                elif op.signal:
                    ins.then_inc(op.sem, 1)
                self.n_ops += 1
            if e_name == "sp":
                for e in ENGS:
                    if e != "sp" and self.ecnt[e] > 0:
                        eh.wait_ge(self.esem[e], self.ecnt[e])
                for sem, cnt in used_dsems:
                    if cnt > 0:
                        eh.wait_ge(sem, cnt)
                eh.sem_inc(self.bar, 1)
            else:
                eh.wait_ge(self.bar, barval)

        with nc.Block() as block:

            @block.sync
            def _(eh):
                run("sp", eh)

            @block.tensor
            def _(eh):
                run("pe", eh)

            @block.scalar
            def _(eh):
                run("act", eh)

            @block.vector
            def _(eh):
                run("dve", eh)

            @block.gpsimd
            def _(eh):
                run("pool", eh)

        self._reset()


class Ctx:
    pass


def _vcol(vec_sb, c):
    return vec_sb[:, c:c + 1]


def build_program(layers=(0, 1, 2, 3), debug=False, stop_after=None):
    nc = bass.Bass("TRN2", target_bir_lowering=False)

    def din(name, shape, dt=F32):
        return nc.dram_tensor(name, shape, dt, kind="ExternalInput").ap()

    scratch_kind = "ExternalOutput" if debug else "Internal"

    def dscr(name, shape, dt):
        return nc.dram_tensor(name, shape, dt, kind=scratch_kind).ap()

    xT = din("xT", [D, S])
    vecs = din("vecs", [128, NV])
    consts = din("consts", [128, NC_])
    biasT = din("biasT", [128, 6144])
    w_in_even = din("w_in_even", [2, 1024, 3072])
    w_out_even = din("w_out_even", [2, 1024, 1024])
    w_in_odd = din("w_in_odd", [2, 1024, 2560])
    w_out_odd = din("w_out_odd", [2, 1024, 1024])
    w_up = din("w_up", [4, 1024, 5632])
    w_down = din("w_down", [4, 2816, 1024])
    yT = nc.dram_tensor("yT", [D, S], F32, kind="ExternalOutput").ap()

    XA = dscr("XA", [D, S], F32)
    XB = dscr("XB", [D, S], F32)
    QT = dscr("QT", [512, S], BF16)
    KT = dscr("KT", [512, S], BF16)
    VN = dscr("VN", [S, 512], BF16)
    MIX = dscr("MIX", [D, S], BF16)
    AT = dscr("AT", [DFF, S], BF16)

    with ExitStack() as es0:
        P = Prog(nc, es0)
        G = Ctx()
        G.nc = nc
        G.P = P
        sb0 = lambda n, s, d: es0.enter_context(nc.sbuf_tensor(n, s, d))
        G.PS = es0.enter_context(nc.psum_tensor("PS", [128, 4096], F32))
        G.vec = sb0("vec_sb", [128, NV], F32)
        G.ones_f = sb0("ones_f", [128, 128], F32)
        G.ones_b = sb0("ones_b", [128, 64], BF16)
        G.ones_bb = sb0("ones_bb", [128, 128], BF16)
        G.eps = sb0("eps_t", [128, 1], F32)
        G.one = sb0("one_t", [128, 1], F32)
        G.tri = sb0("tri_b", [128, 128], BF16)
        G.su = sb0("su_b", [128, 128], BF16)
        G.ident = sb0("ident_b", [128, 128], BF16)
        G.consts = consts
        G.biasT = biasT

        P.dma("sp", out=G.vec[:, :], in_=vecs[:, :])
        P.dma("pool", out=G.tri[:, :], in_=consts[:, C_TRI:C_TRI + 128])
        P.dma("pool", out=G.su[:, :], in_=consts[:, C_SU:C_SU + 128])
        P.dma("pool", out=G.ident[:, :], in_=consts[:, C_ID:C_ID + 128])
        P.I("dve", "memset", ap=G.ones_f[:, :], constant=1.0)
        P.I("dve", "memset", ap=G.ones_b[:, :], constant=1.0)
        P.I("dve", "memset", ap=G.ones_bb[:, :], constant=1.0)
        P.I("dve", "memset", ap=G.eps[:, :], constant=EPS)
        P.I("dve", "memset", ap=G.one[:, :], constant=1.0)
        P.flush()

        n_layers = len(layers)
        for li, l in enumerate(layers):
            src = xT if li == 0 else XA
            last = li == n_layers - 1
            dst = yT if last else XA
            even = (l % 2 == 0)
            i2 = l // 2
            if even:
                w_in = w_in_even[i2]
                w_out = w_out_even[i2]
            else:
                w_in = w_in_odd[i2]
                w_out = w_out_odd[i2]
            phase1(G, l, src, w_in, QT, KT, VN, MIX)
            if stop_after == (l, 1):
                break
            if even:
                phase2_even(G, l, QT, KT, VN, MIX)
            else:
                phase2_odd(G, l, QT, KT, VN, MIX)
            if stop_after == (l, 2):
                break
            phase3(G, l, src, XB, MIX, AT, w_out, w_up[l])
            if stop_after == (l, 3):
                break
            phase4(G, l, XB, dst, AT, w_down[l])
        G.n_ops = P.n_ops
        G.n_waits = P.n_waits
    return nc


def bank(G, i):
    return G.PS[:, i * 512:(i + 1) * 512]


class Rot:
    def __init__(self, items):
        self.items = list(items)
        self.i = 0

    def next(self):
        v = self.items[self.i % len(self.items)]
        self.i += 1
        return v


def act(P, out, in_, func, bias=None, scale=None):
    kw = dict(out=out, in_=in_, func=func)
    if bias is not None:
        kw["bias"] = bias
    if scale is not None:
        kw["scale"] = scale
    return P.I("act", "activation", **kw)


def mm(P, out, lhsT, rhs, start, stop):
    return P.I("pe", "matmul", out=out, lhsT=lhsT, rhs=rhs, start=start, stop=stop)


def rms_rstd(G, sb_alloc, xs, nchunks, rstd, sqrot, lnv, ps_bank, inv_n):
    P = G.P
    for c in range(nchunks):
        sq = sqrot.next()
        act(P, sq[:, :], xs[:, c, :], AF.Square)
        mm(P, ps_bank, G.ones_bb[:, :], sq[:, :], c == 0, c == nchunks - 1)
    act(P, lnv[:, :], ps_bank, AF.Ln, bias=EPS, scale=inv_n)
    act(P, rstd[:, :], lnv[:, :], AF.Exp, scale=-0.5)


def load_w(G, wsb, w_ap, nk, groups, eng="pool"):
    wr = w_ap.rearrange("(c p) n -> p c n", p=128)
    for gi, (c0, c1) in enumerate(groups):
        G.P.dma(eng, out=wsb[:, :, c0:c1], in_=wr[:, :, c0:c1], key=wsb.name + "_%d" % (gi % 6))


def interleave_events(main, events):
    ev = sorted(enumerate(events), key=lambda t: (t[1][0], t[0]))
    k = 0
    for i, f in enumerate(main):
        f()
        while k < len(ev) and ev[k][1][0] < i + 1:
            ev[k][1][1]()
            k += 1
    while k < len(ev):
        ev[k][1][1]()
        k += 1


def phase1(G, l, src, w_in, QT, KT, VN, MIX):
    nc, P, vec = G.nc, G.P, G.vec
    even = (l % 2 == 0)
    i2 = l // 2
    NIN = 3072 if even else 2560
    srcr = src.rearrange("(c p) t -> p c t", p=128)
    with ExitStack() as es:
        sb = lambda n, s, d: es.enter_context(nc.sbuf_tensor("p1_%d_" % l + n, s, d))
        W = sb("W", [128, 8, NIN], BF16)
        if even:
            grp = [(512, 768), (1024, 1280), (0, 256), (768, 1024), (1280, 1536), (256, 512)]
            grp += [(c, c + 256) for c in range(1536, 3072, 256)]
        else:
            grp = [(512, 768), (0, 256), (768, 1024), (256, 512)]
            grp += [(c, c + 256) for c in range(1024, 2560, 256)]
        load_w(G, W, w_in, 8, grp)
        xbuf = [sb("x%d" % i, [128, 8, TT], F32) for i in range(2)]
        hbuf = [sb("h%d" % i, [128, 8, TT], BF16) for i in range(2)]
        sqrot = Rot([sb("sq%d" % i, [128, TT], BF16) for i in range(7)])
        sqfrot = Rot([sb("sqf%d" % i, [128, TT], F32) for i in range(2)])
        lnv = sb("lnv", [128, TT], F32)
        rstd = sb("rstd", [128, TT], F32)
        qkrot = Rot([sb("qk%d" % i, [128, TT], BF16) for i in range(4)])
        vrot = Rot([sb("v%d" % i, [128, 512], BF16) for i in range(2)])
        yrot = Rot([sb("y%d" % i, [128, TT], BF16) for i in range(2)])
        brot = Rot([1, 2, 3, 4, 5, 6, 7])
        if even:
            gcrot = Rot([sb("gc%d" % i, [128, TT], F32) for i in range(2)])
            Z = [sb("Z%d" % i, [128, TT + 2], F32) for i in range(4)]
            trot = Rot([sb("t%d" % i, [128, TT], F32) for i in range(2)])
            for i in range(4):
                P.I("pool", "memset", ap=Z[i][:, 0:2], constant=0.0)
        else:
            sgrot = Rot([sb("sg%d" % i, [128, TT], F32) for i in range(2)])
            GC = [sb("GC%d" % i, [128, TT + 30], BF16) for i in range(4)]
            diag = sb("diag", [128, 124, 128], BF16)
            Y = sb("Y", [128, 4, TT], F32)
            mean = sb("mean", [128, TT], F32)
            msq = sb("msq", [128, TT], F32)
            var = sb("var", [128, TT], F32)
            lnr = sb("lnr", [128, TT], F32)
            rs2 = sb("rs2", [128, TT], F32)
            ynrot = Rot([sb("yn%d" % i, [128, TT], F32) for i in range(2)])
            for i in range(4):
                P.I("pool", "memset", ap=GC[i][:, 0:30], constant=0.0)
                for j in range(31):
                    P.I("dve", "tensor_scalar", out=diag[:, i * 31 + j, :], in0=G.ident[:, :],
                        scalar1=_vcol(vec, WCC0 + (i2 * 31 + j) * 4 + i), scalar2=None, op0=ALU.mult)

        def proj(h, oc):
            b = bank(G, brot.next())
            for c in range(8):
                mm(P, b, W[:, c, oc * 128:(oc + 1) * 128], h[:, c, :], c == 0, c == 7)
            return b

        OWN = {}

        def norm_events(tt):
            xs = xbuf[tt % 2]
            h = hbuf[tt % 2]
            ev = []
            sqs = {}

            def mk_sq(c):
                def f():
                    sq = sqrot.next()
                    act(P, sq[:, :], xs[:, c, :], AF.Square)
                    sqs[c] = sq
                    OWN[sq.name] = (tt, c)
                return f

            def mk_st(c):
                def f():
                    assert OWN[sqs[c].name] == (tt, c)
                    mm(P, bank(G, 0), G.ones_bb[:, :], sqs[c][:, :], c == 0, c == 7)
                return f
            for c in range(8):
                ev.append((0.0 + 0.5 * c, mk_sq(c)))
                ev.append((0.0 + 0.5 * c + 2.5, mk_st(c)))

            def rs():
                act(P, rstd[:, :], bank(G, 0), AF.Ln, bias=EPS, scale=1.0 / D)
                act(P, rstd[:, :], rstd[:, :], AF.Exp, scale=-0.5)
            ev.append((6.2, rs))

            def mk_h(c0):
                def f():
                    for c in (c0, c0 + 1):
                        P.I("dve", "scalar_tensor_tensor", out=h[:, c, :], in0=xs[:, c, :],
                            scalar=_vcol(vec, GN0 + (l * 4 + 0) * 8 + c), in1=rstd[:, :], op0=ALU.mult, op1=ALU.mult)
                return f
            for j, c0 in enumerate(range(0, 8, 2)):
                ev.append((7.2 + 1.0 * j, mk_h(c0)))
            return ev

        def tile_main(tt):
            t0 = tt * TT
            h = hbuf[tt % 2]
            cl = []
            if even:
                qoc, koc, vcol = 12, 16, 2560

                def mk_conv(i):
                    def f():
                        pc = proj(h, 4 + i)
                        gcb = gcrot.next()
                        act(P, gcb[:, :], pc, AF.Copy)
                        px = proj(h, 8 + i)
                        if tt > 0:
                            P.I("pool", "tensor_copy", out=Z[i][:, 0:2], in_=Z[i][:, TT:TT + 2])
                        P.I("dve", "tensor_tensor", out=Z[i][:, 2:TT + 2], in0=px, in1=gcb[:, :], op=ALU.mult)
                        pb = proj(h, i)
                        t = trot.next()
                        wc = lambda j: _vcol(vec, WSC0 + (i2 * 3 + j) * 4 + i)
                        P.I("dve", "tensor_scalar", out=t[:, :], in0=Z[i][:, 0:TT], scalar1=wc(0), scalar2=None, op0=ALU.mult)
                        P.I("dve", "scalar_tensor_tensor", out=t[:, :], in0=Z[i][:, 1:TT + 1], scalar=wc(1), in1=t[:, :], op0=ALU.mult, op1=ALU.add)
                        P.I("dve", "scalar_tensor_tensor", out=t[:, :], in0=Z[i][:, 2:TT + 2], scalar=wc(2), in1=t[:, :], op0=ALU.mult, op1=ALU.add)
                        ya = yrot.next()
                        P.I("dve", "tensor_tensor", out=ya[:, :], in0=pb, in1=t[:, :], op=ALU.mult)
                        P.dma("sp", out=MIX[i * 128:(i + 1) * 128, t0:t0 + TT], in_=ya[:, :])
                    return f
                for i in range(4):
                    cl.append(mk_conv(i))
            else:
                qoc, koc, vcol = 8, 12, 2048

                def mk_cconv(i):
                    def f():
                        pg = proj(h, 4 + i)
                        sg = sgrot.next()
                        act(P, sg[:, :], pg, AF.Sigmoid)
                        pa = proj(h, i)
                        if tt > 0:
                            P.I("pool", "tensor_copy", out=GC[i][:, 0:30], in_=GC[i][:, TT:TT + 30])
                        P.I("dve", "tensor_tensor", out=GC[i][:, 30:TT + 30], in0=pa, in1=sg[:, :], op=ALU.mult)
                        pcv = bank(G, brot.next())
                        for j in range(31):
                            mm(P, pcv, diag[:, i * 31 + j, :], GC[i][:, j:j + TT], j == 0, j == 30)
                        act(P, Y[:, i, :], pcv, AF.Identity, bias=_vcol(vec, BCC0 + i2 * 4 + i), scale=1.0)
                    return f
                for i in range(4):
                    cl.append(mk_cconv(i))

                def ln_part():
                    sbk = bank(G, brot.next())
                    sqk = bank(G, brot.next())
                    for i in range(4):
                        mm(P, sbk, G.ones_f[:, :], Y[:, i, :], i == 0, i == 3)
                    for i in range(4):
                        sq = sqfrot.next()
                        act(P, sq[:, :], Y[:, i, :], AF.Square)
                        mm(P, sqk, G.ones_f[:, :], sq[:, :], i == 0, i == 3)
                    act(P, mean[:, :], sbk, AF.Copy, scale=1.0 / 512)
                    P.I("dve", "tensor_tensor", out=msq[:, :], in0=mean[:, :], in1=mean[:, :], op=ALU.mult)
                    P.I("dve", "scalar_tensor_tensor", out=var[:, :], in0=sqk, scalar=1.0 / 512, in1=msq[:, :], op0=ALU.mult, op1=ALU.subtract)
                    act(P, lnr[:, :], var[:, :], AF.Ln, bias=EPS, scale=1.0)
                    act(P, rs2[:, :], lnr[:, :], AF.Exp, scale=-0.5)
                    for i in range(4):
                        yn = ynrot.next()
                        P.I("dve", "tensor_tensor", out=yn[:, :], in0=Y[:, i, :], in1=mean[:, :], op=ALU.subtract)
                        P.I("dve", "tensor_tensor", out=yn[:, :], in0=yn[:, :], in1=rs2[:, :], op=ALU.mult)
                        yc = yrot.next()
                        act(P, yc[:, :], yn[:, :], AF.Silu, bias=_vcol(vec, LNB0 + i2 * 4 + i), scale=_vcol(vec, LNG0 + i2 * 4 + i))
                        P.dma("sp", out=MIX[i * 128:(i + 1) * 128, t0:t0 + TT], in_=yc[:, :])

            def mk_qk(i):
                def f():
                    pq = proj(h, qoc + i)
                    qs = qkrot.next()
                    act(P, qs[:, :], pq, AF.Copy, scale=0.125)
                    P.dma("sp", out=QT[i * 128:(i + 1) * 128, t0:t0 + TT], in_=qs[:, :])
                    pk = proj(h, koc + i)
                    ks = qkrot.next()
                    P.I("dve", "tensor_copy", out=ks[:, :], in_=pk)
                    P.dma("sp", out=KT[i * 128:(i + 1) * 128, t0:t0 + TT], in_=ks[:, :])
                return f

            def mk_v(sub):
                def f():
                    b = bank(G, brot.next())
                    for c in range(8):
                        mm(P, b, h[:, c, sub * 128:(sub + 1) * 128], W[:, c, vcol:vcol + 512], c == 0, c == 7)
                    vs = vrot.next()
                    act(P, vs[:, :], b, AF.Copy)
                    P.dma("sp", out=VN[t0 + sub * 128:t0 + (sub + 1) * 128, :], in_=vs[:, :])
                return f
            if not even:
                cl.append(mk_qk(0))
                cl.append(mk_qk(1))
                cl.append(ln_part)
                cl.append(mk_qk(2))
                cl.append(mk_qk(3))
            else:
                for i in range(4):
                    cl.append(mk_qk(i))
            for sub in range(4):
                cl.append(mk_v(sub))
            return cl

        P.dma("sp", out=xbuf[0][:, :, :], in_=srcr[:, :, 0:TT])
        P.dma("sp", out=xbuf[1][:, :, :], in_=srcr[:, :, TT:2 * TT])
        interleave_events([], norm_events(0))
        for tt in range(NT):
            interleave_events(tile_main(tt), norm_events(tt + 1) if tt + 1 < NT else [])
            if tt + 2 < NT:
                P.dma("sp", out=xbuf[tt % 2][:, :, :], in_=srcr[:, :, (tt + 2) * TT:(tt + 3) * TT])
        P.flush()


def phase2_even(G, l, QT, KT, VN, MIX):
    nc, P = G.nc, G.P
    with ExitStack() as es:
        sb = lambda n, s, d: es.enter_context(nc.sbuf_tensor("p2e_%d_" % l + n, s, d))
        VD = [sb("VD%d" % g, [128, 32, 512], BF16) for g in range(3)]
        bias = sb("bias", [128, 6144], F32)
        maskd = sb("maskd", [128, 256], F32)
        qsb = [sb("q%d" % i, [128, S], BF16) for i in range(1)]
        ksb = [sb("k%d" % i, [128, S], BF16) for i in range(1)]
        acc = sb("acc", [128, 2, S], F32)
        ysb = sb("ysb", [128, S], BF16)
        sbf = [sb("sbf%d" % i, [128, 1024], F32) for i in range(2)]
        Ab = [sb("A%d" % i, [128, 1024], BF16) for i in range(4)]
        P.dma("sp", out=qsb[0][:, :], in_=QT[0:128, :])
        P.dma("sp", out=ksb[0][:, :], in_=KT[0:128, :])
        P.dma("sp", out=maskd[:, :], in_=G.consts[:, C_MASKD:C_MASKD + 256])
        for g in range(3):
            P.dma("sp", out=bias[:, g * 2048:(g + 1) * 2048], in_=G.biasT[:, g * 2048:(g + 1) * 2048], key="bias%d" % g)
        for gh in range(24):
            P.I("pool", "tensor_tensor", out=bias[:, gh * 256:(gh + 1) * 256], in0=bias[:, gh * 256:(gh + 1) * 256], in1=maskd[:, :], op=ALU.add)
        for g, (win, d) in enumerate(DA_PAIRS):
            vr = VN.rearrange("(blk p r) f -> p blk r f", p=128, r=d)
            for blk in range(32 // d):
                P.dma("sp", out=VD[g][:, blk * d:(blk + 1) * d, :], in_=vr[:, blk, :, :], key="VD%d_%d" % (g, blk % 4))
        srot = Rot([0, 1])
        arot = Rot([0, 1, 2, 3])
        fr = Rot([0, 1])
        for hp in range(4):
            q = qsb[0]
            k = ksb[0]
            if hp > 0:
                P.dma("sp", out=q[:, :], in_=QT[hp * 128:(hp + 1) * 128, :])
                P.dma("sp", out=k[:, :], in_=KT[hp * 128:(hp + 1) * 128, :])
            units = []
            for g, (win, d) in enumerate(DA_PAIRS):
                if d == 1:
                    for gi in range(8):
                        units.append(dict(g=g, d=d, blocks=[((gi * 4 + bi) * 128, 0) for bi in range(4)], aview=("c", gi * 512)))
                elif d == 4:
                    for gi in range(8):
                        units.append(dict(g=g, d=d, blocks=[(gi * 128, r) for r in range(4)], aview=("s4", gi * 512)))
                else:
                    for sg in range(2):
                        for rq in range(4):
                            units.append(dict(g=g, d=d, blocks=[(sg * 128, rq * 4 + bi) for bi in range(4)], aview=("s16", sg * 2048, rq)))
            for ui, u in enumerate(units):
                u["o_bank"] = 4 + (ui % 2) * 2

            def stageA(u):
                g, d, blocks = u["g"], u["d"], u["blocks"]
                u["A"] = []
                for e in range(2):
                    hh = hp * 2 + e
                    hr = slice(e * 64, (e + 1) * 64)
                    sbase = srot.next() * 1024
                    Sps = G.PS[:, sbase:sbase + 1024]
                    for bi, (m0, r) in enumerate(blocks):
                        qs0 = m0 * d + r
                        for jt in range(2):
                            mk0 = m0 - 128 + jt * 128
                            if mk0 < 0:
                                continue
                            ks0 = mk0 * d + r
                            c0 = sbase + (bi * 2 + jt) * 128
                            mm(P, G.PS[:, c0:c0 + 128], k[hr, ks0:ks0 + 127 * d + 1:d], q[hr, qs0:qs0 + 127 * d + 1:d], True, True)
                    sf = sbf[fr.next()]
                    bsl = bias[:, (g * 8 + hh) * 256:(g * 8 + hh + 1) * 256].unsqueeze(1).broadcast_to([128, 4, 256])
                    P.I("dve", "tensor_tensor", out=sf[:, :].rearrange("p (b f) -> p b f", b=4),
                        in0=Sps.rearrange("p (b f) -> p b f", b=4), in1=bsl, op=ALU.add)
                    A = Ab[arot.next()]
                    act(P, A[:, :], sf[:, :], AF.Exp)
                    u["A"].append(A)

            def stageB(u):
                g, d, blocks = u["g"], u["d"], u["blocks"]
                o_bank = u["o_bank"]
                den_bank = o_bank + 1
                for e in range(2):
                    hh = hp * 2 + e
                    hr = slice(e * 64, (e + 1) * 64)
                    A = u["A"][e]
                    for bi, (m0, r) in enumerate(blocks):
                        first = True
                        for jt in range(2):
                            mk0 = m0 - 128 + jt * 128
                            if mk0 < 0:
                                continue
                            tile = (mk0 // 128) * d + r
                            asl = A[:, (bi * 2 + jt) * 128:(bi * 2 + jt) * 128 + 128]
                            co = o_bank * 512 + bi * 128
                            cd = den_bank * 512 + bi * 128
                            mm(P, G.PS[hr, co:co + 128], VD[g][:, tile, hh * 64:(hh + 1) * 64], asl, first, jt == 1)
                            mm(P, G.PS[hr, cd:cd + 128], G.ones_b[:, :], asl, first, jt == 1)
                            first = False

            def stageC(u):
                aview = u["aview"]
                o_bank = u["o_bank"]
                for w_, bk in ((0, o_bank), (1, o_bank + 1)):
                    src_ps = bank(G, bk)
                    if aview[0] == "c":
                        c0 = aview[1]
                        act(P, acc[:, w_, c0:c0 + 512], src_ps, AF.Copy)
                    elif aview[0] == "s4":
                        c0 = aview[1]
                        av = acc[:, w_, c0:c0 + 512].rearrange("p (i r) -> p r i", r=4)
                        P.I("dve", "tensor_tensor", out=av, in0=src_ps.rearrange("p (r i) -> p r i", r=4), in1=av, op=ALU.add)
                    else:
                        c0, rq = aview[1], aview[2]
                        av = acc[:, w_, c0:c0 + 2048].rearrange("p (i r) -> p r i", r=16)[:, rq * 4:(rq + 1) * 4, :]
                        P.I("dve", "tensor_tensor", out=av, in0=src_ps.rearrange("p (r i) -> p r i", r=4), in1=av, op=ALU.add)

            nu = len(units)
            for it in range(nu + 2):
                if it < nu:
                    stageA(units[it])
                if 0 <= it - 1 < nu:
                    stageB(units[it - 1])
                if 0 <= it - 2 < nu:
                    stageC(units[it - 2])
            for qh in range(4):
                cs = slice(qh * 1024, (qh + 1) * 1024)
                act(P, acc[:, 1, cs], acc[:, 1, cs], AF.Ln)
                act(P, acc[:, 1, cs], acc[:, 1, cs], AF.Exp, scale=-1.0)
                P.I("pool", "tensor_tensor", out=ysb[:, cs], in0=acc[:, 0, cs], in1=acc[:, 1, cs], op=ALU.mult)
            P.dma("sp", out=MIX[512 + hp * 128:512 + (hp + 1) * 128, :], in_=ysb[:, :])
        P.flush()


def phase2_odd(G, l, QT, KT, VN, MIX):
    nc, P = G.nc, G.P
    with ExitStack() as es:
        sb = lambda n, s, d: es.enter_context(nc.sbuf_tensor("p2o_%d_" % l + n, s, d))
        V = sb("V", [128, 32, 512], BF16)
        masks = sb("masks", [128, 4, 512], BF16)
        qsb = [sb("q%d" % i, [128, S], BF16) for i in range(2)]
        ksb = [sb("k%d" % i, [128, S], BF16) for i in range(2)]
        ysb = [sb("ysb%d" % i, [128, S], BF16) for i in range(2)]
        NSL = 2
        E = [[sb("E%d_%d" % (i, j), [128, 2, 512], F32) for j in range(3)] for i in range(NSL)]
        SP = [[sb("SP%d_%d" % (i, j), [128, 2, 512], BF16) for j in range(3)] for i in range(NSL)]
        Gt = [[sb("G%d_%d" % (i, j), [128, 2, 512], F32) for j in range(2)] for i in range(NSL)]
        At = [[sb("At%d_%d" % (i, j), [128, 2, 512], BF16) for j in range(2)] for i in range(NSL)]
        vr = VN.rearrange("(t p) f -> p t f", p=128)
        P.dma("sp", out=qsb[0][:, :], in_=QT[0:128, :])
        P.dma("sp", out=ksb[0][:, :], in_=KT[0:128, :])
        P.dma("pool", out=masks[:, :, :], in_=G.consts[:, C_MASKS:C_MASKS + 2048].rearrange("p (o f) -> p o f", o=4))
        for qq in range(7, -1, -1):
            P.dma("sp", out=V[:, qq * 4:(qq + 1) * 4, :], in_=vr[:, qq * 4:(qq + 1) * 4, :], key="Vo_%d" % (qq % 4))
        slot_qts = [[7, 4, 3, 0], [6, 5, 2, 1]]
        for hp in range(4):
            q = qsb[hp % 2]
            k = ksb[hp % 2]
            ys = ysb[hp % 2]
            if hp + 1 < 4:
                P.dma("sp", out=qsb[(hp + 1) % 2][:, :], in_=QT[(hp + 1) * 128:(hp + 2) * 128, :])
                P.dma("sp", out=ksb[(hp + 1) % 2][:, :], in_=KT[(hp + 1) * 128:(hp + 2) * 128, :])
            streams = []
            for s_ in range(NSL):
                st = []
                for qt in slot_qts[s_]:
                    n = 4 * qt + 4
                    for j in range(n):
                        st.append(dict(qt=qt, kb=4 * qt + 3 - j, first=(j == 0), last=(j == n - 1), idx=len(st)))
                streams.append(st)
            nit = max(len(st) for st in streams) + 2

            def stageA(s_, stp):
                qt, kb, idx = stp["qt"], stp["kb"], stp["idx"]
                o = kb - 4 * qt
                w0 = max(o, 0) * 128
                for e in range(2):
                    hr = slice(e * 64, (e + 1) * 64)
                    zc = e * 512
                    mm(P, G.PS[:, zc + w0:zc + 512], k[hr, kb * 128:(kb + 1) * 128], q[hr, qt * 512 + w0:(qt + 1) * 512], True, o < 0)
                    if o >= 0:
                        mm(P, G.PS[:, zc + w0:zc + 512], G.ident[:, :], masks[:, o, w0:512], False, True)
                Ec = E[s_][idx % 3]
                act(P, Ec[:, :, w0:512], G.PS[:, 0:1024].rearrange("p (e f) -> p e f", e=2)[:, :, w0:512], AF.Exp)
                spc = SP[s_][idx % 3]
                act(P, spc[:, :, w0:512], Ec[:, :, w0:512], AF.Ln, bias=1.0, scale=1.0)

            def stageB(s_, stp):
                qt, kb, idx = stp["qt"], stp["kb"], stp["idx"]
                o = kb - 4 * qt
                w0 = max(o, 0) * 128
                spc = SP[s_][idx % 3]
                spp = SP[s_][(idx - 1) % 3]
                for e in range(2):
                    pc = (2 + 2 * s_ + e) * 512
                    if o >= 0:
                        mm(P, G.PS[:, pc + w0:pc + w0 + 128], G.tri[:, :], spc[:, e, w0:w0 + 128], stp["first"], True)
                        if w0 + 128 < 512:
                            mm(P, G.PS[:, pc + w0 + 128:pc + 512], G.su[:, :], spp[:, e, w0 + 128:512], False, False)
                            mm(P, G.PS[:, pc + w0 + 128:pc + 512], G.tri[:, :], spc[:, e, w0 + 128:512], False, True)
                    else:
                        mm(P, G.PS[:, pc:pc + 512], G.su[:, :], spp[:, e, :], False, False)
                        mm(P, G.PS[:, pc:pc + 512], G.tri[:, :], spc[:, e, :], False, True)
                Gc = Gt[s_][idx % 2]
                c0 = (2 + 2 * s_) * 512
                act(P, Gc[:, :, w0:512], G.PS[:, c0:c0 + 1024].rearrange("p (e f) -> p e f", e=2)[:, :, w0:512], AF.Exp, scale=-1.0)
                P.I("dve", "tensor_tensor", out=At[s_][idx % 2][:, :, w0:512], in0=E[s_][idx % 3][:, :, w0:512], in1=Gc[:, :, w0:512], op=ALU.mult)

            def stageC(s_, stp):
                qt, kb, idx = stp["qt"], stp["kb"], stp["idx"]
                o = kb - 4 * qt
                w0 = max(o, 0) * 128
                for e in range(2):
                    hh = hp * 2 + e
                    hr = slice(e * 64, (e + 1) * 64)
                    oc0 = (6 + s_) * 512
                    vt = V[:, kb, hh * 64:(hh + 1) * 64]
                    at = At[s_][idx % 2]
                    if o >= 0:
                        mm(P, G.PS[hr, oc0 + w0:oc0 + w0 + 128], vt, at[:, e, w0:w0 + 128], stp["first"], False)
                        if w0 + 128 < 512:
                            mm(P, G.PS[hr, oc0 + w0 + 128:oc0 + 512], vt, at[:, e, w0 + 128:512], False, False)
                    else:
                        mm(P, G.PS[hr, oc0:oc0 + 512], vt, at[:, e, :], False, stp["last"])
                if stp["last"]:
                    P.I("dve", "tensor_copy", out=ys[:, qt * 512:(qt + 1) * 512], in_=bank(G, 6 + s_))

            for it in range(nit):
                for s_ in range(NSL):
                    if it < len(streams[s_]):
                        stageA(s_, streams[s_][it])
                for s_ in range(NSL):
                    if 0 <= it - 1 < len(streams[s_]):
                        stageB(s_, streams[s_][it - 1])
                for s_ in range(NSL):
                    if 0 <= it - 2 < len(streams[s_]):
                        stageC(s_, streams[s_][it - 2])
            P.dma("sp", out=MIX[512 + hp * 128:512 + (hp + 1) * 128, :], in_=ys[:, :])
        P.flush()


def post_norm_residual(G, l, j, M, xs, rstd, sqrot, lnv, ps_bank, tmprot):
    P, vec = G.P, G.vec
    rms_rstd(G, None, M, 8, rstd, sqrot, lnv, ps_bank, 1.0 / D)
    for c in range(8):
        tmp = tmprot.next()
        P.I("dve", "scalar_tensor_tensor", out=tmp[:, :], in0=M[:, c, :], scalar=_vcol(vec, GN0 + (l * 4 + j) * 8 + c),
            in1=rstd[:, :], op0=ALU.mult, op1=ALU.mult)
        P.I("pool", "tensor_tensor", out=xs[:, c, :], in0=xs[:, c, :], in1=tmp[:, :], op=ALU.add)


def interleave(main, side, offset=1, stride=1):
    k = 0
    for i, f in enumerate(main):
        f()
        while k < len(side) and i >= offset + k * stride:
            side[k]()
            k += 1
    while k < len(side):
        side[k]()
        k += 1


def phase3(G, l, src, XB, MIX, AT, w_out, w_up):
    nc, P, vec = G.nc, G.P, G.vec
    srcr = src.rearrange("(c p) t -> p c t", p=128)
    xbr = XB.rearrange("(c p) t -> p c t", p=128)
    mixr = MIX.rearrange("(c p) t -> p c t", p=128)
    with ExitStack() as es:
        sb = lambda n, s, d: es.enter_context(nc.sbuf_tensor("p3_%d_" % l + n, s, d))
        WO = sb("WO", [128, 8, 1024], BF16)
        WU = sb("WU", [128, 8, 2 * DFF], BF16)
        load_w(G, WO, w_out, 8, [(c, c + 256) for c in range(0, 1024, 256)])
        grp = []
        for c in range(0, DFF, 256):
            grp.append((c, c + 256))
            grp.append((DFF + c, DFF + c + 256))
        load_w(G, WU, w_up, 8, grp)
        xs = sb("x", [128, 8, TT], F32)
        mbuf = [sb("mix%d" % i, [128, 8, TT], BF16) for i in range(2)]
        M = sb("M", [128, 8, TT], F32)
        h2b = [sb("h2_%d" % i, [128, 8, TT], BF16) for i in range(2)]
        sqrot = Rot([sb("sq%d" % i, [128, TT], BF16) for i in range(6)])
        sq2rot = Rot([sb("sqb%d" % i, [128, TT], BF16) for i in range(6)])
        tmprot = Rot([sb("tmp%d" % i, [128, TT], F32) for i in range(2)])
        rstd = sb("rstd", [128, TT], F32)
        Gb = [sb("Gb%d" % i, [128, TT + 2], F32) for i in range(2)]
        H = sb("H", [128, NFC, 2], F32)
        trot = Rot([sb("t%d" % i, [128, TT], F32) for i in range(2)])
        srot = Rot([sb("s%d" % i, [128, TT], F32) for i in range(2)])
        arot = Rot([sb("a%d" % i, [128, TT], BF16) for i in range(3)])
        brot = Rot([1, 2, 3, 4, 5, 6, 7])
        P.I("pool", "memset", ap=H[:, :, :], constant=0.0)
        P.dma("sp", out=mbuf[0][:, :, :], in_=mixr[:, :, 0:TT])

        OWN = {}

        def A_events(tt):
            t0 = tt * TT
            mx = mbuf[tt % 2]
            h2 = h2b[tt % 2]
            ev = []

            def a_load():
                P.dma("sp", out=xs[:, :, :], in_=srcr[:, :, t0:t0 + TT])
                if tt + 1 < NT:
                    P.dma("sp", out=mbuf[(tt + 1) % 2][:, :, :], in_=mixr[:, :, t0 + TT:t0 + 2 * TT])
            ev.append((0.0, a_load))
            sqs = {}

            def mk_oc(oc):
                def f():
                    b = bank(G, brot.next())
                    for c in range(8):
                        mm(P, b, WO[:, c, oc * 128:(oc + 1) * 128], mx[:, c, :], c == 0, c == 7)
                    act(P, M[:, oc, :], b, AF.Copy)
                    sq = sqrot.next()
                    act(P, sq[:, :], M[:, oc, :], AF.Square)
                    sqs[oc] = sq
                    OWN[sq.name] = ("a", tt, oc)
                return f

            def mk_st1(oc):
                def f():
                    assert OWN[sqs[oc].name] == ("a", tt, oc)
                    mm(P, bank(G, 0), G.ones_bb[:, :], sqs[oc][:, :], oc == 0, oc == 7)
                return f
            for oc in range(8):
                ev.append((0.1 + 0.6 * oc, mk_oc(oc)))
                ev.append((0.1 + 0.6 * oc + 3.0, mk_st1(oc)))

            def a_rstd1():
                act(P, rstd[:, :], bank(G, 0), AF.Ln, bias=EPS, scale=1.0 / D)
                act(P, rstd[:, :], rstd[:, :], AF.Exp, scale=-0.5)
            ev.append((7.5, a_rstd1))
            sq2 = {}

            def mk_res(c):
                def f():
                    tmp = tmprot.next()
                    P.I("dve", "scalar_tensor_tensor", out=tmp[:, :], in0=M[:, c, :], scalar=_vcol(vec, GN0 + (l * 4 + 1) * 8 + c),
                        in1=rstd[:, :], op0=ALU.mult, op1=ALU.mult)
                    P.I("dve", "tensor_tensor", out=xs[:, c, :], in0=xs[:, c, :], in1=tmp[:, :], op=ALU.add)
                    sq = sq2rot.next()
                    act(P, sq[:, :], xs[:, c, :], AF.Square)
                    sq2[c] = sq
                    OWN[sq.name] = ("b", tt, c)
                return f

            def mk_st2(c):
                def f():
                    assert OWN[sq2[c].name] == ("b", tt, c)
                    mm(P, bank(G, 0), G.ones_bb[:, :], sq2[c][:, :], c == 0, c == 7)
                return f
            for c in range(8):
                ev.append((8.3 + 0.7 * c, mk_res(c)))
                ev.append((8.3 + 0.7 * c + 3.2, mk_st2(c)))

            def a_rstd2():
                P.dma("sp", out=xbr[:, :, t0:t0 + TT], in_=xs[:, :, :])
                act(P, rstd[:, :], bank(G, 0), AF.Ln, bias=EPS, scale=1.0 / D)
                act(P, rstd[:, :], rstd[:, :], AF.Exp, scale=-0.5)
            ev.append((16.6, a_rstd2))

            def mk_h(c0):
                def f():
                    for c in (c0, c0 + 1):
                        P.I("dve", "scalar_tensor_tensor", out=h2[:, c, :], in0=xs[:, c, :],
                            scalar=_vcol(vec, GN0 + (l * 4 + 2) * 8 + c), in1=rstd[:, :], op0=ALU.mult, op1=ALU.mult)
                return f
            for j, c0 in enumerate(range(0, 8, 2)):
                ev.append((17.4 + 1.0 * j, mk_h(c0)))
            return ev

        def F_closures(tt):
            t0 = tt * TT
            h2 = h2b[tt % 2]
            cl = []

            svs = {}

            def g_part(fc):
                pg = bank(G, brot.next())
                for c in range(8):
                    mm(P, pg, WU[:, c, fc * 128:(fc + 1) * 128], h2[:, c, :], c == 0, c == 7)
                gb = Gb[fc % 2]
                P.I("pool", "tensor_copy", out=gb[:, 0:2], in_=H[:, fc, :])
                act(P, gb[:, 2:TT + 2], pg, AF.Copy)
                P.I("pool", "tensor_copy", out=H[:, fc, :], in_=gb[:, TT:TT + 2])
                t = trot.next()
                wc = lambda j: _vcol(vec, WFC0 + (l * 3 + j) * NFC + fc)
                act(P, t[:, :], pg, AF.Identity, bias=_vcol(vec, BFC0 + l * NFC + fc), scale=wc(2))
                P.I("dve", "scalar_tensor_tensor", out=t[:, :], in0=gb[:, 0:TT], scalar=wc(0), in1=t[:, :], op0=ALU.mult, op1=ALU.add)
                P.I("dve", "scalar_tensor_tensor", out=t[:, :], in0=gb[:, 1:TT + 1], scalar=wc(1), in1=t[:, :], op0=ALU.mult, op1=ALU.add)
                sv = srot.next()
                act(P, sv[:, :], t[:, :], AF.Silu)
                svs[fc] = sv

            def u_part(fc):
                pu = bank(G, brot.next())
                for c in range(8):
                    mm(P, pu, WU[:, c, DFF + fc * 128:DFF + (fc + 1) * 128], h2[:, c, :], c == 0, c == 7)
                a = arot.next()
                P.I("dve", "tensor_tensor", out=a[:, :], in0=pu, in1=svs[fc][:, :], op=ALU.mult)
                P.dma("sp", out=AT[fc * 128:(fc + 1) * 128, t0:t0 + TT], in_=a[:, :])

            def mk_fc(fc):
                def f():
                    g_part(fc)
                    if fc > 0:
                        u_part(fc - 1)
                    if fc == NFC - 1:
                        u_part(fc)
                return f
            for fc in range(NFC):
                cl.append(mk_fc(fc))
            return cl

        interleave_events([], A_events(0))
        for tt in range(NT):
            side = A_events(tt + 1) if tt + 1 < NT else []
            interleave_events(F_closures(tt), side)
        P.flush()


def phase4(G, l, XB, dst, AT, w_down):
    nc, P, vec = G.nc, G.P, G.vec
    xbr = XB.rearrange("(c p) t -> p c t", p=128)
    dstr = dst.rearrange("(c p) t -> p c t", p=128)
    atr = AT.rearrange("(c p) t -> p c t", p=128)
    with ExitStack() as es:
        sb = lambda n, s, d: es.enter_context(nc.sbuf_tensor("p4_%d_" % l + n, s, d))
        WD = sb("WD", [128, NFC, 1024], BF16)
        load_w(G, WD, w_down, NFC, [(c, c + 256) for c in range(0, 1024, 256)])
        xbuf = [sb("x%d" % i, [128, 8, TT], F32) for i in range(2)]
        abuf = [sb("a%d" % i, [128, NFC, TT], BF16) for i in range(2)]
        Mb = [sb("M%d" % i, [128, 8, TT], F32) for i in range(2)]
        sqrot = Rot([sb("sq%d" % i, [128, TT], BF16) for i in range(8)])
        tmprot = Rot([sb("tmp%d" % i, [128, TT], F32) for i in range(2)])
        lnv = sb("lnv", [128, TT], F32)
        rstd = sb("rstd", [128, TT], F32)
        brot = Rot([1, 2, 3, 4, 5, 6, 7])
        P.dma("sp", out=abuf[0][:, :, :], in_=atr[:, :, 0:TT])
        P.dma("sp", out=xbuf[0][:, :, :], in_=xbr[:, :, 0:TT])
        P.dma("sp", out=xbuf[1][:, :, :], in_=xbr[:, :, TT:2 * TT])

        def D_closures(tt):
            t0 = tt * TT
            a = abuf[tt % 2]
            M = Mb[tt % 2]
            cl = []

            def d_load():
                if tt + 1 < NT:
                    P.dma("sp", out=abuf[(tt + 1) % 2][:, :, :], in_=atr[:, :, t0 + TT:t0 + 2 * TT])
            cl.append(d_load)

            def mk_oc(oc):
                def f():
                    b = bank(G, brot.next())
                    for c in range(NFC):
                        mm(P, b, WD[:, c, oc * 128:(oc + 1) * 128], a[:, c, :], c == 0, c == NFC - 1)
                    act(P, M[:, oc, :], b, AF.Copy)
                return f
            for oc in range(8):
                cl.append(mk_oc(oc))
            return cl

        OWN = {}

        def N_events(tt):
            t0 = tt * TT
            xs = xbuf[tt % 2]
            M = Mb[tt % 2]
            ev = []
            sqs = {}

            def n_sq(c0):
                def f():
                    for c in range(c0, c0 + 4):
                        sq = sqrot.next()
                        act(P, sq[:, :], M[:, c, :], AF.Square)
                        sqs[c] = sq
                        OWN[sq.name] = (tt, c)
                return f

            def n_mm(c0):
                def f():
                    for c in range(c0, c0 + 4):
                        assert OWN[sqs[c].name] == (tt, c)
                        mm(P, bank(G, 0), G.ones_bb[:, :], sqs[c][:, :], c == 0, c == 7)
                return f

            def n_rstd():
                act(P, lnv[:, :], bank(G, 0), AF.Ln, bias=EPS, scale=1.0 / D)
                act(P, rstd[:, :], lnv[:, :], AF.Exp, scale=-0.5)
            ev.append((0.0, n_sq(0)))
            ev.append((0.5, n_sq(4)))
            ev.append((2.2, n_mm(0)))
            ev.append((2.6, n_mm(4)))
            ev.append((3.0, n_rstd))

            def n_res(c0):
                def f():
                    for c in range(c0, c0 + 4):
                        tmp = tmprot.next()
                        P.I("dve", "scalar_tensor_tensor", out=tmp[:, :], in0=M[:, c, :], scalar=_vcol(vec, GN0 + (l * 4 + 3) * 8 + c),
                            in1=rstd[:, :], op0=ALU.mult, op1=ALU.mult)
                        P.I("pool", "tensor_tensor", out=xs[:, c, :], in0=xs[:, c, :], in1=tmp[:, :], op=ALU.add)
                    if c0 == 4:
                        P.dma("sp", out=dstr[:, :, t0:t0 + TT], in_=xs[:, :, :])
                return f
            ev.append((4.0, n_res(0)))
            ev.append((5.0, n_res(4)))
            return ev

        for f in D_closures(0):
            f()
        for tt in range(NT):
            main = D_closures(tt + 1) if tt + 1 < NT else []
            interleave_events(main, N_events(tt))
            if tt + 2 < NT:
                P.dma("sp", out=xbuf[tt % 2][:, :, :], in_=xbr[:, :, (tt + 2) * TT:(tt + 3) * TT])
        P.flush()


def _t5_bucket(dist):
    max_exact = 16
    d = np.maximum(dist, 1).astype(np.float32)
    large = max_exact + (np.log(d / np.float32(max_exact)) / np.float32(math.log(2048 / 16)) * np.float32(16)).astype(np.int32)
    large = np.minimum(large, 31)
    return np.where(dist < max_exact, dist, large)


def _host_tables(norm_g, rel_bias, w_sc, w_cc, b_cc, ln_cc_g, ln_cc_b, w_ffn_conv, b_ffn_conv):
    vecs = np.zeros((128, NV), np.float32)

    def put(col0, arr):
        a = np.asarray(arr, np.float32)
        lead = a.shape[:-1]
        C = a.shape[-1] // 128
        a = a.reshape(lead + (C, 128))
        a = np.moveaxis(a, -1, 0).reshape(128, -1)
        vecs[:, col0:col0 + a.shape[1]] = a

    put(GN0, norm_g)
    put(WSC0, w_sc)
    put(WCC0, w_cc)
    put(BCC0, b_cc)
    put(LNG0, ln_cc_g)
    put(LNB0, ln_cc_b)
    put(WFC0, w_ffn_conv)
    put(BFC0, b_ffn_conv)

    consts = np.zeros((128, NC_), np.float32)
    p = np.arange(128)[:, None]
    i = np.arange(128)[None, :]
    consts[:, C_MASKD:C_MASKD + 128] = np.where(i <= p, 0.0, NEG)
    consts[:, C_MASKD + 128:C_MASKD + 256] = np.where(i >= p, 0.0, NEG)
    iq = np.arange(512)[None, :]
    for o in range(4):
        consts[:, C_MASKS + o * 512:C_MASKS + (o + 1) * 512] = np.where(o * 128 + p < iq, 0.0, NEG)
    consts[:, C_TRI:C_TRI + 128] = (p >= i).astype(np.float32)
    consts[:, C_SU:C_SU + 128] = (p < i).astype(np.float32)
    consts[:, C_ID:C_ID + 128] = (p == i).astype(np.float32)

    rb = np.asarray(rel_bias, np.float32)
    biasT = np.zeros((128, 3, 8, 2, 128), np.float32)
    for g, (win, d) in enumerate(DA_PAIRS):
        for jt in range(2):
            rel = 128 + i - (jt * 128 + p)
            relc = np.clip(rel, 0, 128)
            bk = _t5_bucket(relc * d)
            biasT[:, g, :, jt, :] = np.transpose(rb[bk], (0, 2, 1))
    return vecs, consts, biasT.reshape(128, 6144)


_NC_CACHE = {}


def kernel(x, norm_g, rel_bias, w_in_even, w_out_even, w_sc, w_in_odd, w_out_odd,
           w_cc, b_cc, ln_cc_g, ln_cc_b, w_up, w_ffn_conv, b_ffn_conv, w_down):
    x = np.asarray(x, np.float32)
    B = x.shape[0]
    vecs, consts, biasT = _host_tables(norm_g, rel_bias, w_sc, w_cc, b_cc, ln_cc_g, ln_cc_b, w_ffn_conv, b_ffn_conv)
    if "nc" not in _NC_CACHE:
        _NC_CACHE["nc"] = build_program()
    nc = _NC_CACHE["nc"]
    shared = dict(
        vecs=vecs, consts=consts, biasT=biasT,
        w_in_even=np.ascontiguousarray(w_in_even, np.float32), w_out_even=np.ascontiguousarray(w_out_even, np.float32),
        w_in_odd=np.ascontiguousarray(w_in_odd, np.float32), w_out_odd=np.ascontiguousarray(w_out_odd, np.float32),
        w_up=np.ascontiguousarray(w_up, np.float32), w_down=np.ascontiguousarray(w_down, np.float32),
    )
    in_maps = []
    for b in range(B):
        m = dict(shared)
        m["xT"] = np.ascontiguousarray(x[b].T)
        in_maps.append(m)
    res = run_bass_kernel_spmd(nc, in_maps, core_ids=list(range(B)))
    out = np.stack([np.ascontiguousarray(r["yT"].T) for r in res.results], axis=0)
    return out.astype(np.float32)
```

```python
import math
from contextlib import ExitStack

import numpy as np
import concourse.bass as bass
import concourse.mybir as mybir
from concourse.bass_utils import run_bass_kernel_spmd

F32 = mybir.dt.float32
BF16 = mybir.dt.bfloat16
AF = mybir.ActivationFunctionType
ALU = mybir.AluOpType

S = 4096
D = 1024
NT = 8
TT = 512
DFF = 2816
NFC = 22
EPS = 1e-6
DA_PAIRS = ((128, 1), (512, 4), (2048, 16))
NEG = -30000.0

GN0 = 0
WSC0 = 128
WCC0 = 152
BCC0 = 400
LNG0 = 408
LNB0 = 416
WFC0 = 424
BFC0 = 688
NV = 776
C_MASKD = 0
C_MASKS = 256
C_TRI = 2304
C_SU = 2432
C_ID = 2560
NC_ = 2688


def _is_ap(v):
    return hasattr(v, "tensor") and hasattr(v, "offset") and hasattr(v, "ap")


def _box(ap):
    t = ap.tensor
    shape = list(t.shape)
    nd = len(shape)
    strides = [1] * nd
    for k in range(nd - 2, -1, -1):
        strides[k] = strides[k + 1] * shape[k + 1]
    off = ap.offset
    lo = []
    for k in range(nd):
        lo.append(off // strides[k])
        off = off % strides[k]
    ext = [0] * nd
    for step, cnt in ap.ap:
        if cnt <= 1 or step == 0:
            continue
        k = 0
        while k < nd - 1 and step < strides[k]:
            k += 1
        ext[k] += (step // strides[k]) * (cnt - 1)
        rem = step % strides[k]
        if rem != 0:
            for kk in range(k + 1, nd):
                ext[kk] = shape[kk]
    return (t.name, tuple((lo[k], lo[k] + ext[k] + 1) for k in range(nd)))


def _ovl(a, b):
    for (al, ah), (bl, bh) in zip(a[1], b[1]):
        if not (al < bh and bl < ah):
            return False
    return True


def _contains(a, b):
    for (al, ah), (bl, bh) in zip(a[1], b[1]):
        if not (al <= bl and bh <= ah):
            return False
    return True


class Op:
    __slots__ = ("eng", "emit", "deps", "is_dma", "sem", "val", "signal")

    def __init__(self, eng, emit):
        self.eng = eng
        self.emit = emit
        self.deps = []
        self.is_dma = False
        self.sem = None
        self.val = None
        self.signal = False


ENGS = ("pe", "act", "dve", "pool", "sp")


class Prog:
    def __init__(self, nc, es, n_dsem=60):
        self.nc = nc
        self.esem = {e: es.enter_context(nc.semaphore("s_" + e)) for e in ENGS}
        self.ecnt = {e: 0 for e in ENGS}
        self.bar = es.enter_context(nc.semaphore("s_bar"))
        self.barcnt = 0
        self.dpool = [[es.enter_context(nc.semaphore("d%d" % i)), 0] for i in range(n_dsem)]
        self.n_ops = 0
        self.n_waits = 0
        self._reset()

    def _reset(self):
        self.ops = {e: [] for e in ENGS}
        self.hist = {}
        self.dma_last = {}
        self.dkey2idx = {}
        self.all_ops = []

    def _track(self, op, reads, writes):
        deps = []
        rb_ = [_box(a) for a in reads]
        wb_ = [_box(a) for a in writes]
        for b in rb_:
            h = self.hist.setdefault(b[0], {"w": [], "r": []})
            for wb, wop in h["w"]:
                if _ovl(wb, b):
                    deps.append(wop)
        for b in wb_:
            h = self.hist.setdefault(b[0], {"w": [], "r": []})
            for wb, wop in h["w"]:
                if _ovl(wb, b):
                    deps.append(wop)
            for rb, rop in h["r"]:
                if _ovl(rb, b):
                    deps.append(rop)
        for b in rb_:
            h = self.hist[b[0]]
            if len(h["r"]) > 64:
                h["r"] = [(rb, rop) for rb, rop in h["r"] if not (rop.eng == op.eng and not rop.is_dma and not op.is_dma and _contains(b, rb))]
            h["r"].append((b, op))
        for b in wb_:
            h = self.hist[b[0]]
            h["w"] = [(wb, wop) for wb, wop in h["w"] if not _contains(b, wb)]
            h["r"] = [(rb, rop) for rb, rop in h["r"] if not _contains(b, rb) or rop is op]
            h["w"].append((b, op))
        seen = set()
        for d in deps:
            if d is op or id(d) in seen:
                continue
            if d.eng == "pe" and op.eng == "pe" and not d.is_dma and not op.is_dma:
                continue
            seen.add(id(d))
            op.deps.append(d)

    def I(self, eng, method, extra_reads=(), extra_writes=(), **kw):
        reads = list(extra_reads)
        writes = list(extra_writes)
        for k, v in kw.items():
            if _is_ap(v):
                if k in ("out", "accum_out", "ap"):
                    writes.append(v)
                else:
                    reads.append(v)

        def emit(e, method=method, kw=kw):
            return getattr(e, method)(**kw)

        op = Op(eng, emit)
        self._track(op, reads, writes)
        self.ops[eng].append(op)
        self.all_ops.append(op)
        return op

    def dma(self, eng, out, in_, key=None):
        o_dram = "dram" in str(out.tensor.space).lower()
        sb = in_ if o_dram else out
        if key is None:
            key = sb.tensor.name

        def emit(e, out=out, in_=in_):
            return e.dma_start(out=out, in_=in_)

        op = Op(eng, emit)
        op.is_dma = True
        self._track(op, [in_], [out])
        prev = self.dma_last.get(key)
        if prev is not None and all(prev is not d for d in op.deps):
            op.deps.append(prev)
        self.dma_last[key] = op
        idx = self.dkey2idx.setdefault(key, len(self.dkey2idx))
        ent = self.dpool[idx]
        ent[1] += 16
        op.sem = ent[0]
        op.val = ent[1]
        self.ops[eng].append(op)
        self.all_ops.append(op)
        return op

    def flush(self):
        nc = self.nc
        for op in self.all_ops:
            for d in op.deps:
                d.signal = True
        for e in ENGS:
            for op in reversed(self.ops[e]):
                if not op.is_dma:
                    op.signal = True
                    break
        for e in ENGS:
            for op in self.ops[e]:
                if (not op.is_dma) and op.signal:
                    self.ecnt[e] += 1
                    op.sem = self.esem[e]
                    op.val = self.ecnt[e]
        self.barcnt += 1
        barval = self.barcnt
        used_dsems = [self.dpool[i] for i in range(len(self.dkey2idx))]

        def run(e_name, eh):
            waited = {}
            for op in self.ops[e_name]:
                need = {}
                for d in op.deps:
                    k = id(d.sem)
                    if waited.get(k, 0) >= d.val:
                        continue
                    if k not in need or need[k][1] < d.val:
                        need[k] = (d.sem, d.val)
                for k, (s, v) in need.items():
                    eh.wait_ge(s, v)
                    waited[k] = v
                    self.n_waits += 1
                ins = op.emit(eh)
                if op.is_dma:
                    ins.then_inc(op.sem, 16)
                elif op.signal:
                    ins.then_inc(op.sem, 1)
                self.n_ops += 1
            if e_name == "sp":
                for e in ENGS:
                    if e != "sp" and self.ecnt[e] > 0:
                        eh.wait_ge(self.esem[e], self.ecnt[e])
                for sem, cnt in used_dsems:
                    if cnt > 0:
                        eh.wait_ge(sem, cnt)
                eh.sem_inc(self.bar, 1)
            else:
                eh.wait_ge(self.bar, barval)

        with nc.Block() as block:

            @block.sync
            def _(eh):
                run("sp", eh)

            @block.tensor
            def _(eh):
                run("pe", eh)

            @block.scalar
            def _(eh):
                run("act", eh)

            @block.vector
            def _(eh):
                run("dve", eh)

            @block.gpsimd
            def _(eh):
                run("pool", eh)

        self._reset()


class Ctx:
    pass


def _vcol(vec_sb, c):
    return vec_sb[:, c:c + 1]


def build_program(layers=(0, 1, 2, 3), debug=False, stop_after=None):
    nc = bass.Bass("TRN2", target_bir_lowering=False)

    def din(name, shape, dt=F32):
        return nc.dram_tensor(name, shape, dt, kind="ExternalInput").ap()

    scratch_kind = "ExternalOutput" if debug else "Internal"

    def dscr(name, shape, dt):
        return nc.dram_tensor(name, shape, dt, kind=scratch_kind).ap()

    xT = din("xT", [D, S])
    vecs = din("vecs", [128, NV])
    consts = din("consts", [128, NC_])
    biasT = din("biasT", [128, 6144])
    w_in_even = din("w_in_even", [2, 1024, 3072])
    w_out_even = din("w_out_even", [2, 1024, 1024])
    w_in_odd = din("w_in_odd", [2, 1024, 2560])
    w_out_odd = din("w_out_odd", [2, 1024, 1024])
    w_up = din("w_up", [4, 1024, 5632])
    w_down = din("w_down", [4, 2816, 1024])
    yT = nc.dram_tensor("yT", [D, S], F32, kind="ExternalOutput").ap()

    XA = dscr("XA", [D, S], F32)
    XB = dscr("XB", [D, S], F32)
    QT = dscr("QT", [512, S], BF16)
    KT = dscr("KT", [512, S], BF16)
    VN = dscr("VN", [S, 512], BF16)
    MIX = dscr("MIX", [D, S], BF16)
    AT = dscr("AT", [DFF, S], BF16)

    with ExitStack() as es0:
        P = Prog(nc, es0)
        G = Ctx()
        G.nc = nc
        G.P = P
        sb0 = lambda n, s, d: es0.enter_context(nc.sbuf_tensor(n, s, d))
        G.PS = es0.enter_context(nc.psum_tensor("PS", [128, 4096], F32))
        G.vec = sb0("vec_sb", [128, NV], F32)
        G.ones_f = sb0("ones_f", [128, 128], F32)
        G.ones_b = sb0("ones_b", [128, 64], BF16)
        G.ones_bb = sb0("ones_bb", [128, 128], BF16)
        G.eps = sb0("eps_t", [128, 1], F32)
        G.one = sb0("one_t", [128, 1], F32)
        G.tri = sb0("tri_b", [128, 128], BF16)
        G.su = sb0("su_b", [128, 128], BF16)
        G.ident = sb0("ident_b", [128, 128], BF16)
        G.consts = consts
        G.biasT = biasT

        P.dma("sp", out=G.vec[:, :], in_=vecs[:, :])
        P.dma("pool", out=G.tri[:, :], in_=consts[:, C_TRI:C_TRI + 128])
        P.dma("pool", out=G.su[:, :], in_=consts[:, C_SU:C_SU + 128])
        P.dma("pool", out=G.ident[:, :], in_=consts[:, C_ID:C_ID + 128])
        P.I("dve", "memset", ap=G.ones_f[:, :], constant=1.0)
        P.I("dve", "memset", ap=G.ones_b[:, :], constant=1.0)
        P.I("dve", "memset", ap=G.ones_bb[:, :], constant=1.0)
        P.I("dve", "memset", ap=G.eps[:, :], constant=EPS)
        P.I("dve", "memset", ap=G.one[:, :], constant=1.0)
        P.flush()

        n_layers = len(layers)
        for li, l in enumerate(layers):
            src = xT if li == 0 else XA
            last = li == n_layers - 1
            dst = yT if last else XA
            even = (l % 2 == 0)
            i2 = l // 2
            if even:
                w_in = w_in_even[i2]
                w_out = w_out_even[i2]
            else:
                w_in = w_in_odd[i2]
                w_out = w_out_odd[i2]
            phase1(G, l, src, w_in, QT, KT, VN, MIX)
            if stop_after == (l, 1):
                break
            if even:
                phase2_even(G, l, QT, KT, VN, MIX)
            else:
                phase2_odd(G, l, QT, KT, VN, MIX)
            if stop_after == (l, 2):
                break
            phase3(G, l, src, XB, MIX, AT, w_out, w_up[l])
            if stop_after == (l, 3):
                break
            phase4(G, l, XB, dst, AT, w_down[l])
        G.n_ops = P.n_ops
        G.n_waits = P.n_waits
    return nc


def bank(G, i):
    return G.PS[:, i * 512:(i + 1) * 512]


class Rot:
    def __init__(self, items):
        self.items = list(items)
        self.i = 0

    def next(self):
        v = self.items[self.i % len(self.items)]
        self.i += 1
        return v


def act(P, out, in_, func, bias=None, scale=None):
    kw = dict(out=out, in_=in_, func=func)
    if bias is not None:
        kw["bias"] = bias
    if scale is not None:
        kw["scale"] = scale
    return P.I("act", "activation", **kw)


def mm(P, out, lhsT, rhs, start, stop):
    return P.I("pe", "matmul", out=out, lhsT=lhsT, rhs=rhs, start=start, stop=stop)


def rms_rstd(G, sb_alloc, xs, nchunks, rstd, sqrot, lnv, ps_bank, inv_n):
    P = G.P
    for c in range(nchunks):
        sq = sqrot.next()
        act(P, sq[:, :], xs[:, c, :], AF.Square)
        mm(P, ps_bank, G.ones_bb[:, :], sq[:, :], c == 0, c == nchunks - 1)
    act(P, lnv[:, :], ps_bank, AF.Ln, bias=EPS, scale=inv_n)
    act(P, rstd[:, :], lnv[:, :], AF.Exp, scale=-0.5)


def load_w(G, wsb, w_ap, nk, groups, eng="pool"):
    wr = w_ap.rearrange("(c p) n -> p c n", p=128)
    for gi, (c0, c1) in enumerate(groups):
        G.P.dma(eng, out=wsb[:, :, c0:c1], in_=wr[:, :, c0:c1], key=wsb.name + "_%d" % (gi % 6))


def interleave_events(main, events):
    ev = sorted(enumerate(events), key=lambda t: (t[1][0], t[0]))
    k = 0
    for i, f in enumerate(main):
        f()
        while k < len(ev) and ev[k][1][0] < i + 1:
            ev[k][1][1]()
            k += 1
    while k < len(ev):
        ev[k][1][1]()
        k += 1


def phase1(G, l, src, w_in, QT, KT, VN, MIX):
    nc, P, vec = G.nc, G.P, G.vec
    even = (l % 2 == 0)
    i2 = l // 2
    NIN = 3072 if even else 2560
    srcr = src.rearrange("(c p) t -> p c t", p=128)
    with ExitStack() as es:
        sb = lambda n, s, d: es.enter_context(nc.sbuf_tensor("p1_%d_" % l + n, s, d))
        W = sb("W", [128, 8, NIN], BF16)
        if even:
            grp = [(512, 768), (1024, 1280), (0, 256), (768, 1024), (1280, 1536), (256, 512)]
            grp += [(c, c + 256) for c in range(1536, 3072, 256)]
        else:
            grp = [(512, 768), (0, 256), (768, 1024), (256, 512)]
            grp += [(c, c + 256) for c in range(1024, 2560, 256)]
        load_w(G, W, w_in, 8, grp)
        xbuf = [sb("x%d" % i, [128, 8, TT], F32) for i in range(2)]
        hbuf = [sb("h%d" % i, [128, 8, TT], BF16) for i in range(2)]
        sqrot = Rot([sb("sq%d" % i, [128, TT], BF16) for i in range(7)])
        sqfrot = Rot([sb("sqf%d" % i, [128, TT], F32) for i in range(2)])
        lnv = sb("lnv", [128, TT], F32)
        rstd = sb("rstd", [128, TT], F32)
        qkrot = Rot([sb("qk%d" % i, [128, TT], BF16) for i in range(4)])
        vrot = Rot([sb("v%d" % i, [128, 512], BF16) for i in range(2)])
        yrot = Rot([sb("y%d" % i, [128, TT], BF16) for i in range(2)])
        brot = Rot([1, 2, 3, 4, 5, 6, 7])
        if even:
            gcrot = Rot([sb("gc%d" % i, [128, TT], F32) for i in range(2)])
            Z = [sb("Z%d" % i, [128, TT + 2], F32) for i in range(4)]
            trot = Rot([sb("t%d" % i, [128, TT], F32) for i in range(2)])
            for i in range(4):
                P.I("pool", "memset", ap=Z[i][:, 0:2], constant=0.0)
        else:
            sgrot = Rot([sb("sg%d" % i, [128, TT], F32) for i in range(2)])
            GC = [sb("GC%d" % i, [128, TT + 30], BF16) for i in range(4)]
            diag = sb("diag", [128, 124, 128], BF16)
            Y = sb("Y", [128, 4, TT], F32)
            mean = sb("mean", [128, TT], F32)
            msq = sb("msq", [128, TT], F32)
            var = sb("var", [128, TT], F32)
            lnr = sb("lnr", [128, TT], F32)
            rs2 = sb("rs2", [128, TT], F32)
            ynrot = Rot([sb("yn%d" % i, [128, TT], F32) for i in range(2)])
            for i in range(4):
                P.I("pool", "memset", ap=GC[i][:, 0:30], constant=0.0)
                for j in range(31):
                    P.I("dve", "tensor_scalar", out=diag[:, i * 31 + j, :], in0=G.ident[:, :],
                        scalar1=_vcol(vec, WCC0 + (i2 * 31 + j) * 4 + i), scalar2=None, op0=ALU.mult)

        def proj(h, oc):
            b = bank(G, brot.next())
            for c in range(8):
                mm(P, b, W[:, c, oc * 128:(oc + 1) * 128], h[:, c, :], c == 0, c == 7)
            return b

        OWN = {}

        def norm_events(tt):
            xs = xbuf[tt % 2]
            h = hbuf[tt % 2]
            ev = []
            sqs = {}

            def mk_sq(c):
                def f():
                    sq = sqrot.next()
                    act(P, sq[:, :], xs[:, c, :], AF.Square)
                    sqs[c] = sq
                    OWN[sq.name] = (tt, c)
                return f

            def mk_st(c):
                def f():
                    assert OWN[sqs[c].name] == (tt, c)
                    mm(P, bank(G, 0), G.ones_bb[:, :], sqs[c][:, :], c == 0, c == 7)
                return f
            for c in range(8):
                ev.append((0.0 + 0.5 * c, mk_sq(c)))
                ev.append((0.0 + 0.5 * c + 2.5, mk_st(c)))

            def rs():
                act(P, rstd[:, :], bank(G, 0), AF.Ln, bias=EPS, scale=1.0 / D)
                act(P, rstd[:, :], rstd[:, :], AF.Exp, scale=-0.5)
            ev.append((6.2, rs))

            def mk_h(c0):
                def f():
                    for c in (c0, c0 + 1):
                        P.I("dve", "scalar_tensor_tensor", out=h[:, c, :], in0=xs[:, c, :],
                            scalar=_vcol(vec, GN0 + (l * 4 + 0) * 8 + c), in1=rstd[:, :], op0=ALU.mult, op1=ALU.mult)
                return f
            for j, c0 in enumerate(range(0, 8, 2)):
                ev.append((7.2 + 1.0 * j, mk_h(c0)))
            return ev

        def tile_main(tt):
            t0 = tt * TT
            h = hbuf[tt % 2]
            cl = []
            if even:
                qoc, koc, vcol = 12, 16, 2560

                def mk_conv(i):
                    def f():
                        pc = proj(h, 4 + i)
                        gcb = gcrot.next()
                        act(P, gcb[:, :], pc, AF.Copy)
                        px = proj(h, 8 + i)
                        if tt > 0:
                            P.I("pool", "tensor_copy", out=Z[i][:, 0:2], in_=Z[i][:, TT:TT + 2])
                        P.I("dve", "tensor_tensor", out=Z[i][:, 2:TT + 2], in0=px, in1=gcb[:, :], op=ALU.mult)
                        pb = proj(h, i)
                        t = trot.next()
                        wc = lambda j: _vcol(vec, WSC0 + (i2 * 3 + j) * 4 + i)
                        P.I("dve", "tensor_scalar", out=t[:, :], in0=Z[i][:, 0:TT], scalar1=wc(0), scalar2=None, op0=ALU.mult)
                        P.I("dve", "scalar_tensor_tensor", out=t[:, :], in0=Z[i][:, 1:TT + 1], scalar=wc(1), in1=t[:, :], op0=ALU.mult, op1=ALU.add)
                        P.I("dve", "scalar_tensor_tensor", out=t[:, :], in0=Z[i][:, 2:TT + 2], scalar=wc(2), in1=t[:, :], op0=ALU.mult, op1=ALU.add)
                        ya = yrot.next()
                        P.I("dve", "tensor_tensor", out=ya[:, :], in0=pb, in1=t[:, :], op=ALU.mult)
                        P.dma("sp", out=MIX[i * 128:(i + 1) * 128, t0:t0 + TT], in_=ya[:, :])
                    return f
                for i in range(4):
                    cl.append(mk_conv(i))
            else:
                qoc, koc, vcol = 8, 12, 2048

                def mk_cconv(i):
                    def f():
                        pg = proj(h, 4 + i)
                        sg = sgrot.next()
                        act(P, sg[:, :], pg, AF.Sigmoid)
                        pa = proj(h, i)
                        if tt > 0:
                            P.I("pool", "tensor_copy", out=GC[i][:, 0:30], in_=GC[i][:, TT:TT + 30])
                        P.I("dve", "tensor_tensor", out=GC[i][:, 30:TT + 30], in0=pa, in1=sg[:, :], op=ALU.mult)
                        pcv = bank(G, brot.next())
                        for j in range(31):
                            mm(P, pcv, diag[:, i * 31 + j, :], GC[i][:, j:j + TT], j == 0, j == 30)
                        act(P, Y[:, i, :], pcv, AF.Identity, bias=_vcol(vec, BCC0 + i2 * 4 + i), scale=1.0)
                    return f
                for i in range(4):
                    cl.append(mk_cconv(i))

                def ln_part():
                    sbk = bank(G, brot.next())
                    sqk = bank(G, brot.next())
                    for i in range(4):
                        mm(P, sbk, G.ones_f[:, :], Y[:, i, :], i == 0, i == 3)
                    for i in range(4):
                        sq = sqfrot.next()
                        act(P, sq[:, :], Y[:, i, :], AF.Square)
                        mm(P, sqk, G.ones_f[:, :], sq[:, :], i == 0, i == 3)
                    act(P, mean[:, :], sbk, AF.Copy, scale=1.0 / 512)
                    P.I("dve", "tensor_tensor", out=msq[:, :], in0=mean[:, :], in1=mean[:, :], op=ALU.mult)
                    P.I("dve", "scalar_tensor_tensor", out=var[:, :], in0=sqk, scalar=1.0 / 512, in1=msq[:, :], op0=ALU.mult, op1=ALU.subtract)
                    act(P, lnr[:, :], var[:, :], AF.Ln, bias=EPS, scale=1.0)
                    act(P, rs2[:, :], lnr[:, :], AF.Exp, scale=-0.5)
                    for i in range(4):
                        yn = ynrot.next()
                        P.I("dve", "tensor_tensor", out=yn[:, :], in0=Y[:, i, :], in1=mean[:, :], op=ALU.subtract)
                        P.I("dve", "tensor_tensor", out=yn[:, :], in0=yn[:, :], in1=rs2[:, :], op=ALU.mult)
                        yc = yrot.next()
                        act(P, yc[:, :], yn[:, :], AF.Silu, bias=_vcol(vec, LNB0 + i2 * 4 + i), scale=_vcol(vec, LNG0 + i2 * 4 + i))
                        P.dma("sp", out=MIX[i * 128:(i + 1) * 128, t0:t0 + TT], in_=yc[:, :])

            def mk_qk(i):
                def f():
                    pq = proj(h, qoc + i)
                    qs = qkrot.next()
                    act(P, qs[:, :], pq, AF.Copy, scale=0.125)
                    P.dma("sp", out=QT[i * 128:(i + 1) * 128, t0:t0 + TT], in_=qs[:, :])
                    pk = proj(h, koc + i)
                    ks = qkrot.next()
                    P.I("dve", "tensor_copy", out=ks[:, :], in_=pk)
                    P.dma("sp", out=KT[i * 128:(i + 1) * 128, t0:t0 + TT], in_=ks[:, :])
                return f

            def mk_v(sub):
                def f():
                    b = bank(G, brot.next())
                    for c in range(8):
                        mm(P, b, h[:, c, sub * 128:(sub + 1) * 128], W[:, c, vcol:vcol + 512], c == 0, c == 7)
                    vs = vrot.next()
                    act(P, vs[:, :], b, AF.Copy)
                    P.dma("sp", out=VN[t0 + sub * 128:t0 + (sub + 1) * 128, :], in_=vs[:, :])
                return f
            if not even:
                cl.append(mk_qk(0))
                cl.append(mk_qk(1))
                cl.append(ln_part)
                cl.append(mk_qk(2))
                cl.append(mk_qk(3))
            else:
                for i in range(4):
                    cl.append(mk_qk(i))
            for sub in range(4):
                cl.append(mk_v(sub))
            return cl

        P.dma("sp", out=xbuf[0][:, :, :], in_=srcr[:, :, 0:TT])
        P.dma("sp", out=xbuf[1][:, :, :], in_=srcr[:, :, TT:2 * TT])
        interleave_events([], norm_events(0))
        for tt in range(NT):
            interleave_events(tile_main(tt), norm_events(tt + 1) if tt + 1 < NT else [])
            if tt + 2 < NT:
                P.dma("sp", out=xbuf[tt % 2][:, :, :], in_=srcr[:, :, (tt + 2) * TT:(tt + 3) * TT])
        P.flush()


def phase2_even(G, l, QT, KT, VN, MIX):
    nc, P = G.nc, G.P
    with ExitStack() as es:
        sb = lambda n, s, d: es.enter_context(nc.sbuf_tensor("p2e_%d_" % l + n, s, d))
        VD = [sb("VD%d" % g, [128, 32, 512], BF16) for g in range(3)]
        bias = sb("bias", [128, 6144], F32)
        maskd = sb("maskd", [128, 256], F32)
        qsb = [sb("q%d" % i, [128, S], BF16) for i in range(1)]
        ksb = [sb("k%d" % i, [128, S], BF16) for i in range(1)]
        acc = sb("acc", [128, 2, S], F32)
        ysb = sb("ysb", [128, S], BF16)
        sbf = [sb("sbf%d" % i, [128, 1024], F32) for i in range(2)]
        Ab = [sb("A%d" % i, [128, 1024], BF16) for i in range(4)]
        P.dma("sp", out=qsb[0][:, :], in_=QT[0:128, :])
        P.dma("sp", out=ksb[0][:, :], in_=KT[0:128, :])
        P.dma("sp", out=maskd[:, :], in_=G.consts[:, C_MASKD:C_MASKD + 256])
        for g in range(3):
            P.dma("sp", out=bias[:, g * 2048:(g + 1) * 2048], in_=G.biasT[:, g * 2048:(g + 1) * 2048], key="bias%d" % g)
        for gh in range(24):
            P.I("pool", "tensor_tensor", out=bias[:, gh * 256:(gh + 1) * 256], in0=bias[:, gh * 256:(gh + 1) * 256], in1=maskd[:, :], op=ALU.add)
        for g, (win, d) in enumerate(DA_PAIRS):
            vr = VN.rearrange("(blk p r) f -> p blk r f", p=128, r=d)
            for blk in range(32 // d):
                P.dma("sp", out=VD[g][:, blk * d:(blk + 1) * d, :], in_=vr[:, blk, :, :], key="VD%d_%d" % (g, blk % 4))
        srot = Rot([0, 1])
        arot = Rot([0, 1, 2, 3])
        fr = Rot([0, 1])
        for hp in range(4):
            q = qsb[0]
            k = ksb[0]
            if hp > 0:
                P.dma("sp", out=q[:, :], in_=QT[hp * 128:(hp + 1) * 128, :])
                P.dma("sp", out=k[:, :], in_=KT[hp * 128:(hp + 1) * 128, :])
            units = []
            for g, (win, d) in enumerate(DA_PAIRS):
                if d == 1:
                    for gi in range(8):
                        units.append(dict(g=g, d=d, blocks=[((gi * 4 + bi) * 128, 0) for bi in range(4)], aview=("c", gi * 512)))
                elif d == 4:
                    for gi in range(8):
                        units.append(dict(g=g, d=d, blocks=[(gi * 128, r) for r in range(4)], aview=("s4", gi * 512)))
                else:
                    for sg in range(2):
                        for rq in range(4):
                            units.append(dict(g=g, d=d, blocks=[(sg * 128, rq * 4 + bi) for bi in range(4)], aview=("s16", sg * 2048, rq)))
            for ui, u in enumerate(units):
                u["o_bank"] = 4 + (ui % 2) * 2

            def stageA(u):
                g, d, blocks = u["g"], u["d"], u["blocks"]
                u["A"] = []
                for e in range(2):
                    hh = hp * 2 + e
                    hr = slice(e * 64, (e + 1) * 64)
                    sbase = srot.next() * 1024
                    Sps = G.PS[:, sbase:sbase + 1024]
                    for bi, (m0, r) in enumerate(blocks):
                        qs0 = m0 * d + r
                        for jt in range(2):
                            mk0 = m0 - 128 + jt * 128
                            if mk0 < 0:
                                continue
                            ks0 = mk0 * d + r
                            c0 = sbase + (bi * 2 + jt) * 128
                            mm(P, G.PS[:, c0:c0 + 128], k[hr, ks0:ks0 + 127 * d + 1:d], q[hr, qs0:qs0 + 127 * d + 1:d], True, True)
                    sf = sbf[fr.next()]
                    bsl = bias[:, (g * 8 + hh) * 256:(g * 8 + hh + 1) * 256].unsqueeze(1).broadcast_to([128, 4, 256])
                    P.I("dve", "tensor_tensor", out=sf[:, :].rearrange("p (b f) -> p b f", b=4),
                        in0=Sps.rearrange("p (b f) -> p b f", b=4), in1=bsl, op=ALU.add)
                    A = Ab[arot.next()]
                    act(P, A[:, :], sf[:, :], AF.Exp)
                    u["A"].append(A)

            def stageB(u):
                g, d, blocks = u["g"], u["d"], u["blocks"]
                o_bank = u["o_bank"]
                den_bank = o_bank + 1
                for e in range(2):
                    hh = hp * 2 + e
                    hr = slice(e * 64, (e + 1) * 64)
                    A = u["A"][e]
                    for bi, (m0, r) in enumerate(blocks):
                        first = True
                        for jt in range(2):
                            mk0 = m0 - 128 + jt * 128
                            if mk0 < 0:
                                continue
                            tile = (mk0 // 128) * d + r
                            asl = A[:, (bi * 2 + jt) * 128:(bi * 2 + jt) * 128 + 128]
                            co = o_bank * 512 + bi * 128
                            cd = den_bank * 512 + bi * 128
                            mm(P, G.PS[hr, co:co + 128], VD[g][:, tile, hh * 64:(hh + 1) * 64], asl, first, jt == 1)
                            mm(P, G.PS[hr, cd:cd + 128], G.ones_b[:, :], asl, first, jt == 1)
                            first = False

            def stageC(u):
                aview = u["aview"]
                o_bank = u["o_bank"]
                pc = o_bank * 512
                if aview[0] == "c":
                    c0 = aview[1]
                    for w_, bk in ((0, o_bank), (1, o_bank + 1)):
                        act(P, acc[:, w_, c0:c0 + 512], bank(G, bk), AF.Copy)
                else:
                    src_ps = G.PS[:, pc:pc + 1024].rearrange("p (w r i) -> p w r i", w=2, r=4)
                    if aview[0] == "s4":
                        c0 = aview[1]
                        av = acc[:, :, c0:c0 + 512].rearrange("p w (i r) -> p w r i", r=4)
                    else:
                        c0, rq = aview[1], aview[2]
                        av = acc[:, :, c0:c0 + 2048].rearrange("p w (i r) -> p w r i", r=16)[:, :, rq * 4:(rq + 1) * 4, :]
                    P.I("dve", "tensor_tensor", out=av, in0=src_ps, in1=av, op=ALU.add)

            nu = len(units)
            for it in range(nu + 2):
                if it < nu:
                    stageA(units[it])
                if 0 <= it - 1 < nu:
                    stageB(units[it - 1])
                if 0 <= it - 2 < nu:
                    stageC(units[it - 2])
            for qh in range(4):
                cs = slice(qh * 1024, (qh + 1) * 1024)
                act(P, acc[:, 1, cs], acc[:, 1, cs], AF.Ln)
                act(P, acc[:, 1, cs], acc[:, 1, cs], AF.Exp, scale=-1.0)
                P.I("pool", "tensor_tensor", out=ysb[:, cs], in0=acc[:, 0, cs], in1=acc[:, 1, cs], op=ALU.mult)
            P.dma("sp", out=MIX[512 + hp * 128:512 + (hp + 1) * 128, :], in_=ysb[:, :])
        P.flush()


def phase2_odd(G, l, QT, KT, VN, MIX):
    nc, P = G.nc, G.P
    with ExitStack() as es:
        sb = lambda n, s, d: es.enter_context(nc.sbuf_tensor("p2o_%d_" % l + n, s, d))
        V = sb("V", [128, 32, 512], BF16)
        masks = sb("masks", [128, 4, 512], BF16)
        qsb = [sb("q%d" % i, [128, S], BF16) for i in range(2)]
        ksb = [sb("k%d" % i, [128, S], BF16) for i in range(2)]
        ysb = [sb("ysb%d" % i, [128, S], BF16) for i in range(2)]
        NSL = 2
        E = [[sb("E%d_%d" % (i, j), [128, 2, 512], F32) for j in range(3)] for i in range(NSL)]
        SP = [[sb("SP%d_%d" % (i, j), [128, 2, 512], BF16) for j in range(3)] for i in range(NSL)]
        Gt = [[sb("G%d_%d" % (i, j), [128, 2, 512], F32) for j in range(2)] for i in range(NSL)]
        At = [[sb("At%d_%d" % (i, j), [128, 2, 512], BF16) for j in range(2)] for i in range(NSL)]
        vr = VN.rearrange("(t p) f -> p t f", p=128)
        P.dma("sp", out=qsb[0][:, :], in_=QT[0:128, :])
        P.dma("sp", out=ksb[0][:, :], in_=KT[0:128, :])
        P.dma("pool", out=masks[:, :, :], in_=G.consts[:, C_MASKS:C_MASKS + 2048].rearrange("p (o f) -> p o f", o=4))
        for qq in range(7, -1, -1):
            P.dma("sp", out=V[:, qq * 4:(qq + 1) * 4, :], in_=vr[:, qq * 4:(qq + 1) * 4, :], key="Vo_%d" % (qq % 4))
        slot_qts = [[7, 4, 3, 0], [6, 5, 2, 1]]
        for hp in range(4):
            q = qsb[hp % 2]
            k = ksb[hp % 2]
            ys = ysb[hp % 2]
            if hp + 1 < 4:
                P.dma("sp", out=qsb[(hp + 1) % 2][:, :], in_=QT[(hp + 1) * 128:(hp + 2) * 128, :])
                P.dma("sp", out=ksb[(hp + 1) % 2][:, :], in_=KT[(hp + 1) * 128:(hp + 2) * 128, :])
            streams = []
            for s_ in range(NSL):
                st = []
                for qt in slot_qts[s_]:
                    n = 4 * qt + 4
                    for j in range(n):
                        st.append(dict(qt=qt, kb=4 * qt + 3 - j, first=(j == 0), last=(j == n - 1), idx=len(st)))
                streams.append(st)
            nit = max(len(st) for st in streams) + 2

            def stageA(s_, stp):
                qt, kb, idx = stp["qt"], stp["kb"], stp["idx"]
                o = kb - 4 * qt
                w0 = max(o, 0) * 128
                for e in range(2):
                    hr = slice(e * 64, (e + 1) * 64)
                    zc = e * 512
                    mm(P, G.PS[:, zc + w0:zc + 512], k[hr, kb * 128:(kb + 1) * 128], q[hr, qt * 512 + w0:(qt + 1) * 512], True, o < 0)
                    if o >= 0:
                        mm(P, G.PS[:, zc + w0:zc + 512], G.ident[:, :], masks[:, o, w0:512], False, True)
                Ec = E[s_][idx % 3]
                act(P, Ec[:, :, w0:512], G.PS[:, 0:1024].rearrange("p (e f) -> p e f", e=2)[:, :, w0:512], AF.Exp)
                spc = SP[s_][idx % 3]
                act(P, spc[:, :, w0:512], Ec[:, :, w0:512], AF.Ln, bias=1.0, scale=1.0)

            def stageB(s_, stp):
                qt, kb, idx = stp["qt"], stp["kb"], stp["idx"]
                o = kb - 4 * qt
                w0 = max(o, 0) * 128
                spc = SP[s_][idx % 3]
                spp = SP[s_][(idx - 1) % 3]
                for e in range(2):
                    pc = (2 + 2 * s_ + e) * 512
                    if o >= 0:
                        mm(P, G.PS[:, pc + w0:pc + w0 + 128], G.tri[:, :], spc[:, e, w0:w0 + 128], stp["first"], True)
                        if w0 + 128 < 512:
                            mm(P, G.PS[:, pc + w0 + 128:pc + 512], G.su[:, :], spp[:, e, w0 + 128:512], False, False)
                            mm(P, G.PS[:, pc + w0 + 128:pc + 512], G.tri[:, :], spc[:, e, w0 + 128:512], False, True)
                    else:
                        mm(P, G.PS[:, pc:pc + 512], G.su[:, :], spp[:, e, :], False, False)
                        mm(P, G.PS[:, pc:pc + 512], G.tri[:, :], spc[:, e, :], False, True)
                Gc = Gt[s_][idx % 2]
                c0 = (2 + 2 * s_) * 512
                act(P, Gc[:, :, w0:512], G.PS[:, c0:c0 + 1024].rearrange("p (e f) -> p e f", e=2)[:, :, w0:512], AF.Exp, scale=-1.0)
                P.I("dve", "tensor_tensor", out=At[s_][idx % 2][:, :, w0:512], in0=E[s_][idx % 3][:, :, w0:512], in1=Gc[:, :, w0:512], op=ALU.mult)

            def stageC(s_, stp):
                qt, kb, idx = stp["qt"], stp["kb"], stp["idx"]
                o = kb - 4 * qt
                w0 = max(o, 0) * 128
                for e in range(2):
                    hh = hp * 2 + e
                    hr = slice(e * 64, (e + 1) * 64)
                    oc0 = (6 + s_) * 512
                    vt = V[:, kb, hh * 64:(hh + 1) * 64]
                    at = At[s_][idx % 2]
                    if o >= 0:
                        mm(P, G.PS[hr, oc0 + w0:oc0 + w0 + 128], vt, at[:, e, w0:w0 + 128], stp["first"], False)
                        if w0 + 128 < 512:
                            mm(P, G.PS[hr, oc0 + w0 + 128:oc0 + 512], vt, at[:, e, w0 + 128:512], False, False)
                    else:
                        mm(P, G.PS[hr, oc0:oc0 + 512], vt, at[:, e, :], False, stp["last"])
                if stp["last"]:
                    P.I("dve", "tensor_copy", out=ys[:, qt * 512:(qt + 1) * 512], in_=bank(G, 6 + s_))

            for it in range(nit):
                for s_ in range(NSL):
                    if it < len(streams[s_]):
                        stageA(s_, streams[s_][it])
                for s_ in range(NSL):
                    if 0 <= it - 1 < len(streams[s_]):
                        stageB(s_, streams[s_][it - 1])
                for s_ in range(NSL):
                    if 0 <= it - 2 < len(streams[s_]):
                        stageC(s_, streams[s_][it - 2])
            P.dma("sp", out=MIX[512 + hp * 128:512 + (hp + 1) * 128, :], in_=ys[:, :])
        P.flush()


def post_norm_residual(G, l, j, M, xs, rstd, sqrot, lnv, ps_bank, tmprot):
    P, vec = G.P, G.vec
    rms_rstd(G, None, M, 8, rstd, sqrot, lnv, ps_bank, 1.0 / D)
    for c in range(8):
        tmp = tmprot.next()
        P.I("dve", "scalar_tensor_tensor", out=tmp[:, :], in0=M[:, c, :], scalar=_vcol(vec, GN0 + (l * 4 + j) * 8 + c),
            in1=rstd[:, :], op0=ALU.mult, op1=ALU.mult)
        P.I("pool", "tensor_tensor", out=xs[:, c, :], in0=xs[:, c, :], in1=tmp[:, :], op=ALU.add)


def interleave(main, side, offset=1, stride=1):
    k = 0
    for i, f in enumerate(main):
        f()
        while k < len(side) and i >= offset + k * stride:
            side[k]()
            k += 1
    while k < len(side):
        side[k]()
        k += 1


def phase3(G, l, src, XB, MIX, AT, w_out, w_up):
    nc, P, vec = G.nc, G.P, G.vec
    srcr = src.rearrange("(c p) t -> p c t", p=128)
    xbr = XB.rearrange("(c p) t -> p c t", p=128)
    mixr = MIX.rearrange("(c p) t -> p c t", p=128)
    with ExitStack() as es:
        sb = lambda n, s, d: es.enter_context(nc.sbuf_tensor("p3_%d_" % l + n, s, d))
        WO = sb("WO", [128, 8, 1024], BF16)
        WU = sb("WU", [128, 8, 2 * DFF], BF16)
        load_w(G, WO, w_out, 8, [(c, c + 256) for c in range(0, 1024, 256)])
        grp = []
        for c in range(0, DFF, 256):
            grp.append((c, c + 256))
            grp.append((DFF + c, DFF + c + 256))
        load_w(G, WU, w_up, 8, grp)
        xs = sb("x", [128, 8, TT], F32)
        mbuf = [sb("mix%d" % i, [128, 8, TT], BF16) for i in range(2)]
        M = sb("M", [128, 8, TT], F32)
        h2b = [sb("h2_%d" % i, [128, 8, TT], BF16) for i in range(2)]
        sqrot = Rot([sb("sq%d" % i, [128, TT], BF16) for i in range(6)])
        sq2rot = Rot([sb("sqb%d" % i, [128, TT], BF16) for i in range(6)])
        tmprot = Rot([sb("tmp%d" % i, [128, TT], F32) for i in range(2)])
        rstd = sb("rstd", [128, TT], F32)
        Gb = [sb("Gb%d" % i, [128, TT + 2], F32) for i in range(2)]
        H = sb("H", [128, NFC, 2], F32)
        trot = Rot([sb("t%d" % i, [128, TT], F32) for i in range(2)])
        srot = Rot([sb("s%d" % i, [128, TT], F32) for i in range(2)])
        arot = Rot([sb("a%d" % i, [128, TT], BF16) for i in range(3)])
        brot = Rot([1, 2, 3, 4, 5, 6, 7])
        P.I("pool", "memset", ap=H[:, :, :], constant=0.0)
        P.dma("sp", out=mbuf[0][:, :, :], in_=mixr[:, :, 0:TT])

        OWN = {}

        def A_events(tt):
            t0 = tt * TT
            mx = mbuf[tt % 2]
            h2 = h2b[tt % 2]
            ev = []

            def a_load():
                P.dma("sp", out=xs[:, :, :], in_=srcr[:, :, t0:t0 + TT])
                if tt + 1 < NT:
                    P.dma("sp", out=mbuf[(tt + 1) % 2][:, :, :], in_=mixr[:, :, t0 + TT:t0 + 2 * TT])
            ev.append((0.0, a_load))
            sqs = {}

            def mk_oc(oc):
                def f():
                    b = bank(G, brot.next())
                    for c in range(8):
                        mm(P, b, WO[:, c, oc * 128:(oc + 1) * 128], mx[:, c, :], c == 0, c == 7)
                    act(P, M[:, oc, :], b, AF.Copy)
                    sq = sqrot.next()
                    act(P, sq[:, :], M[:, oc, :], AF.Square)
                    sqs[oc] = sq
                    OWN[sq.name] = ("a", tt, oc)
                return f

            def mk_st1(oc):
                def f():
                    assert OWN[sqs[oc].name] == ("a", tt, oc)
                    mm(P, bank(G, 0), G.ones_bb[:, :], sqs[oc][:, :], oc == 0, oc == 7)
                return f
            for oc in range(8):
                ev.append((0.1 + 0.6 * oc, mk_oc(oc)))
                ev.append((0.1 + 0.6 * oc + 3.0, mk_st1(oc)))

            def a_rstd1():
                act(P, rstd[:, :], bank(G, 0), AF.Ln, bias=EPS, scale=1.0 / D)
                act(P, rstd[:, :], rstd[:, :], AF.Exp, scale=-0.5)
            ev.append((7.5, a_rstd1))
            sq2 = {}

            def mk_res(c):
                def f():
                    tmp = tmprot.next()
                    P.I("dve", "scalar_tensor_tensor", out=tmp[:, :], in0=M[:, c, :], scalar=_vcol(vec, GN0 + (l * 4 + 1) * 8 + c),
                        in1=rstd[:, :], op0=ALU.mult, op1=ALU.mult)
                    P.I("dve", "tensor_tensor", out=xs[:, c, :], in0=xs[:, c, :], in1=tmp[:, :], op=ALU.add)
                    sq = sq2rot.next()
                    act(P, sq[:, :], xs[:, c, :], AF.Square)
                    sq2[c] = sq
                    OWN[sq.name] = ("b", tt, c)
                return f

            def mk_st2(c):
                def f():
                    assert OWN[sq2[c].name] == ("b", tt, c)
                    mm(P, bank(G, 0), G.ones_bb[:, :], sq2[c][:, :], c == 0, c == 7)
                return f
            for c in range(8):
                ev.append((8.3 + 0.7 * c, mk_res(c)))
                ev.append((8.3 + 0.7 * c + 3.2, mk_st2(c)))

            def a_rstd2():
                P.dma("sp", out=xbr[:, :, t0:t0 + TT], in_=xs[:, :, :])
                act(P, rstd[:, :], bank(G, 0), AF.Ln, bias=EPS, scale=1.0 / D)
                act(P, rstd[:, :], rstd[:, :], AF.Exp, scale=-0.5)
            ev.append((16.6, a_rstd2))

            def mk_h(c0):
                def f():
                    for c in (c0, c0 + 1):
                        P.I("dve", "scalar_tensor_tensor", out=h2[:, c, :], in0=xs[:, c, :],
                            scalar=_vcol(vec, GN0 + (l * 4 + 2) * 8 + c), in1=rstd[:, :], op0=ALU.mult, op1=ALU.mult)
                return f
            for j, c0 in enumerate(range(0, 8, 2)):
                ev.append((17.4 + 1.0 * j, mk_h(c0)))
            return ev

        def F_closures(tt):
            t0 = tt * TT
            h2 = h2b[tt % 2]
            cl = []

            svs = {}

            def g_part(fc):
                pg = bank(G, brot.next())
                for c in range(8):
                    mm(P, pg, WU[:, c, fc * 128:(fc + 1) * 128], h2[:, c, :], c == 0, c == 7)
                gb = Gb[fc % 2]
                P.I("pool", "tensor_copy", out=gb[:, 0:2], in_=H[:, fc, :])
                act(P, gb[:, 2:TT + 2], pg, AF.Copy)
                P.I("pool", "tensor_copy", out=H[:, fc, :], in_=gb[:, TT:TT + 2])
                t = trot.next()
                wc = lambda j: _vcol(vec, WFC0 + (l * 3 + j) * NFC + fc)
                act(P, t[:, :], pg, AF.Identity, bias=_vcol(vec, BFC0 + l * NFC + fc), scale=wc(2))
                P.I("dve", "scalar_tensor_tensor", out=t[:, :], in0=gb[:, 0:TT], scalar=wc(0), in1=t[:, :], op0=ALU.mult, op1=ALU.add)
                P.I("dve", "scalar_tensor_tensor", out=t[:, :], in0=gb[:, 1:TT + 1], scalar=wc(1), in1=t[:, :], op0=ALU.mult, op1=ALU.add)
                sv = srot.next()
                act(P, sv[:, :], t[:, :], AF.Silu)
                svs[fc] = sv

            def u_part(fc):
                pu = bank(G, brot.next())
                for c in range(8):
                    mm(P, pu, WU[:, c, DFF + fc * 128:DFF + (fc + 1) * 128], h2[:, c, :], c == 0, c == 7)
                a = arot.next()
                P.I("dve", "tensor_tensor", out=a[:, :], in0=pu, in1=svs[fc][:, :], op=ALU.mult)
                P.dma("sp", out=AT[fc * 128:(fc + 1) * 128, t0:t0 + TT], in_=a[:, :])

            def mk_fc(fc):
                def f():
                    g_part(fc)
                    if fc > 0:
                        u_part(fc - 1)
                    if fc == NFC - 1:
                        u_part(fc)
                return f
            for fc in range(NFC):
                cl.append(mk_fc(fc))
            return cl

        interleave_events([], A_events(0))
        for tt in range(NT):
            side = A_events(tt + 1) if tt + 1 < NT else []
            interleave_events(F_closures(tt), side)
        P.flush()


def phase4(G, l, XB, dst, AT, w_down):
    nc, P, vec = G.nc, G.P, G.vec
    xbr = XB.rearrange("(c p) t -> p c t", p=128)
    dstr = dst.rearrange("(c p) t -> p c t", p=128)
    atr = AT.rearrange("(c p) t -> p c t", p=128)
    with ExitStack() as es:
        sb = lambda n, s, d: es.enter_context(nc.sbuf_tensor("p4_%d_" % l + n, s, d))
        WD = sb("WD", [128, NFC, 1024], BF16)
        load_w(G, WD, w_down, NFC, [(c, c + 256) for c in range(0, 1024, 256)])
        xbuf = [sb("x%d" % i, [128, 8, TT], F32) for i in range(2)]
        abuf = [sb("a%d" % i, [128, NFC, TT], BF16) for i in range(2)]
        Mb = [sb("M%d" % i, [128, 8, TT], F32) for i in range(2)]
        sqrot = Rot([sb("sq%d" % i, [128, TT], BF16) for i in range(8)])
        tmprot = Rot([sb("tmp%d" % i, [128, TT], F32) for i in range(2)])
        lnv = sb("lnv", [128, TT], F32)
        rstd = sb("rstd", [128, TT], F32)
        brot = Rot([1, 2, 3, 4, 5, 6, 7])
        P.dma("sp", out=abuf[0][:, :, :], in_=atr[:, :, 0:TT])
        P.dma("sp", out=xbuf[0][:, :, :], in_=xbr[:, :, 0:TT])
        P.dma("sp", out=xbuf[1][:, :, :], in_=xbr[:, :, TT:2 * TT])

        def D_closures(tt):
            t0 = tt * TT
            a = abuf[tt % 2]
            M = Mb[tt % 2]
            cl = []

            def d_load():
                if tt + 1 < NT:
                    P.dma("sp", out=abuf[(tt + 1) % 2][:, :, :], in_=atr[:, :, t0 + TT:t0 + 2 * TT])
            cl.append(d_load)

            def mk_oc(oc):
                def f():
                    b = bank(G, brot.next())
                    for c in range(NFC):
                        mm(P, b, WD[:, c, oc * 128:(oc + 1) * 128], a[:, c, :], c == 0, c == NFC - 1)
                    act(P, M[:, oc, :], b, AF.Copy)
                return f
            for oc in range(8):
                cl.append(mk_oc(oc))
            return cl

        OWN = {}

        def N_events(tt):
            t0 = tt * TT
            xs = xbuf[tt % 2]
            M = Mb[tt % 2]
            ev = []
            sqs = {}

            def n_sq(c0):
                def f():
                    for c in range(c0, c0 + 4):
                        sq = sqrot.next()
                        act(P, sq[:, :], M[:, c, :], AF.Square)
                        sqs[c] = sq
                        OWN[sq.name] = (tt, c)
                return f

            def n_mm(c0):
                def f():
                    for c in range(c0, c0 + 4):
                        assert OWN[sqs[c].name] == (tt, c)
                        mm(P, bank(G, 0), G.ones_bb[:, :], sqs[c][:, :], c == 0, c == 7)
                return f

            def n_rstd():
                act(P, lnv[:, :], bank(G, 0), AF.Ln, bias=EPS, scale=1.0 / D)
                act(P, rstd[:, :], lnv[:, :], AF.Exp, scale=-0.5)
            ev.append((0.0, n_sq(0)))
            ev.append((0.5, n_sq(4)))
            ev.append((2.2, n_mm(0)))
            ev.append((2.6, n_mm(4)))
            ev.append((3.0, n_rstd))

            def n_res(c0):
                def f():
                    for c in range(c0, c0 + 4):
                        tmp = tmprot.next()
                        P.I("dve", "scalar_tensor_tensor", out=tmp[:, :], in0=M[:, c, :], scalar=_vcol(vec, GN0 + (l * 4 + 3) * 8 + c),
                            in1=rstd[:, :], op0=ALU.mult, op1=ALU.mult)
                        P.I("pool", "tensor_tensor", out=xs[:, c, :], in0=xs[:, c, :], in1=tmp[:, :], op=ALU.add)
                    if c0 == 4:
                        P.dma("sp", out=dstr[:, :, t0:t0 + TT], in_=xs[:, :, :])
                return f
            ev.append((4.0, n_res(0)))
            ev.append((5.0, n_res(4)))
            return ev

        for f in D_closures(0):
            f()
        for tt in range(NT):
            main = D_closures(tt + 1) if tt + 1 < NT else []
            interleave_events(main, N_events(tt))
            if tt + 2 < NT:
                P.dma("sp", out=xbuf[tt % 2][:, :, :], in_=xbr[:, :, (tt + 2) * TT:(tt + 3) * TT])
        P.flush()


def _t5_bucket(dist):
    max_exact = 16
    d = np.maximum(dist, 1).astype(np.float32)
    large = max_exact + (np.log(d / np.float32(max_exact)) / np.float32(math.log(2048 / 16)) * np.float32(16)).astype(np.int32)
    large = np.minimum(large, 31)
    return np.where(dist < max_exact, dist, large)


def _host_tables(norm_g, rel_bias, w_sc, w_cc, b_cc, ln_cc_g, ln_cc_b, w_ffn_conv, b_ffn_conv):
    vecs = np.zeros((128, NV), np.float32)

    def put(col0, arr):
        a = np.asarray(arr, np.float32)
        lead = a.shape[:-1]
        C = a.shape[-1] // 128
        a = a.reshape(lead + (C, 128))
        a = np.moveaxis(a, -1, 0).reshape(128, -1)
        vecs[:, col0:col0 + a.shape[1]] = a

    put(GN0, norm_g)
    put(WSC0, w_sc)
    put(WCC0, w_cc)
    put(BCC0, b_cc)
    put(LNG0, ln_cc_g)
    put(LNB0, ln_cc_b)
    put(WFC0, w_ffn_conv)
    put(BFC0, b_ffn_conv)

    consts = np.zeros((128, NC_), np.float32)
    p = np.arange(128)[:, None]
    i = np.arange(128)[None, :]
    consts[:, C_MASKD:C_MASKD + 128] = np.where(i <= p, 0.0, NEG)
    consts[:, C_MASKD + 128:C_MASKD + 256] = np.where(i >= p, 0.0, NEG)
    iq = np.arange(512)[None, :]
    for o in range(4):
        consts[:, C_MASKS + o * 512:C_MASKS + (o + 1) * 512] = np.where(o * 128 + p < iq, 0.0, NEG)
    consts[:, C_TRI:C_TRI + 128] = (p >= i).astype(np.float32)
    consts[:, C_SU:C_SU + 128] = (p < i).astype(np.float32)
    consts[:, C_ID:C_ID + 128] = (p == i).astype(np.float32)

    rb = np.asarray(rel_bias, np.float32)
    biasT = np.zeros((128, 3, 8, 2, 128), np.float32)
    for g, (win, d) in enumerate(DA_PAIRS):
        for jt in range(2):
            rel = 128 + i - (jt * 128 + p)
            relc = np.clip(rel, 0, 128)
            bk = _t5_bucket(relc * d)
            biasT[:, g, :, jt, :] = np.transpose(rb[bk], (0, 2, 1))
    return vecs, consts, biasT.reshape(128, 6144)


_NC_CACHE = {}


def kernel(x, norm_g, rel_bias, w_in_even, w_out_even, w_sc, w_in_odd, w_out_odd,
           w_cc, b_cc, ln_cc_g, ln_cc_b, w_up, w_ffn_conv, b_ffn_conv, w_down):
    x = np.asarray(x, np.float32)
    B = x.shape[0]
    vecs, consts, biasT = _host_tables(norm_g, rel_bias, w_sc, w_cc, b_cc, ln_cc_g, ln_cc_b, w_ffn_conv, b_ffn_conv)
    if "nc" not in _NC_CACHE:
        _NC_CACHE["nc"] = build_program()
    nc = _NC_CACHE["nc"]
    shared = dict(
        vecs=vecs, consts=consts, biasT=biasT,
        w_in_even=np.ascontiguousarray(w_in_even, np.float32), w_out_even=np.ascontiguousarray(w_out_even, np.float32),
        w_in_odd=np.ascontiguousarray(w_in_odd, np.float32), w_out_odd=np.ascontiguousarray(w_out_odd, np.float32),
        w_up=np.ascontiguousarray(w_up, np.float32), w_down=np.ascontiguousarray(w_down, np.float32),
    )
    in_maps = []
    for b in range(B):
        m = dict(shared)
        m["xT"] = np.ascontiguousarray(x[b].T)
        in_maps.append(m)
    res = run_bass_kernel_spmd(nc, in_maps, core_ids=list(range(B)))
    out = np.stack([np.ascontiguousarray(r["yT"].T) for r in res.results], axis=0)
    return out.astype(np.float32)
```
